# Optimizing a Trainium2 kernel written in Bass

```python
import math
import jax
import jax.numpy as jnp
from jax import lax
import numpy as np

D_MODEL = 1024
BATCH = 4
SEQ = 4096
DEPTH = 2
DEC_BATCH = 128
DEC_SEQ = 4
PAST_LEN = 16384
PAGE_SIZE = 128

BRANCH_W = D_MODEL // 2
N_BRANCH = 3
HEAD_DIM = 64
N_HEADS = BRANCH_W // HEAD_DIM
N_KV_HEADS = 2
KV_GROUP = N_HEADS // N_KV_HEADS
WINDOW = 128
ROPE_THETA = 10000.0
SSM_W = BRANCH_W
SSM_GROUP_CH = 16
SSM_GROUPS = SSM_W // SSM_GROUP_CH
SSM_STATE = 64
N_MEM = 256
MEM_HEADS = 4
MEM_HEAD_DIM = BRANCH_W // MEM_HEADS
MEM_W = MEM_HEADS * MEM_HEAD_DIM
D_FF = -(-8 * D_MODEL // (3 * 256)) * 256
RMS_EPS = 1e-6

Q_OFF = 0
K_OFF = Q_OFF + N_HEADS * HEAD_DIM
V_OFF = K_OFF + N_KV_HEADS * HEAD_DIM
U_OFF = V_OFF + N_KV_HEADS * HEAD_DIM
MQ_OFF = U_OFF + SSM_W
G_OFF = MQ_OFF + MEM_W
IN_W = G_OFF + N_BRANCH * D_MODEL

kernel_name = 'hybrid_swa_s5_memxattn_decode_step'


def rms_norm(x, g):
    xf = x.astype(jnp.float32)
    xf = xf * lax.rsqrt(jnp.mean(xf * xf, axis=-1, keepdims=True) + RMS_EPS)
    return xf.astype(x.dtype) * g


def rotary(x, pos):
    half = x.shape[-1] // 2
    inv = ROPE_THETA ** (-jnp.arange(half, dtype=jnp.float32) / half)
    ang = pos[:, None] * inv[None, :]
    cos = jnp.cos(ang)[:, None, :]
    sin = jnp.sin(ang)[:, None, :]
    xf = x.astype(jnp.float32)
    x1, x2 = xf[..., :half], xf[..., half:]
    return jnp.concatenate([x1 * cos - x2 * sin, x2 * cos + x1 * sin], axis=-1).astype(x.dtype)


def sink_attention(q, k, v, mask, sinks):
    scale = 1.0 / math.sqrt(q.shape[-1])
    s = jnp.einsum('...qhgd,...khd->...hgqk', q.astype(jnp.float32), k.astype(jnp.float32)) * scale
    s = jnp.where(mask[..., None, None, :, :], s, -jnp.inf)
    sink = sinks.astype(jnp.float32)[:, :, None, None]
    m = jnp.maximum(jnp.max(s, axis=-1, keepdims=True), sink)
    p = jnp.exp(s - m)
    denom = jnp.sum(p, axis=-1, keepdims=True) + jnp.exp(sink - m)
    o = jnp.einsum('...hgqk,...khd->...qhgd', p / denom, v.astype(jnp.float32))
    return o.astype(q.dtype)


def swa_prompt(q, k, v, sinks):
    b, s = q.shape[0], q.shape[1]
    nb = s // WINDOW
    qb = q.reshape(b, nb, WINDOW, N_KV_HEADS, KV_GROUP, HEAD_DIM)

    def band(t):
        tb = t.reshape(b, nb, WINDOW, N_KV_HEADS, HEAD_DIM)
        prev = jnp.pad(tb, ((0, 0), (1, 0), (0, 0), (0, 0), (0, 0)))[:, :-1]
        return jnp.concatenate([prev, tb], axis=2)

    blk = jnp.arange(nb)[:, None] * WINDOW
    qpos = blk + jnp.arange(WINDOW)[None, :]
    kpos = blk - WINDOW + jnp.arange(2 * WINDOW)[None, :]
    diff = qpos[:, :, None] - kpos[:, None, :]
    mask = (diff >= 0) & (diff < WINDOW) & (kpos[:, None, :] >= 0)
    o = sink_attention(qb, band(k), band(v), mask, sinks.reshape(N_KV_HEADS, KV_GROUP))
    return o.reshape(b, s, BRANCH_W)


def swa_sample(q, k, v, past_k, past_v, sinks):
    b, t = q.shape[0], q.shape[1]
    kcat = jnp.concatenate([past_k.astype(k.dtype), k], axis=1)
    vcat = jnp.concatenate([past_v.astype(v.dtype), v], axis=1)
    kpos = jnp.concatenate([PAST_LEN - WINDOW + jnp.arange(WINDOW), PAST_LEN + jnp.arange(t)])
    qpos = PAST_LEN + jnp.arange(t)
    diff = qpos[:, None] - kpos[None, :]
    mask = ((diff >= 0) & (diff < WINDOW) & (kpos[None, :] >= 0))[None]
    qg = q.reshape(b, t, N_KV_HEADS, KV_GROUP, HEAD_DIM)
    o = sink_attention(qg, kcat, vcat, mask, sinks.reshape(N_KV_HEADS, KV_GROUP))
    return o.reshape(b, t, BRANCH_W), kcat[:, -WINDOW:], vcat[:, -WINDOW:]


def ssm_branch(u, h0_re, h0_im, p):
    b, t = u.shape[0], u.shape[1]
    f32 = jnp.float32
    ug = u.astype(f32).reshape(b, t, SSM_GROUPS, SSM_GROUP_CH)
    a_re = p['ssm_a_re'].astype(f32)
    a_im = p['ssm_a_im'].astype(f32)
    dt = jnp.exp(p['ssm_log_dt'].astype(f32))[:, None]
    mag = jnp.exp(a_re * dt)
    lam_re = mag * jnp.cos(a_im * dt)
    lam_im = mag * jnp.sin(a_im * dt)
    den = a_re * a_re + a_im * a_im
    nr, ni = lam_re - 1.0, lam_im
    g_re = (nr * a_re + ni * a_im) / den
    g_im = (ni * a_re - nr * a_im) / den
    bu_re = jnp.einsum('btgc,gpc->btgp', ug, p['ssm_b_re'].astype(f32))
    bu_im = jnp.einsum('btgc,gpc->btgp', ug, p['ssm_b_im'].astype(f32))
    bb_re = g_re * bu_re - g_im * bu_im
    bb_im = g_re * bu_im + g_im * bu_re
    h0r, h0i = h0_re.astype(f32), h0_im.astype(f32)
    bb_re = bb_re.at[:, 0].add(lam_re * h0r - lam_im * h0i)
    bb_im = bb_im.at[:, 0].add(lam_re * h0i + lam_im * h0r)
    ar = jnp.broadcast_to(lam_re, bb_re.shape)
    ai = jnp.broadcast_to(lam_im, bb_im.shape)

    def combine(e1, e2):
        a1r, a1i, b1r, b1i = e1
        a2r, a2i, b2r, b2i = e2
        return (a2r * a1r - a2i * a1i,
                a2r * a1i + a2i * a1r,
                a2r * b1r - a2i * b1i + b2r,
                a2r * b1i + a2i * b1r + b2i)

    _, _, xr, xi = lax.associative_scan(combine, (ar, ai, bb_re, bb_im), axis=1)
    y = (jnp.einsum('btgp,gcp->btgc', xr, p['ssm_c_re'].astype(f32))
         - jnp.einsum('btgp,gcp->btgc', xi, p['ssm_c_im'].astype(f32))
         + p['ssm_d'].astype(f32).reshape(SSM_GROUPS, SSM_GROUP_CH) * ug)
    z = jax.nn.gelu(y.reshape(b, t, SSM_W))
    out = z * jax.nn.sigmoid(z @ p['ssm_w_glu'].astype(f32))
    return out.astype(u.dtype), xr[:, -1], xi[:, -1]


def mem_kv(mem, p):
    b = mem.shape[0]
    kv = rms_norm(mem, p['mem_norm']) @ p['w_mem_kv']
    k = kv[..., :MEM_W].reshape(b, N_MEM, MEM_HEADS, MEM_HEAD_DIM)
    v = kv[..., MEM_W:].reshape(b, N_MEM, MEM_HEADS, MEM_HEAD_DIM)
    return rms_norm(k, p['mem_k_norm']), v


def mem_attend(qm, mk, mv):
    scale = 1.0 / math.sqrt(MEM_HEAD_DIM)
    s = jnp.einsum('bqhd,bkhd->bhqk', qm.astype(jnp.float32), mk.astype(jnp.float32)) * scale
    pr = jax.nn.softmax(s, axis=-1)
    o = jnp.einsum('bhqk,bkhd->bqhd', pr, mv.astype(jnp.float32))
    return o.reshape(qm.shape[0], qm.shape[1], MEM_W).astype(qm.dtype)


def trunk_layer(x, pos, past_k, past_v, h0_re, h0_im, mk, mv, p):
    b, t = x.shape[0], x.shape[1]
    h = rms_norm(x, p['attn_norm'])
    z = h @ p['w_in']
    q = z[..., Q_OFF:K_OFF].reshape(b, t, N_HEADS, HEAD_DIM)
    k = z[..., K_OFF:V_OFF].reshape(b, t, N_KV_HEADS, HEAD_DIM)
    v = z[..., V_OFF:U_OFF].reshape(b, t, N_KV_HEADS, HEAD_DIM)
    u = z[..., U_OFF:MQ_OFF]
    qm = z[..., MQ_OFF:G_OFF].reshape(b, t, MEM_HEADS, MEM_HEAD_DIM)
    gates = jax.nn.sigmoid(z[..., G_OFF:].reshape(b, t, N_BRANCH, D_MODEL))
    q = rotary(rms_norm(q, p['q_norm']), pos)
    k = rotary(rms_norm(k, p['k_norm']), pos)
    if past_k is None:
        o_a = swa_prompt(q, k, v, p['attn_sinks'])
        new_k, new_v = k[:, -WINDOW:], v[:, -WINDOW:]
    else:
        o_a, new_k, new_v = swa_sample(q, k, v, past_k, past_v, p['attn_sinks'])
    o_b, hr, hi = ssm_branch(u, h0_re, h0_im, p)
    o_c = mem_attend(rms_norm(qm, p['mem_q_norm']), mk, mv)
    branches = jnp.stack([o_a, o_b, o_c], axis=2)
    proj = jnp.einsum('btnc,ncd->btnd', branches, p['w_branch'])
    merged = jnp.sum(gates * proj, axis=2)
    x = x + merged @ p['w_out']
    gu = rms_norm(x, p['ffn_norm']) @ p['w_ffn_up']
    x = x + (jax.nn.silu(gu[..., :D_FF]) * gu[..., D_FF:]) @ p['w_ffn_down']
    return x, new_k, new_v, hr, hi


def setup_inputs(seed: int = 0) -> dict:
    key = jax.random.key(seed)
    keys = jax.random.split(key, 40)
    counter = [0]
    f32 = jnp.float32

    def nxt():
        kk = keys[counter[0]]
        counter[0] += 1
        return kk

    def nrm(shape, scale=1.0):
        return jax.random.normal(nxt(), shape, f32) * scale

    def gain(shape):
        return 1.0 + 0.1 * nrm(shape)

    L = DEPTH
    return {
        'x_prompt': nrm((BATCH, SEQ, D_MODEL)),
        'x_sample': nrm((DEC_BATCH, DEC_SEQ, D_MODEL)),
        'cache_swa_k': nrm((L, DEC_BATCH, WINDOW, N_KV_HEADS, HEAD_DIM)),
        'cache_swa_v': nrm((L, DEC_BATCH, WINDOW, N_KV_HEADS, HEAD_DIM)),
        'state_ssm_re': nrm((L, DEC_BATCH, SSM_GROUPS, SSM_STATE), 0.5),
        'state_ssm_im': nrm((L, DEC_BATCH, SSM_GROUPS, SSM_STATE), 0.5),
        'cache_mem_k': nrm((L, DEC_BATCH, N_MEM, MEM_HEADS, MEM_HEAD_DIM)),
        'cache_mem_v': nrm((L, DEC_BATCH, N_MEM, MEM_HEADS, MEM_HEAD_DIM)),
        'mem_prompt': nrm((BATCH, N_MEM, D_MODEL)),
        'attn_norm': gain((L, D_MODEL)),
        'w_in': nrm((L, D_MODEL, IN_W), D_MODEL ** -0.5),
        'q_norm': gain((L, HEAD_DIM)),
        'k_norm': gain((L, HEAD_DIM)),
        'attn_sinks': nrm((L, N_HEADS), 0.5),
        'ssm_a_re': -0.5 + 0.01 * nrm((L, SSM_GROUPS, SSM_STATE)),
        'ssm_a_im': math.pi * jnp.arange(SSM_STATE, dtype=f32) + 0.01 * nrm((L, SSM_GROUPS, SSM_STATE)),
        'ssm_log_dt': jax.random.uniform(nxt(), (L, SSM_GROUPS), f32, math.log(1e-3), math.log(1e-1)),
        'ssm_b_re': nrm((L, SSM_GROUPS, SSM_STATE, SSM_GROUP_CH), (2 * SSM_GROUP_CH) ** -0.5),
        'ssm_b_im': nrm((L, SSM_GROUPS, SSM_STATE, SSM_GROUP_CH), (2 * SSM_GROUP_CH) ** -0.5),
        'ssm_c_re': nrm((L, SSM_GROUPS, SSM_GROUP_CH, SSM_STATE), SSM_STATE ** -0.5),
        'ssm_c_im': nrm((L, SSM_GROUPS, SSM_GROUP_CH, SSM_STATE), SSM_STATE ** -0.5),
        'ssm_d': nrm((L, SSM_W)),
        'ssm_w_glu': nrm((L, SSM_W, SSM_W), SSM_W ** -0.5),
        'mem_norm': gain((L, D_MODEL)),
        'w_mem_kv': nrm((L, D_MODEL, 2 * MEM_W), D_MODEL ** -0.5),
        'mem_q_norm': gain((L, MEM_HEAD_DIM)),
        'mem_k_norm': gain((L, MEM_HEAD_DIM)),
        'w_branch': nrm((L, N_BRANCH, BRANCH_W, D_MODEL), BRANCH_W ** -0.5),
        'w_out': nrm((L, D_MODEL, D_MODEL), D_MODEL ** -0.5),
        'ffn_norm': gain((L, D_MODEL)),
        'w_ffn_up': nrm((L, D_MODEL, 2 * D_FF), D_MODEL ** -0.5),
        'w_ffn_down': nrm((L, D_FF, D_MODEL), D_FF ** -0.5),
    }


def reference(x_prompt, x_sample, cache_swa_k, cache_swa_v, state_ssm_re, state_ssm_im,
              cache_mem_k, cache_mem_v, mem_prompt, attn_norm, w_in, q_norm, k_norm, attn_sinks,
              ssm_a_re, ssm_a_im, ssm_log_dt, ssm_b_re, ssm_b_im, ssm_c_re, ssm_c_im, ssm_d,
              ssm_w_glu, mem_norm, w_mem_kv, mem_q_norm, mem_k_norm, w_branch, w_out, ffn_norm,
              w_ffn_up, w_ffn_down):
    pos_p = jnp.arange(x_prompt.shape[1], dtype=jnp.float32)
    pos_s = PAST_LEN + jnp.arange(x_sample.shape[1], dtype=jnp.float32)
    h0 = jnp.zeros((x_prompt.shape[0], SSM_GROUPS, SSM_STATE), jnp.float32)
    y_p, y_s = x_prompt, x_sample
    kp_l, vp_l, hrp_l, hip_l, mkp_l, mvp_l = [], [], [], [], [], []
    ks_l, vs_l, hrs_l, his_l = [], [], [], []
    for l in range(DEPTH):
        p = {
            'attn_norm': attn_norm[l], 'w_in': w_in[l], 'q_norm': q_norm[l], 'k_norm': k_norm[l],
            'attn_sinks': attn_sinks[l], 'ssm_a_re': ssm_a_re[l], 'ssm_a_im': ssm_a_im[l],
            'ssm_log_dt': ssm_log_dt[l], 'ssm_b_re': ssm_b_re[l], 'ssm_b_im': ssm_b_im[l],
            'ssm_c_re': ssm_c_re[l], 'ssm_c_im': ssm_c_im[l], 'ssm_d': ssm_d[l],
            'ssm_w_glu': ssm_w_glu[l], 'mem_norm': mem_norm[l], 'w_mem_kv': w_mem_kv[l],
            'mem_q_norm': mem_q_norm[l], 'mem_k_norm': mem_k_norm[l], 'w_branch': w_branch[l],
            'w_out': w_out[l], 'ffn_norm': ffn_norm[l], 'w_ffn_up': w_ffn_up[l],
            'w_ffn_down': w_ffn_down[l],
        }
        mk_p, mv_p = mem_kv(mem_prompt, p)
        y_p, kp, vp, hrp, hip = trunk_layer(y_p, pos_p, None, None, h0, h0, mk_p, mv_p, p)
        y_s, ks, vs, hrs, his = trunk_layer(y_s, pos_s, cache_swa_k[l], cache_swa_v[l],
                                            state_ssm_re[l], state_ssm_im[l],
                                            cache_mem_k[l], cache_mem_v[l], p)
        kp_l.append(kp); vp_l.append(vp); hrp_l.append(hrp); hip_l.append(hip)
        mkp_l.append(mk_p); mvp_l.append(mv_p)
        ks_l.append(ks); vs_l.append(vs); hrs_l.append(hrs); his_l.append(his)
    swa_k_prompt = jnp.stack(kp_l)
    swa_v_prompt = jnp.stack(vp_l)
    ssm_re_prompt = jnp.stack(hrp_l)
    ssm_im_prompt = jnp.stack(hip_l)
    mem_k_prompt = jnp.stack(mkp_l)
    mem_v_prompt = jnp.stack(mvp_l)
    swa_k_sample = jnp.stack(ks_l)
    swa_v_sample = jnp.stack(vs_l)
    ssm_re_sample = jnp.stack(hrs_l)
    ssm_im_sample = jnp.stack(his_l)
    return (y_p, y_s, swa_k_prompt, swa_v_prompt, ssm_re_prompt, ssm_im_prompt,
            mem_k_prompt, mem_v_prompt, swa_k_sample, swa_v_sample, ssm_re_sample, ssm_im_sample)
```

```python
import contextlib
import math
import types
import numpy as np
import concourse.bass as bass
import concourse.mybir as mybir
from concourse.bass_utils import run_bass_kernel_spmd

F32 = mybir.dt.float32
BF16 = mybir.dt.bfloat16
I32 = mybir.dt.int32
AF = mybir.ActivationFunctionType
ALU = mybir.AluOpType

D = 1024
SEQ = 4096
NL = 2
INW = 4864
DFF = 2816
K_OFF, V_OFF, U_OFF, MQ_OFF, G_OFF = 512, 640, 768, 1280, 1792
PAST = 16384
TB = 256
NBLK = SEQ // TB
NSB = 16
NS = NSB * 4
NLEV = 9
EPS = 1e-6
NWS = 4
ENGS = ("pe", "act", "dve", "pool", "sp")


def _freeze(fn):
    if fn is None or fn.__closure__ is None:
        return fn
    cells = []
    for c in fn.__closure__:
        try:
            cells.append(types.CellType(c.cell_contents))
        except ValueError:
            cells.append(c)
    return types.FunctionType(fn.__code__, fn.__globals__, fn.__name__, fn.__defaults__, tuple(cells))


class Sched:
    def __init__(self, nc, stack):
        self.nc = nc
        self.stack = stack
        self.q = {e: [] for e in ENGS}
        self.cnt = {e: 0 for e in ENGS}
        self.esem = {e: stack.enter_context(nc.semaphore("sem_" + e)) for e in ENGS}
        self.dsem = {}
        self.dcnt = {}
        self.last_w = {}
        self.readers = {}
        self.seen = {e: {} for e in ENGS}
        self.alias = {}
        self.cur_ns = 0
        self.glob = set()

    def _ns(self, k):
        if isinstance(k, tuple):
            return (self.cur_ns,) + k if k[0] == "RA" else k
        return k if k in self.glob else (self.cur_ns, k)

    def _exp(self, keys):
        out = []
        for k in keys:
            out.append(k)
            out.extend(self.alias.get(k, ()))
        return out

    def op(self, eng, fn, reads=(), writes=(), dma=None):
        reads = [self._ns(k) for k in self._exp(reads)]
        writes = [self._ns(k) for k in self._exp(writes)]
        if dma is not None:
            dma = self._ns(dma)
        fn = _freeze(fn)
        deps = []
        for r in reads:
            t = self.last_w.get(r)
            if t is not None:
                deps.append((t, True))
        for w in writes:
            t = self.last_w.get(w)
            if t is not None:
                deps.append((t, False))
            for t in self.readers.get(w, ()):
                deps.append((t, False))
        waits = {}
        for (kind, key, val), raw in deps:
            if kind == "eng" and key == eng and (eng == "pe" or not raw):
                continue
            sk = (kind, key)
            if val > self.seen[eng].get(sk, 0):
                waits[sk] = max(waits.get(sk, 0), val)
        for sk, val in waits.items():
            self.seen[eng][sk] = val
        if dma is not None:
            if dma not in self.dsem:
                self.dsem[dma] = self.stack.enter_context(self.nc.semaphore("dq%d" % len(self.dsem)))
                self.dcnt[dma] = 0
            self.dcnt[dma] += 16
            tok = ("dma", dma, self.dcnt[dma])
        else:
            self.cnt[eng] += 1
            tok = ("eng", eng, self.cnt[eng])
        self.q[eng].append((fn, list(waits.items()), tok))
        for w in writes:
            self.last_w[w] = tok
            self.readers[w] = []
        for r in reads:
            self.readers.setdefault(r, []).append(tok)
        return tok

    def wait_all(self, eng, toks):
        waits = {}
        for kind, key, val in toks:
            sk = (kind, key)
            if val > self.seen[eng].get(sk, 0):
                waits[sk] = max(waits.get(sk, 0), val)
        for sk, val in waits.items():
            self.seen[eng][sk] = val
        self.q[eng].append((None, list(waits.items()), None))

    def emit(self):
        nc = self.nc
        sem = lambda sk: self.esem[sk[1]] if sk[0] == "eng" else self.dsem[sk[1]]
        with nc.Block() as block:
            def run(e, engobj):
                for fn, waits, tok in self.q[e]:
                    for sk, val in waits:
                        engobj.wait_ge(sem(sk), val)
                    if fn is None:
                        continue
                    ins = fn(engobj)
                    if tok[0] == "dma":
                        ins.then_inc(self.dsem[tok[1]], 16)
                    else:
                        ins.then_inc(self.esem[e], 1)

            @block.tensor
            def _(t):
                run("pe", t)

            @block.scalar
            def _(t):
                run("act", t)

            @block.vector
            def _(t):
                run("dve", t)

            @block.gpsimd
            def _(t):
                run("pool", t)

            @block.sync
            def _(t):
                run("sp", t)


def build_program():
    nc = bass.Bass("TRN2", target_bir_lowering=False)

    def din(name, shape):
        return nc.dram_tensor(name, list(shape), F32, kind="ExternalInput").ap()

    def dout(name, shape):
        return nc.dram_tensor(name, list(shape), F32, kind="ExternalOutput").ap()

    def dscr(name, shape, dt=BF16):
        return nc.dram_tensor(name, list(shape), dt).ap()

    xp = din("xp", [SEQ, D]); xs = din("xs", [NS, D])
    csk = din("csk", [NL, NSB, 128, 128]); csv = din("csv", [NL, NSB, 128, 128])
    sre = din("sre", [NL, NSB, 32, 64]); sim = din("sim", [NL, NSB, 32, 64])
    cmk = din("cmk", [NL, NSB, 256, 512]); cmv = din("cmv", [NL, NSB, 256, 512])
    memp = din("memp", [256, D])
    attn_norm = din("attn_norm", [NL, D]); w_in = din("w_in", [NL, D, INW])
    q_norm = din("q_norm", [NL, 64]); k_norm = din("k_norm", [NL, 64]); attn_sinks = din("attn_sinks", [NL, 8])
    a_re = din("ssm_a_re", [NL, 32, 64]); a_im = din("ssm_a_im", [NL, 32, 64]); log_dt = din("ssm_log_dt", [NL, 32])
    b_re = din("ssm_b_re", [NL, 32, 64, 16]); b_im = din("ssm_b_im", [NL, 32, 64, 16])
    c_re = din("ssm_c_re", [NL, 32, 16, 64]); c_im = din("ssm_c_im", [NL, 32, 16, 64])
    ssm_d = din("ssm_d", [NL, 512]); w_glu = din("ssm_w_glu", [NL, 512, 512])
    mem_norm = din("mem_norm", [NL, D]); w_kv = din("w_mem_kv", [NL, D, D])
    mq_norm = din("mem_q_norm", [NL, 128]); mk_norm = din("mem_k_norm", [NL, 128])
    w_br = din("w_branch", [NL, 3, 512, D]); w_out = din("w_out", [NL, D, D])
    ffn_norm = din("ffn_norm", [NL, D]); w_up = din("w_ffn_up", [NL, D, 2 * DFF]); w_dn = din("w_ffn_down", [NL, DFF, D])
    c_ident = din("c_ident", [128, 128]); c_blk = din("c_blk", [128, 128]); c_perm = din("c_perm", [128, 128])
    c_mprev = din("c_mprev", [128, 128]); c_mcur = din("c_mcur", [128, 128])
    c_cos = din("c_cos", [128, SEQ]); c_sin = din("c_sin", [128, SEQ])
    c_cos_s = din("c_cos_s", [128, NS]); c_sin_s = din("c_sin_s", [128, NS])
    c_mc = din("c_mc", [128, 4]); c_mnew = din("c_mnew", [NS, NS]); c_rowm = din("c_rowm", [128, 4])

    yp = dout("yp", [SEQ, D]); ys = dout("ys", [NS, D])
    kp = dout("kp", [NL, 128, 128]); vp = dout("vp", [NL, 128, 128])
    hrp = dout("hrp", [NL, 32, 64]); hip = dout("hip", [NL, 32, 64])
    mkp = dout("mkp", [NL, 256, 512]); mvp = dout("mvp", [NL, 256, 512])
    ks = dout("ks", [NL, NSB, 128, 128]); vs = dout("vs", [NL, NSB, 128, 128])
    hrs = dout("hrs", [NL, NSB, 32, 64]); his = dout("his", [NL, NSB, 32, 64])

    wb_in = dscr("wb_in", [NL, D, INW]); wb_glu = dscr("wb_glu", [NL, 512, 512]); wb_kv = dscr("wb_kv", [NL, D, D])
    wb_br = dscr("wb_br", [NL, 3, 512, D]); wb_out = dscr("wb_out", [NL, D, D])
    wb_up = dscr("wb_up", [NL, D, 2 * DFF]); wb_dn = dscr("wb_dn", [NL, DFF, D])

    out_toks = []

    with contextlib.ExitStack() as st:
        S = Sched(nc, st)

        def sb(name, shape, dt):
            return st.enter_context(nc.sbuf_tensor(name, list(shape), dt))

        PS = [st.enter_context(nc.psum_tensor("ps%d" % i, [128, 512], F32)) for i in range(8)]
        psn = [0, 0]
        held = set()

        def bank():
            ns = S.cur_ns
            while True:
                i = 4 * ns + psn[ns] % 4
                psn[ns] += 1
                if i not in held:
                    return i

        ident_f = sb("ident_f", [128, 128], F32); ident_b = sb("ident_b", [128, 128], BF16)
        ones_b = sb("ones_b", [128, 128], BF16); blk_b = sb("blk_b", [128, 128], BF16); perm_b = sb("perm_b", [128, 128], BF16)
        mprev_b = sb("mprev_b", [128, 128], BF16); mcur_b = sb("mcur_b", [128, 128], BF16)
        mc_b = sb("mc_b", [128, 4], BF16); mnew_b = sb("mnew_b", [NS, NS], BF16); rowm = sb("rowm", [128, 4], F32)
        gA = sb("gA", [128, NL, 8], F32); gF = sb("gF", [128, NL, 8], F32)
        gq = sb("gq", [128, NL], F32); gk = sb("gk", [128, NL], F32); gmq = sb("gmq", [128, NL], F32)
        esink = sb("esink", [128, NL, 4], F32); dcol = sb("dcol", [128, NL, 4], F32)
        W2 = sb("W2", [128, NL, 16, 2, 128], BF16); CP = sb("CP", [128, NL, 16, 2, 128], BF16)
        Dd = sb("Dd", [128, NL, 4, 128], BF16)
        LR = sb("LR", [128, NL, NLEV, 16], F32); LI = sb("LI", [128, NL, NLEV, 16], F32); LIn = sb("LIn", [128, NL, NLEV, 16], F32)
        MKT = sb("MKT", [128, NL, 4, 256], BF16); MV = sb("MV", [128, NL, 2, 512], BF16)
        car_r = sb("car_r", [128, NL, 16], F32); car_i = sb("car_i", [128, NL, 16], F32)
        WS = [sb("ws%d" % i, [128, 4096], BF16) for i in range(NWS)]
        small = sb("small", [128, 64], F32)
        smi = sb("smi", [128, 16], I32)
        hs_r = sb("hs_r", [128, 16, NSB], F32); hs_i = sb("hs_i", [128, 16, NSB], F32)
        S.glob.update(["ident_f", "ident_b", "ones_b", "blk_b", "perm_b", "mprev_b", "mcur_b", "mc_b", "mnew_b", "rowm", "gA", "gF", "gq", "gk",
                       "gmq", "esink", "dcol", "W2", "CP", "Dd", "LR", "LI", "LIn", "MKT", "MV", "car", "small", "smi", "hs", "mnb", "pk1", "pk2",
                       "pkb", "gmk_b", "are_t", "aim_t", "dt_t", "tT", "kTc0", "kTc1", "vtc0", "vtc1"] + ["sA%d" % i for i in range(8)])
        RAB = 16384
        RAK = [("RA", i) for i in range(32)]

        def alloc_stream(sid):
            x = "_%d" % sid
            d = {}
            d["cosb"] = sb("cosb" + x, [128, TB], F32); d["sinb"] = sb("sinb" + x, [128, TB], F32)
            d["kTc"] = sb("kTc" + x, [128, NL, 128 + TB], BF16); d["vtc"] = sb("vtc" + x, [128, NL, 3, 128], BF16)
            d["xT"] = sb("xT" + x, [128, 8, TB], F32); d["hT"] = sb("hT" + x, [128, 8, TB], BF16)
            d["qf"] = sb("qf" + x, [128, TB], F32); d["sqb"] = sb("sqb" + x, [128, TB], BF16)
            d["sdv"] = sb("sdv" + x, [128, TB], F32); d["rstd"] = sb("rstd" + x, [128, TB], F32)
            d["qn"] = sb("qn" + x, [128, TB], BF16); d["t1"] = sb("t1" + x, [128, TB], F32); d["t2"] = sb("t2" + x, [128, TB], F32)
            d["qr"] = sb("qr" + x, [128, 4, TB], BF16); d["k32"] = sb("k32" + x, [128, TB], F32); d["v32"] = sb("v32" + x, [128, 2, 128], F32)
            d["uT"] = sb("uT" + x, [128, 4, TB], BF16); d["qmn"] = sb("qmn" + x, [128, 4, TB], BF16)
            d["oa"] = sb("oa" + x, [128, 4, TB], BF16); d["ob"] = sb("ob" + x, [128, 4, TB], BF16); d["oc"] = sb("oc" + x, [128, 4, TB], BF16)
            d["PT"] = [sb("PT%d" % i + x, [128, 2, 512], BF16) for i in range(2)]
            d["dn"] = [sb("dn%d" % i + x, [128, 512], F32) for i in range(2)]
            d["sg"] = [sb("sg%d" % i + x, [128, TB], BF16) for i in range(2)]
            d["kcT"] = sb("kcT" + x, [128, 2, 128], BF16)
            RA_ = sb("RA" + x, [128, RAB // 2], BF16)
            d["RA"] = RA_
            d["zT"] = d["qr"]
            d["HN"] = [dict(qf=d["qf"], sqb=d["sqb"], sdv=d["sdv"], rstd=d["rstd"], qn=d["qn"], t1=d["t1"], t2=d["t2"], sfx="")] * 2

            def ra_(off_b, nbytes, dt):
                a_ = RA_[:, off_b // 2:(off_b + nbytes) // 2]
                return (a_.bitcast(F32) if dt == F32 else a_)
            d["xtok"] = ra_(0, 8192, F32).rearrange("p (s d) -> p s d", s=2)
            d["memt"] = ra_(0, 16384, F32).rearrange("p (s d) -> p s d", s=4)
            d["sqT"] = ra_(12288, 4096, BF16).rearrange("p (k n) -> p k n", k=8)
            d["XSr"] = ra_(0, 4096, F32); d["XSi"] = ra_(4096, 4096, F32)
            d["TD"] = [(ra_(8192 + 2048 * i, 2048, F32), RAK[16 + 4 * i:20 + 4 * i]) for i in range(4)]
            d["xbr"] = ra_(8192, 2048, BF16); d["xbi"] = ra_(10240, 2048, BF16)
            d["macc"] = ra_(0, 8192, F32); d["mgT"] = ra_(8192, 4096, BF16)
            return d

        SB = [alloc_stream(0), alloc_stream(1)]
        RA1 = SB[1]["RA"]
        mnb = RA1[:, 0:2048].rearrange("p (a d) -> p a d", a=2)
        pk1 = RA1[:, 2048:3072].bitcast(F32); pk2 = RA1[:, 3072:4096].bitcast(F32); pkb = RA1[:, 4096:4608]
        S.alias["mnb"] = [("NSRA", 1, i) for i in range(0, 8)]
        S.alias["pk1"] = [("NSRA", 1, i) for i in range(8, 12)]
        S.alias["pk2"] = [("NSRA", 1, i) for i in range(12, 16)]
        S.alias["pkb"] = [("NSRA", 1, i) for i in range(16, 18)]

        def ra1(off_b, nbytes, nm, dt=F32):
            v = RA1[:, off_b // 2:(off_b + nbytes) // 2]
            S.alias[nm] = [("NSRA", 1, i) for i in range(off_b // 512, (off_b + nbytes + 511) // 512)]
            return v.bitcast(dt) if dt != BF16 else v
        S.alias["xtok"] = RAK[0:16]; S.alias["memt"] = RAK[0:32]; S.alias["sqT"] = RAK[24:32]; S.alias["zT"] = ["qr"]
        kXSr = RAK[0:8]; kXSi = RAK[8:16]; kxbr = RAK[16:20]; kxbi = RAK[20:24]; kmacc = RAK[0:16]; kmgT = RAK[16:24]
        hn_i = [0]
        cosb = sinb = kTc = vtc = xT = hT = qf = sqb = sdv = rstd = qn = t1 = t2 = qr = k32 = v32 = uT = qmn = oa = ob = oc = None
        PT = dn = sg = kcT = RA = zT = HN = xtok = memt = sqT = XSr = XSi = TD = xbr = xbi = macc = mgT = None
        kTo = vto = None

        def bind(sid):
            nonlocal cosb, sinb, kTc, vtc, xT, hT, qf, sqb, sdv, rstd, qn, t1, t2, qr, k32, v32, uT, qmn, oa, ob, oc
            nonlocal PT, dn, sg, kcT, RA, zT, HN, xtok, memt, sqT, XSr, XSi, TD, xbr, xbi, macc, mgT, kTo, vto
            S.cur_ns = sid
            d = SB[sid]
            cosb, sinb, kTc, vtc, xT, hT = d["cosb"], d["sinb"], d["kTc"], d["vtc"], d["xT"], d["hT"]
            qf, sqb, sdv, rstd, qn, t1, t2 = d["qf"], d["sqb"], d["sdv"], d["rstd"], d["qn"], d["t1"], d["t2"]
            qr, k32, v32, uT, qmn, oa, ob, oc = d["qr"], d["k32"], d["v32"], d["uT"], d["qmn"], d["oa"], d["ob"], d["oc"]
            PT, dn, sg, kcT, RA, zT, HN = d["PT"], d["dn"], d["sg"], d["kcT"], d["RA"], d["zT"], d["HN"]
            xtok, memt, sqT, XSr, XSi, TD, xbr, xbi, macc, mgT = (d["xtok"], d["memt"], d["sqT"], d["XSr"], d["XSi"], d["TD"], d["xbr"],
                                                                 d["xbi"], d["macc"], d["mgT"])
            kTo, vto = SB[1 - sid]["kTc"], SB[1 - sid]["vtc"]

        bind(0)

        def kcar(which, sid):
            return "%s%d" % (which, sid)

        wsn = [0, 0]

        def wk(i):
            return [("ws", i), ("wsT", i)]

        def wslot():
            ns = S.cur_ns
            i = 2 * ns + wsn[ns] % 2
            wsn[ns] += 1
            return i

        def dq():
            return "sp" if S.cur_ns == 0 else "pool"

        S.op("sp", lambda e: e.dma_start(out=ident_f[:], in_=c_ident), writes=["ident_f"], dma="c0")
        for (dst, src, nm) in [(ident_b, c_ident, "ident_b"), (blk_b, c_blk, "blk_b"), (perm_b, c_perm, "perm_b"),
                               (mprev_b, c_mprev, "mprev_b"), (mcur_b, c_mcur, "mcur_b"), (mc_b, c_mc, "mc_b"),
                               (mnew_b, c_mnew, "mnew_b")]:
            S.op("pool", lambda e, dst=dst, src=src: e.dma_start(out=dst[:], in_=src), writes=[nm], dma=nm)
        S.op("sp", lambda e: e.dma_start(out=rowm[:], in_=c_rowm), writes=["rowm"], dma="rowm")
        S.op("dve", lambda e: e.memset(ones_b[:], 1.0), writes=["ones_b"])
        S.op("dve", lambda e: e.memset(small[:], 0.0), writes=["small"])
        S.op("dve", lambda e: e.memset(small[:, 0:1], math.pi / 2), writes=["small"])
        S.op("dve", lambda e: e.memset(small[:, 1:2], EPS), writes=["small"])

        def conv(dst, src, key, rows):
            r0 = 0
            while r0 < rows:
                r1 = min(rows, r0 + 256)
                S.op("pool", lambda e, r0=r0, r1=r1: e.dma_start(out=dst[r0:r1, :], in_=src[r0:r1, :]),
                     writes=[key], dma=key)
                r0 = r1

        for l in range(NL):
            conv(wb_kv[l], w_kv[l], ("wb_kv", l), D)
        for l in range(NL):
            conv(wb_in[l], w_in[l], ("wb_in", l), D)
            conv(wb_glu[l], w_glu[l], ("wb_glu", l), 512)
            for n in range(3):
                conv(wb_br[l, n], w_br[l, n], ("wb_br", l), 512)
            conv(wb_out[l], w_out[l], ("wb_out", l), D)
            conv(wb_up[l], w_up[l], ("wb_up", l), D)
            conv(wb_dn[l], w_dn[l], ("wb_dn", l), DFF)

        for l in range(NL):
            S.op("sp", lambda e, l=l: e.dma_start(out=gA[:, l, :], in_=attn_norm[l].rearrange("(k p) -> p k", p=128),
                                                  allow_slow_non_contiguous=True), writes=["gA"], dma="gA")
            S.op("sp", lambda e, l=l: e.dma_start(out=gF[:, l, :], in_=ffn_norm[l].rearrange("(k p) -> p k", p=128),
                                                  allow_slow_non_contiguous=True), writes=["gF"], dma="gF")
            S.op("sp", lambda e, l=l: e.dma_start(out=dcol[:, l, :], in_=ssm_d[l].rearrange("(k p) -> p k", p=128),
                                                  allow_slow_non_contiguous=True), writes=["dcol"], dma="dcol")
            for two in range(2):
                sl = slice(64 * two, 64 * two + 64)
                S.op("sp", lambda e, l=l, sl=sl: e.dma_start(out=gq[sl, l:l + 1], in_=q_norm[l].rearrange("(p o) -> p o", o=1)),
                     writes=["gq"], dma="gq")
                S.op("sp", lambda e, l=l, sl=sl: e.dma_start(out=gk[sl, l:l + 1], in_=k_norm[l].rearrange("(p o) -> p o", o=1)),
                     writes=["gk"], dma="gk")
                S.op("sp", lambda e, l=l, sl=sl, two=two: e.dma_start(out=esink[sl, l, :],
                                                                      in_=attn_sinks[l, 4 * two:4 * two + 4].partition_broadcast(64)),
                     writes=["esink"], dma="esink")
            S.op("sp", lambda e, l=l: e.dma_start(out=gmq[:, l:l + 1], in_=mq_norm[l].rearrange("(p o) -> p o", o=1)),
                 writes=["gmq"], dma="gmq")
        S.op("act", lambda e: e.activation(out=esink[:], in_=esink[:], func=AF.Exp), reads=["esink"], writes=["esink"])

        are_t = ra1(10240, 64, "are_t"); aim_t = ra1(10304, 64, "aim_t"); dt_t = ra1(10368, 64, "dt_t")
        sA = [ra1(10432 + 64 * i, 64, "sA%d" % i) for i in range(8)]
        def ra3(idx, nm):
            v = RA[:, 1024 * idx:1024 * idx + 1024].bitcast(F32)
            S.alias[nm] = RAK[4 * idx:4 * idx + 4]
            return v.rearrange("p (t c) -> p t c", t=16)
        Bb = [ra3(i, "Bb%d" % i) for i in range(2)]
        Cb = [ra3(2 + i, "Cb%d" % i) for i in range(2)]
        GB = [ra3(4 + i, "GB%d" % i) for i in range(2)]
        tG = [ra3(6 + i, "tG%d" % i) for i in range(2)]
        tT = ra1(9216, 512, "tT")

        def dve(fn, R, W):
            return S.op("dve", fn, reads=R, writes=W)

        def act(fn, R, W):
            return S.op("act", fn, reads=R, writes=W)

        TWO_PI = 2.0 * math.pi
        for l in range(NL):
            for gl in range(2):
                sl = slice(64 * gl, 64 * gl + 64)
                S.op("sp", lambda e, l=l, gl=gl, sl=sl: e.dma_start(
                    out=are_t[sl, :], in_=a_re[l].rearrange("(tp gl) p -> gl p tp", gl=2)[gl], allow_slow_non_contiguous=True),
                    writes=["are_t"], dma="are_t")
                S.op("sp", lambda e, l=l, gl=gl, sl=sl: e.dma_start(
                    out=aim_t[sl, :], in_=a_im[l].rearrange("(tp gl) p -> gl p tp", gl=2)[gl], allow_slow_non_contiguous=True),
                    writes=["aim_t"], dma="aim_t")
                S.op("sp", lambda e, l=l, gl=gl, sl=sl: e.dma_start(
                    out=dt_t[sl, :], in_=log_dt[l].rearrange("(tp gl) -> gl tp", gl=2)[gl].partition_broadcast(64)),
                    writes=["dt_t"], dma="dt_t")
            for ri, (bsrc, csrc) in enumerate([(b_re, c_re), (b_im, c_im)]):
                S.op("pool", lambda e, ri=ri: e.memset(Bb[ri][:], 0.0), writes=["Bb%d" % ri])
                S.op("pool", lambda e, ri=ri: e.memset(Cb[ri][:], 0.0), writes=["Cb%d" % ri])
                for gl in range(2):
                    sl = slice(64 * gl, 64 * gl + 64)
                    cs = slice(16 * gl, 16 * gl + 16)
                    S.op("sp", lambda e, l=l, gl=gl, sl=sl, cs=cs, ri=ri, bsrc=bsrc: e.dma_start(
                        out=Bb[ri][sl, :, cs], in_=bsrc[l].rearrange("(tp gl) p c -> gl p tp c", gl=2)[gl]),
                        writes=["Bb%d" % ri], dma="Bb%d" % ri)
                    for tp in range(16):
                        S.op("sp", lambda e, l=l, gl=gl, sl=sl, cs=cs, ri=ri, csrc=csrc, tp=tp: e.dma_start(
                            out=Cb[ri][sl, tp, cs], in_=csrc[l, 2 * tp + gl].rearrange("c p -> p c"), allow_slow_non_contiguous=True),
                            writes=["Cb%d" % ri], dma="Cb%d" % ri)
            dtv, ard, mag, th, kf, s_, c_, tmp = sA
            act(lambda e: e.activation(out=dtv[:], in_=dt_t[:], func=AF.Exp), ["dt_t"], ["sA0"])
            dve(lambda e: e.tensor_tensor(out=ard[:], in0=are_t[:], in1=dtv[:], op=ALU.mult), ["are_t", "sA0"], ["sA1"])
            act(lambda e: e.activation(out=mag[:], in_=ard[:], func=AF.Exp), ["sA1"], ["sA2"])
            dve(lambda e: e.tensor_tensor(out=th[:], in0=aim_t[:], in1=dtv[:], op=ALU.mult), ["aim_t", "sA0"], ["sA3"])
            dve(lambda e: e.tensor_scalar(out=kf[:], in0=th[:], scalar1=1.0 / TWO_PI, scalar2=None, op0=ALU.mult), ["sA3"], ["sA4"])
            dve(lambda e: e.tensor_copy(out=smi[:], in_=kf[:]), ["sA4"], ["smi"])
            dve(lambda e: e.tensor_copy(out=kf[:], in_=smi[:]), ["smi"], ["sA4"])
            dve(lambda e: e.scalar_tensor_tensor(out=th[:], in0=kf[:], scalar=-TWO_PI, in1=th[:], op0=ALU.mult, op1=ALU.add),
                ["sA4", "sA3"], ["sA3"])
            act(lambda e: e.activation(out=s_[:], in_=th[:], func=AF.Sin, scale=0.5), ["sA3"], ["sA5"])
            act(lambda e: e.activation(out=c_[:], in_=th[:], func=AF.Sin, scale=0.5, bias=small[:, 0:1]), ["sA3", "small"], ["sA6"])
            lr0 = LR[:, l, 0, :]; li0 = LI[:, l, 0, :]
            dve(lambda e: e.tensor_tensor(out=tmp[:], in0=s_[:], in1=c_[:], op=ALU.mult), ["sA5", "sA6"], ["sA7"])
            dve(lambda e: e.scalar_tensor_tensor(out=li0, in0=tmp[:], scalar=2.0, in1=mag[:], op0=ALU.mult, op1=ALU.mult),
                ["sA7", "sA2"], ["LI"])
            dve(lambda e: e.tensor_tensor(out=tmp[:], in0=s_[:], in1=s_[:], op=ALU.mult), ["sA5"], ["sA7"])
            dve(lambda e: e.tensor_scalar(out=tmp[:], in0=tmp[:], scalar1=-2.0, scalar2=1.0, op0=ALU.mult, op1=ALU.add), ["sA7"], ["sA7"])
            dve(lambda e: e.tensor_tensor(out=lr0, in0=tmp[:], in1=mag[:], op=ALU.mult), ["sA7", "sA2"], ["LR"])
            den_, nr_, gr_, gi_, rd_ = sA[0], sA[1], sA[2], sA[3], sA[4]
            dve(lambda e: e.tensor_tensor(out=den_[:], in0=are_t[:], in1=are_t[:], op=ALU.mult), ["are_t"], ["sA0"])
            dve(lambda e: e.tensor_tensor(out=tmp[:], in0=aim_t[:], in1=aim_t[:], op=ALU.mult), ["aim_t"], ["sA7"])
            dve(lambda e: e.tensor_tensor(out=den_[:], in0=den_[:], in1=tmp[:], op=ALU.add), ["sA0", "sA7"], ["sA0"])
            dve(lambda e: e.reciprocal(out=rd_[:], in_=den_[:]), ["sA0"], ["sA4"])
            dve(lambda e: e.tensor_scalar(out=nr_[:], in0=lr0, scalar1=-1.0, scalar2=None, op0=ALU.add), ["LR"], ["sA1"])
            dve(lambda e: e.tensor_tensor(out=gr_[:], in0=nr_[:], in1=are_t[:], op=ALU.mult), ["sA1", "are_t"], ["sA2"])
            dve(lambda e: e.tensor_tensor(out=tmp[:], in0=li0, in1=aim_t[:], op=ALU.mult), ["LI", "aim_t"], ["sA7"])
            dve(lambda e: e.tensor_tensor(out=gr_[:], in0=gr_[:], in1=tmp[:], op=ALU.add), ["sA2", "sA7"], ["sA2"])
            dve(lambda e: e.tensor_tensor(out=gr_[:], in0=gr_[:], in1=rd_[:], op=ALU.mult), ["sA2", "sA4"], ["sA2"])
            dve(lambda e: e.tensor_tensor(out=gi_[:], in0=li0, in1=are_t[:], op=ALU.mult), ["LI", "are_t"], ["sA3"])
            dve(lambda e: e.tensor_tensor(out=tmp[:], in0=nr_[:], in1=aim_t[:], op=ALU.mult), ["sA1", "aim_t"], ["sA7"])
            dve(lambda e: e.tensor_tensor(out=gi_[:], in0=gi_[:], in1=tmp[:], op=ALU.subtract), ["sA3", "sA7"], ["sA3"])
            dve(lambda e: e.tensor_tensor(out=gi_[:], in0=gi_[:], in1=rd_[:], op=ALU.mult), ["sA3", "sA4"], ["sA3"])
            for i in range(NLEV - 1):
                a, b = LR[:, l, i, :], LI[:, l, i, :]
                a2, b2 = LR[:, l, i + 1, :], LI[:, l, i + 1, :]
                dve(lambda e, a=a, b=b: e.tensor_tensor(out=tmp[:], in0=b, in1=b, op=ALU.mult), ["LI"], ["sA7"])
                dve(lambda e, a=a, a2=a2: e.tensor_tensor(out=a2, in0=a, in1=a, op=ALU.mult), ["LR"], ["LR"])
                dve(lambda e, a2=a2: e.tensor_tensor(out=a2, in0=a2, in1=tmp[:], op=ALU.subtract), ["LR", "sA7"], ["LR"])
                dve(lambda e, a=a, b=b, b2=b2: e.scalar_tensor_tensor(out=b2, in0=a, scalar=2.0, in1=b, op0=ALU.mult, op1=ALU.mult),
                    ["LR", "LI"], ["LI"])
            dve(lambda e, l=l: e.tensor_scalar(out=LIn[:, l], in0=LI[:, l], scalar1=-1.0, scalar2=None, op0=ALU.mult), ["LI"], ["LIn"])
            grb = gr_[:].rearrange("p (t o) -> p t o", o=1).to_broadcast([128, 16, 32])
            gib = gi_[:].rearrange("p (t o) -> p t o", o=1).to_broadcast([128, 16, 32])
            dve(lambda e: e.tensor_tensor(out=GB[0][:], in0=Bb[0][:], in1=grb, op=ALU.mult), ["Bb0", "sA2"], ["GB0"])
            dve(lambda e: e.tensor_tensor(out=tG[0][:], in0=Bb[1][:], in1=gib, op=ALU.mult), ["Bb1", "sA3"], ["tG0"])
            dve(lambda e: e.tensor_tensor(out=GB[0][:], in0=GB[0][:], in1=tG[0][:], op=ALU.subtract), ["GB0", "tG0"], ["GB0"])
            dve(lambda e: e.tensor_tensor(out=GB[1][:], in0=Bb[1][:], in1=grb, op=ALU.mult), ["Bb1", "sA2"], ["GB1"])
            dve(lambda e: e.tensor_tensor(out=tG[1][:], in0=Bb[0][:], in1=gib, op=ALU.mult), ["Bb0", "sA3"], ["tG1"])
            dve(lambda e: e.tensor_tensor(out=GB[1][:], in0=GB[1][:], in1=tG[1][:], op=ALU.add), ["GB1", "tG1"], ["GB1"])
            for ri in range(2):
                for ct in range(4):
                    b = bank()
                    S.op("pe", lambda e, b=b, ri=ri, ct=ct: e.transpose(
                        out=PS[b][:, 0:128], in_=GB[ri][:, 4 * ct:4 * ct + 4, :].rearrange("p a b -> p (a b)"), identity=ident_f[:]),
                        reads=["GB%d" % ri, "ident_f"], writes=[("ps", b)])
                    act(lambda e, b=b: e.copy(out=tT[:], in_=PS[b][:, 0:128]), [("ps", b)], ["tT"])
                    for i in range(4):
                        dve(lambda e, l=l, ri=ri, ct=ct, i=i: e.tensor_scalar(
                            out=W2[:, l, 4 * ct + i, ri, :], in0=tT[:], scalar1=rowm[:, i:i + 1], scalar2=None, op0=ALU.mult),
                            ["tT", "rowm"], ["W2"])
            S.op("pool", lambda e, l=l: e.memset(CP[:, l], 0.0), writes=["CP"])
            for i in range(4):
                act(lambda e, l=l, i=i: e.copy(out=CP[:, l, i::4, 0, 32 * i:32 * i + 32], in_=Cb[0][:, i::4, :]), ["Cb0", "CP"], ["CP"])
                act(lambda e, l=l, i=i: e.mul(out=CP[:, l, i::4, 1, 32 * i:32 * i + 32], in_=Cb[1][:, i::4, :], mul=-1.0), ["Cb1", "CP"], ["CP"])
            for ct in range(4):
                dve(lambda e, l=l, ct=ct: e.tensor_scalar(out=Dd[:, l, ct, :], in0=ident_f[:], scalar1=dcol[:, l, ct:ct + 1], scalar2=None,
                                                          op0=ALU.mult), ["ident_f", "dcol"], ["Dd"])

        mnT = hT
        gmk_b = ra1(9728, 512, "gmk_b")
        S.op("sp", lambda e: e.dma_start(out=memt[:, 0:2, :], in_=memp.rearrange("(a p) d -> p a d", p=128)), writes=["memt"], dma="memt")
        for l in range(NL):
            S.op("sp", lambda e, l=l: e.dma_start(out=memt[:, 2, :], in_=mem_norm[l].partition_broadcast(128)), writes=["memt"], dma="memt")
            S.op("sp", lambda e, l=l: e.dma_start(out=gmk_b[:], in_=mk_norm[l].partition_broadcast(128)), writes=["gmk_b"], dma="gmk_b")
            dve(lambda e: e.memset(small[:, 8:10], 0.0), [], ["small"])
            for a in range(2):
                act(lambda e, a=a: e.activation(out=memt[:, 3, :], in_=memt[:, a, :], func=AF.Square, accum_out=small[:, 8 + a:9 + a]),
                    ["memt"], ["memt", "small"])
            act(lambda e: e.activation(out=small[:, 10:12], in_=small[:, 8:10], func=AF.Sqrt, scale=1.0 / D, bias=small[:, 1:2]),
                ["small"], ["small"])
            dve(lambda e: e.reciprocal(out=small[:, 12:14], in_=small[:, 10:12]), ["small"], ["small"])
            for a in range(2):
                dve(lambda e, a=a: e.scalar_tensor_tensor(out=mnb[:, a, :], in0=memt[:, a, :], scalar=small[:, 12 + a:13 + a],
                                                          in1=memt[:, 2, :], op0=ALU.mult, op1=ALU.mult), ["memt", "small"], ["mnb"])
            for a in range(2):
                for k in range(8):
                    b = bank()
                    pb = PS[b][:].bitcast(BF16)
                    S.op("pe", lambda e, a=a, k=k, pb=pb: e.transpose(out=pb[:, 0:128], in_=mnb[:, a, 128 * k:128 * k + 128],
                                                                      identity=ident_b[:]),
                         reads=["mnb", "ident_b"], writes=[("ps", b)])
                    act(lambda e, a=a, k=k, pb=pb: e.copy(out=mnT[:, k, 128 * a:128 * a + 128], in_=pb[:, 0:128]), [("ps", b)], ["hT"])
            for half in range(2):
                ws = wslot()
                wv = WS[ws][:].rearrange("p (k c) -> p k c", k=8)
                S.op("sp", lambda e, l=l, half=half, wv=wv: e.dma_start(
                    out=wv, in_=wb_kv[l].rearrange("(k p) c -> p k c", p=128)[:, :, 512 * half:512 * half + 512]),
                    reads=[("wb_kv", l)], writes=wk(ws), dma=("ws", ws))
                for a in range(2):
                    b = bank()
                    for k in range(8):
                        S.op("pe", lambda e, a=a, k=k, b=b, wv=wv: e.matmul(PS[b][:], lhsT=mnT[:, k, 128 * a:128 * a + 128], rhs=wv[:, k, :],
                                                                          start=(k == 0), stop=(k == 7)),
                             reads=["hT", ("ws", ws)], writes=[("ps", b)])
                    if half == 0:
                        kk = pk1
                        dve(lambda e: e.memset(small[:, 16:20], 0.0), [], ["small"])
                        for h in range(4):
                            act(lambda e, b=b, h=h: e.activation(out=pk2[:, 128 * h:128 * h + 128], in_=PS[b][:, 128 * h:128 * h + 128],
                                                                 func=AF.Square, accum_out=small[:, 16 + h:17 + h]),
                                [("ps", b)], ["pk2", "small"])
                        act(lambda e: e.activation(out=small[:, 20:24], in_=small[:, 16:20], func=AF.Sqrt, scale=1.0 / 128, bias=small[:, 1:2]),
                            ["small"], ["small"])
                        dve(lambda e: e.reciprocal(out=small[:, 24:28], in_=small[:, 20:24]), ["small"], ["small"])
                        for h in range(4):
                            dve(lambda e, b=b, h=h: e.scalar_tensor_tensor(
                                out=kk[:, 128 * h:128 * h + 128], in0=PS[b][:, 128 * h:128 * h + 128], scalar=small[:, 24 + h:25 + h],
                                in1=gmk_b[:], op0=ALU.mult, op1=ALU.mult), [("ps", b), "small", "gmk_b"], ["pk1"])
                        out_toks.append(S.op("sp", lambda e, l=l, a=a: e.dma_start(out=mkp[l, 128 * a:128 * a + 128, :], in_=kk[:]),
                                             reads=["pk1"], dma="o_mkp"))
                        act(lambda e: e.copy(out=pkb[:], in_=kk[:]), ["pk1"], ["pkb"])
                        for h in range(4):
                            b2 = bank()
                            pb = PS[b2][:].bitcast(BF16)
                            S.op("pe", lambda e, h=h, pb=pb: e.transpose(out=pb[:, 0:128], in_=pkb[:, 128 * h:128 * h + 128], identity=ident_b[:]),
                                 reads=["pkb", "ident_b"], writes=[("ps", b2)])
                            act(lambda e, l=l, a=a, h=h, pb=pb: e.copy(out=MKT[:, l, h, 128 * a:128 * a + 128], in_=pb[:, 0:128]),
                                [("ps", b2)], ["MKT"])
                    else:
                        vv = pk2
                        act(lambda e, b=b: e.copy(out=vv[:], in_=PS[b][:]), [("ps", b)], ["pk2"])
                        out_toks.append(S.op("sp", lambda e, l=l, a=a: e.dma_start(out=mvp[l, 128 * a:128 * a + 128, :], in_=vv[:]),
                                             reads=["pk2"], dma="o_mvp"))
                        dve(lambda e, l=l, a=a: e.tensor_copy(out=MV[:, l, a, :], in_=vv[:]), ["pk2"], ["MV"])

        def load_w(dst_view, src_ap, srckey):
            ws = wslot()
            dv = dst_view(ws)
            S.op(dq(), lambda e: e.dma_start(out=dv, in_=src_ap), reads=[srckey], writes=wk(ws), dma=("ws", ws))
            return ws

        def rms_rstd(N, ssb, inv_n, R):
            act(lambda e: e.activation(out=sdv[:, 0:N], in_=PS[ssb][:, 0:N], func=AF.Sqrt, scale=inv_n, bias=small[:, 1:2]),
                [("ps", ssb), "small"], ["sdv"])
            dve(lambda e: e.reciprocal(out=rstd[:, 0:N], in_=sdv[:, 0:N]), ["sdv"], ["rstd"])

        def norm_block(N, gtab):
            act(lambda e: e.activation(out=sqT[:, :, 0:N], in_=xT[:, :, 0:N], func=AF.Square), ["xT"], ["sqT"])
            b = bank()
            for k in range(8):
                S.op("pe", lambda e, k=k, b=b: e.matmul(PS[b][:, 0:N], lhsT=ones_b[:], rhs=sqT[:, k, 0:N], start=(k == 0), stop=(k == 7)),
                     reads=["sqT", "ones_b"], writes=[("ps", b)])
            rms_rstd(N, b, 1.0 / D, None)
            for k in range(8):
                dve(lambda e, k=k: e.scalar_tensor_tensor(out=hT[:, k, 0:N], in0=xT[:, k, 0:N], scalar=gtab[:, k:k + 1], in1=rstd[:, 0:N],
                                                          op0=ALU.mult, op1=ALU.mult), ["xT", "rstd", "gA", "gF"], ["hT"])

        def proj_tile(N, wv, c0, ws, b=None):
            if b is None:
                b = bank()
            for k in range(8):
                S.op("pe", lambda e, k=k, b=b: e.matmul(PS[b][:, 0:N], lhsT=wv[:, k, c0:c0 + 128], rhs=hT[:, k, 0:N],
                                                        start=(k == 0), stop=(k == 7)),
                     reads=["hT", ("ws", ws)], writes=[("ps", b)])
            return b

        def headnorm_rope(N, b, l, gcol, onesm, inv_n, rope, out_bf, out_keys, out32=None, out32_keys=()):
            H = HN[hn_i[0] % 2]
            hn_i[0] += 1
            x = H["sfx"]
            qf_, sqb_, sdv_, rstd_, qn_, t1_, t2_ = H["qf"], H["sqb"], H["sdv"], H["rstd"], H["qn"], H["t1"], H["t2"]
            act(lambda e: e.copy(out=qf_[:, 0:N], in_=PS[b][:, 0:N]), [("ps", b)], ["qf" + x])
            act(lambda e: e.activation(out=sqb_[:, 0:N], in_=qf_[:, 0:N], func=AF.Square), ["qf" + x], ["sqb" + x])
            b2 = bank()
            S.op("pe", lambda e: e.matmul(PS[b2][:, 0:N], lhsT=onesm[:], rhs=sqb_[:, 0:N], start=True, stop=True),
                 reads=["sqb" + x, "blk_b", "ones_b"], writes=[("ps", b2)])
            act(lambda e: e.activation(out=sdv_[:, 0:N], in_=PS[b2][:, 0:N], func=AF.Sqrt, scale=inv_n, bias=small[:, 1:2]),
                [("ps", b2), "small"], ["sdv"])
            dve(lambda e: e.reciprocal(out=rstd_[:, 0:N], in_=sdv_[:, 0:N]), ["sdv"], ["rstd" + x])
            if not rope:
                S.op("dve", lambda e: e.scalar_tensor_tensor(out=out_bf, in0=qf_[:, 0:N], scalar=gcol, in1=rstd_[:, 0:N],
                                                             op0=ALU.mult, op1=ALU.mult),
                     reads=["qf" + x, "rstd" + x, "gmq"], writes=out_keys)
                return
            S.op("dve", lambda e: e.scalar_tensor_tensor(out=qn_[:, 0:N], in0=qf_[:, 0:N], scalar=gcol, in1=rstd_[:, 0:N],
                                                         op0=ALU.mult, op1=ALU.mult),
                 reads=["qf" + x, "rstd" + x, "gq", "gk"], writes=["qn" + x])
            b3 = bank()
            S.op("pe", lambda e: e.matmul(PS[b3][:, 0:N], lhsT=perm_b[:], rhs=qn_[:, 0:N], start=True, stop=True),
                 reads=["qn" + x, "perm_b"], writes=[("ps", b3)])
            S.op("pool", lambda e: e.tensor_tensor(out=t1_[:, 0:N], in0=qn_[:, 0:N], in1=cosb[:, 0:N], op=ALU.mult),
                 reads=["qn" + x, "cosb"], writes=["t1"])
            dve(lambda e: e.tensor_tensor(out=t2_[:, 0:N], in0=PS[b3][:, 0:N], in1=sinb[:, 0:N], op=ALU.mult), [("ps", b3), "sinb"], ["t2"])
            if out32 is not None:
                dve(lambda e: e.tensor_tensor(out=out32, in0=t1_[:, 0:N], in1=t2_[:, 0:N], op=ALU.add), ["t1", "t2"], list(out32_keys))
                act(lambda e: e.copy(out=out_bf, in_=out32), list(out32_keys), out_keys)
            else:
                dve(lambda e: e.tensor_tensor(out=out_bf, in0=t1_[:, 0:N], in1=t2_[:, 0:N], op=ALU.add), ["t1", "t2"], out_keys)

        def cmul_add(eng, dr, di, sr, si, lr, li, lin, kdr, kdi, ksr, ksi, T, kT):
            o = lambda fn, R, W: S.op(eng, fn, reads=R, writes=W)
            o(lambda e: e.tensor_tensor(out=T[0], in0=sr, in1=lr, op=ALU.mult), ksr + ["LR"], kT[0])
            o(lambda e: e.tensor_tensor(out=T[1], in0=si, in1=lin, op=ALU.mult), ksi + ["LIn"], kT[1])
            o(lambda e: e.tensor_tensor(out=T[2], in0=si, in1=lr, op=ALU.mult), ksi + ["LR"], kT[2])
            o(lambda e: e.tensor_tensor(out=T[3], in0=sr, in1=li, op=ALU.mult), ksr + ["LI"], kT[3])
            o(lambda e: e.tensor_tensor(out=dr, in0=dr, in1=T[0], op=ALU.add), kdr + kT[0], kdr)
            o(lambda e: e.tensor_tensor(out=di, in0=di, in1=T[2], op=ALU.add), kdi + kT[2], kdi)
            o(lambda e: e.tensor_tensor(out=dr, in0=dr, in1=T[1], op=ALU.add), kdr + kT[1], kdr)
            o(lambda e: e.tensor_tensor(out=di, in0=di, in1=T[3], op=ALU.add), kdi + kT[3], kdi)

        kXS = kXSr + kXSi

        def lam_b(tab, l, lev, tp0, ntp, shape):
            return tab[:, l, lev, tp0:tp0 + ntp].rearrange("p (t o) -> p t o", o=1).to_broadcast(shape)

        ybank = []

        def ssm_group_prompt(l, ct, N, first_block):
            for i in range(4):
                tp = 4 * ct + i
                for ri, X, kX in ((0, XSr, kXSr), (1, XSi, kXSi)):
                    b = bank()
                    S.op("pe", lambda e, tp=tp, ri=ri, b=b: e.matmul(PS[b][:, 0:N], lhsT=W2[:, l, tp, ri, :], rhs=uT[:, ct, 0:N],
                                                                    start=True, stop=True), reads=["W2", "uT"], writes=[("ps", b)])
                    act(lambda e, X=X, i=i, b=b: e.copy(out=X[:, i * TB:i * TB + N], in_=PS[b][:, 0:N]), [("ps", b)], kX[2 * i:2 * i + 2])
            Xr3 = XSr.rearrange("p (t n) -> p t n", t=4)
            Xi3 = XSi.rearrange("p (t n) -> p t n", t=4)
            parts = [("dve", 0, 4, TD)]
            nlev = int(math.log2(N))
            steps = []
            if not first_block:
                steps.append(("carry", 0))
            for lev in range(nlev):
                steps.append(("up", lev))
            for lev in range(nlev - 2, -1, -1):
                steps.append(("down", lev))
            for kind, lev in steps:
                for eng, a0, na, TT in parts:
                    kr = kXSr[2 * a0:2 * (a0 + na)]; ki = kXSi[2 * a0:2 * (a0 + na)]
                    Xr = Xr3[:, a0:a0 + na, :]; Xi = Xi3[:, a0:a0 + na, :]
                    if kind == "carry":
                        m = 1
                        dr, di = Xr[:, :, 0:1], Xi[:, :, 0:1]
                        sr = car_r[:, l, 4 * ct + a0:4 * ct + a0 + na].rearrange("p (t o) -> p t o", o=1)
                        si = car_i[:, l, 4 * ct + a0:4 * ct + a0 + na].rearrange("p (t o) -> p t o", o=1)
                        ksr = ksi = ["car"]
                    else:
                        d = 1 << lev
                        if kind == "up":
                            m = N // (2 * d)
                            Xr4 = Xr.rearrange("p t (m s) -> p t m s", s=2 * d)
                            Xi4 = Xi.rearrange("p t (m s) -> p t m s", s=2 * d)
                        else:
                            m = N // (2 * d) - 1
                            Xr4 = Xr[:, :, d:N - d].rearrange("p t (m s) -> p t m s", s=2 * d)
                            Xi4 = Xi[:, :, d:N - d].rearrange("p t (m s) -> p t m s", s=2 * d)
                        dr, di = Xr4[:, :, :, 2 * d - 1], Xi4[:, :, :, 2 * d - 1]
                        sr, si = Xr4[:, :, :, d - 1], Xi4[:, :, :, d - 1]
                        ksr, ksi = kr, ki
                    sh = [128, na, m]
                    T = [t_[0].rearrange("p (t n) -> p t n", t=na)[:, :, 0:m] for t_ in TT]
                    kT = [t_[1] for t_ in TT]
                    cmul_add(eng, dr, di, sr, si, lam_b(LR, l, lev, 4 * ct + a0, na, sh), lam_b(LI, l, lev, 4 * ct + a0, na, sh),
                             lam_b(LIn, l, lev, 4 * ct + a0, na, sh), kr, ki, ksr, ksi, T, kT)
                yield
            dve(lambda e: e.tensor_copy(out=car_r[:, l, 4 * ct:4 * ct + 4], in_=Xr3[:, :, N - 1]), kXSr, ["car"])
            dve(lambda e: e.tensor_copy(out=car_i[:, l, 4 * ct:4 * ct + 4], in_=Xi3[:, :, N - 1]), kXSi, ["car"])
            act(lambda e: e.copy(out=xbr[:], in_=XSr), kXSr, kxbr)
            act(lambda e: e.copy(out=xbi[:], in_=XSi), kXSi, kxbi)
            ybank.append(ssm_y(l, ct, N))

        def ssm_y(l, ct, N):
            b = bank()
            n = 0
            for i in range(4):
                tp = 4 * ct + i
                for ri, xb_, kx in ((0, xbr, kxbr), (1, xbi, kxbi)):
                    S.op("pe", lambda e, tp=tp, ri=ri, xb_=xb_, i=i, b=b, n=n: e.matmul(
                        PS[b][:, 0:N], lhsT=CP[:, l, tp, ri, :], rhs=xb_[:, i * TB:i * TB + N], start=(n == 0), stop=False),
                        reads=["CP"] + kx, writes=[("ps", b)])
                    n += 1
            S.op("pe", lambda e, b=b: e.matmul(PS[b][:, 0:N], lhsT=Dd[:, l, ct, :], rhs=uT[:, ct, 0:N], start=False, stop=True),
                 reads=["Dd", "uT"], writes=[("ps", b)])
            return b

        def ssm_group_sample(l, ct):
            N = NS
            for i in range(4):
                tp = 4 * ct + i
                for ri, X in ((0, XSr), (1, XSi)):
                    b = bank()
                    S.op("pe", lambda e, tp=tp, ri=ri, b=b: e.matmul(PS[b][:, 0:N], lhsT=W2[:, l, tp, ri, :], rhs=uT[:, ct, 0:N],
                                                                    start=True, stop=True), reads=["W2", "uT"], writes=[("ps", b)])
                    act(lambda e, X=X, i=i, b=b: e.copy(out=X[:, i * TB:i * TB + N], in_=PS[b][:, 0:N]), [("ps", b)], kXS)
            Xr4 = XSr.rearrange("p (t n) -> p t n", t=4)[:, :, 0:NS].rearrange("p t (b i) -> p t b i", i=4)
            Xi4 = XSi.rearrange("p (t n) -> p t n", t=4)[:, :, 0:NS].rearrange("p t (b i) -> p t b i", i=4)
            sh = [128, 4, NSB]
            T = [t_[0][:, 0:4 * NSB].rearrange("p (t n) -> p t n", t=4) for t_ in TD]
            kT = [t_[1] for t_ in TD]
            lr, li, lin = lam_b(LR, l, 0, 4 * ct, 4, sh), lam_b(LI, l, 0, 4 * ct, 4, sh), lam_b(LIn, l, 0, 4 * ct, 4, sh)
            for i in range(4):
                if i == 0:
                    sr, si, ksr, ksi = hs_r[:, 4 * ct:4 * ct + 4, :], hs_i[:, 4 * ct:4 * ct + 4, :], ["hs"], ["hs"]
                else:
                    sr, si, ksr, ksi = Xr4[:, :, :, i - 1], Xi4[:, :, :, i - 1], kXSr, kXSi
                cmul_add("dve", Xr4[:, :, :, i], Xi4[:, :, :, i], sr, si, lr, li, lin, kXSr, kXSi, ksr, ksi, T, kT)
            dve(lambda e: e.tensor_copy(out=hs_r[:, 4 * ct:4 * ct + 4, :], in_=Xr4[:, :, :, 3]), kXS, ["hs"])
            dve(lambda e: e.tensor_copy(out=hs_i[:, 4 * ct:4 * ct + 4, :], in_=Xi4[:, :, :, 3]), kXS, ["hs"])
            act(lambda e: e.copy(out=xbr[:], in_=XSr), kXS, kxbr)
            act(lambda e: e.copy(out=xbi[:], in_=XSi), kXS, kxbi)
            return ssm_y(l, ct, N)

        def layer_block(l, N, sample, blk):
            first_block = (blk == 0)
            last_block = (blk == NBLK - 1)
            wvin = wb_in[l].rearrange("(k p) c -> p k c", p=128)
            v8 = lambda ws: WS[ws][:].rearrange("p (k c) -> p k c", k=8)
            norm_block(N, gA[:, l, :])
            ws = wslot()
            wq = WS[ws][:].rearrange("p (k t two d) -> p k t two d", k=8, t=4, two=2)
            for two in range(2):
                for k in range(8):
                    S.op(dq(), lambda e, two=two, k=k: e.dma_start(
                        out=wq[:, k, :, two, :], in_=wvin[:, k, 256 * two:256 * two + 256].rearrange("p (t d) -> p t d", d=64)),
                        reads=[("wb_in", l)], writes=wk(ws), dma=("ws", ws))
            wqv = v8(ws)
            for t in range(4):
                b = proj_tile(N, wqv, 128 * t, ws)
                headnorm_rope(N, b, l, gq[:, l:l + 1], blk_b, 1.0 / 64, True, qr[:, t, 0:N], ["qr"])
                yield
            ws = load_w(lambda w: v8(w)[:, :, 0:256], wvin[:, :, K_OFF:K_OFF + 256], ("wb_in", l))
            wkv_ = v8(ws)
            b = proj_tile(N, wkv_, 0, ws)
            kdst = kTc[:, l, 128:128 + N] if not sample else kTc[:, l, 0:N]
            headnorm_rope(N, b, l, gk[:, l:l + 1], blk_b, 1.0 / 64, True, kdst, [kcar("kTc", S.cur_ns)], out32=k32[:, 0:N], out32_keys=["k32"])
            nsub = max(1, N // 128)
            pn = min(N, 128)
            b = bank()
            for s in range(nsub):
                for k in range(8):
                    S.op("pe", lambda e, s=s, k=k, b=b: e.matmul(PS[b][0:pn, 128 * s:128 * s + 128], lhsT=hT[:, k, 128 * s:128 * s + pn],
                                                                rhs=wkv_[:, k, 128:256], start=(k == 0), stop=(k == 7)),
                         reads=["hT", ("ws", ws)], writes=[("ps", b)])
            act(lambda e, b=b: e.copy(out=v32[0:pn, 0:nsub, :], in_=PS[b][0:pn, 0:128 * nsub].rearrange("p (s c) -> p s c", c=128)),
                [("ps", b)], ["v32"])
            vdst = vtc[0:pn, l, 1:1 + nsub, :] if not sample else vtc[0:pn, l, 0:1, :]
            dve(lambda e: e.tensor_copy(out=vdst, in_=v32[0:pn, 0:nsub, :]), ["v32"], [kcar("vtc", S.cur_ns)])
            if not sample:
                for s in range(nsub):
                    for h in range(2):
                        hs = slice(64 * h, 64 * h + 64)
                        pt = PT[(2 * s + h) % 2]; kpt = "PT%d" % ((2 * s + h) % 2)
                        use_prev = not (first_block and s == 0)
                        parts = ([0] if use_prev else []) + [1]
                        sbk = {}
                        for part in parts:
                            b = bank(); sbk[part] = b
                            c0 = 128 * s + 128 * part
                            S.op("pe", lambda e, b=b, c0=c0, hs=hs, s=s: e.matmul(
                                PS[b][:].rearrange("p (t c) -> p t c", t=4), lhsT=kTc[hs, l, c0:c0 + 128],
                                rhs=qr[hs, :, 128 * s:128 * s + 128], start=True, stop=True),
                                reads=[kcar("kTc", S.cur_ns), "qr"], writes=[("ps", b)])
                            act(lambda e, b=b, part=part, pt=pt: e.activation(out=pt[:, part, :], in_=PS[b][:], func=AF.Exp, scale=0.125),
                                [("ps", b)], [kpt])
                            mk_ = mprev_b if part == 0 else mcur_b
                            S.op("pool", lambda e, part=part, pt=pt, mk_=mk_: e.tensor_tensor(
                                out=pt[:, part, :].rearrange("p (t c) -> p t c", t=4), in0=pt[:, part, :].rearrange("p (t c) -> p t c", t=4),
                                in1=mk_[:].rearrange("p (o c) -> p o c", o=1).to_broadcast([128, 4, 128]), op=ALU.mult),
                                reads=[kpt, "mprev_b", "mcur_b"], writes=[kpt])
                        bo = bank(); bd = bank()
                        for n_, part in enumerate(parts):
                            S.op("pe", lambda e, part=part, n_=n_, bo=bo, pt=pt, s=s, hs=hs: e.matmul(
                                PS[bo][hs, :], lhsT=vtc[:, l, s + part, hs], rhs=pt[:, part, :], start=(n_ == 0), stop=(n_ == len(parts) - 1)),
                                reads=[kcar("vtc", S.cur_ns), kpt], writes=[("ps", bo)])
                        for n_, part in enumerate(parts):
                            S.op("pe", lambda e, part=part, n_=n_, bd=bd, pt=pt, hs=hs: e.matmul(
                                PS[bd][hs, :], lhsT=ones_b[:, hs], rhs=pt[:, part, :], start=(n_ == 0), stop=(n_ == len(parts) - 1)),
                                reads=["ones_b", kpt], writes=[("ps", bd)])
                        dd = dn[h]; kd = "dn%d" % h
                        dve(lambda e, bd=bd, dd=dd, hs=hs: e.tensor_tensor(
                            out=dd[hs, :].rearrange("p (t c) -> p t c", t=4), in0=PS[bd][hs, :].rearrange("p (t c) -> p t c", t=4),
                            in1=esink[hs, l, :].rearrange("p (t o) -> p t o", o=1).to_broadcast([64, 4, 128]), op=ALU.add),
                            [("ps", bd), "esink"], [kd])
                        dve(lambda e, dd=dd, hs=hs: e.reciprocal(out=dd[hs, :], in_=dd[hs, :]), [kd], [kd])
                        dve(lambda e, bo=bo, dd=dd, hs=hs, s=s: e.tensor_tensor(
                            out=oa[hs, :, 128 * s:128 * s + 128], in0=PS[bo][hs, :].rearrange("p (t c) -> p t c", t=4),
                            in1=dd[hs, :].rearrange("p (t c) -> p t c", t=4), op=ALU.mult), [("ps", bo), kd], ["oa"])
                        yield
                if last_block:
                    b = bank()
                    S.op("pe", lambda e, b=b: e.transpose(out=PS[b][:, 0:128], in_=k32[:, N - 128:N], identity=ident_f[:]),
                         reads=["k32", "ident_f"], writes=[("ps", b)])
                    act(lambda e, b=b: e.copy(out=t1[:, 0:128], in_=PS[b][:, 0:128]), [("ps", b)], ["t1"])
                    out_toks.append(S.op(dq(), lambda e: e.dma_start(out=kp[l], in_=t1[:, 0:128]), reads=["t1"], dma="o_kp"))
                    out_toks.append(S.op(dq(), lambda e: e.dma_start(out=vp[l], in_=v32[:, nsub - 1, :]), reads=["v32"], dma="o_vp"))
                else:
                    act(lambda e: e.copy(out=kTo[:, l, 0:128], in_=kTc[:, l, N:N + 128]), [kcar("kTc", S.cur_ns)], [kcar("kTc", 1 - S.cur_ns)])
                    act(lambda e: e.copy(out=vto[:, l, 0, :], in_=vtc[:, l, nsub, :]), [kcar("vtc", S.cur_ns)], [kcar("vtc", 1 - S.cur_ns)])
            else:
                b = bank()
                S.op("pe", lambda e, b=b: e.transpose(out=PS[b][0:NS, 0:128], in_=k32[:, 0:NS], identity=ident_f[:]),
                     reads=["k32", "ident_f"], writes=[("ps", b)])
                act(lambda e, b=b: e.copy(out=t1[0:NS, 0:128], in_=PS[b][0:NS, 0:128]), [("ps", b)], ["t1"])
                for bb in range(NSB):
                    out_toks.append(S.op(dq(), lambda e, bb=bb: e.dma_start(out=ks[l, bb, 124:128, :], in_=t1[4 * bb:4 * bb + 4, 0:128]),
                                         reads=["t1"], dma="o_ks"))
                    out_toks.append(S.op(dq(), lambda e, bb=bb: e.dma_start(out=vs[l, bb, 124:128, :], in_=v32[4 * bb:4 * bb + 4, 0, :]),
                                         reads=["v32"], dma="o_vs"))
                out_toks.append(S.op(dq(), lambda e: e.dma_start(out=ks[l, :, 0:124, :], in_=csk[l, :, 4:128, :]), dma="o_ks"))
                out_toks.append(S.op(dq(), lambda e: e.dma_start(out=vs[l, :, 0:124, :], in_=csv[l, :, 4:128, :]), dma="o_vs"))
                ptn = PT[0]; ptc = PT[1]
                for h in range(2):
                    hs = slice(64 * h, 64 * h + 64)
                    b = bank()
                    S.op("pe", lambda e, b=b, hs=hs: e.matmul(PS[b][0:NS, 0:4 * NS].rearrange("p (t c) -> p t c", t=4),
                                                             lhsT=kTc[hs, l, 0:NS], rhs=qr[hs, :, 0:NS], start=True, stop=True),
                         reads=[kcar("kTc", S.cur_ns), "qr"], writes=[("ps", b)])
                    act(lambda e, b=b, h=h: e.activation(out=ptn[0:NS, h, 0:4 * NS], in_=PS[b][0:NS, 0:4 * NS], func=AF.Exp, scale=0.125),
                        [("ps", b)], ["PT0"])
                    dve(lambda e, h=h: e.tensor_tensor(
                        out=ptn[0:NS, h, 0:4 * NS].rearrange("p (t c) -> p t c", t=4), in0=ptn[0:NS, h, 0:4 * NS].rearrange("p (t c) -> p t c", t=4),
                        in1=mnew_b[:].rearrange("p (o c) -> p o c", o=1).to_broadcast([NS, 4, NS]), op=ALU.mult), ["PT0", "mnew_b"], ["PT0"])
                bsc = bank(); held.add(bsc)
                for bb in range(NSB):
                    kst = sg[bb % 2]; kk_ = "sg%d" % (bb % 2)
                    S.op("pool", lambda e, bb=bb, kst=kst: e.dma_start(out=kst[:, 0:128], in_=csk[l, bb]), writes=[kk_], dma=kk_)
                    S.op("pool", lambda e, bb=bb, kst=kst: e.dma_start(out=kst[:, 128:256], in_=csv[l, bb]), writes=[kk_], dma=kk_)
                    b = bank()
                    pb = PS[b][:].bitcast(BF16)
                    S.op("pe", lambda e, kst=kst, pb=pb: e.transpose(out=pb[:, 0:128], in_=kst[:, 0:128], identity=ident_b[:]),
                         reads=[kk_, "ident_b"], writes=[("ps", b)])
                    act(lambda e, pb=pb, bb=bb: e.copy(out=kcT[:, bb % 2, :], in_=pb[:, 0:128]), [("ps", b)], ["kcT%d" % (bb % 2)])
                    for h in range(2):
                        hs = slice(64 * h, 64 * h + 64)
                        c0 = (bb * 2 + h) * 16
                        S.op("pe", lambda e, bb=bb, hs=hs, c0=c0: e.matmul(
                            PS[bsc][:, c0:c0 + 16].rearrange("p (t i) -> p t i", t=4), lhsT=kcT[hs, bb % 2, :],
                            rhs=qr[hs, :, 4 * bb:4 * bb + 4], start=True, stop=True),
                            reads=["kcT%d" % (bb % 2), "qr"], writes=[("ps", bsc)])
                    dve(lambda e, bb=bb, kst=kst: e.tensor_copy(out=RA[:, 128 * bb:128 * bb + 128], in_=kst[:, 128:256]), [kk_], RAK[0:8])
                act(lambda e: e.activation(out=ptc[:, 0, :], in_=PS[bsc][:], func=AF.Exp, scale=0.125), [("ps", bsc)], ["PT1"])
                held.discard(bsc)
                dve(lambda e: e.tensor_tensor(
                    out=ptc[:, 0, :].rearrange("p (a i) -> p a i", i=4), in0=ptc[:, 0, :].rearrange("p (a i) -> p a i", i=4),
                    in1=mc_b[:].rearrange("p (o i) -> p o i", o=1).to_broadcast([128, 128, 4]), op=ALU.mult), ["PT1", "mc_b"], ["PT1"])
                for h in range(2):
                    hs = slice(64 * h, 64 * h + 64)
                    for which, lw in ((0, None), (1, None)):
                        bo = bank()
                        lhs_new = vtc[0:NS, l, 0, hs] if which == 0 else ones_b[0:NS, hs]
                        S.op("pe", lambda e, bo=bo, hs=hs, h=h, lhs_new=lhs_new: e.matmul(
                            PS[bo][hs, 0:4 * NS], lhsT=lhs_new, rhs=ptn[0:NS, h, 0:4 * NS], start=True, stop=False),
                            reads=[kcar("vtc", S.cur_ns), "ones_b", "PT0"], writes=[("ps", bo)])
                        for bb in range(NSB):
                            c0 = (bb * 2 + h) * 16
                            lhs_c = RA[:, 128 * bb + 64 * h:128 * bb + 64 * h + 64] if which == 0 else ones_b[:, hs]
                            S.op("pe", lambda e, bo=bo, hs=hs, bb=bb, c0=c0, lhs_c=lhs_c: e.matmul(
                                PS[bo][hs, 0:4 * NS].rearrange("p (t c) -> p t c", t=4)[:, :, 4 * bb:4 * bb + 4], lhsT=lhs_c,
                                rhs=ptc[:, 0, c0:c0 + 16].rearrange("p (t i) -> p t i", t=4), start=False, stop=(bb == NSB - 1)),
                                reads=RAK[0:8] + ["ones_b", "PT1"], writes=[("ps", bo)])
                        if which == 0:
                            bnum = bo
                        else:
                            bden = bo
                    dd = dn[h]; kd = "dn%d" % h
                    dve(lambda e, bden=bden, dd=dd, hs=hs: e.tensor_tensor(
                        out=dd[hs, 0:4 * NS].rearrange("p (t c) -> p t c", t=4), in0=PS[bden][hs, 0:4 * NS].rearrange("p (t c) -> p t c", t=4),
                        in1=esink[hs, l, :].rearrange("p (t o) -> p t o", o=1).to_broadcast([64, 4, NS]), op=ALU.add),
                        [("ps", bden), "esink"], [kd])
                    dve(lambda e, dd=dd, hs=hs: e.reciprocal(out=dd[hs, 0:4 * NS], in_=dd[hs, 0:4 * NS]), [kd], [kd])
                    dve(lambda e, bnum=bnum, dd=dd, hs=hs: e.tensor_tensor(
                        out=oa[hs, :, 0:NS], in0=PS[bnum][hs, 0:4 * NS].rearrange("p (t c) -> p t c", t=4),
                        in1=dd[hs, 0:4 * NS].rearrange("p (t c) -> p t c", t=4), op=ALU.mult), [("ps", bnum), kd], ["oa"])
            ws = load_w(lambda w: v8(w), wvin[:, :, U_OFF:U_OFF + 512], ("wb_in", l))
            for t in range(4):
                b = proj_tile(N, v8(ws), 128 * t, ws)
                act(lambda e, b=b, t=t: e.copy(out=uT[:, t, 0:N], in_=PS[b][:, 0:N]), [("ps", b)], ["uT"])
                yield
            if sample:
                for gl in range(2):
                    sl = slice(64 * gl, 64 * gl + 64)
                    for bb in range(NSB):
                        S.op(dq(), lambda e, gl=gl, sl=sl, bb=bb: e.dma_start(
                            out=hs_r[sl, :, bb], in_=sre[l, bb].rearrange("(tp gl) p -> gl p tp", gl=2)[gl], allow_slow_non_contiguous=True),
                            writes=["hs"], dma="hs")
                        S.op(dq(), lambda e, gl=gl, sl=sl, bb=bb: e.dma_start(
                            out=hs_i[sl, :, bb], in_=sim[l, bb].rearrange("(tp gl) p -> gl p tp", gl=2)[gl], allow_slow_non_contiguous=True),
                            writes=["hs"], dma="hs")
            for ct in range(4):
                if sample:
                    by = ssm_group_sample(l, ct)
                else:
                    yield from ssm_group_prompt(l, ct, N, first_block)
                    by = ybank.pop()
                act(lambda e, by=by, ct=ct: e.activation(out=zT[:, ct, 0:N], in_=PS[by][:, 0:N], func=AF.Gelu), [("ps", by)], ["zT"])
                yield
            if sample:
                for gl in range(2):
                    sl = slice(64 * gl, 64 * gl + 64)
                    for bb in range(NSB):
                        out_toks.append(S.op(dq(), lambda e, gl=gl, sl=sl, bb=bb: e.dma_start(
                            out=hrs[l, bb].rearrange("(tp gl) p -> gl p tp", gl=2)[gl], in_=hs_r[sl, :, bb], allow_slow_non_contiguous=True),
                            reads=["hs"], dma="o_hs"))
                        out_toks.append(S.op(dq(), lambda e, gl=gl, sl=sl, bb=bb: e.dma_start(
                            out=his[l, bb].rearrange("(tp gl) p -> gl p tp", gl=2)[gl], in_=hs_i[sl, :, bb], allow_slow_non_contiguous=True),
                            reads=["hs"], dma="o_hs"))
            elif last_block:
                for gl in range(2):
                    sl = slice(64 * gl, 64 * gl + 64)
                    out_toks.append(S.op(dq(), lambda e, gl=gl, sl=sl: e.dma_start(
                        out=hrp[l].rearrange("(tp gl) p -> gl p tp", gl=2)[gl], in_=car_r[sl, l, :], allow_slow_non_contiguous=True),
                        reads=["car"], dma="o_hp"))
                    out_toks.append(S.op(dq(), lambda e, gl=gl, sl=sl: e.dma_start(
                        out=hip[l].rearrange("(tp gl) p -> gl p tp", gl=2)[gl], in_=car_i[sl, l, :], allow_slow_non_contiguous=True),
                        reads=["car"], dma="o_hp"))
            ws = load_w(lambda w: WS[w][:, 0:2048].rearrange("p (k c) -> p k c", k=4), wb_glu[l].rearrange("(k p) c -> p k c", p=128),
                        ("wb_glu", l))
            wg = WS[ws][:, 0:2048].rearrange("p (k c) -> p k c", k=4)
            for t in range(4):
                b = bank()
                for k in range(4):
                    S.op("pe", lambda e, b=b, k=k, t=t: e.matmul(PS[b][:, 0:N], lhsT=wg[:, k, 128 * t:128 * t + 128], rhs=zT[:, k, 0:N],
                                                                start=(k == 0), stop=(k == 3)), reads=["zT", ("ws", ws)], writes=[("ps", b)])
                s_ = sg[t % 2]; ks_ = "sg%d" % (t % 2)
                act(lambda e, b=b, s_=s_: e.activation(out=s_[:, 0:N], in_=PS[b][:, 0:N], func=AF.Sigmoid), [("ps", b)], [ks_])
                dve(lambda e, t=t, s_=s_: e.tensor_tensor(out=ob[:, t, 0:N], in0=zT[:, t, 0:N], in1=s_[:, 0:N], op=ALU.mult), ["zT", ks_], ["ob"])
                yield
            ws = load_w(lambda w: v8(w), wvin[:, :, MQ_OFF:MQ_OFF + 512], ("wb_in", l))
            for t in range(4):
                b = proj_tile(N, v8(ws), 128 * t, ws)
                headnorm_rope(N, b, l, gmq[:, l:l + 1], ones_b, 1.0 / 128, False, qmn[:, t, 0:N], ["qmn"])
                yield
            sc_m = 1.0 / math.sqrt(128.0)
            if not sample:
                for h in range(4):
                    pt = PT[h % 2]; kpt = "PT%d" % (h % 2)
                    for kt in range(2):
                        b = bank()
                        S.op("pe", lambda e, b=b, h=h, kt=kt: e.matmul(PS[b][:, 0:N], lhsT=MKT[:, l, h, 128 * kt:128 * kt + 128], rhs=qmn[:, h, 0:N],
                                                                      start=True, stop=True), reads=["MKT", "qmn"], writes=[("ps", b)])
                        act(lambda e, b=b, kt=kt, pt=pt: e.activation(out=pt[:, kt, 0:N], in_=PS[b][:, 0:N], func=AF.Exp, scale=sc_m),
                            [("ps", b)], [kpt])
                    bo = bank(); bd = bank()
                    for kt in range(2):
                        S.op("pe", lambda e, bo=bo, h=h, kt=kt, pt=pt: e.matmul(PS[bo][:, 0:N], lhsT=MV[:, l, kt, 128 * h:128 * h + 128],
                                                                               rhs=pt[:, kt, 0:N], start=(kt == 0), stop=(kt == 1)),
                             reads=["MV", kpt], writes=[("ps", bo)])
                    for kt in range(2):
                        S.op("pe", lambda e, bd=bd, kt=kt, pt=pt: e.matmul(PS[bd][:, 0:N], lhsT=ones_b[:], rhs=pt[:, kt, 0:N],
                                                                          start=(kt == 0), stop=(kt == 1)),
                             reads=["ones_b", kpt], writes=[("ps", bd)])
                    dd = dn[h % 2]; kd = "dn%d" % (h % 2)
                    dve(lambda e, bd=bd, dd=dd: e.reciprocal(out=dd[:, 0:N], in_=PS[bd][:, 0:N]), [("ps", bd)], [kd])
                    dve(lambda e, bo=bo, dd=dd, h=h: e.tensor_tensor(out=oc[:, h, 0:N], in0=PS[bo][:, 0:N], in1=dd[:, 0:N], op=ALU.mult),
                        [("ps", bo), kd], ["oc"])
                    yield
            else:
                bsc = bank(); held.add(bsc)
                bo = bank(); held.add(bo)
                ptc = PT[1]
                for bb in range(NSB):
                    wsk = wslot()
                    kcb = WS[wsk][:, 0:1024].rearrange("p (a c) -> p a c", a=2)
                    vcb = WS[wsk][:, 1024:2048].rearrange("p (a c) -> p a c", a=2)
                    S.op("pool", lambda e, bb=bb, kcb=kcb: e.dma_start(out=kcb, in_=cmk[l, bb].rearrange("(a p) c -> p a c", p=128)),
                         writes=wk(wsk), dma=("ws", wsk))
                    S.op("pool", lambda e, bb=bb, vcb=vcb: e.dma_start(out=vcb, in_=cmv[l, bb].rearrange("(a p) c -> p a c", p=128)),
                         writes=wk(wsk), dma=("ws", wsk))
                    for h in range(4):
                        for kt in range(2):
                            b = bank()
                            pb = PS[b][:].bitcast(BF16)
                            o0 = 2048 + 256 * h + 128 * kt
                            S.op("pe", lambda e, pb=pb, kcb=kcb, h=h, kt=kt: e.transpose(out=pb[:, 0:128], in_=kcb[:, kt, 128 * h:128 * h + 128],
                                                                                        identity=ident_b[:]),
                                 reads=[("ws", wsk), "ident_b"], writes=[("ps", b)])
                            act(lambda e, pb=pb, wsk=wsk, o0=o0: e.copy(out=WS[wsk][:, o0:o0 + 128], in_=pb[:, 0:128]),
                                [("ps", b)], [("wsT", wsk)])
                    for h in range(4):
                        for kt in range(2):
                            c0 = ((bb * 4 + h) * 2 + kt) * 4
                            o0 = 2048 + 256 * h + 128 * kt
                            S.op("pe", lambda e, h=h, c0=c0, bb=bb, wsk=wsk, o0=o0: e.matmul(
                                PS[bsc][:, c0:c0 + 4], lhsT=WS[wsk][:, o0:o0 + 128],
                                rhs=qmn[:, h, 4 * bb:4 * bb + 4], start=True, stop=True),
                                reads=[("wsT", wsk), "qmn"], writes=[("ps", bsc)])
                    c0 = bb * 32
                    act(lambda e, c0=c0: e.activation(out=ptc[:, 0, c0:c0 + 32], in_=PS[bsc][:, c0:c0 + 32], func=AF.Exp, scale=sc_m),
                        [("ps", bsc)], ["PT1"])
                    for h in range(4):
                        for kt in range(2):
                            c1 = ((bb * 4 + h) * 2 + kt) * 4
                            S.op("pe", lambda e, h=h, kt=kt, c1=c1, bb=bb, vcb=vcb: e.matmul(
                                PS[bo][:, h * NS + 4 * bb:h * NS + 4 * bb + 4], lhsT=vcb[:, kt, 128 * h:128 * h + 128],
                                rhs=ptc[:, 0, c1:c1 + 4], start=(kt == 0), stop=(kt == 1)),
                                reads=[("ws", wsk), "PT1"], writes=[("ps", bo)])
                bd = bank()
                pv = ptc[:, 0, :].rearrange("p (b h kt i) -> p h kt b i", b=NSB, h=4, kt=2)
                for h in range(4):
                    for kt in range(2):
                        S.op("pe", lambda e, bd=bd, kt=kt, h=h: e.matmul(PS[bd][:, h * NS:(h + 1) * NS].rearrange("p (b i) -> p b i", i=4),
                                                                        lhsT=ones_b[:], rhs=pv[:, h, kt, :, :], start=(kt == 0), stop=(kt == 1)),
                             reads=["ones_b", "PT1"], writes=[("ps", bd)])
                dd = dn[0]
                dve(lambda e, bd=bd: e.reciprocal(out=dd[:, 0:4 * NS], in_=PS[bd][:, 0:4 * NS]), [("ps", bd)], ["dn0"])
                dve(lambda e, bo=bo: e.tensor_tensor(out=oc[:, :, 0:NS], in0=PS[bo][:, 0:4 * NS].rearrange("p (h c) -> p h c", h=4),
                                                     in1=dd[:, 0:4 * NS].rearrange("p (h c) -> p h c", h=4), op=ALU.mult),
                    [("ps", bo), "dn0"], ["oc"])
                held.discard(bsc); held.discard(bo)
            yield "HALF"
            macc3 = macc.rearrange("p (m n) -> p m n", m=8)
            mgT3 = mgT.rearrange("p (m n) -> p m n", m=8)
            for n_, on_ in enumerate((oa, ob, oc)):
                okey = ("oa", "ob", "oc")[n_]
                for half in range(2):
                    wsg = load_w(lambda w: v8(w), wvin[:, :, G_OFF + n_ * 1024 + 512 * half:G_OFF + n_ * 1024 + 512 * half + 512], ("wb_in", l))
                    wsb = wslot()
                    wbv = WS[wsb][:, 0:2048].rearrange("p (k c) -> p k c", k=4)
                    cs_ = slice(512 * half, 512 * half + 512)
                    if n_ == 0:
                        for t in range(4):
                            for two in range(2):
                                r0 = (two * 4 + t) * 64
                                S.op(dq(), lambda e, t=t, two=two, r0=r0: e.dma_start(out=wbv[64 * two:64 * two + 64, t, :],
                                                                                     in_=wb_br[l, 0, r0:r0 + 64, cs_]),
                                     reads=[("wb_br", l)], writes=wk(wsb), dma=("ws", wsb))
                    else:
                        S.op(dq(), lambda e, n_=n_: e.dma_start(out=wbv, in_=wb_br[l, n_].rearrange("(k p) c -> p k c", p=128)[:, :, cs_]),
                             reads=[("wb_br", l)], writes=wk(wsb), dma=("ws", wsb))
                    for mm_ in range(4):
                        m = 4 * half + mm_
                        bg = proj_tile(N, v8(wsg), 128 * mm_, wsg)
                        s_ = sg[m % 2]; ks_ = "sg%d" % (m % 2)
                        act(lambda e, bg=bg, s_=s_: e.activation(out=s_[:, 0:N], in_=PS[bg][:, 0:N], func=AF.Sigmoid), [("ps", bg)], [ks_])
                        bp = bank()
                        for k in range(4):
                            S.op("pe", lambda e, bp=bp, k=k, mm_=mm_, on_=on_: e.matmul(PS[bp][:, 0:N], lhsT=wbv[:, k, 128 * mm_:128 * mm_ + 128],
                                                                                    rhs=on_[:, k, 0:N], start=(k == 0), stop=(k == 3)),
                                 reads=[okey, ("ws", wsb)], writes=[("ps", bp)])
                        if n_ == 0:
                            dve(lambda e, bp=bp, s_=s_, m=m: e.tensor_tensor(out=macc3[:, m, 0:N], in0=PS[bp][:, 0:N], in1=s_[:, 0:N], op=ALU.mult),
                                [("ps", bp), ks_], kmacc)
                        else:
                            dve(lambda e, bp=bp, s_=s_: e.tensor_tensor(out=t1[:, 0:N], in0=PS[bp][:, 0:N], in1=s_[:, 0:N], op=ALU.mult),
                                [("ps", bp), ks_], ["t1"])
                            if n_ == 1:
                                dve(lambda e, m=m: e.tensor_tensor(out=macc3[:, m, 0:N], in0=macc3[:, m, 0:N], in1=t1[:, 0:N], op=ALU.add),
                                    kmacc + ["t1"], kmacc)
                            else:
                                dve(lambda e, m=m: e.tensor_tensor(out=mgT3[:, m, 0:N], in0=macc3[:, m, 0:N], in1=t1[:, 0:N], op=ALU.add),
                                    kmacc + ["t1"], kmgT)
                        yield
            for half in range(2):
                wso = load_w(lambda w: v8(w), wb_out[l].rearrange("(k p) c -> p k c", p=128)[:, :, 512 * half:512 * half + 512], ("wb_out", l))
                for mm_ in range(4):
                    m = 4 * half + mm_
                    b = bank()
                    for k in range(8):
                        S.op("pe", lambda e, b=b, k=k, mm_=mm_, wso=wso: e.matmul(PS[b][:, 0:N], lhsT=v8(wso)[:, k, 128 * mm_:128 * mm_ + 128],
                                                                                 rhs=mgT3[:, k, 0:N], start=(k == 0), stop=(k == 7)),
                             reads=kmgT + [("ws", wso)], writes=[("ps", b)])
                    dve(lambda e, b=b, m=m: e.tensor_tensor(out=xT[:, m, 0:N], in0=xT[:, m, 0:N], in1=PS[b][:, 0:N], op=ALU.add),
                        [("ps", b), "xT"], ["xT"])
                    yield
            norm_block(N, gF[:, l, :])
            wvup = wb_up[l].rearrange("(k p) c -> p k c", p=128)

            def actT(j):
                return RA[:, j * TB:(j + 1) * TB], [RAK[j]]

            for grp in range(6):
                nt = 4 if grp < 5 else 2
                wsg = load_w(lambda w: v8(w)[:, :, 0:128 * nt], wvup[:, :, 512 * grp:512 * grp + 128 * nt], ("wb_up", l))
                wsu = load_w(lambda w: v8(w)[:, :, 0:128 * nt], wvup[:, :, DFF + 512 * grp:DFF + 512 * grp + 128 * nt], ("wb_up", l))
                for jj in range(nt):
                    j = 4 * grp + jj
                    bg = proj_tile(N, v8(wsg), 128 * jj, wsg)
                    bu = proj_tile(N, v8(wsu), 128 * jj, wsu)
                    s_ = sg[j % 2]; ks_ = "sg%d" % (j % 2)
                    act(lambda e, bg=bg, s_=s_: e.activation(out=s_[:, 0:N], in_=PS[bg][:, 0:N], func=AF.Silu), [("ps", bg)], [ks_])
                    av, ak = actT(j)
                    dve(lambda e, bu=bu, s_=s_, av=av: e.tensor_tensor(out=av[:, 0:N], in0=PS[bu][:, 0:N], in1=s_[:, 0:N], op=ALU.mult),
                        [("ps", bu), ks_], ak)
                    yield
            wvdn = wb_dn[l].rearrange("(k p) c -> p k c", p=128)
            for q4 in range(4):
                wsl = []
                for hh in range(2):
                    w_ = wslot()
                    S.op(dq(), lambda e, w_=w_, hh=hh, q4=q4: e.dma_start(
                        out=WS[w_][:, 0:11 * 256].rearrange("p (k c) -> p k c", k=11), in_=wvdn[:, 11 * hh:11 * hh + 11, 256 * q4:256 * q4 + 256]),
                        reads=[("wb_dn", l)], writes=wk(w_), dma=("ws", w_))
                    wsl.append(w_)
                for mm_ in range(2):
                    m = 2 * q4 + mm_
                    b = bank()
                    for j in range(22):
                        w_ = wsl[j // 11]
                        wv_ = WS[w_][:, 0:11 * 256].rearrange("p (k c) -> p k c", k=11)
                        av, ak = actT(j)
                        S.op("pe", lambda e, b=b, j=j, mm_=mm_, wv_=wv_, av=av: e.matmul(
                            PS[b][:, 0:N], lhsT=wv_[:, j % 11, 128 * mm_:128 * mm_ + 128], rhs=av[:, 0:N], start=(j == 0), stop=(j == 21)),
                            reads=ak + [("ws", w_)], writes=[("ps", b)])
                    dve(lambda e, b=b, m=m: e.tensor_tensor(out=xT[:, m, 0:N], in0=xT[:, m, 0:N], in1=PS[b][:, 0:N], op=ALU.add),
                        [("ps", b), "xT"], ["xT"])
                    yield


        def run_block(blk, sample):
            N = NS if sample else TB
            nsub = max(1, N // 128)
            pn = min(N, 128)
            if sample:
                S.op(dq(), lambda e: e.dma_start(out=xtok[0:NS, 0, :], in_=xs), writes=["xtok"], dma="xtok")
                S.op(dq(), lambda e: e.dma_start(out=cosb[:, 0:NS], in_=c_cos_s), writes=["cosb"], dma="cosb")
                S.op(dq(), lambda e: e.dma_start(out=sinb[:, 0:NS], in_=c_sin_s), writes=["sinb"], dma="sinb")
            else:
                t0 = blk * TB
                S.op(dq(), lambda e: e.dma_start(out=xtok[:], in_=xp[t0:t0 + TB, :].rearrange("(s p) d -> p s d", p=128)),
                     writes=["xtok"], dma="xtok")
                S.op(dq(), lambda e: e.dma_start(out=cosb[:], in_=c_cos[:, t0:t0 + TB]), writes=["cosb"], dma="cosb")
                S.op(dq(), lambda e: e.dma_start(out=sinb[:], in_=c_sin[:, t0:t0 + TB]), writes=["sinb"], dma="sinb")
            for s in range(nsub):
                for k in range(8):
                    b = bank()
                    S.op("pe", lambda e, s=s, k=k, b=b: e.transpose(out=PS[b][:, 0:pn], in_=xtok[0:pn, s, 128 * k:128 * k + 128],
                                                                    identity=ident_f[0:pn, 0:pn]),
                         reads=["xtok", "ident_f"], writes=[("ps", b)])
                    act(lambda e, s=s, k=k, b=b: e.copy(out=xT[:, k, 128 * s:128 * s + pn], in_=PS[b][:, 0:pn]), [("ps", b)], ["xT"])
            yield
            for l in range(NL):
                yield from layer_block(l, N, sample, blk)
                if l < NL - 1:
                    yield "HALF"
            for s in range(nsub):
                for k in range(8):
                    b = bank()
                    S.op("pe", lambda e, s=s, k=k, b=b: e.transpose(out=PS[b][0:pn, 0:128], in_=xT[:, k, 128 * s:128 * s + pn],
                                                                    identity=ident_f[:]),
                         reads=["xT", "ident_f"], writes=[("ps", b)])
                    act(lambda e, s=s, k=k, b=b: e.copy(out=xtok[0:pn, s, 128 * k:128 * k + 128], in_=PS[b][0:pn, 0:128]),
                        [("ps", b)], ["xtok"])
            if sample:
                out_toks.append(S.op(dq(), lambda e: e.dma_start(out=ys, in_=xtok[0:NS, 0, :]), reads=["xtok"], dma="o_y"))
            else:
                t0 = blk * TB
                out_toks.append(S.op(dq(), lambda e: e.dma_start(out=yp[t0:t0 + TB, :].rearrange("(s p) d -> p s d", p=128), in_=xtok[:]),
                                     reads=["xtok"], dma="o_y"))

            yield "HALF"

        def stream_gen(blocks):
            for blk in blocks:
                yield from run_block(blk, False)

        gens = [stream_gen(range(0, NBLK, 2)), stream_gen(range(1, NBLK, 2))]
        alive = [True, True]
        half_no = [0, 0]
        est = {}

        def step(sid):
            bind(sid)
            try:
                r = next(gens[sid])
            except StopIteration:
                alive[sid] = False
                return False
            return r != "HALF"

        def run_half_alone(sid):
            n = 0
            while step(sid):
                n += 1
            est[half_no[sid] % 2] = n + 1
            half_no[sid] += 1

        def run_pair():
            cnt = [0, 0]
            act_ = [alive[0], alive[1]]
            want = [est.get(half_no[i] % 2, 64) for i in range(2)]
            while act_[0] or act_[1]:
                if act_[0] and act_[1]:
                    sid = 0 if cnt[0] * want[1] <= cnt[1] * want[0] else 1
                else:
                    sid = 0 if act_[0] else 1
                cnt[sid] += 1
                if not step(sid):
                    act_[sid] = False
            for i in range(2):
                if cnt[i] > 0:
                    est[half_no[i] % 2] = cnt[i]
                    half_no[i] += 1

        run_half_alone(0)
        while alive[0] or alive[1]:
            run_pair()
        bind(0)
        for _ in run_block(0, True):
            bind(0)
        S.wait_all("sp", out_toks)
        with nc.allow_non_contiguous_dma(reason="small strided parameter / state transfers"):
            S.emit()
    return nc


_NC_CACHE = {}


def _consts():
    c = {}
    c["c_ident"] = np.eye(128, dtype=np.float32)
    blk = np.zeros((128, 128), np.float32)
    blk[:64, :64] = 1.0
    blk[64:, 64:] = 1.0
    c["c_blk"] = blk
    p = np.arange(128)
    lo = (p % 64) < 32
    partner = np.where(lo, p + 32, p - 32)
    perm = np.zeros((128, 128), np.float32)
    perm[partner, p] = 1.0
    c["c_perm"] = perm
    j = np.arange(128)[:, None]
    i = np.arange(128)[None, :]
    c["c_mprev"] = (j > i).astype(np.float32)
    c["c_mcur"] = (j <= i).astype(np.float32)
    half = 32
    inv = (np.float32(10000.0) ** (-(np.arange(half, dtype=np.float32) / np.float32(half)))).astype(np.float32)
    invp = inv[p % 32]
    sign = np.where(lo, -1.0, 1.0).astype(np.float32)

    def tables(pos):
        ang = (pos[None, :].astype(np.float32) * invp[:, None]).astype(np.float32)
        return np.cos(ang).astype(np.float32), (np.sin(ang) * sign[:, None]).astype(np.float32)

    c["c_cos"], c["c_sin"] = tables(np.arange(SEQ, dtype=np.float32))
    pos_s = np.float32(PAST) + np.tile(np.arange(4, dtype=np.float32), NSB)
    c["c_cos_s"], c["c_sin_s"] = tables(pos_s)
    r = np.arange(128)[:, None]
    ii = np.arange(4)[None, :]
    c["c_mc"] = (r > ii).astype(np.float32)
    kb, kj = np.divmod(np.arange(NS), 4)
    c["c_mnew"] = ((kb[:, None] == kb[None, :]) & (kj[:, None] <= kj[None, :])).astype(np.float32)
    c["c_rowm"] = (np.arange(128)[:, None] // 32 == np.arange(4)[None, :]).astype(np.float32)
    return c


def kernel(**inputs):
    f = lambda a: np.ascontiguousarray(np.asarray(a, dtype=np.float32))
    inp = {k: f(v) for k, v in inputs.items()}
    if "nc" not in _NC_CACHE:
        _NC_CACHE["nc"] = build_program()
    nc = _NC_CACHE["nc"]
    consts = _consts()
    wnames = ["attn_norm", "w_in", "q_norm", "k_norm", "attn_sinks", "ssm_a_re", "ssm_a_im", "ssm_log_dt", "ssm_b_re", "ssm_b_im",
              "ssm_c_re", "ssm_c_im", "ssm_d", "ssm_w_glu", "mem_norm", "w_mem_kv", "mem_q_norm", "mem_k_norm", "w_branch", "w_out",
              "ffn_norm", "w_ffn_up", "w_ffn_down"]
    in_maps = []
    for c in range(8):
        b0 = NSB * c
        m = {
            "xp": inp["x_prompt"][c % 4],
            "xs": inp["x_sample"][b0:b0 + NSB].reshape(NS, D),
            "csk": inp["cache_swa_k"][:, b0:b0 + NSB].reshape(NL, NSB, 128, 128),
            "csv": inp["cache_swa_v"][:, b0:b0 + NSB].reshape(NL, NSB, 128, 128),
            "sre": inp["state_ssm_re"][:, b0:b0 + NSB],
            "sim": inp["state_ssm_im"][:, b0:b0 + NSB],
            "cmk": inp["cache_mem_k"][:, b0:b0 + NSB].reshape(NL, NSB, 256, 512),
            "cmv": inp["cache_mem_v"][:, b0:b0 + NSB].reshape(NL, NSB, 256, 512),
            "memp": inp["mem_prompt"][c % 4],
        }
        for w in wnames:
            m[w] = inp[w]
        m.update(consts)
        in_maps.append({k: np.ascontiguousarray(v) for k, v in m.items()})
    res = run_bass_kernel_spmd(nc, in_maps, core_ids=list(range(8)))
    R = res.results
    cat = lambda name, cores: np.stack([R[c][name] for c in cores])
    y_p = cat("yp", range(4))
    y_s = np.concatenate([R[c]["ys"].reshape(NSB, 4, D) for c in range(8)], axis=0)
    per_l = lambda name, shape: np.stack([R[c][name] for c in range(4)], axis=1).reshape(shape)
    swa_k_p = per_l("kp", (NL, 4, 128, 2, 64))
    swa_v_p = per_l("vp", (NL, 4, 128, 2, 64))
    ssm_re_p = per_l("hrp", (NL, 4, 32, 64))
    ssm_im_p = per_l("hip", (NL, 4, 32, 64))
    mem_k_p = per_l("mkp", (NL, 4, 256, 4, 128))
    mem_v_p = per_l("mvp", (NL, 4, 256, 4, 128))
    cat_s = lambda name, shape: np.concatenate([R[c][name] for c in range(8)], axis=1).reshape(shape)
    swa_k_s = cat_s("ks", (NL, 128, 128, 2, 64))
    swa_v_s = cat_s("vs", (NL, 128, 128, 2, 64))
    ssm_re_s = cat_s("hrs", (NL, 128, 32, 64))
    ssm_im_s = cat_s("his", (NL, 128, 32, 64))
    outs = (y_p, y_s, swa_k_p, swa_v_p, ssm_re_p, ssm_im_p, mem_k_p, mem_v_p, swa_k_s, swa_v_s, ssm_re_s, ssm_im_s)
    return tuple(np.ascontiguousarray(o, dtype=np.float32) for o in outs)
```

```python
import contextlib
import math
import types
import numpy as np
import concourse.bass as bass
import concourse.mybir as mybir
from concourse.bass_utils import run_bass_kernel_spmd

F32 = mybir.dt.float32
BF16 = mybir.dt.bfloat16
I32 = mybir.dt.int32
AF = mybir.ActivationFunctionType
ALU = mybir.AluOpType

D = 1024
SEQ = 4096
NL = 2
INW = 4864
DFF = 2816
K_OFF, V_OFF, U_OFF, MQ_OFF, G_OFF = 512, 640, 768, 1280, 1792
PAST = 16384
TB = 512
NBLK = SEQ // TB
NSB = 16
NS = NSB * 4
NLEV = 9
EPS = 1e-6
NWS = 4
ENGS = ("pe", "act", "dve", "pool", "sp")


def _freeze(fn):
    if fn is None or fn.__closure__ is None:
        return fn
    cells = []
    for c in fn.__closure__:
        try:
            cells.append(types.CellType(c.cell_contents))
        except ValueError:
            cells.append(c)
    return types.FunctionType(fn.__code__, fn.__globals__, fn.__name__, fn.__defaults__, tuple(cells))


class Sched:
    def __init__(self, nc, stack):
        self.nc = nc
        self.stack = stack
        self.q = {e: [] for e in ENGS}
        self.cnt = {e: 0 for e in ENGS}
        self.esem = {e: stack.enter_context(nc.semaphore("sem_" + e)) for e in ENGS}
        self.dsem = {}
        self.dcnt = {}
        self.last_w = {}
        self.readers = {}
        self.seen = {e: {} for e in ENGS}
        self.alias = {}

    def _exp(self, keys):
        out = []
        for k in keys:
            out.append(k)
            out.extend(self.alias.get(k, ()))
        return out

    def op(self, eng, fn, reads=(), writes=(), dma=None):
        reads = self._exp(reads)
        writes = self._exp(writes)
        fn = _freeze(fn)
        deps = []
        for r in reads:
            t = self.last_w.get(r)
            if t is not None:
                deps.append((t, True))
        for w in writes:
            t = self.last_w.get(w)
            if t is not None:
                deps.append((t, False))
            for t in self.readers.get(w, ()):
                deps.append((t, False))
        waits = {}
        for (kind, key, val), raw in deps:
            if kind == "eng" and key == eng and (eng == "pe" or not raw):
                continue
            sk = (kind, key)
            if val > self.seen[eng].get(sk, 0):
                waits[sk] = max(waits.get(sk, 0), val)
        for sk, val in waits.items():
            self.seen[eng][sk] = val
        if dma is not None:
            if dma not in self.dsem:
                self.dsem[dma] = self.stack.enter_context(self.nc.semaphore("dq%d" % len(self.dsem)))
                self.dcnt[dma] = 0
            self.dcnt[dma] += 16
            tok = ("dma", dma, self.dcnt[dma])
        else:
            self.cnt[eng] += 1
            tok = ("eng", eng, self.cnt[eng])
        self.q[eng].append((fn, list(waits.items()), tok))
        for w in writes:
            self.last_w[w] = tok
            self.readers[w] = []
        for r in reads:
            self.readers.setdefault(r, []).append(tok)
        return tok

    def wait_all(self, eng, toks):
        waits = {}
        for kind, key, val in toks:
            sk = (kind, key)
            if val > self.seen[eng].get(sk, 0):
                waits[sk] = max(waits.get(sk, 0), val)
        for sk, val in waits.items():
            self.seen[eng][sk] = val
        self.q[eng].append((None, list(waits.items()), None))

    def emit(self):
        nc = self.nc
        sem = lambda sk: self.esem[sk[1]] if sk[0] == "eng" else self.dsem[sk[1]]
        with nc.Block() as block:
            def run(e, engobj):
                for fn, waits, tok in self.q[e]:
                    for sk, val in waits:
                        engobj.wait_ge(sem(sk), val)
                    if fn is None:
                        continue
                    ins = fn(engobj)
                    if tok[0] == "dma":
                        ins.then_inc(self.dsem[tok[1]], 16)
                    else:
                        ins.then_inc(self.esem[e], 1)

            @block.tensor
            def _(t):
                run("pe", t)

            @block.scalar
            def _(t):
                run("act", t)

            @block.vector
            def _(t):
                run("dve", t)

            @block.gpsimd
            def _(t):
                run("pool", t)

            @block.sync
            def _(t):
                run("sp", t)


def build_program():
    nc = bass.Bass("TRN2", target_bir_lowering=False)

    def din(name, shape):
        return nc.dram_tensor(name, list(shape), F32, kind="ExternalInput").ap()

    def dout(name, shape):
        return nc.dram_tensor(name, list(shape), F32, kind="ExternalOutput").ap()

    def dscr(name, shape, dt=BF16):
        return nc.dram_tensor(name, list(shape), dt).ap()

    xp = din("xp", [SEQ, D]); xs = din("xs", [NS, D])
    csk = din("csk", [NL, NSB, 128, 128]); csv = din("csv", [NL, NSB, 128, 128])
    sre = din("sre", [NL, NSB, 32, 64]); sim = din("sim", [NL, NSB, 32, 64])
    cmk = din("cmk", [NL, NSB, 256, 512]); cmv = din("cmv", [NL, NSB, 256, 512])
    memp = din("memp", [256, D])
    attn_norm = din("attn_norm", [NL, D]); w_in = din("w_in", [NL, D, INW])
    q_norm = din("q_norm", [NL, 64]); k_norm = din("k_norm", [NL, 64]); attn_sinks = din("attn_sinks", [NL, 8])
    a_re = din("ssm_a_re", [NL, 32, 64]); a_im = din("ssm_a_im", [NL, 32, 64]); log_dt = din("ssm_log_dt", [NL, 32])
    b_re = din("ssm_b_re", [NL, 32, 64, 16]); b_im = din("ssm_b_im", [NL, 32, 64, 16])
    c_re = din("ssm_c_re", [NL, 32, 16, 64]); c_im = din("ssm_c_im", [NL, 32, 16, 64])
    ssm_d = din("ssm_d", [NL, 512]); w_glu = din("ssm_w_glu", [NL, 512, 512])
    mem_norm = din("mem_norm", [NL, D]); w_kv = din("w_mem_kv", [NL, D, D])
    mq_norm = din("mem_q_norm", [NL, 128]); mk_norm = din("mem_k_norm", [NL, 128])
    w_br = din("w_branch", [NL, 3, 512, D]); w_out = din("w_out", [NL, D, D])
    ffn_norm = din("ffn_norm", [NL, D]); w_up = din("w_ffn_up", [NL, D, 2 * DFF]); w_dn = din("w_ffn_down", [NL, DFF, D])
    c_ident = din("c_ident", [128, 128]); c_blk = din("c_blk", [128, 128]); c_perm = din("c_perm", [128, 128])
    c_mprev = din("c_mprev", [128, 128]); c_mcur = din("c_mcur", [128, 128])
    c_cos = din("c_cos", [128, SEQ]); c_sin = din("c_sin", [128, SEQ])
    c_cos_s = din("c_cos_s", [128, NS]); c_sin_s = din("c_sin_s", [128, NS])
    c_mc = din("c_mc", [128, 4]); c_mnew = din("c_mnew", [NS, NS]); c_rowm = din("c_rowm", [128, 4])

    yp = dout("yp", [SEQ, D]); ys = dout("ys", [NS, D])
    kp = dout("kp", [NL, 128, 128]); vp = dout("vp", [NL, 128, 128])
    hrp = dout("hrp", [NL, 32, 64]); hip = dout("hip", [NL, 32, 64])
    mkp = dout("mkp", [NL, 256, 512]); mvp = dout("mvp", [NL, 256, 512])
    ks = dout("ks", [NL, NSB, 128, 128]); vs = dout("vs", [NL, NSB, 128, 128])
    hrs = dout("hrs", [NL, NSB, 32, 64]); his = dout("his", [NL, NSB, 32, 64])

    wb_in = dscr("wb_in", [NL, D, INW]); wb_glu = dscr("wb_glu", [NL, 512, 512]); wb_kv = dscr("wb_kv", [NL, D, D])
    wb_br = dscr("wb_br", [NL, 3, 512, D]); wb_out = dscr("wb_out", [NL, D, D])
    wb_up = dscr("wb_up", [NL, D, 2 * DFF]); wb_dn = dscr("wb_dn", [NL, DFF, D])

    out_toks = []

    with contextlib.ExitStack() as st:
        S = Sched(nc, st)

        def sb(name, shape, dt):
            return st.enter_context(nc.sbuf_tensor(name, list(shape), dt))

        PS = [st.enter_context(nc.psum_tensor("ps%d" % i, [128, 512], F32)) for i in range(8)]
        psn = [0]

        held = set()

        def bank():
            while True:
                i = psn[0] % 8
                psn[0] += 1
                if i not in held:
                    return i

        ident_f = sb("ident_f", [128, 128], F32); ident_b = sb("ident_b", [128, 128], BF16)
        ones_b = sb("ones_b", [128, 128], BF16); blk_b = sb("blk_b", [128, 128], BF16); perm_b = sb("perm_b", [128, 128], BF16)
        mprev_b = sb("mprev_b", [128, 128], BF16); mcur_b = sb("mcur_b", [128, 128], BF16)
        mc_b = sb("mc_b", [128, 4], BF16); mnew_b = sb("mnew_b", [NS, NS], BF16); rowm = sb("rowm", [128, 4], F32)
        cosb = sb("cosb", [128, TB], F32); sinb = sb("sinb", [128, TB], F32)
        gA = sb("gA", [128, NL, 8], F32); gF = sb("gF", [128, NL, 8], F32)
        gq = sb("gq", [128, NL], F32); gk = sb("gk", [128, NL], F32); gmq = sb("gmq", [128, NL], F32)
        esink = sb("esink", [128, NL, 4], F32); dcol = sb("dcol", [128, NL, 4], F32)
        W2 = sb("W2", [128, NL, 16, 2, 128], BF16); CP = sb("CP", [128, NL, 16, 2, 128], BF16)
        Dd = sb("Dd", [128, NL, 4, 128], BF16)
        LR = sb("LR", [128, NL, NLEV, 16], F32); LI = sb("LI", [128, NL, NLEV, 16], F32); LIn = sb("LIn", [128, NL, NLEV, 16], F32)
        MKT = sb("MKT", [128, NL, 4, 256], BF16); MV = sb("MV", [128, NL, 2, 512], BF16)
        kTc = sb("kTc", [128, NL, 128 + TB], BF16); vtc = sb("vtc", [128, NL, 5, 128], BF16)
        car_r = sb("car_r", [128, NL, 16], F32); car_i = sb("car_i", [128, NL, 16], F32)
        xT = sb("xT", [128, 8, TB], F32)
        hT = sb("hT", [128, 8, TB], BF16)
        qf = sb("qf", [128, TB], F32); sqb = sb("sqb", [128, TB], BF16); sdv = sb("sdv", [128, TB], F32); rstd = sb("rstd", [128, TB], F32)
        qn = sb("qn", [128, TB], BF16); t1 = sb("t1", [128, TB], F32); t2 = sb("t2", [128, TB], F32)
        HN = [dict(qf=qf, sqb=sqb, sdv=sdv, rstd=rstd, qn=qn, t1=t1, t2=t2, sfx=""),
              dict(qf=sb("qfB", [128, TB], F32), sqb=sb("sqbB", [128, TB], BF16), sdv=sdv,
                   rstd=sb("rstdB", [128, TB], F32), qn=sb("qnB", [128, TB], BF16), t1=t1, t2=t2, sfx="B")]
        hn_i = [0]
        S.alias["zT"] = ["qr"]
        qr = sb("qr", [128, 4, TB], BF16); k32 = sb("k32", [128, TB], F32); v32 = sb("v32", [128, 4, 128], F32)
        uT = sb("uT", [128, 4, TB], BF16); qmn = sb("qmn", [128, 4, TB], BF16)
        oa = sb("oa", [128, 4, TB], BF16); ob = sb("ob", [128, 4, TB], BF16); oc = sb("oc", [128, 4, TB], BF16)
        PT = [sb("PT%d" % i, [128, 2, TB], BF16) for i in range(2)]
        dn = [sb("dn%d" % i, [128, TB], F32) for i in range(2)]
        zT = qr; sg = [sb("sg%d" % i, [128, TB], BF16) for i in range(2)]
        WS = [sb("ws%d" % i, [128, 4096], BF16) for i in range(NWS)]
        RA = sb("RA", [128, 16384], BF16)
        small = sb("small", [128, 64], F32)
        smi = sb("smi", [128, 16], I32)
        hs_r = sb("hs_r", [128, 16, NSB], F32); hs_i = sb("hs_i", [128, 16, NSB], F32)
        kcT = sb("kcT", [128, 2, 128], BF16)

        RAK = [("RA", i) for i in range(32)]

        def ra(off_b, nbytes, dt):
            a = RA[:, off_b // 2:(off_b + nbytes) // 2]
            keys = RAK[off_b // 1024:(off_b + nbytes + 1023) // 1024]
            return (a.bitcast(F32) if dt == F32 else a), keys

        xtok = ra(0, 16384, F32)[0].rearrange("p (s d) -> p s d", s=4)
        S.alias["xtok"] = RAK[0:16]
        sqT = ra(24576, 8192, BF16)[0].rearrange("p (k n) -> p k n", k=8)
        S.alias["sqT"] = RAK[24:32]
        XSr, kXSr = ra(0, 8192, F32); XSi, kXSi = ra(8192, 8192, F32)
        TD = [ra(16384 + 4096 * i, 4096, F32) for i in range(4)]
        xbr, kxbr = ra(16384, 4096, BF16); xbi, kxbi = ra(20480, 4096, BF16)
        macc, kmacc = ra(0, 16384, F32); mgT, kmgT = ra(16384, 8192, BF16)
        wsn = [0]

        def wk(i):
            return [("ws", i), ("wsT", i)]

        def wslot():
            i = wsn[0] % NWS
            wsn[0] += 1
            return i

        S.op("sp", lambda e: e.dma_start(out=ident_f[:], in_=c_ident), writes=["ident_f"], dma="c0")
        for (dst, src, nm) in [(ident_b, c_ident, "ident_b"), (blk_b, c_blk, "blk_b"), (perm_b, c_perm, "perm_b"),
                               (mprev_b, c_mprev, "mprev_b"), (mcur_b, c_mcur, "mcur_b"), (mc_b, c_mc, "mc_b"),
                               (mnew_b, c_mnew, "mnew_b")]:
            S.op("pool", lambda e, dst=dst, src=src: e.dma_start(out=dst[:], in_=src), writes=[nm], dma=nm)
        S.op("sp", lambda e: e.dma_start(out=rowm[:], in_=c_rowm), writes=["rowm"], dma="rowm")
        S.op("dve", lambda e: e.memset(ones_b[:], 1.0), writes=["ones_b"])
        S.op("dve", lambda e: e.memset(small[:], 0.0), writes=["small"])
        S.op("dve", lambda e: e.memset(small[:, 0:1], math.pi / 2), writes=["small"])
        S.op("dve", lambda e: e.memset(small[:, 1:2], EPS), writes=["small"])

        def conv(dst, src, key, rows):
            r0 = 0
            while r0 < rows:
                r1 = min(rows, r0 + 256)
                S.op("pool", lambda e, r0=r0, r1=r1: e.dma_start(out=dst[r0:r1, :], in_=src[r0:r1, :]),
                     writes=[key], dma=key)
                r0 = r1

        for l in range(NL):
            conv(wb_kv[l], w_kv[l], ("wb_kv", l), D)
        for l in range(NL):
            conv(wb_in[l], w_in[l], ("wb_in", l), D)
            conv(wb_glu[l], w_glu[l], ("wb_glu", l), 512)
            for n in range(3):
                conv(wb_br[l, n], w_br[l, n], ("wb_br", l), 512)
            conv(wb_out[l], w_out[l], ("wb_out", l), D)
            conv(wb_up[l], w_up[l], ("wb_up", l), D)
            conv(wb_dn[l], w_dn[l], ("wb_dn", l), DFF)

        for l in range(NL):
            S.op("sp", lambda e, l=l: e.dma_start(out=gA[:, l, :], in_=attn_norm[l].rearrange("(k p) -> p k", p=128),
                                                  allow_slow_non_contiguous=True), writes=["gA"], dma="gA")
            S.op("sp", lambda e, l=l: e.dma_start(out=gF[:, l, :], in_=ffn_norm[l].rearrange("(k p) -> p k", p=128),
                                                  allow_slow_non_contiguous=True), writes=["gF"], dma="gF")
            S.op("sp", lambda e, l=l: e.dma_start(out=dcol[:, l, :], in_=ssm_d[l].rearrange("(k p) -> p k", p=128),
                                                  allow_slow_non_contiguous=True), writes=["dcol"], dma="dcol")
            for two in range(2):
                sl = slice(64 * two, 64 * two + 64)
                S.op("sp", lambda e, l=l, sl=sl: e.dma_start(out=gq[sl, l:l + 1], in_=q_norm[l].rearrange("(p o) -> p o", o=1)),
                     writes=["gq"], dma="gq")
                S.op("sp", lambda e, l=l, sl=sl: e.dma_start(out=gk[sl, l:l + 1], in_=k_norm[l].rearrange("(p o) -> p o", o=1)),
                     writes=["gk"], dma="gk")
                S.op("sp", lambda e, l=l, sl=sl, two=two: e.dma_start(out=esink[sl, l, :],
                                                                      in_=attn_sinks[l, 4 * two:4 * two + 4].partition_broadcast(64)),
                     writes=["esink"], dma="esink")
            S.op("sp", lambda e, l=l: e.dma_start(out=gmq[:, l:l + 1], in_=mq_norm[l].rearrange("(p o) -> p o", o=1)),
                 writes=["gmq"], dma="gmq")
        S.op("act", lambda e: e.activation(out=esink[:], in_=esink[:], func=AF.Exp), reads=["esink"], writes=["esink"])

        are_t = sb("are_t", [128, 16], F32); aim_t = sb("aim_t", [128, 16], F32); dt_t = sb("dt_t", [128, 16], F32)
        sA = [sb("sA%d" % i, [128, 16], F32) for i in range(8)]
        def ra3(idx, nm):
            v, kk = ra(16384 + 2048 * idx, 2048, F32)
            S.alias[nm] = kk
            return v.rearrange("p (t c) -> p t c", t=16)
        Bb = [ra3(i, "Bb%d" % i) for i in range(2)]
        Cb = [ra3(2 + i, "Cb%d" % i) for i in range(2)]
        GB = [ra3(4 + i, "GB%d" % i) for i in range(2)]
        tG = [ra3(6 + i, "tG%d" % i) for i in range(2)]
        tT = sb("tT", [128, 128], F32)

        def dve(fn, R, W):
            return S.op("dve", fn, reads=R, writes=W)

        def act(fn, R, W):
            return S.op("act", fn, reads=R, writes=W)

        TWO_PI = 2.0 * math.pi
        for l in range(NL):
            for gl in range(2):
                sl = slice(64 * gl, 64 * gl + 64)
                S.op("sp", lambda e, l=l, gl=gl, sl=sl: e.dma_start(
                    out=are_t[sl, :], in_=a_re[l].rearrange("(tp gl) p -> gl p tp", gl=2)[gl], allow_slow_non_contiguous=True),
                    writes=["are_t"], dma="are_t")
                S.op("sp", lambda e, l=l, gl=gl, sl=sl: e.dma_start(
                    out=aim_t[sl, :], in_=a_im[l].rearrange("(tp gl) p -> gl p tp", gl=2)[gl], allow_slow_non_contiguous=True),
                    writes=["aim_t"], dma="aim_t")
                S.op("sp", lambda e, l=l, gl=gl, sl=sl: e.dma_start(
                    out=dt_t[sl, :], in_=log_dt[l].rearrange("(tp gl) -> gl tp", gl=2)[gl].partition_broadcast(64)),
                    writes=["dt_t"], dma="dt_t")
            for ri, (bsrc, csrc) in enumerate([(b_re, c_re), (b_im, c_im)]):
                S.op("pool", lambda e, ri=ri: e.memset(Bb[ri][:], 0.0), writes=["Bb%d" % ri])
                S.op("pool", lambda e, ri=ri: e.memset(Cb[ri][:], 0.0), writes=["Cb%d" % ri])
                for gl in range(2):
                    sl = slice(64 * gl, 64 * gl + 64)
                    cs = slice(16 * gl, 16 * gl + 16)
                    S.op("sp", lambda e, l=l, gl=gl, sl=sl, cs=cs, ri=ri, bsrc=bsrc: e.dma_start(
                        out=Bb[ri][sl, :, cs], in_=bsrc[l].rearrange("(tp gl) p c -> gl p tp c", gl=2)[gl]),
                        writes=["Bb%d" % ri], dma="Bb%d" % ri)
                    for tp in range(16):
                        S.op("sp", lambda e, l=l, gl=gl, sl=sl, cs=cs, ri=ri, csrc=csrc, tp=tp: e.dma_start(
                            out=Cb[ri][sl, tp, cs], in_=csrc[l, 2 * tp + gl].rearrange("c p -> p c"), allow_slow_non_contiguous=True),
                            writes=["Cb%d" % ri], dma="Cb%d" % ri)
            dtv, ard, mag, th, kf, s_, c_, tmp = sA
            act(lambda e: e.activation(out=dtv[:], in_=dt_t[:], func=AF.Exp), ["dt_t"], ["sA0"])
            dve(lambda e: e.tensor_tensor(out=ard[:], in0=are_t[:], in1=dtv[:], op=ALU.mult), ["are_t", "sA0"], ["sA1"])
            act(lambda e: e.activation(out=mag[:], in_=ard[:], func=AF.Exp), ["sA1"], ["sA2"])
            dve(lambda e: e.tensor_tensor(out=th[:], in0=aim_t[:], in1=dtv[:], op=ALU.mult), ["aim_t", "sA0"], ["sA3"])
            dve(lambda e: e.tensor_scalar(out=kf[:], in0=th[:], scalar1=1.0 / TWO_PI, scalar2=None, op0=ALU.mult), ["sA3"], ["sA4"])
            dve(lambda e: e.tensor_copy(out=smi[:], in_=kf[:]), ["sA4"], ["smi"])
            dve(lambda e: e.tensor_copy(out=kf[:], in_=smi[:]), ["smi"], ["sA4"])
            dve(lambda e: e.scalar_tensor_tensor(out=th[:], in0=kf[:], scalar=-TWO_PI, in1=th[:], op0=ALU.mult, op1=ALU.add),
                ["sA4", "sA3"], ["sA3"])
            act(lambda e: e.activation(out=s_[:], in_=th[:], func=AF.Sin, scale=0.5), ["sA3"], ["sA5"])
            act(lambda e: e.activation(out=c_[:], in_=th[:], func=AF.Sin, scale=0.5, bias=small[:, 0:1]), ["sA3", "small"], ["sA6"])
            lr0 = LR[:, l, 0, :]; li0 = LI[:, l, 0, :]
            dve(lambda e: e.tensor_tensor(out=tmp[:], in0=s_[:], in1=c_[:], op=ALU.mult), ["sA5", "sA6"], ["sA7"])
            dve(lambda e: e.scalar_tensor_tensor(out=li0, in0=tmp[:], scalar=2.0, in1=mag[:], op0=ALU.mult, op1=ALU.mult),
                ["sA7", "sA2"], ["LI"])
            dve(lambda e: e.tensor_tensor(out=tmp[:], in0=s_[:], in1=s_[:], op=ALU.mult), ["sA5"], ["sA7"])
            dve(lambda e: e.tensor_scalar(out=tmp[:], in0=tmp[:], scalar1=-2.0, scalar2=1.0, op0=ALU.mult, op1=ALU.add), ["sA7"], ["sA7"])
            dve(lambda e: e.tensor_tensor(out=lr0, in0=tmp[:], in1=mag[:], op=ALU.mult), ["sA7", "sA2"], ["LR"])
            den_, nr_, gr_, gi_, rd_ = sA[0], sA[1], sA[2], sA[3], sA[4]
            dve(lambda e: e.tensor_tensor(out=den_[:], in0=are_t[:], in1=are_t[:], op=ALU.mult), ["are_t"], ["sA0"])
            dve(lambda e: e.tensor_tensor(out=tmp[:], in0=aim_t[:], in1=aim_t[:], op=ALU.mult), ["aim_t"], ["sA7"])
            dve(lambda e: e.tensor_tensor(out=den_[:], in0=den_[:], in1=tmp[:], op=ALU.add), ["sA0", "sA7"], ["sA0"])
            dve(lambda e: e.reciprocal(out=rd_[:], in_=den_[:]), ["sA0"], ["sA4"])
            dve(lambda e: e.tensor_scalar(out=nr_[:], in0=lr0, scalar1=-1.0, scalar2=None, op0=ALU.add), ["LR"], ["sA1"])
            dve(lambda e: e.tensor_tensor(out=gr_[:], in0=nr_[:], in1=are_t[:], op=ALU.mult), ["sA1", "are_t"], ["sA2"])
            dve(lambda e: e.tensor_tensor(out=tmp[:], in0=li0, in1=aim_t[:], op=ALU.mult), ["LI", "aim_t"], ["sA7"])
            dve(lambda e: e.tensor_tensor(out=gr_[:], in0=gr_[:], in1=tmp[:], op=ALU.add), ["sA2", "sA7"], ["sA2"])
            dve(lambda e: e.tensor_tensor(out=gr_[:], in0=gr_[:], in1=rd_[:], op=ALU.mult), ["sA2", "sA4"], ["sA2"])
            dve(lambda e: e.tensor_tensor(out=gi_[:], in0=li0, in1=are_t[:], op=ALU.mult), ["LI", "are_t"], ["sA3"])
            dve(lambda e: e.tensor_tensor(out=tmp[:], in0=nr_[:], in1=aim_t[:], op=ALU.mult), ["sA1", "aim_t"], ["sA7"])
            dve(lambda e: e.tensor_tensor(out=gi_[:], in0=gi_[:], in1=tmp[:], op=ALU.subtract), ["sA3", "sA7"], ["sA3"])
            dve(lambda e: e.tensor_tensor(out=gi_[:], in0=gi_[:], in1=rd_[:], op=ALU.mult), ["sA3", "sA4"], ["sA3"])
            for i in range(NLEV - 1):
                a, b = LR[:, l, i, :], LI[:, l, i, :]
                a2, b2 = LR[:, l, i + 1, :], LI[:, l, i + 1, :]
                dve(lambda e, a=a, b=b: e.tensor_tensor(out=tmp[:], in0=b, in1=b, op=ALU.mult), ["LI"], ["sA7"])
                dve(lambda e, a=a, a2=a2: e.tensor_tensor(out=a2, in0=a, in1=a, op=ALU.mult), ["LR"], ["LR"])
                dve(lambda e, a2=a2: e.tensor_tensor(out=a2, in0=a2, in1=tmp[:], op=ALU.subtract), ["LR", "sA7"], ["LR"])
                dve(lambda e, a=a, b=b, b2=b2: e.scalar_tensor_tensor(out=b2, in0=a, scalar=2.0, in1=b, op0=ALU.mult, op1=ALU.mult),
                    ["LR", "LI"], ["LI"])
            dve(lambda e, l=l: e.tensor_scalar(out=LIn[:, l], in0=LI[:, l], scalar1=-1.0, scalar2=None, op0=ALU.mult), ["LI"], ["LIn"])
            grb = gr_[:].rearrange("p (t o) -> p t o", o=1).to_broadcast([128, 16, 32])
            gib = gi_[:].rearrange("p (t o) -> p t o", o=1).to_broadcast([128, 16, 32])
            dve(lambda e: e.tensor_tensor(out=GB[0][:], in0=Bb[0][:], in1=grb, op=ALU.mult), ["Bb0", "sA2"], ["GB0"])
            dve(lambda e: e.tensor_tensor(out=tG[0][:], in0=Bb[1][:], in1=gib, op=ALU.mult), ["Bb1", "sA3"], ["tG0"])
            dve(lambda e: e.tensor_tensor(out=GB[0][:], in0=GB[0][:], in1=tG[0][:], op=ALU.subtract), ["GB0", "tG0"], ["GB0"])
            dve(lambda e: e.tensor_tensor(out=GB[1][:], in0=Bb[1][:], in1=grb, op=ALU.mult), ["Bb1", "sA2"], ["GB1"])
            dve(lambda e: e.tensor_tensor(out=tG[1][:], in0=Bb[0][:], in1=gib, op=ALU.mult), ["Bb0", "sA3"], ["tG1"])
            dve(lambda e: e.tensor_tensor(out=GB[1][:], in0=GB[1][:], in1=tG[1][:], op=ALU.add), ["GB1", "tG1"], ["GB1"])
            for ri in range(2):
                for ct in range(4):
                    b = bank()
                    S.op("pe", lambda e, b=b, ri=ri, ct=ct: e.transpose(
                        out=PS[b][:, 0:128], in_=GB[ri][:, 4 * ct:4 * ct + 4, :].rearrange("p a b -> p (a b)"), identity=ident_f[:]),
                        reads=["GB%d" % ri, "ident_f"], writes=[("ps", b)])
                    act(lambda e, b=b: e.copy(out=tT[:], in_=PS[b][:, 0:128]), [("ps", b)], ["tT"])
                    for i in range(4):
                        dve(lambda e, l=l, ri=ri, ct=ct, i=i: e.tensor_scalar(
                            out=W2[:, l, 4 * ct + i, ri, :], in0=tT[:], scalar1=rowm[:, i:i + 1], scalar2=None, op0=ALU.mult),
                            ["tT", "rowm"], ["W2"])
            S.op("pool", lambda e, l=l: e.memset(CP[:, l], 0.0), writes=["CP"])
            for i in range(4):
                act(lambda e, l=l, i=i: e.copy(out=CP[:, l, i::4, 0, 32 * i:32 * i + 32], in_=Cb[0][:, i::4, :]), ["Cb0", "CP"], ["CP"])
                act(lambda e, l=l, i=i: e.mul(out=CP[:, l, i::4, 1, 32 * i:32 * i + 32], in_=Cb[1][:, i::4, :], mul=-1.0), ["Cb1", "CP"], ["CP"])
            for ct in range(4):
                dve(lambda e, l=l, ct=ct: e.tensor_scalar(out=Dd[:, l, ct, :], in0=ident_f[:], scalar1=dcol[:, l, ct:ct + 1], scalar2=None,
                                                          op0=ALU.mult), ["ident_f", "dcol"], ["Dd"])

        memt = xtok
        mnb = qr[:].rearrange("p t n -> p (t n)").rearrange("p (a d) -> p a d", a=2)
        S.alias["mnb"] = ["qr"]
        mnT = hT
        gmk_b = sb("gmk_b", [128, 128], F32)
        S.op("sp", lambda e: e.dma_start(out=memt[:, 0:2, :], in_=memp.rearrange("(a p) d -> p a d", p=128)), writes=["xtok"], dma="xtok")
        for l in range(NL):
            S.op("sp", lambda e, l=l: e.dma_start(out=memt[:, 2, :], in_=mem_norm[l].partition_broadcast(128)), writes=["xtok"], dma="xtok")
            S.op("sp", lambda e, l=l: e.dma_start(out=gmk_b[:], in_=mk_norm[l].partition_broadcast(128)), writes=["gmk_b"], dma="gmk_b")
            dve(lambda e: e.memset(small[:, 8:10], 0.0), [], ["small"])
            for a in range(2):
                act(lambda e, a=a: e.activation(out=memt[:, 3, :], in_=memt[:, a, :], func=AF.Square, accum_out=small[:, 8 + a:9 + a]),
                    ["xtok"], ["xtok", "small"])
            act(lambda e: e.activation(out=small[:, 10:12], in_=small[:, 8:10], func=AF.Sqrt, scale=1.0 / D, bias=small[:, 1:2]),
                ["small"], ["small"])
            dve(lambda e: e.reciprocal(out=small[:, 12:14], in_=small[:, 10:12]), ["small"], ["small"])
            for a in range(2):
                dve(lambda e, a=a: e.scalar_tensor_tensor(out=mnb[:, a, :], in0=memt[:, a, :], scalar=small[:, 12 + a:13 + a],
                                                          in1=memt[:, 2, :], op0=ALU.mult, op1=ALU.mult), ["xtok", "small"], ["mnb"])
            for a in range(2):
                for k in range(8):
                    b = bank()
                    pb = PS[b][:].bitcast(BF16)
                    S.op("pe", lambda e, a=a, k=k, pb=pb: e.transpose(out=pb[:, 0:128], in_=mnb[:, a, 128 * k:128 * k + 128],
                                                                      identity=ident_b[:]),
                         reads=["mnb", "ident_b"], writes=[("ps", b)])
                    act(lambda e, a=a, k=k, pb=pb: e.copy(out=mnT[:, k, 128 * a:128 * a + 128], in_=pb[:, 0:128]), [("ps", b)], ["hT"])
            for half in range(2):
                ws = wslot()
                wv = WS[ws][:].rearrange("p (k c) -> p k c", k=8)
                S.op("sp", lambda e, l=l, half=half, wv=wv: e.dma_start(
                    out=wv, in_=wb_kv[l].rearrange("(k p) c -> p k c", p=128)[:, :, 512 * half:512 * half + 512]),
                    reads=[("wb_kv", l)], writes=wk(ws), dma=("ws", ws))
                for a in range(2):
                    b = bank()
                    for k in range(8):
                        S.op("pe", lambda e, a=a, k=k, b=b, wv=wv: e.matmul(PS[b][:], lhsT=mnT[:, k, 128 * a:128 * a + 128], rhs=wv[:, k, :],
                                                                          start=(k == 0), stop=(k == 7)),
                             reads=["hT", ("ws", ws)], writes=[("ps", b)])
                    if half == 0:
                        kk = t1
                        dve(lambda e: e.memset(small[:, 16:20], 0.0), [], ["small"])
                        for h in range(4):
                            act(lambda e, b=b, h=h: e.activation(out=t2[:, 128 * h:128 * h + 128], in_=PS[b][:, 128 * h:128 * h + 128],
                                                                 func=AF.Square, accum_out=small[:, 16 + h:17 + h]),
                                [("ps", b)], ["t2", "small"])
                        act(lambda e: e.activation(out=small[:, 20:24], in_=small[:, 16:20], func=AF.Sqrt, scale=1.0 / 128, bias=small[:, 1:2]),
                            ["small"], ["small"])
                        dve(lambda e: e.reciprocal(out=small[:, 24:28], in_=small[:, 20:24]), ["small"], ["small"])
                        for h in range(4):
                            dve(lambda e, b=b, h=h: e.scalar_tensor_tensor(
                                out=kk[:, 128 * h:128 * h + 128], in0=PS[b][:, 128 * h:128 * h + 128], scalar=small[:, 24 + h:25 + h],
                                in1=gmk_b[:], op0=ALU.mult, op1=ALU.mult), [("ps", b), "small", "gmk_b"], ["t1"])
                        out_toks.append(S.op("sp", lambda e, l=l, a=a: e.dma_start(out=mkp[l, 128 * a:128 * a + 128, :], in_=kk[:]),
                                             reads=["t1"], dma="o_mkp"))
                        act(lambda e: e.copy(out=sqb[:], in_=kk[:]), ["t1"], ["sqb"])
                        for h in range(4):
                            b2 = bank()
                            pb = PS[b2][:].bitcast(BF16)
                            S.op("pe", lambda e, h=h, pb=pb: e.transpose(out=pb[:, 0:128], in_=sqb[:, 128 * h:128 * h + 128], identity=ident_b[:]),
                                 reads=["sqb", "ident_b"], writes=[("ps", b2)])
                            act(lambda e, l=l, a=a, h=h, pb=pb: e.copy(out=MKT[:, l, h, 128 * a:128 * a + 128], in_=pb[:, 0:128]),
                                [("ps", b2)], ["MKT"])
                    else:
                        vv = t2
                        act(lambda e, b=b: e.copy(out=vv[:], in_=PS[b][:]), [("ps", b)], ["t2"])
                        out_toks.append(S.op("sp", lambda e, l=l, a=a: e.dma_start(out=mvp[l, 128 * a:128 * a + 128, :], in_=vv[:]),
                                             reads=["t2"], dma="o_mvp"))
                        dve(lambda e, l=l, a=a: e.tensor_copy(out=MV[:, l, a, :], in_=vv[:]), ["t2"], ["MV"])

        def load_w(dst_view, src_ap, srckey):
            ws = wslot()
            dv = dst_view(ws)
            S.op("sp", lambda e: e.dma_start(out=dv, in_=src_ap), reads=[srckey], writes=wk(ws), dma=("ws", ws))
            return ws

        def rms_rstd(N, ssb, inv_n, R):
            act(lambda e: e.activation(out=sdv[:, 0:N], in_=PS[ssb][:, 0:N], func=AF.Sqrt, scale=inv_n, bias=small[:, 1:2]),
                [("ps", ssb), "small"], ["sdv"])
            dve(lambda e: e.reciprocal(out=rstd[:, 0:N], in_=sdv[:, 0:N]), ["sdv"], ["rstd"])

        def norm_block(N, gtab):
            act(lambda e: e.activation(out=sqT[:, :, 0:N], in_=xT[:, :, 0:N], func=AF.Square), ["xT"], ["sqT"])
            b = bank()
            for k in range(8):
                S.op("pe", lambda e, k=k, b=b: e.matmul(PS[b][:, 0:N], lhsT=ones_b[:], rhs=sqT[:, k, 0:N], start=(k == 0), stop=(k == 7)),
                     reads=["sqT", "ones_b"], writes=[("ps", b)])
            rms_rstd(N, b, 1.0 / D, None)
            for k in range(8):
                dve(lambda e, k=k: e.scalar_tensor_tensor(out=hT[:, k, 0:N], in0=xT[:, k, 0:N], scalar=gtab[:, k:k + 1], in1=rstd[:, 0:N],
                                                          op0=ALU.mult, op1=ALU.mult), ["xT", "rstd", "gA", "gF"], ["hT"])

        def proj_tile(N, wv, c0, ws, b=None):
            if b is None:
                b = bank()
            for k in range(8):
                S.op("pe", lambda e, k=k, b=b: e.matmul(PS[b][:, 0:N], lhsT=wv[:, k, c0:c0 + 128], rhs=hT[:, k, 0:N],
                                                        start=(k == 0), stop=(k == 7)),
                     reads=["hT", ("ws", ws)], writes=[("ps", b)])
            return b

        def headnorm_rope(N, b, l, gcol, onesm, inv_n, rope, out_bf, out_keys, out32=None, out32_keys=()):
            H = HN[hn_i[0] % 2]
            hn_i[0] += 1
            x = H["sfx"]
            qf_, sqb_, sdv_, rstd_, qn_, t1_, t2_ = H["qf"], H["sqb"], H["sdv"], H["rstd"], H["qn"], H["t1"], H["t2"]
            act(lambda e: e.copy(out=qf_[:, 0:N], in_=PS[b][:, 0:N]), [("ps", b)], ["qf" + x])
            act(lambda e: e.activation(out=sqb_[:, 0:N], in_=qf_[:, 0:N], func=AF.Square), ["qf" + x], ["sqb" + x])
            b2 = bank()
            S.op("pe", lambda e: e.matmul(PS[b2][:, 0:N], lhsT=onesm[:], rhs=sqb_[:, 0:N], start=True, stop=True),
                 reads=["sqb" + x, "blk_b", "ones_b"], writes=[("ps", b2)])
            act(lambda e: e.activation(out=sdv_[:, 0:N], in_=PS[b2][:, 0:N], func=AF.Sqrt, scale=inv_n, bias=small[:, 1:2]),
                [("ps", b2), "small"], ["sdv"])
            dve(lambda e: e.reciprocal(out=rstd_[:, 0:N], in_=sdv_[:, 0:N]), ["sdv"], ["rstd" + x])
            if not rope:
                S.op("dve", lambda e: e.scalar_tensor_tensor(out=out_bf, in0=qf_[:, 0:N], scalar=gcol, in1=rstd_[:, 0:N],
                                                             op0=ALU.mult, op1=ALU.mult),
                     reads=["qf" + x, "rstd" + x, "gmq"], writes=out_keys)
                return
            S.op("dve", lambda e: e.scalar_tensor_tensor(out=qn_[:, 0:N], in0=qf_[:, 0:N], scalar=gcol, in1=rstd_[:, 0:N],
                                                         op0=ALU.mult, op1=ALU.mult),
                 reads=["qf" + x, "rstd" + x, "gq", "gk"], writes=["qn" + x])
            b3 = bank()
            S.op("pe", lambda e: e.matmul(PS[b3][:, 0:N], lhsT=perm_b[:], rhs=qn_[:, 0:N], start=True, stop=True),
                 reads=["qn" + x, "perm_b"], writes=[("ps", b3)])
            S.op("pool", lambda e: e.tensor_tensor(out=t1_[:, 0:N], in0=qn_[:, 0:N], in1=cosb[:, 0:N], op=ALU.mult),
                 reads=["qn" + x, "cosb"], writes=["t1"])
            dve(lambda e: e.tensor_tensor(out=t2_[:, 0:N], in0=PS[b3][:, 0:N], in1=sinb[:, 0:N], op=ALU.mult), [("ps", b3), "sinb"], ["t2"])
            if out32 is not None:
                dve(lambda e: e.tensor_tensor(out=out32, in0=t1_[:, 0:N], in1=t2_[:, 0:N], op=ALU.add), ["t1", "t2"], list(out32_keys))
                act(lambda e: e.copy(out=out_bf, in_=out32), list(out32_keys), out_keys)
            else:
                dve(lambda e: e.tensor_tensor(out=out_bf, in0=t1_[:, 0:N], in1=t2_[:, 0:N], op=ALU.add), ["t1", "t2"], out_keys)

        def cmul_add(eng, dr, di, sr, si, lr, li, lin, kdr, kdi, ksr, ksi, T, kT):
            o = lambda fn, R, W: S.op(eng, fn, reads=R, writes=W)
            o(lambda e: e.tensor_tensor(out=T[0], in0=sr, in1=lr, op=ALU.mult), ksr + ["LR"], kT[0])
            o(lambda e: e.tensor_tensor(out=T[1], in0=si, in1=lin, op=ALU.mult), ksi + ["LIn"], kT[1])
            o(lambda e: e.tensor_tensor(out=T[2], in0=si, in1=lr, op=ALU.mult), ksi + ["LR"], kT[2])
            o(lambda e: e.tensor_tensor(out=T[3], in0=sr, in1=li, op=ALU.mult), ksr + ["LI"], kT[3])
            o(lambda e: e.tensor_tensor(out=dr, in0=dr, in1=T[0], op=ALU.add), kdr + kT[0], kdr)
            o(lambda e: e.tensor_tensor(out=di, in0=di, in1=T[2], op=ALU.add), kdi + kT[2], kdi)
            o(lambda e: e.tensor_tensor(out=dr, in0=dr, in1=T[1], op=ALU.add), kdr + kT[1], kdr)
            o(lambda e: e.tensor_tensor(out=di, in0=di, in1=T[3], op=ALU.add), kdi + kT[3], kdi)

        kXS = kXSr + kXSi

        def lam_b(tab, l, lev, tp0, ntp, shape):
            return tab[:, l, lev, tp0:tp0 + ntp].rearrange("p (t o) -> p t o", o=1).to_broadcast(shape)

        def ssm_group_prompt(l, ct, N, first_block):
            for i in range(4):
                tp = 4 * ct + i
                for ri, X, kX in ((0, XSr, kXSr), (1, XSi, kXSi)):
                    b = bank()
                    S.op("pe", lambda e, tp=tp, ri=ri, b=b: e.matmul(PS[b][:, 0:N], lhsT=W2[:, l, tp, ri, :], rhs=uT[:, ct, 0:N],
                                                                    start=True, stop=True), reads=["W2", "uT"], writes=[("ps", b)])
                    act(lambda e, X=X, i=i, b=b: e.copy(out=X[:, i * TB:i * TB + N], in_=PS[b][:, 0:N]), [("ps", b)], kX[2 * i:2 * i + 2])
            Xr3 = XSr.rearrange("p (t n) -> p t n", t=4)
            Xi3 = XSi.rearrange("p (t n) -> p t n", t=4)
            parts = [("dve", 0, 4, TD)]
            nlev = int(math.log2(N))
            steps = []
            if not first_block:
                steps.append(("carry", 0))
            for lev in range(nlev):
                steps.append(("up", lev))
            for lev in range(nlev - 2, -1, -1):
                steps.append(("down", lev))
            for kind, lev in steps:
                for eng, a0, na, TT in parts:
                    kr = kXSr[2 * a0:2 * (a0 + na)]; ki = kXSi[2 * a0:2 * (a0 + na)]
                    Xr = Xr3[:, a0:a0 + na, :]; Xi = Xi3[:, a0:a0 + na, :]
                    if kind == "carry":
                        m = 1
                        dr, di = Xr[:, :, 0:1], Xi[:, :, 0:1]
                        sr = car_r[:, l, 4 * ct + a0:4 * ct + a0 + na].rearrange("p (t o) -> p t o", o=1)
                        si = car_i[:, l, 4 * ct + a0:4 * ct + a0 + na].rearrange("p (t o) -> p t o", o=1)
                        ksr = ksi = ["car"]
                    else:
                        d = 1 << lev
                        if kind == "up":
                            m = N // (2 * d)
                            Xr4 = Xr.rearrange("p t (m s) -> p t m s", s=2 * d)
                            Xi4 = Xi.rearrange("p t (m s) -> p t m s", s=2 * d)
                        else:
                            m = N // (2 * d) - 1
                            Xr4 = Xr[:, :, d:N - d].rearrange("p t (m s) -> p t m s", s=2 * d)
                            Xi4 = Xi[:, :, d:N - d].rearrange("p t (m s) -> p t m s", s=2 * d)
                        dr, di = Xr4[:, :, :, 2 * d - 1], Xi4[:, :, :, 2 * d - 1]
                        sr, si = Xr4[:, :, :, d - 1], Xi4[:, :, :, d - 1]
                        ksr, ksi = kr, ki
                    sh = [128, na, m]
                    T = [t_[0].rearrange("p (t n) -> p t n", t=na)[:, :, 0:m] for t_ in TT]
                    kT = [t_[1] for t_ in TT]
                    cmul_add(eng, dr, di, sr, si, lam_b(LR, l, lev, 4 * ct + a0, na, sh), lam_b(LI, l, lev, 4 * ct + a0, na, sh),
                             lam_b(LIn, l, lev, 4 * ct + a0, na, sh), kr, ki, ksr, ksi, T, kT)
            dve(lambda e: e.tensor_copy(out=car_r[:, l, 4 * ct:4 * ct + 4], in_=Xr3[:, :, N - 1]), kXSr, ["car"])
            dve(lambda e: e.tensor_copy(out=car_i[:, l, 4 * ct:4 * ct + 4], in_=Xi3[:, :, N - 1]), kXSi, ["car"])
            act(lambda e: e.copy(out=xbr[:], in_=XSr), kXSr, kxbr)
            act(lambda e: e.copy(out=xbi[:], in_=XSi), kXSi, kxbi)
            return ssm_y(l, ct, N)

        def ssm_y(l, ct, N):
            b = bank()
            n = 0
            for i in range(4):
                tp = 4 * ct + i
                for ri, xb_, kx in ((0, xbr, kxbr), (1, xbi, kxbi)):
                    S.op("pe", lambda e, tp=tp, ri=ri, xb_=xb_, i=i, b=b, n=n: e.matmul(
                        PS[b][:, 0:N], lhsT=CP[:, l, tp, ri, :], rhs=xb_[:, i * TB:i * TB + N], start=(n == 0), stop=False),
                        reads=["CP"] + kx, writes=[("ps", b)])
                    n += 1
            S.op("pe", lambda e, b=b: e.matmul(PS[b][:, 0:N], lhsT=Dd[:, l, ct, :], rhs=uT[:, ct, 0:N], start=False, stop=True),
                 reads=["Dd", "uT"], writes=[("ps", b)])
            return b

        def ssm_group_sample(l, ct):
            N = NS
            for i in range(4):
                tp = 4 * ct + i
                for ri, X in ((0, XSr), (1, XSi)):
                    b = bank()
                    S.op("pe", lambda e, tp=tp, ri=ri, b=b: e.matmul(PS[b][:, 0:N], lhsT=W2[:, l, tp, ri, :], rhs=uT[:, ct, 0:N],
                                                                    start=True, stop=True), reads=["W2", "uT"], writes=[("ps", b)])
                    act(lambda e, X=X, i=i, b=b: e.copy(out=X[:, i * TB:i * TB + N], in_=PS[b][:, 0:N]), [("ps", b)], kXS)
            Xr4 = XSr.rearrange("p (t n) -> p t n", t=4)[:, :, 0:NS].rearrange("p t (b i) -> p t b i", i=4)
            Xi4 = XSi.rearrange("p (t n) -> p t n", t=4)[:, :, 0:NS].rearrange("p t (b i) -> p t b i", i=4)
            sh = [128, 4, NSB]
            T = [t_[0][:, 0:4 * NSB].rearrange("p (t n) -> p t n", t=4) for t_ in TD]
            kT = [t_[1] for t_ in TD]
            lr, li, lin = lam_b(LR, l, 0, 4 * ct, 4, sh), lam_b(LI, l, 0, 4 * ct, 4, sh), lam_b(LIn, l, 0, 4 * ct, 4, sh)
            for i in range(4):
                if i == 0:
                    sr, si, ksr, ksi = hs_r[:, 4 * ct:4 * ct + 4, :], hs_i[:, 4 * ct:4 * ct + 4, :], ["hs"], ["hs"]
                else:
                    sr, si, ksr, ksi = Xr4[:, :, :, i - 1], Xi4[:, :, :, i - 1], kXSr, kXSi
                cmul_add("dve", Xr4[:, :, :, i], Xi4[:, :, :, i], sr, si, lr, li, lin, kXSr, kXSi, ksr, ksi, T, kT)
            dve(lambda e: e.tensor_copy(out=hs_r[:, 4 * ct:4 * ct + 4, :], in_=Xr4[:, :, :, 3]), kXS, ["hs"])
            dve(lambda e: e.tensor_copy(out=hs_i[:, 4 * ct:4 * ct + 4, :], in_=Xi4[:, :, :, 3]), kXS, ["hs"])
            act(lambda e: e.copy(out=xbr[:], in_=XSr), kXS, kxbr)
            act(lambda e: e.copy(out=xbi[:], in_=XSi), kXS, kxbi)
            return ssm_y(l, ct, N)

        def layer_block(l, N, sample, blk):
            first_block = (blk == 0)
            last_block = (blk == NBLK - 1)
            wvin = wb_in[l].rearrange("(k p) c -> p k c", p=128)
            v8 = lambda ws: WS[ws][:].rearrange("p (k c) -> p k c", k=8)
            norm_block(N, gA[:, l, :])
            ws = wslot()
            wq = WS[ws][:].rearrange("p (k t two d) -> p k t two d", k=8, t=4, two=2)
            for two in range(2):
                for k in range(8):
                    S.op("sp", lambda e, two=two, k=k: e.dma_start(
                        out=wq[:, k, :, two, :], in_=wvin[:, k, 256 * two:256 * two + 256].rearrange("p (t d) -> p t d", d=64)),
                        reads=[("wb_in", l)], writes=wk(ws), dma=("ws", ws))
            wqv = v8(ws)
            for t in range(4):
                b = proj_tile(N, wqv, 128 * t, ws)
                headnorm_rope(N, b, l, gq[:, l:l + 1], blk_b, 1.0 / 64, True, qr[:, t, 0:N], ["qr"])
            ws = load_w(lambda w: v8(w)[:, :, 0:256], wvin[:, :, K_OFF:K_OFF + 256], ("wb_in", l))
            wkv_ = v8(ws)
            b = proj_tile(N, wkv_, 0, ws)
            kdst = kTc[:, l, 128:128 + N] if not sample else kTc[:, l, 0:N]
            headnorm_rope(N, b, l, gk[:, l:l + 1], blk_b, 1.0 / 64, True, kdst, ["kTc"], out32=k32[:, 0:N], out32_keys=["k32"])
            nsub = max(1, N // 128)
            pn = min(N, 128)
            b = bank()
            for s in range(nsub):
                for k in range(8):
                    S.op("pe", lambda e, s=s, k=k, b=b: e.matmul(PS[b][0:pn, 128 * s:128 * s + 128], lhsT=hT[:, k, 128 * s:128 * s + pn],
                                                                rhs=wkv_[:, k, 128:256], start=(k == 0), stop=(k == 7)),
                         reads=["hT", ("ws", ws)], writes=[("ps", b)])
            act(lambda e, b=b: e.copy(out=v32[0:pn, 0:nsub, :], in_=PS[b][0:pn, 0:128 * nsub].rearrange("p (s c) -> p s c", c=128)),
                [("ps", b)], ["v32"])
            vdst = vtc[0:pn, l, 1:1 + nsub, :] if not sample else vtc[0:pn, l, 0:1, :]
            dve(lambda e: e.tensor_copy(out=vdst, in_=v32[0:pn, 0:nsub, :]), ["v32"], ["vtc"])
            if not sample:
                for s in range(nsub):
                    for h in range(2):
                        hs = slice(64 * h, 64 * h + 64)
                        pt = PT[(2 * s + h) % 2]; kpt = "PT%d" % ((2 * s + h) % 2)
                        use_prev = not (first_block and s == 0)
                        parts = ([0] if use_prev else []) + [1]
                        sbk = {}
                        for part in parts:
                            b = bank(); sbk[part] = b
                            c0 = 128 * s + 128 * part
                            S.op("pe", lambda e, b=b, c0=c0, hs=hs, s=s: e.matmul(
                                PS[b][:].rearrange("p (t c) -> p t c", t=4), lhsT=kTc[hs, l, c0:c0 + 128],
                                rhs=qr[hs, :, 128 * s:128 * s + 128], start=True, stop=True),
                                reads=["kTc", "qr"], writes=[("ps", b)])
                            act(lambda e, b=b, part=part, pt=pt: e.activation(out=pt[:, part, :], in_=PS[b][:], func=AF.Exp, scale=0.125),
                                [("ps", b)], [kpt])
                            mk_ = mprev_b if part == 0 else mcur_b
                            S.op("pool", lambda e, part=part, pt=pt, mk_=mk_: e.tensor_tensor(
                                out=pt[:, part, :].rearrange("p (t c) -> p t c", t=4), in0=pt[:, part, :].rearrange("p (t c) -> p t c", t=4),
                                in1=mk_[:].rearrange("p (o c) -> p o c", o=1).to_broadcast([128, 4, 128]), op=ALU.mult),
                                reads=[kpt, "mprev_b", "mcur_b"], writes=[kpt])
                        bo = bank(); bd = bank()
                        for n_, part in enumerate(parts):
                            S.op("pe", lambda e, part=part, n_=n_, bo=bo, pt=pt, s=s, hs=hs: e.matmul(
                                PS[bo][hs, :], lhsT=vtc[:, l, s + part, hs], rhs=pt[:, part, :], start=(n_ == 0), stop=(n_ == len(parts) - 1)),
                                reads=["vtc", kpt], writes=[("ps", bo)])
                        for n_, part in enumerate(parts):
                            S.op("pe", lambda e, part=part, n_=n_, bd=bd, pt=pt, hs=hs: e.matmul(
                                PS[bd][hs, :], lhsT=ones_b[:, hs], rhs=pt[:, part, :], start=(n_ == 0), stop=(n_ == len(parts) - 1)),
                                reads=["ones_b", kpt], writes=[("ps", bd)])
                        dd = dn[h]; kd = "dn%d" % h
                        dve(lambda e, bd=bd, dd=dd, hs=hs: e.tensor_tensor(
                            out=dd[hs, :].rearrange("p (t c) -> p t c", t=4), in0=PS[bd][hs, :].rearrange("p (t c) -> p t c", t=4),
                            in1=esink[hs, l, :].rearrange("p (t o) -> p t o", o=1).to_broadcast([64, 4, 128]), op=ALU.add),
                            [("ps", bd), "esink"], [kd])
                        dve(lambda e, dd=dd, hs=hs: e.reciprocal(out=dd[hs, :], in_=dd[hs, :]), [kd], [kd])
                        dve(lambda e, bo=bo, dd=dd, hs=hs, s=s: e.tensor_tensor(
                            out=oa[hs, :, 128 * s:128 * s + 128], in0=PS[bo][hs, :].rearrange("p (t c) -> p t c", t=4),
                            in1=dd[hs, :].rearrange("p (t c) -> p t c", t=4), op=ALU.mult), [("ps", bo), kd], ["oa"])
                if last_block:
                    b = bank()
                    S.op("pe", lambda e, b=b: e.transpose(out=PS[b][:, 0:128], in_=k32[:, N - 128:N], identity=ident_f[:]),
                         reads=["k32", "ident_f"], writes=[("ps", b)])
                    act(lambda e, b=b: e.copy(out=t1[:, 0:128], in_=PS[b][:, 0:128]), [("ps", b)], ["t1"])
                    out_toks.append(S.op("sp", lambda e: e.dma_start(out=kp[l], in_=t1[:, 0:128]), reads=["t1"], dma="o_kp"))
                    out_toks.append(S.op("sp", lambda e: e.dma_start(out=vp[l], in_=v32[:, 3, :]), reads=["v32"], dma="o_vp"))
                else:
                    act(lambda e: e.copy(out=kTc[:, l, 0:128], in_=kTc[:, l, N:N + 128]), ["kTc"], ["kTc"])
                    act(lambda e: e.copy(out=vtc[:, l, 0, :], in_=vtc[:, l, 4, :]), ["vtc"], ["vtc"])
            else:
                b = bank()
                S.op("pe", lambda e, b=b: e.transpose(out=PS[b][0:NS, 0:128], in_=k32[:, 0:NS], identity=ident_f[:]),
                     reads=["k32", "ident_f"], writes=[("ps", b)])
                act(lambda e, b=b: e.copy(out=t1[0:NS, 0:128], in_=PS[b][0:NS, 0:128]), [("ps", b)], ["t1"])
                for bb in range(NSB):
                    out_toks.append(S.op("sp", lambda e, bb=bb: e.dma_start(out=ks[l, bb, 124:128, :], in_=t1[4 * bb:4 * bb + 4, 0:128]),
                                         reads=["t1"], dma="o_ks"))
                    out_toks.append(S.op("sp", lambda e, bb=bb: e.dma_start(out=vs[l, bb, 124:128, :], in_=v32[4 * bb:4 * bb + 4, 0, :]),
                                         reads=["v32"], dma="o_vs"))
                out_toks.append(S.op("sp", lambda e: e.dma_start(out=ks[l, :, 0:124, :], in_=csk[l, :, 4:128, :]), dma="o_ks"))
                out_toks.append(S.op("sp", lambda e: e.dma_start(out=vs[l, :, 0:124, :], in_=csv[l, :, 4:128, :]), dma="o_vs"))
                ptn = PT[0]; ptc = PT[1]
                for h in range(2):
                    hs = slice(64 * h, 64 * h + 64)
                    b = bank()
                    S.op("pe", lambda e, b=b, hs=hs: e.matmul(PS[b][0:NS, 0:4 * NS].rearrange("p (t c) -> p t c", t=4),
                                                             lhsT=kTc[hs, l, 0:NS], rhs=qr[hs, :, 0:NS], start=True, stop=True),
                         reads=["kTc", "qr"], writes=[("ps", b)])
                    act(lambda e, b=b, h=h: e.activation(out=ptn[0:NS, h, 0:4 * NS], in_=PS[b][0:NS, 0:4 * NS], func=AF.Exp, scale=0.125),
                        [("ps", b)], ["PT0"])
                    dve(lambda e, h=h: e.tensor_tensor(
                        out=ptn[0:NS, h, 0:4 * NS].rearrange("p (t c) -> p t c", t=4), in0=ptn[0:NS, h, 0:4 * NS].rearrange("p (t c) -> p t c", t=4),
                        in1=mnew_b[:].rearrange("p (o c) -> p o c", o=1).to_broadcast([NS, 4, NS]), op=ALU.mult), ["PT0", "mnew_b"], ["PT0"])
                bsc = bank(); held.add(bsc)
                for bb in range(NSB):
                    kst = sg[bb % 2]; kk_ = "sg%d" % (bb % 2)
                    S.op("pool", lambda e, bb=bb, kst=kst: e.dma_start(out=kst[:, 0:128], in_=csk[l, bb]), writes=[kk_], dma=kk_)
                    S.op("pool", lambda e, bb=bb, kst=kst: e.dma_start(out=kst[:, 128:256], in_=csv[l, bb]), writes=[kk_], dma=kk_)
                    b = bank()
                    pb = PS[b][:].bitcast(BF16)
                    S.op("pe", lambda e, kst=kst, pb=pb: e.transpose(out=pb[:, 0:128], in_=kst[:, 0:128], identity=ident_b[:]),
                         reads=[kk_, "ident_b"], writes=[("ps", b)])
                    act(lambda e, pb=pb, bb=bb: e.copy(out=kcT[:, bb % 2, :], in_=pb[:, 0:128]), [("ps", b)], ["kcT%d" % (bb % 2)])
                    for h in range(2):
                        hs = slice(64 * h, 64 * h + 64)
                        c0 = (bb * 2 + h) * 16
                        S.op("pe", lambda e, bb=bb, hs=hs, c0=c0: e.matmul(
                            PS[bsc][:, c0:c0 + 16].rearrange("p (t i) -> p t i", t=4), lhsT=kcT[hs, bb % 2, :],
                            rhs=qr[hs, :, 4 * bb:4 * bb + 4], start=True, stop=True),
                            reads=["kcT%d" % (bb % 2), "qr"], writes=[("ps", bsc)])
                    dve(lambda e, bb=bb, kst=kst: e.tensor_copy(out=RA[:, 128 * bb:128 * bb + 128], in_=kst[:, 128:256]), [kk_], RAK[0:4])
                act(lambda e: e.activation(out=ptc[:, 0, :], in_=PS[bsc][:], func=AF.Exp, scale=0.125), [("ps", bsc)], ["PT1"])
                held.discard(bsc)
                dve(lambda e: e.tensor_tensor(
                    out=ptc[:, 0, :].rearrange("p (a i) -> p a i", i=4), in0=ptc[:, 0, :].rearrange("p (a i) -> p a i", i=4),
                    in1=mc_b[:].rearrange("p (o i) -> p o i", o=1).to_broadcast([128, 128, 4]), op=ALU.mult), ["PT1", "mc_b"], ["PT1"])
                for h in range(2):
                    hs = slice(64 * h, 64 * h + 64)
                    for which, lw in ((0, None), (1, None)):
                        bo = bank()
                        lhs_new = vtc[0:NS, l, 0, hs] if which == 0 else ones_b[0:NS, hs]
                        S.op("pe", lambda e, bo=bo, hs=hs, h=h, lhs_new=lhs_new: e.matmul(
                            PS[bo][hs, 0:4 * NS], lhsT=lhs_new, rhs=ptn[0:NS, h, 0:4 * NS], start=True, stop=False),
                            reads=["vtc", "ones_b", "PT0"], writes=[("ps", bo)])
                        for bb in range(NSB):
                            c0 = (bb * 2 + h) * 16
                            lhs_c = RA[:, 128 * bb + 64 * h:128 * bb + 64 * h + 64] if which == 0 else ones_b[:, hs]
                            S.op("pe", lambda e, bo=bo, hs=hs, bb=bb, c0=c0, lhs_c=lhs_c: e.matmul(
                                PS[bo][hs, 0:4 * NS].rearrange("p (t c) -> p t c", t=4)[:, :, 4 * bb:4 * bb + 4], lhsT=lhs_c,
                                rhs=ptc[:, 0, c0:c0 + 16].rearrange("p (t i) -> p t i", t=4), start=False, stop=(bb == NSB - 1)),
                                reads=RAK[0:4] + ["ones_b", "PT1"], writes=[("ps", bo)])
                        if which == 0:
                            bnum = bo
                        else:
                            bden = bo
                    dd = dn[h]; kd = "dn%d" % h
                    dve(lambda e, bden=bden, dd=dd, hs=hs: e.tensor_tensor(
                        out=dd[hs, 0:4 * NS].rearrange("p (t c) -> p t c", t=4), in0=PS[bden][hs, 0:4 * NS].rearrange("p (t c) -> p t c", t=4),
                        in1=esink[hs, l, :].rearrange("p (t o) -> p t o", o=1).to_broadcast([64, 4, NS]), op=ALU.add),
                        [("ps", bden), "esink"], [kd])
                    dve(lambda e, dd=dd, hs=hs: e.reciprocal(out=dd[hs, 0:4 * NS], in_=dd[hs, 0:4 * NS]), [kd], [kd])
                    dve(lambda e, bnum=bnum, dd=dd, hs=hs: e.tensor_tensor(
                        out=oa[hs, :, 0:NS], in0=PS[bnum][hs, 0:4 * NS].rearrange("p (t c) -> p t c", t=4),
                        in1=dd[hs, 0:4 * NS].rearrange("p (t c) -> p t c", t=4), op=ALU.mult), [("ps", bnum), kd], ["oa"])
            ws = load_w(lambda w: v8(w), wvin[:, :, U_OFF:U_OFF + 512], ("wb_in", l))
            for t in range(4):
                b = proj_tile(N, v8(ws), 128 * t, ws)
                act(lambda e, b=b, t=t: e.copy(out=uT[:, t, 0:N], in_=PS[b][:, 0:N]), [("ps", b)], ["uT"])
            if sample:
                for gl in range(2):
                    sl = slice(64 * gl, 64 * gl + 64)
                    for bb in range(NSB):
                        S.op("sp", lambda e, gl=gl, sl=sl, bb=bb: e.dma_start(
                            out=hs_r[sl, :, bb], in_=sre[l, bb].rearrange("(tp gl) p -> gl p tp", gl=2)[gl], allow_slow_non_contiguous=True),
                            writes=["hs"], dma="hs")
                        S.op("sp", lambda e, gl=gl, sl=sl, bb=bb: e.dma_start(
                            out=hs_i[sl, :, bb], in_=sim[l, bb].rearrange("(tp gl) p -> gl p tp", gl=2)[gl], allow_slow_non_contiguous=True),
                            writes=["hs"], dma="hs")
            for ct in range(4):
                by = ssm_group_sample(l, ct) if sample else ssm_group_prompt(l, ct, N, first_block)
                act(lambda e, by=by, ct=ct: e.activation(out=zT[:, ct, 0:N], in_=PS[by][:, 0:N], func=AF.Gelu), [("ps", by)], ["zT"])
            if sample:
                for gl in range(2):
                    sl = slice(64 * gl, 64 * gl + 64)
                    for bb in range(NSB):
                        out_toks.append(S.op("sp", lambda e, gl=gl, sl=sl, bb=bb: e.dma_start(
                            out=hrs[l, bb].rearrange("(tp gl) p -> gl p tp", gl=2)[gl], in_=hs_r[sl, :, bb], allow_slow_non_contiguous=True),
                            reads=["hs"], dma="o_hs"))
                        out_toks.append(S.op("sp", lambda e, gl=gl, sl=sl, bb=bb: e.dma_start(
                            out=his[l, bb].rearrange("(tp gl) p -> gl p tp", gl=2)[gl], in_=hs_i[sl, :, bb], allow_slow_non_contiguous=True),
                            reads=["hs"], dma="o_hs"))
            elif last_block:
                for gl in range(2):
                    sl = slice(64 * gl, 64 * gl + 64)
                    out_toks.append(S.op("sp", lambda e, gl=gl, sl=sl: e.dma_start(
                        out=hrp[l].rearrange("(tp gl) p -> gl p tp", gl=2)[gl], in_=car_r[sl, l, :], allow_slow_non_contiguous=True),
                        reads=["car"], dma="o_hp"))
                    out_toks.append(S.op("sp", lambda e, gl=gl, sl=sl: e.dma_start(
                        out=hip[l].rearrange("(tp gl) p -> gl p tp", gl=2)[gl], in_=car_i[sl, l, :], allow_slow_non_contiguous=True),
                        reads=["car"], dma="o_hp"))
            ws = load_w(lambda w: WS[w][:, 0:2048].rearrange("p (k c) -> p k c", k=4), wb_glu[l].rearrange("(k p) c -> p k c", p=128),
                        ("wb_glu", l))
            wg = WS[ws][:, 0:2048].rearrange("p (k c) -> p k c", k=4)
            for t in range(4):
                b = bank()
                for k in range(4):
                    S.op("pe", lambda e, b=b, k=k, t=t: e.matmul(PS[b][:, 0:N], lhsT=wg[:, k, 128 * t:128 * t + 128], rhs=zT[:, k, 0:N],
                                                                start=(k == 0), stop=(k == 3)), reads=["zT", ("ws", ws)], writes=[("ps", b)])
                s_ = sg[t % 2]; ks_ = "sg%d" % (t % 2)
                act(lambda e, b=b, s_=s_: e.activation(out=s_[:, 0:N], in_=PS[b][:, 0:N], func=AF.Sigmoid), [("ps", b)], [ks_])
                dve(lambda e, t=t, s_=s_: e.tensor_tensor(out=ob[:, t, 0:N], in0=zT[:, t, 0:N], in1=s_[:, 0:N], op=ALU.mult), ["zT", ks_], ["ob"])
            ws = load_w(lambda w: v8(w), wvin[:, :, MQ_OFF:MQ_OFF + 512], ("wb_in", l))
            for t in range(4):
                b = proj_tile(N, v8(ws), 128 * t, ws)
                headnorm_rope(N, b, l, gmq[:, l:l + 1], ones_b, 1.0 / 128, False, qmn[:, t, 0:N], ["qmn"])
            sc_m = 1.0 / math.sqrt(128.0)
            if not sample:
                for h in range(4):
                    pt = PT[h % 2]; kpt = "PT%d" % (h % 2)
                    for kt in range(2):
                        b = bank()
                        S.op("pe", lambda e, b=b, h=h, kt=kt: e.matmul(PS[b][:, 0:N], lhsT=MKT[:, l, h, 128 * kt:128 * kt + 128], rhs=qmn[:, h, 0:N],
                                                                      start=True, stop=True), reads=["MKT", "qmn"], writes=[("ps", b)])
                        act(lambda e, b=b, kt=kt, pt=pt: e.activation(out=pt[:, kt, 0:N], in_=PS[b][:, 0:N], func=AF.Exp, scale=sc_m),
                            [("ps", b)], [kpt])
                    bo = bank(); bd = bank()
                    for kt in range(2):
                        S.op("pe", lambda e, bo=bo, h=h, kt=kt, pt=pt: e.matmul(PS[bo][:, 0:N], lhsT=MV[:, l, kt, 128 * h:128 * h + 128],
                                                                               rhs=pt[:, kt, 0:N], start=(kt == 0), stop=(kt == 1)),
                             reads=["MV", kpt], writes=[("ps", bo)])
                    for kt in range(2):
                        S.op("pe", lambda e, bd=bd, kt=kt, pt=pt: e.matmul(PS[bd][:, 0:N], lhsT=ones_b[:], rhs=pt[:, kt, 0:N],
                                                                          start=(kt == 0), stop=(kt == 1)),
                             reads=["ones_b", kpt], writes=[("ps", bd)])
                    dd = dn[h % 2]; kd = "dn%d" % (h % 2)
                    dve(lambda e, bd=bd, dd=dd: e.reciprocal(out=dd[:, 0:N], in_=PS[bd][:, 0:N]), [("ps", bd)], [kd])
                    dve(lambda e, bo=bo, dd=dd, h=h: e.tensor_tensor(out=oc[:, h, 0:N], in0=PS[bo][:, 0:N], in1=dd[:, 0:N], op=ALU.mult),
                        [("ps", bo), kd], ["oc"])
            else:
                bsc = bank(); held.add(bsc)
                bo = bank(); held.add(bo)
                ptc = PT[1]
                for bb in range(NSB):
                    wsk = wslot()
                    kcb = WS[wsk][:, 0:1024].rearrange("p (a c) -> p a c", a=2)
                    vcb = WS[wsk][:, 1024:2048].rearrange("p (a c) -> p a c", a=2)
                    S.op("pool", lambda e, bb=bb, kcb=kcb: e.dma_start(out=kcb, in_=cmk[l, bb].rearrange("(a p) c -> p a c", p=128)),
                         writes=wk(wsk), dma=("ws", wsk))
                    S.op("pool", lambda e, bb=bb, vcb=vcb: e.dma_start(out=vcb, in_=cmv[l, bb].rearrange("(a p) c -> p a c", p=128)),
                         writes=wk(wsk), dma=("ws", wsk))
                    for h in range(4):
                        for kt in range(2):
                            b = bank()
                            pb = PS[b][:].bitcast(BF16)
                            o0 = 2048 + 256 * h + 128 * kt
                            S.op("pe", lambda e, pb=pb, kcb=kcb, h=h, kt=kt: e.transpose(out=pb[:, 0:128], in_=kcb[:, kt, 128 * h:128 * h + 128],
                                                                                        identity=ident_b[:]),
                                 reads=[("ws", wsk), "ident_b"], writes=[("ps", b)])
                            act(lambda e, pb=pb, wsk=wsk, o0=o0: e.copy(out=WS[wsk][:, o0:o0 + 128], in_=pb[:, 0:128]),
                                [("ps", b)], [("wsT", wsk)])
                    for h in range(4):
                        for kt in range(2):
                            c0 = ((bb * 4 + h) * 2 + kt) * 4
                            o0 = 2048 + 256 * h + 128 * kt
                            S.op("pe", lambda e, h=h, c0=c0, bb=bb, wsk=wsk, o0=o0: e.matmul(
                                PS[bsc][:, c0:c0 + 4], lhsT=WS[wsk][:, o0:o0 + 128],
                                rhs=qmn[:, h, 4 * bb:4 * bb + 4], start=True, stop=True),
                                reads=[("wsT", wsk), "qmn"], writes=[("ps", bsc)])
                    c0 = bb * 32
                    act(lambda e, c0=c0: e.activation(out=ptc[:, 0, c0:c0 + 32], in_=PS[bsc][:, c0:c0 + 32], func=AF.Exp, scale=sc_m),
                        [("ps", bsc)], ["PT1"])
                    for h in range(4):
                        for kt in range(2):
                            c1 = ((bb * 4 + h) * 2 + kt) * 4
                            S.op("pe", lambda e, h=h, kt=kt, c1=c1, bb=bb, vcb=vcb: e.matmul(
                                PS[bo][:, h * NS + 4 * bb:h * NS + 4 * bb + 4], lhsT=vcb[:, kt, 128 * h:128 * h + 128],
                                rhs=ptc[:, 0, c1:c1 + 4], start=(kt == 0), stop=(kt == 1)),
                                reads=[("ws", wsk), "PT1"], writes=[("ps", bo)])
                bd = bank()
                pv = ptc[:, 0, :].rearrange("p (b h kt i) -> p h kt b i", b=NSB, h=4, kt=2)
                for h in range(4):
                    for kt in range(2):
                        S.op("pe", lambda e, bd=bd, kt=kt, h=h: e.matmul(PS[bd][:, h * NS:(h + 1) * NS].rearrange("p (b i) -> p b i", i=4),
                                                                        lhsT=ones_b[:], rhs=pv[:, h, kt, :, :], start=(kt == 0), stop=(kt == 1)),
                             reads=["ones_b", "PT1"], writes=[("ps", bd)])
                dd = dn[0]
                dve(lambda e, bd=bd: e.reciprocal(out=dd[:, 0:4 * NS], in_=PS[bd][:, 0:4 * NS]), [("ps", bd)], ["dn0"])
                dve(lambda e, bo=bo: e.tensor_tensor(out=oc[:, :, 0:NS], in0=PS[bo][:, 0:4 * NS].rearrange("p (h c) -> p h c", h=4),
                                                     in1=dd[:, 0:4 * NS].rearrange("p (h c) -> p h c", h=4), op=ALU.mult),
                    [("ps", bo), "dn0"], ["oc"])
                held.discard(bsc); held.discard(bo)
            macc3 = macc.rearrange("p (m n) -> p m n", m=8)
            mgT3 = mgT.rearrange("p (m n) -> p m n", m=8)
            for n_, on_ in enumerate((oa, ob, oc)):
                okey = ("oa", "ob", "oc")[n_]
                wsb = wslot()
                wbv = WS[wsb][:].rearrange("p (k c) -> p k c", k=4)
                if n_ == 0:
                    for t in range(4):
                        for two in range(2):
                            r0 = (two * 4 + t) * 64
                            S.op("sp", lambda e, t=t, two=two, r0=r0: e.dma_start(out=wbv[64 * two:64 * two + 64, t, :], in_=wb_br[l, 0, r0:r0 + 64, :]),
                                 reads=[("wb_br", l)], writes=wk(wsb), dma=("ws", wsb))
                else:
                    S.op("sp", lambda e, n_=n_: e.dma_start(out=wbv, in_=wb_br[l, n_].rearrange("(k p) c -> p k c", p=128)),
                         reads=[("wb_br", l)], writes=wk(wsb), dma=("ws", wsb))
                for half in range(2):
                    wsg = load_w(lambda w: v8(w), wvin[:, :, G_OFF + n_ * 1024 + 512 * half:G_OFF + n_ * 1024 + 512 * half + 512], ("wb_in", l))
                    for mm_ in range(4):
                        m = 4 * half + mm_
                        bg = proj_tile(N, v8(wsg), 128 * mm_, wsg)
                        s_ = sg[m % 2]; ks_ = "sg%d" % (m % 2)
                        act(lambda e, bg=bg, s_=s_: e.activation(out=s_[:, 0:N], in_=PS[bg][:, 0:N], func=AF.Sigmoid), [("ps", bg)], [ks_])
                        bp = bank()
                        for k in range(4):
                            S.op("pe", lambda e, bp=bp, k=k, m=m, on_=on_: e.matmul(PS[bp][:, 0:N], lhsT=wbv[:, k, 128 * m:128 * m + 128],
                                                                                  rhs=on_[:, k, 0:N], start=(k == 0), stop=(k == 3)),
                                 reads=[okey, ("ws", wsb)], writes=[("ps", bp)])
                        if n_ == 0:
                            dve(lambda e, bp=bp, s_=s_, m=m: e.tensor_tensor(out=macc3[:, m, 0:N], in0=PS[bp][:, 0:N], in1=s_[:, 0:N], op=ALU.mult),
                                [("ps", bp), ks_], kmacc)
                        else:
                            dve(lambda e, bp=bp, s_=s_: e.tensor_tensor(out=t1[:, 0:N], in0=PS[bp][:, 0:N], in1=s_[:, 0:N], op=ALU.mult),
                                [("ps", bp), ks_], ["t1"])
                            if n_ == 1:
                                dve(lambda e, m=m: e.tensor_tensor(out=macc3[:, m, 0:N], in0=macc3[:, m, 0:N], in1=t1[:, 0:N], op=ALU.add),
                                    kmacc + ["t1"], kmacc)
                            else:
                                dve(lambda e, m=m: e.tensor_tensor(out=mgT3[:, m, 0:N], in0=macc3[:, m, 0:N], in1=t1[:, 0:N], op=ALU.add),
                                    kmacc + ["t1"], kmgT)
            for half in range(2):
                wso = load_w(lambda w: v8(w), wb_out[l].rearrange("(k p) c -> p k c", p=128)[:, :, 512 * half:512 * half + 512], ("wb_out", l))
                for mm_ in range(4):
                    m = 4 * half + mm_
                    b = bank()
                    for k in range(8):
                        S.op("pe", lambda e, b=b, k=k, mm_=mm_, wso=wso: e.matmul(PS[b][:, 0:N], lhsT=v8(wso)[:, k, 128 * mm_:128 * mm_ + 128],
                                                                                 rhs=mgT3[:, k, 0:N], start=(k == 0), stop=(k == 7)),
                             reads=kmgT + [("ws", wso)], writes=[("ps", b)])
                    dve(lambda e, b=b, m=m: e.tensor_tensor(out=xT[:, m, 0:N], in0=xT[:, m, 0:N], in1=PS[b][:, 0:N], op=ALU.add),
                        [("ps", b), "xT"], ["xT"])
            norm_block(N, gF[:, l, :])
            wvup = wb_up[l].rearrange("(k p) c -> p k c", p=128)

            def actT(j):
                return RA[:, j * TB:(j + 1) * TB], [RAK[j]]

            for grp in range(6):
                nt = 4 if grp < 5 else 2
                wsg = load_w(lambda w: v8(w)[:, :, 0:128 * nt], wvup[:, :, 512 * grp:512 * grp + 128 * nt], ("wb_up", l))
                wsu = load_w(lambda w: v8(w)[:, :, 0:128 * nt], wvup[:, :, DFF + 512 * grp:DFF + 512 * grp + 128 * nt], ("wb_up", l))
                for jj in range(nt):
                    j = 4 * grp + jj
                    bg = proj_tile(N, v8(wsg), 128 * jj, wsg)
                    bu = proj_tile(N, v8(wsu), 128 * jj, wsu)
                    s_ = sg[j % 2]; ks_ = "sg%d" % (j % 2)
                    act(lambda e, bg=bg, s_=s_: e.activation(out=s_[:, 0:N], in_=PS[bg][:, 0:N], func=AF.Silu), [("ps", bg)], [ks_])
                    av, ak = actT(j)
                    dve(lambda e, bu=bu, s_=s_, av=av: e.tensor_tensor(out=av[:, 0:N], in0=PS[bu][:, 0:N], in1=s_[:, 0:N], op=ALU.mult),
                        [("ps", bu), ks_], ak)
            wvdn = wb_dn[l].rearrange("(k p) c -> p k c", p=128)
            for q4 in range(4):
                wsl = []
                for hh in range(2):
                    w_ = wslot()
                    S.op("sp", lambda e, w_=w_, hh=hh, q4=q4: e.dma_start(
                        out=WS[w_][:, 0:11 * 256].rearrange("p (k c) -> p k c", k=11), in_=wvdn[:, 11 * hh:11 * hh + 11, 256 * q4:256 * q4 + 256]),
                        reads=[("wb_dn", l)], writes=wk(w_), dma=("ws", w_))
                    wsl.append(w_)
                for mm_ in range(2):
                    m = 2 * q4 + mm_
                    b = bank()
                    for j in range(22):
                        w_ = wsl[j // 11]
                        wv_ = WS[w_][:, 0:11 * 256].rearrange("p (k c) -> p k c", k=11)
                        av, ak = actT(j)
                        S.op("pe", lambda e, b=b, j=j, mm_=mm_, wv_=wv_, av=av: e.matmul(
                            PS[b][:, 0:N], lhsT=wv_[:, j % 11, 128 * mm_:128 * mm_ + 128], rhs=av[:, 0:N], start=(j == 0), stop=(j == 21)),
                            reads=ak + [("ws", w_)], writes=[("ps", b)])
                    dve(lambda e, b=b, m=m: e.tensor_tensor(out=xT[:, m, 0:N], in0=xT[:, m, 0:N], in1=PS[b][:, 0:N], op=ALU.add),
                        [("ps", b), "xT"], ["xT"])


        def run_block(blk, sample):
            N = NS if sample else TB
            nsub = max(1, N // 128)
            pn = min(N, 128)
            if sample:
                S.op("sp", lambda e: e.dma_start(out=xtok[0:NS, 0, :], in_=xs), writes=["xtok"], dma="xtok")
                S.op("sp", lambda e: e.dma_start(out=cosb[:, 0:NS], in_=c_cos_s), writes=["cosb"], dma="cosb")
                S.op("sp", lambda e: e.dma_start(out=sinb[:, 0:NS], in_=c_sin_s), writes=["sinb"], dma="sinb")
            else:
                t0 = blk * TB
                S.op("sp", lambda e: e.dma_start(out=xtok[:], in_=xp[t0:t0 + TB, :].rearrange("(s p) d -> p s d", p=128)),
                     writes=["xtok"], dma="xtok")
                S.op("sp", lambda e: e.dma_start(out=cosb[:], in_=c_cos[:, t0:t0 + TB]), writes=["cosb"], dma="cosb")
                S.op("sp", lambda e: e.dma_start(out=sinb[:], in_=c_sin[:, t0:t0 + TB]), writes=["sinb"], dma="sinb")
            for s in range(nsub):
                for k in range(8):
                    b = bank()
                    S.op("pe", lambda e, s=s, k=k, b=b: e.transpose(out=PS[b][:, 0:pn], in_=xtok[0:pn, s, 128 * k:128 * k + 128],
                                                                    identity=ident_f[0:pn, 0:pn]),
                         reads=["xtok", "ident_f"], writes=[("ps", b)])
                    act(lambda e, s=s, k=k, b=b: e.copy(out=xT[:, k, 128 * s:128 * s + pn], in_=PS[b][:, 0:pn]), [("ps", b)], ["xT"])
            for l in range(NL):
                layer_block(l, N, sample, blk)
            for s in range(nsub):
                for k in range(8):
                    b = bank()
                    S.op("pe", lambda e, s=s, k=k, b=b: e.transpose(out=PS[b][0:pn, 0:128], in_=xT[:, k, 128 * s:128 * s + pn],
                                                                    identity=ident_f[:]),
                         reads=["xT", "ident_f"], writes=[("ps", b)])
                    act(lambda e, s=s, k=k, b=b: e.copy(out=xtok[0:pn, s, 128 * k:128 * k + 128], in_=PS[b][0:pn, 0:128]),
                        [("ps", b)], ["xtok"])
            if sample:
                out_toks.append(S.op("sp", lambda e: e.dma_start(out=ys, in_=xtok[0:NS, 0, :]), reads=["xtok"], dma="o_y"))
            else:
                t0 = blk * TB
                out_toks.append(S.op("sp", lambda e: e.dma_start(out=yp[t0:t0 + TB, :].rearrange("(s p) d -> p s d", p=128), in_=xtok[:]),
                                     reads=["xtok"], dma="o_y"))

        for blk in range(NBLK):
            run_block(blk, False)
        run_block(0, True)
        S.wait_all("sp", out_toks)
        with nc.allow_non_contiguous_dma(reason="small strided parameter / state transfers"):
            S.emit()
    return nc


_NC_CACHE = {}


def _consts():
    c = {}
    c["c_ident"] = np.eye(128, dtype=np.float32)
    blk = np.zeros((128, 128), np.float32)
    blk[:64, :64] = 1.0
    blk[64:, 64:] = 1.0
    c["c_blk"] = blk
    p = np.arange(128)
    lo = (p % 64) < 32
    partner = np.where(lo, p + 32, p - 32)
    perm = np.zeros((128, 128), np.float32)
    perm[partner, p] = 1.0
    c["c_perm"] = perm
    j = np.arange(128)[:, None]
    i = np.arange(128)[None, :]
    c["c_mprev"] = (j > i).astype(np.float32)
    c["c_mcur"] = (j <= i).astype(np.float32)
    half = 32
    inv = (np.float32(10000.0) ** (-(np.arange(half, dtype=np.float32) / np.float32(half)))).astype(np.float32)
    invp = inv[p % 32]
    sign = np.where(lo, -1.0, 1.0).astype(np.float32)

    def tables(pos):
        ang = (pos[None, :].astype(np.float32) * invp[:, None]).astype(np.float32)
        return np.cos(ang).astype(np.float32), (np.sin(ang) * sign[:, None]).astype(np.float32)

    c["c_cos"], c["c_sin"] = tables(np.arange(SEQ, dtype=np.float32))
    pos_s = np.float32(PAST) + np.tile(np.arange(4, dtype=np.float32), NSB)
    c["c_cos_s"], c["c_sin_s"] = tables(pos_s)
    r = np.arange(128)[:, None]
    ii = np.arange(4)[None, :]
    c["c_mc"] = (r > ii).astype(np.float32)
    kb, kj = np.divmod(np.arange(NS), 4)
    c["c_mnew"] = ((kb[:, None] == kb[None, :]) & (kj[:, None] <= kj[None, :])).astype(np.float32)
    c["c_rowm"] = (np.arange(128)[:, None] // 32 == np.arange(4)[None, :]).astype(np.float32)
    return c


def kernel(**inputs):
    f = lambda a: np.ascontiguousarray(np.asarray(a, dtype=np.float32))
    inp = {k: f(v) for k, v in inputs.items()}
    if "nc" not in _NC_CACHE:
        _NC_CACHE["nc"] = build_program()
    nc = _NC_CACHE["nc"]
    consts = _consts()
    wnames = ["attn_norm", "w_in", "q_norm", "k_norm", "attn_sinks", "ssm_a_re", "ssm_a_im", "ssm_log_dt", "ssm_b_re", "ssm_b_im",
              "ssm_c_re", "ssm_c_im", "ssm_d", "ssm_w_glu", "mem_norm", "w_mem_kv", "mem_q_norm", "mem_k_norm", "w_branch", "w_out",
              "ffn_norm", "w_ffn_up", "w_ffn_down"]
    in_maps = []
    for c in range(8):
        b0 = NSB * c
        m = {
            "xp": inp["x_prompt"][c % 4],
            "xs": inp["x_sample"][b0:b0 + NSB].reshape(NS, D),
            "csk": inp["cache_swa_k"][:, b0:b0 + NSB].reshape(NL, NSB, 128, 128),
            "csv": inp["cache_swa_v"][:, b0:b0 + NSB].reshape(NL, NSB, 128, 128),
            "sre": inp["state_ssm_re"][:, b0:b0 + NSB],
            "sim": inp["state_ssm_im"][:, b0:b0 + NSB],
            "cmk": inp["cache_mem_k"][:, b0:b0 + NSB].reshape(NL, NSB, 256, 512),
            "cmv": inp["cache_mem_v"][:, b0:b0 + NSB].reshape(NL, NSB, 256, 512),
            "memp": inp["mem_prompt"][c % 4],
        }
        for w in wnames:
            m[w] = inp[w]
        m.update(consts)
        in_maps.append({k: np.ascontiguousarray(v) for k, v in m.items()})
    res = run_bass_kernel_spmd(nc, in_maps, core_ids=list(range(8)))
    R = res.results
    cat = lambda name, cores: np.stack([R[c][name] for c in cores])
    y_p = cat("yp", range(4))
    y_s = np.concatenate([R[c]["ys"].reshape(NSB, 4, D) for c in range(8)], axis=0)
    per_l = lambda name, shape: np.stack([R[c][name] for c in range(4)], axis=1).reshape(shape)
    swa_k_p = per_l("kp", (NL, 4, 128, 2, 64))
    swa_v_p = per_l("vp", (NL, 4, 128, 2, 64))
    ssm_re_p = per_l("hrp", (NL, 4, 32, 64))
    ssm_im_p = per_l("hip", (NL, 4, 32, 64))
    mem_k_p = per_l("mkp", (NL, 4, 256, 4, 128))
    mem_v_p = per_l("mvp", (NL, 4, 256, 4, 128))
    cat_s = lambda name, shape: np.concatenate([R[c][name] for c in range(8)], axis=1).reshape(shape)
    swa_k_s = cat_s("ks", (NL, 128, 128, 2, 64))
    swa_v_s = cat_s("vs", (NL, 128, 128, 2, 64))
    ssm_re_s = cat_s("hrs", (NL, 128, 32, 64))
    ssm_im_s = cat_s("his", (NL, 128, 32, 64))
    outs = (y_p, y_s, swa_k_p, swa_v_p, ssm_re_p, ssm_im_p, mem_k_p, mem_v_p, swa_k_s, swa_v_s, ssm_re_s, ssm_im_s)
    return tuple(np.ascontiguousarray(o, dtype=np.float32) for o in outs)
```

```python
import contextlib
import math
import types
import numpy as np
import concourse.bass as bass
import concourse.mybir as mybir
from concourse.bass_utils import run_bass_kernel_spmd

F32 = mybir.dt.float32
BF16 = mybir.dt.bfloat16
I32 = mybir.dt.int32
AF = mybir.ActivationFunctionType
ALU = mybir.AluOpType

D = 1024
SEQ = 4096
NL = 2
INW = 4864
DFF = 2816
K_OFF, V_OFF, U_OFF, MQ_OFF, G_OFF = 512, 640, 768, 1280, 1792
PAST = 16384
TB = 512
NBLK = SEQ // TB
NSB = 16
NS = NSB * 4
NLEV = 9
EPS = 1e-6
NWS = 4
ENGS = ("pe", "act", "dve", "pool", "sp")


def _freeze(fn):
    if fn is None or fn.__closure__ is None:
        return fn
    cells = []
    for c in fn.__closure__:
        try:
            cells.append(types.CellType(c.cell_contents))
        except ValueError:
            cells.append(c)
    return types.FunctionType(fn.__code__, fn.__globals__, fn.__name__, fn.__defaults__, tuple(cells))


class Sched:
    def __init__(self, nc, stack):
        self.nc = nc
        self.stack = stack
        self.q = {e: [] for e in ENGS}
        self.cnt = {e: 0 for e in ENGS}
        self.esem = {e: stack.enter_context(nc.semaphore("sem_" + e)) for e in ENGS}
        self.dsem = {}
        self.dcnt = {}
        self.last_w = {}
        self.readers = {}
        self.seen = {e: {} for e in ENGS}
        self.alias = {}

    def _exp(self, keys):
        out = []
        for k in keys:
            out.append(k)
            out.extend(self.alias.get(k, ()))
        return out

    def op(self, eng, fn, reads=(), writes=(), dma=None):
        reads = self._exp(reads)
        writes = self._exp(writes)
        fn = _freeze(fn)
        deps = []
        for r in reads:
            t = self.last_w.get(r)
            if t is not None:
                deps.append((t, True))
        for w in writes:
            t = self.last_w.get(w)
            if t is not None:
                deps.append((t, False))
            for t in self.readers.get(w, ()):
                deps.append((t, False))
        waits = {}
        for (kind, key, val), raw in deps:
            if kind == "eng" and key == eng and (eng == "pe" or not raw):
                continue
            sk = (kind, key)
            if val > self.seen[eng].get(sk, 0):
                waits[sk] = max(waits.get(sk, 0), val)
        for sk, val in waits.items():
            self.seen[eng][sk] = val
        if dma is not None:
            if dma not in self.dsem:
                self.dsem[dma] = self.stack.enter_context(self.nc.semaphore("dq%d" % len(self.dsem)))
                self.dcnt[dma] = 0
            self.dcnt[dma] += 16
            tok = ("dma", dma, self.dcnt[dma])
        else:
            self.cnt[eng] += 1
            tok = ("eng", eng, self.cnt[eng])
        self.q[eng].append((fn, list(waits.items()), tok))
        for w in writes:
            self.last_w[w] = tok
            self.readers[w] = []
        for r in reads:
            self.readers.setdefault(r, []).append(tok)
        return tok

    def wait_all(self, eng, toks):
        waits = {}
        for kind, key, val in toks:
            sk = (kind, key)
            if val > self.seen[eng].get(sk, 0):
                waits[sk] = max(waits.get(sk, 0), val)
        for sk, val in waits.items():
            self.seen[eng][sk] = val
        self.q[eng].append((None, list(waits.items()), None))

    def emit(self):
        nc = self.nc
        sem = lambda sk: self.esem[sk[1]] if sk[0] == "eng" else self.dsem[sk[1]]
        with nc.Block() as block:
            def run(e, engobj):
                for fn, waits, tok in self.q[e]:
                    for sk, val in waits:
                        engobj.wait_ge(sem(sk), val)
                    if fn is None:
                        continue
                    ins = fn(engobj)
                    if tok[0] == "dma":
                        ins.then_inc(self.dsem[tok[1]], 16)
                    else:
                        ins.then_inc(self.esem[e], 1)

            @block.tensor
            def _(t):
                run("pe", t)

            @block.scalar
            def _(t):
                run("act", t)

            @block.vector
            def _(t):
                run("dve", t)

            @block.gpsimd
            def _(t):
                run("pool", t)

            @block.sync
            def _(t):
                run("sp", t)


def build_program():
    nc = bass.Bass("TRN2", target_bir_lowering=False)

    def din(name, shape):
        return nc.dram_tensor(name, list(shape), F32, kind="ExternalInput").ap()

    def dout(name, shape):
        return nc.dram_tensor(name, list(shape), F32, kind="ExternalOutput").ap()

    def dscr(name, shape, dt=BF16):
        return nc.dram_tensor(name, list(shape), dt).ap()

    xp = din("xp", [SEQ, D]); xs = din("xs", [NS, D])
    csk = din("csk", [NL, NSB, 128, 128]); csv = din("csv", [NL, NSB, 128, 128])
    sre = din("sre", [NL, NSB, 32, 64]); sim = din("sim", [NL, NSB, 32, 64])
    cmk = din("cmk", [NL, NSB, 256, 512]); cmv = din("cmv", [NL, NSB, 256, 512])
    memp = din("memp", [256, D])
    attn_norm = din("attn_norm", [NL, D]); w_in = din("w_in", [NL, D, INW])
    q_norm = din("q_norm", [NL, 64]); k_norm = din("k_norm", [NL, 64]); attn_sinks = din("attn_sinks", [NL, 8])
    a_re = din("ssm_a_re", [NL, 32, 64]); a_im = din("ssm_a_im", [NL, 32, 64]); log_dt = din("ssm_log_dt", [NL, 32])
    b_re = din("ssm_b_re", [NL, 32, 64, 16]); b_im = din("ssm_b_im", [NL, 32, 64, 16])
    c_re = din("ssm_c_re", [NL, 32, 16, 64]); c_im = din("ssm_c_im", [NL, 32, 16, 64])
    ssm_d = din("ssm_d", [NL, 512]); w_glu = din("ssm_w_glu", [NL, 512, 512])
    mem_norm = din("mem_norm", [NL, D]); w_kv = din("w_mem_kv", [NL, D, D])
    mq_norm = din("mem_q_norm", [NL, 128]); mk_norm = din("mem_k_norm", [NL, 128])
    w_br = din("w_branch", [NL, 3, 512, D]); w_out = din("w_out", [NL, D, D])
    ffn_norm = din("ffn_norm", [NL, D]); w_up = din("w_ffn_up", [NL, D, 2 * DFF]); w_dn = din("w_ffn_down", [NL, DFF, D])
    c_ident = din("c_ident", [128, 128]); c_blk = din("c_blk", [128, 128]); c_perm = din("c_perm", [128, 128])
    c_mprev = din("c_mprev", [128, 128]); c_mcur = din("c_mcur", [128, 128])
    c_cos = din("c_cos", [128, SEQ]); c_sin = din("c_sin", [128, SEQ])
    c_cos_s = din("c_cos_s", [128, NS]); c_sin_s = din("c_sin_s", [128, NS])
    c_mc = din("c_mc", [128, 4]); c_mnew = din("c_mnew", [NS, NS]); c_rowm = din("c_rowm", [128, 4])

    yp = dout("yp", [SEQ, D]); ys = dout("ys", [NS, D])
    kp = dout("kp", [NL, 128, 128]); vp = dout("vp", [NL, 128, 128])
    hrp = dout("hrp", [NL, 32, 64]); hip = dout("hip", [NL, 32, 64])
    mkp = dout("mkp", [NL, 256, 512]); mvp = dout("mvp", [NL, 256, 512])
    ks = dout("ks", [NL, NSB, 128, 128]); vs = dout("vs", [NL, NSB, 128, 128])
    hrs = dout("hrs", [NL, NSB, 32, 64]); his = dout("his", [NL, NSB, 32, 64])

    wb_in = dscr("wb_in", [NL, D, INW]); wb_glu = dscr("wb_glu", [NL, 512, 512]); wb_kv = dscr("wb_kv", [NL, D, D])
    wb_br = dscr("wb_br", [NL, 3, 512, D]); wb_out = dscr("wb_out", [NL, D, D])
    wb_up = dscr("wb_up", [NL, D, 2 * DFF]); wb_dn = dscr("wb_dn", [NL, DFF, D])

    out_toks = []

    with contextlib.ExitStack() as st:
        S = Sched(nc, st)

        def sb(name, shape, dt):
            return st.enter_context(nc.sbuf_tensor(name, list(shape), dt))

        PS = [st.enter_context(nc.psum_tensor("ps%d" % i, [128, 512], F32)) for i in range(8)]
        psn = [0]

        held = set()

        def bank():
            while True:
                i = psn[0] % 8
                psn[0] += 1
                if i not in held:
                    return i

        ident_f = sb("ident_f", [128, 128], F32); ident_b = sb("ident_b", [128, 128], BF16)
        ones_b = sb("ones_b", [128, 128], BF16); blk_b = sb("blk_b", [128, 128], BF16); perm_b = sb("perm_b", [128, 128], BF16)
        mprev_b = sb("mprev_b", [128, 128], BF16); mcur_b = sb("mcur_b", [128, 128], BF16)
        mc_b = sb("mc_b", [128, 4], BF16); mnew_b = sb("mnew_b", [NS, NS], BF16); rowm = sb("rowm", [128, 4], F32)
        cosb = sb("cosb", [128, TB], F32); sinb = sb("sinb", [128, TB], F32)
        gA = sb("gA", [128, NL, 8], F32); gF = sb("gF", [128, NL, 8], F32)
        gq = sb("gq", [128, NL], F32); gk = sb("gk", [128, NL], F32); gmq = sb("gmq", [128, NL], F32)
        esink = sb("esink", [128, NL, 4], F32); dcol = sb("dcol", [128, NL, 4], F32)
        W2 = sb("W2", [128, NL, 16, 2, 128], BF16); CP = sb("CP", [128, NL, 16, 2, 128], BF16)
        Dd = sb("Dd", [128, NL, 4, 128], BF16)
        LR = sb("LR", [128, NL, NLEV, 16], F32); LI = sb("LI", [128, NL, NLEV, 16], F32); LIn = sb("LIn", [128, NL, NLEV, 16], F32)
        MKT = sb("MKT", [128, NL, 4, 256], BF16); MV = sb("MV", [128, NL, 2, 512], BF16)
        kTc = sb("kTc", [128, NL, 128 + TB], BF16); vtc = sb("vtc", [128, NL, 5, 128], BF16)
        car_r = sb("car_r", [128, NL, 16], F32); car_i = sb("car_i", [128, NL, 16], F32)
        xT = sb("xT", [128, 8, TB], F32)
        hT = sb("hT", [128, 8, TB], BF16)
        qf = sb("qf", [128, TB], F32); sqb = sb("sqb", [128, TB], BF16); sdv = sb("sdv", [128, TB], F32); rstd = sb("rstd", [128, TB], F32)
        qn = sb("qn", [128, TB], BF16); t1 = sb("t1", [128, TB], F32); t2 = sb("t2", [128, TB], F32)
        HN = [dict(qf=qf, sqb=sqb, sdv=sdv, rstd=rstd, qn=qn, t1=t1, t2=t2, sfx=""),
              dict(qf=sb("qfB", [128, TB], F32), sqb=sb("sqbB", [128, TB], BF16), sdv=sdv,
                   rstd=sb("rstdB", [128, TB], F32), qn=sb("qnB", [128, TB], BF16), t1=t1, t2=t2, sfx="B")]
        hn_i = [0]
        S.alias["zT"] = ["qr"]
        qr = sb("qr", [128, 4, TB], BF16); k32 = sb("k32", [128, TB], F32); v32 = sb("v32", [128, 4, 128], F32)
        uT = sb("uT", [128, 4, TB], BF16); qmn = sb("qmn", [128, 4, TB], BF16)
        oa = sb("oa", [128, 4, TB], BF16); ob = sb("ob", [128, 4, TB], BF16); oc = sb("oc", [128, 4, TB], BF16)
        PT = [sb("PT%d" % i, [128, 2, TB], BF16) for i in range(2)]
        dn = [sb("dn%d" % i, [128, TB], F32) for i in range(2)]
        zT = qr; sg = [sb("sg%d" % i, [128, TB], BF16) for i in range(2)]
        WS = [sb("ws%d" % i, [128, 4096], BF16) for i in range(NWS)]
        RA = sb("RA", [128, 16384], BF16)
        small = sb("small", [128, 64], F32)
        smi = sb("smi", [128, 16], I32)
        hs_r = sb("hs_r", [128, 16, NSB], F32); hs_i = sb("hs_i", [128, 16, NSB], F32)
        kcT = sb("kcT", [128, 2, 128], BF16)

        RAK = [("RA", i) for i in range(32)]

        def ra(off_b, nbytes, dt):
            a = RA[:, off_b // 2:(off_b + nbytes) // 2]
            keys = RAK[off_b // 1024:(off_b + nbytes + 1023) // 1024]
            return (a.bitcast(F32) if dt == F32 else a), keys

        xtok = ra(0, 16384, F32)[0].rearrange("p (s d) -> p s d", s=4)
        S.alias["xtok"] = RAK[0:16]
        sqT = ra(24576, 8192, BF16)[0].rearrange("p (k n) -> p k n", k=8)
        S.alias["sqT"] = RAK[24:32]
        XSr, kXSr = ra(0, 8192, F32); XSi, kXSi = ra(8192, 8192, F32)
        TD = [ra(16384 + 4096 * i, 4096, F32) for i in range(4)]
        xbr, kxbr = ra(16384, 4096, BF16); xbi, kxbi = ra(20480, 4096, BF16)
        macc, kmacc = ra(0, 16384, F32); mgT, kmgT = ra(16384, 8192, BF16)
        wsn = [0]

        def wk(i):
            return [("ws", i), ("wsT", i)]

        def wslot():
            i = wsn[0] % NWS
            wsn[0] += 1
            return i

        S.op("sp", lambda e: e.dma_start(out=ident_f[:], in_=c_ident), writes=["ident_f"], dma="c0")
        for (dst, src, nm) in [(ident_b, c_ident, "ident_b"), (blk_b, c_blk, "blk_b"), (perm_b, c_perm, "perm_b"),
                               (mprev_b, c_mprev, "mprev_b"), (mcur_b, c_mcur, "mcur_b"), (mc_b, c_mc, "mc_b"),
                               (mnew_b, c_mnew, "mnew_b")]:
            S.op("pool", lambda e, dst=dst, src=src: e.dma_start(out=dst[:], in_=src), writes=[nm], dma=nm)
        S.op("sp", lambda e: e.dma_start(out=rowm[:], in_=c_rowm), writes=["rowm"], dma="rowm")
        S.op("dve", lambda e: e.memset(ones_b[:], 1.0), writes=["ones_b"])
        S.op("dve", lambda e: e.memset(small[:], 0.0), writes=["small"])
        S.op("dve", lambda e: e.memset(small[:, 0:1], math.pi / 2), writes=["small"])
        S.op("dve", lambda e: e.memset(small[:, 1:2], EPS), writes=["small"])

        def conv(dst, src, key, rows):
            r0 = 0
            while r0 < rows:
                r1 = min(rows, r0 + 256)
                S.op("pool", lambda e, r0=r0, r1=r1: e.dma_start(out=dst[r0:r1, :], in_=src[r0:r1, :]),
                     writes=[key], dma=key)
                r0 = r1

        for l in range(NL):
            conv(wb_kv[l], w_kv[l], ("wb_kv", l), D)
        for l in range(NL):
            conv(wb_in[l], w_in[l], ("wb_in", l), D)
            conv(wb_glu[l], w_glu[l], ("wb_glu", l), 512)
            for n in range(3):
                conv(wb_br[l, n], w_br[l, n], ("wb_br", l), 512)
            conv(wb_out[l], w_out[l], ("wb_out", l), D)
            conv(wb_up[l], w_up[l], ("wb_up", l), D)
            conv(wb_dn[l], w_dn[l], ("wb_dn", l), DFF)

        for l in range(NL):
            S.op("sp", lambda e, l=l: e.dma_start(out=gA[:, l, :], in_=attn_norm[l].rearrange("(k p) -> p k", p=128),
                                                  allow_slow_non_contiguous=True), writes=["gA"], dma="gA")
            S.op("sp", lambda e, l=l: e.dma_start(out=gF[:, l, :], in_=ffn_norm[l].rearrange("(k p) -> p k", p=128),
                                                  allow_slow_non_contiguous=True), writes=["gF"], dma="gF")
            S.op("sp", lambda e, l=l: e.dma_start(out=dcol[:, l, :], in_=ssm_d[l].rearrange("(k p) -> p k", p=128),
                                                  allow_slow_non_contiguous=True), writes=["dcol"], dma="dcol")
            for two in range(2):
                sl = slice(64 * two, 64 * two + 64)
                S.op("sp", lambda e, l=l, sl=sl: e.dma_start(out=gq[sl, l:l + 1], in_=q_norm[l].rearrange("(p o) -> p o", o=1)),
                     writes=["gq"], dma="gq")
                S.op("sp", lambda e, l=l, sl=sl: e.dma_start(out=gk[sl, l:l + 1], in_=k_norm[l].rearrange("(p o) -> p o", o=1)),
                     writes=["gk"], dma="gk")
                S.op("sp", lambda e, l=l, sl=sl, two=two: e.dma_start(out=esink[sl, l, :],
                                                                      in_=attn_sinks[l, 4 * two:4 * two + 4].partition_broadcast(64)),
                     writes=["esink"], dma="esink")
            S.op("sp", lambda e, l=l: e.dma_start(out=gmq[:, l:l + 1], in_=mq_norm[l].rearrange("(p o) -> p o", o=1)),
                 writes=["gmq"], dma="gmq")
        S.op("act", lambda e: e.activation(out=esink[:], in_=esink[:], func=AF.Exp), reads=["esink"], writes=["esink"])

        are_t = sb("are_t", [128, 16], F32); aim_t = sb("aim_t", [128, 16], F32); dt_t = sb("dt_t", [128, 16], F32)
        sA = [sb("sA%d" % i, [128, 16], F32) for i in range(8)]
        def ra3(idx, nm):
            v, kk = ra(16384 + 2048 * idx, 2048, F32)
            S.alias[nm] = kk
            return v.rearrange("p (t c) -> p t c", t=16)
        Bb = [ra3(i, "Bb%d" % i) for i in range(2)]
        Cb = [ra3(2 + i, "Cb%d" % i) for i in range(2)]
        GB = [ra3(4 + i, "GB%d" % i) for i in range(2)]
        tG = [ra3(6 + i, "tG%d" % i) for i in range(2)]
        tT = sb("tT", [128, 128], F32)

        def dve(fn, R, W):
            return S.op("dve", fn, reads=R, writes=W)

        def act(fn, R, W):
            return S.op("act", fn, reads=R, writes=W)

        TWO_PI = 2.0 * math.pi
        for l in range(NL):
            for gl in range(2):
                sl = slice(64 * gl, 64 * gl + 64)
                S.op("sp", lambda e, l=l, gl=gl, sl=sl: e.dma_start(
                    out=are_t[sl, :], in_=a_re[l].rearrange("(tp gl) p -> gl p tp", gl=2)[gl], allow_slow_non_contiguous=True),
                    writes=["are_t"], dma="are_t")
                S.op("sp", lambda e, l=l, gl=gl, sl=sl: e.dma_start(
                    out=aim_t[sl, :], in_=a_im[l].rearrange("(tp gl) p -> gl p tp", gl=2)[gl], allow_slow_non_contiguous=True),
                    writes=["aim_t"], dma="aim_t")
                S.op("sp", lambda e, l=l, gl=gl, sl=sl: e.dma_start(
                    out=dt_t[sl, :], in_=log_dt[l].rearrange("(tp gl) -> gl tp", gl=2)[gl].partition_broadcast(64)),
                    writes=["dt_t"], dma="dt_t")
            for ri, (bsrc, csrc) in enumerate([(b_re, c_re), (b_im, c_im)]):
                S.op("pool", lambda e, ri=ri: e.memset(Bb[ri][:], 0.0), writes=["Bb%d" % ri])
                S.op("pool", lambda e, ri=ri: e.memset(Cb[ri][:], 0.0), writes=["Cb%d" % ri])
                for gl in range(2):
                    sl = slice(64 * gl, 64 * gl + 64)
                    cs = slice(16 * gl, 16 * gl + 16)
                    S.op("sp", lambda e, l=l, gl=gl, sl=sl, cs=cs, ri=ri, bsrc=bsrc: e.dma_start(
                        out=Bb[ri][sl, :, cs], in_=bsrc[l].rearrange("(tp gl) p c -> gl p tp c", gl=2)[gl]),
                        writes=["Bb%d" % ri], dma="Bb%d" % ri)
                    for tp in range(16):
                        S.op("sp", lambda e, l=l, gl=gl, sl=sl, cs=cs, ri=ri, csrc=csrc, tp=tp: e.dma_start(
                            out=Cb[ri][sl, tp, cs], in_=csrc[l, 2 * tp + gl].rearrange("c p -> p c"), allow_slow_non_contiguous=True),
                            writes=["Cb%d" % ri], dma="Cb%d" % ri)
            dtv, ard, mag, th, kf, s_, c_, tmp = sA
            act(lambda e: e.activation(out=dtv[:], in_=dt_t[:], func=AF.Exp), ["dt_t"], ["sA0"])
            dve(lambda e: e.tensor_tensor(out=ard[:], in0=are_t[:], in1=dtv[:], op=ALU.mult), ["are_t", "sA0"], ["sA1"])
            act(lambda e: e.activation(out=mag[:], in_=ard[:], func=AF.Exp), ["sA1"], ["sA2"])
            dve(lambda e: e.tensor_tensor(out=th[:], in0=aim_t[:], in1=dtv[:], op=ALU.mult), ["aim_t", "sA0"], ["sA3"])
            dve(lambda e: e.tensor_scalar(out=kf[:], in0=th[:], scalar1=1.0 / TWO_PI, scalar2=None, op0=ALU.mult), ["sA3"], ["sA4"])
            dve(lambda e: e.tensor_copy(out=smi[:], in_=kf[:]), ["sA4"], ["smi"])
            dve(lambda e: e.tensor_copy(out=kf[:], in_=smi[:]), ["smi"], ["sA4"])
            dve(lambda e: e.scalar_tensor_tensor(out=th[:], in0=kf[:], scalar=-TWO_PI, in1=th[:], op0=ALU.mult, op1=ALU.add),
                ["sA4", "sA3"], ["sA3"])
            act(lambda e: e.activation(out=s_[:], in_=th[:], func=AF.Sin, scale=0.5), ["sA3"], ["sA5"])
            act(lambda e: e.activation(out=c_[:], in_=th[:], func=AF.Sin, scale=0.5, bias=small[:, 0:1]), ["sA3", "small"], ["sA6"])
            lr0 = LR[:, l, 0, :]; li0 = LI[:, l, 0, :]
            dve(lambda e: e.tensor_tensor(out=tmp[:], in0=s_[:], in1=c_[:], op=ALU.mult), ["sA5", "sA6"], ["sA7"])
            dve(lambda e: e.scalar_tensor_tensor(out=li0, in0=tmp[:], scalar=2.0, in1=mag[:], op0=ALU.mult, op1=ALU.mult),
                ["sA7", "sA2"], ["LI"])
            dve(lambda e: e.tensor_tensor(out=tmp[:], in0=s_[:], in1=s_[:], op=ALU.mult), ["sA5"], ["sA7"])
            dve(lambda e: e.tensor_scalar(out=tmp[:], in0=tmp[:], scalar1=-2.0, scalar2=1.0, op0=ALU.mult, op1=ALU.add), ["sA7"], ["sA7"])
            dve(lambda e: e.tensor_tensor(out=lr0, in0=tmp[:], in1=mag[:], op=ALU.mult), ["sA7", "sA2"], ["LR"])
            den_, nr_, gr_, gi_, rd_ = sA[0], sA[1], sA[2], sA[3], sA[4]
            dve(lambda e: e.tensor_tensor(out=den_[:], in0=are_t[:], in1=are_t[:], op=ALU.mult), ["are_t"], ["sA0"])
            dve(lambda e: e.tensor_tensor(out=tmp[:], in0=aim_t[:], in1=aim_t[:], op=ALU.mult), ["aim_t"], ["sA7"])
            dve(lambda e: e.tensor_tensor(out=den_[:], in0=den_[:], in1=tmp[:], op=ALU.add), ["sA0", "sA7"], ["sA0"])
            dve(lambda e: e.reciprocal(out=rd_[:], in_=den_[:]), ["sA0"], ["sA4"])
            dve(lambda e: e.tensor_scalar(out=nr_[:], in0=lr0, scalar1=-1.0, scalar2=None, op0=ALU.add), ["LR"], ["sA1"])
            dve(lambda e: e.tensor_tensor(out=gr_[:], in0=nr_[:], in1=are_t[:], op=ALU.mult), ["sA1", "are_t"], ["sA2"])
            dve(lambda e: e.tensor_tensor(out=tmp[:], in0=li0, in1=aim_t[:], op=ALU.mult), ["LI", "aim_t"], ["sA7"])
            dve(lambda e: e.tensor_tensor(out=gr_[:], in0=gr_[:], in1=tmp[:], op=ALU.add), ["sA2", "sA7"], ["sA2"])
            dve(lambda e: e.tensor_tensor(out=gr_[:], in0=gr_[:], in1=rd_[:], op=ALU.mult), ["sA2", "sA4"], ["sA2"])
            dve(lambda e: e.tensor_tensor(out=gi_[:], in0=li0, in1=are_t[:], op=ALU.mult), ["LI", "are_t"], ["sA3"])
            dve(lambda e: e.tensor_tensor(out=tmp[:], in0=nr_[:], in1=aim_t[:], op=ALU.mult), ["sA1", "aim_t"], ["sA7"])
            dve(lambda e: e.tensor_tensor(out=gi_[:], in0=gi_[:], in1=tmp[:], op=ALU.subtract), ["sA3", "sA7"], ["sA3"])
            dve(lambda e: e.tensor_tensor(out=gi_[:], in0=gi_[:], in1=rd_[:], op=ALU.mult), ["sA3", "sA4"], ["sA3"])
            for i in range(NLEV - 1):
                a, b = LR[:, l, i, :], LI[:, l, i, :]
                a2, b2 = LR[:, l, i + 1, :], LI[:, l, i + 1, :]
                dve(lambda e, a=a, b=b: e.tensor_tensor(out=tmp[:], in0=b, in1=b, op=ALU.mult), ["LI"], ["sA7"])
                dve(lambda e, a=a, a2=a2: e.tensor_tensor(out=a2, in0=a, in1=a, op=ALU.mult), ["LR"], ["LR"])
                dve(lambda e, a2=a2: e.tensor_tensor(out=a2, in0=a2, in1=tmp[:], op=ALU.subtract), ["LR", "sA7"], ["LR"])
                dve(lambda e, a=a, b=b, b2=b2: e.scalar_tensor_tensor(out=b2, in0=a, scalar=2.0, in1=b, op0=ALU.mult, op1=ALU.mult),
                    ["LR", "LI"], ["LI"])
            dve(lambda e, l=l: e.tensor_scalar(out=LIn[:, l], in0=LI[:, l], scalar1=-1.0, scalar2=None, op0=ALU.mult), ["LI"], ["LIn"])
            grb = gr_[:].rearrange("p (t o) -> p t o", o=1).to_broadcast([128, 16, 32])
            gib = gi_[:].rearrange("p (t o) -> p t o", o=1).to_broadcast([128, 16, 32])
            dve(lambda e: e.tensor_tensor(out=GB[0][:], in0=Bb[0][:], in1=grb, op=ALU.mult), ["Bb0", "sA2"], ["GB0"])
            dve(lambda e: e.tensor_tensor(out=tG[0][:], in0=Bb[1][:], in1=gib, op=ALU.mult), ["Bb1", "sA3"], ["tG0"])
            dve(lambda e: e.tensor_tensor(out=GB[0][:], in0=GB[0][:], in1=tG[0][:], op=ALU.subtract), ["GB0", "tG0"], ["GB0"])
            dve(lambda e: e.tensor_tensor(out=GB[1][:], in0=Bb[1][:], in1=grb, op=ALU.mult), ["Bb1", "sA2"], ["GB1"])
            dve(lambda e: e.tensor_tensor(out=tG[1][:], in0=Bb[0][:], in1=gib, op=ALU.mult), ["Bb0", "sA3"], ["tG1"])
            dve(lambda e: e.tensor_tensor(out=GB[1][:], in0=GB[1][:], in1=tG[1][:], op=ALU.add), ["GB1", "tG1"], ["GB1"])
            for ri in range(2):
                for ct in range(4):
                    b = bank()
                    S.op("pe", lambda e, b=b, ri=ri, ct=ct: e.transpose(
                        out=PS[b][:, 0:128], in_=GB[ri][:, 4 * ct:4 * ct + 4, :].rearrange("p a b -> p (a b)"), identity=ident_f[:]),
                        reads=["GB%d" % ri, "ident_f"], writes=[("ps", b)])
                    act(lambda e, b=b: e.copy(out=tT[:], in_=PS[b][:, 0:128]), [("ps", b)], ["tT"])
                    for i in range(4):
                        dve(lambda e, l=l, ri=ri, ct=ct, i=i: e.tensor_scalar(
                            out=W2[:, l, 4 * ct + i, ri, :], in0=tT[:], scalar1=rowm[:, i:i + 1], scalar2=None, op0=ALU.mult),
                            ["tT", "rowm"], ["W2"])
            S.op("pool", lambda e, l=l: e.memset(CP[:, l], 0.0), writes=["CP"])
            for i in range(4):
                act(lambda e, l=l, i=i: e.copy(out=CP[:, l, i::4, 0, 32 * i:32 * i + 32], in_=Cb[0][:, i::4, :]), ["Cb0", "CP"], ["CP"])
                act(lambda e, l=l, i=i: e.mul(out=CP[:, l, i::4, 1, 32 * i:32 * i + 32], in_=Cb[1][:, i::4, :], mul=-1.0), ["Cb1", "CP"], ["CP"])
            for ct in range(4):
                dve(lambda e, l=l, ct=ct: e.tensor_scalar(out=Dd[:, l, ct, :], in0=ident_f[:], scalar1=dcol[:, l, ct:ct + 1], scalar2=None,
                                                          op0=ALU.mult), ["ident_f", "dcol"], ["Dd"])

        memt = xtok
        mnb = qr[:].rearrange("p t n -> p (t n)").rearrange("p (a d) -> p a d", a=2)
        S.alias["mnb"] = ["qr"]
        mnT = hT
        gmk_b = sb("gmk_b", [128, 128], F32)
        S.op("sp", lambda e: e.dma_start(out=memt[:, 0:2, :], in_=memp.rearrange("(a p) d -> p a d", p=128)), writes=["xtok"], dma="xtok")
        for l in range(NL):
            S.op("sp", lambda e, l=l: e.dma_start(out=memt[:, 2, :], in_=mem_norm[l].partition_broadcast(128)), writes=["xtok"], dma="xtok")
            S.op("sp", lambda e, l=l: e.dma_start(out=gmk_b[:], in_=mk_norm[l].partition_broadcast(128)), writes=["gmk_b"], dma="gmk_b")
            dve(lambda e: e.memset(small[:, 8:10], 0.0), [], ["small"])
            for a in range(2):
                act(lambda e, a=a: e.activation(out=memt[:, 3, :], in_=memt[:, a, :], func=AF.Square, accum_out=small[:, 8 + a:9 + a]),
                    ["xtok"], ["xtok", "small"])
            act(lambda e: e.activation(out=small[:, 10:12], in_=small[:, 8:10], func=AF.Sqrt, scale=1.0 / D, bias=small[:, 1:2]),
                ["small"], ["small"])
            dve(lambda e: e.reciprocal(out=small[:, 12:14], in_=small[:, 10:12]), ["small"], ["small"])
            for a in range(2):
                dve(lambda e, a=a: e.scalar_tensor_tensor(out=mnb[:, a, :], in0=memt[:, a, :], scalar=small[:, 12 + a:13 + a],
                                                          in1=memt[:, 2, :], op0=ALU.mult, op1=ALU.mult), ["xtok", "small"], ["mnb"])
            for a in range(2):
                for k in range(8):
                    b = bank()
                    pb = PS[b][:].bitcast(BF16)
                    S.op("pe", lambda e, a=a, k=k, pb=pb: e.transpose(out=pb[:, 0:128], in_=mnb[:, a, 128 * k:128 * k + 128],
                                                                      identity=ident_b[:]),
                         reads=["mnb", "ident_b"], writes=[("ps", b)])
                    act(lambda e, a=a, k=k, pb=pb: e.copy(out=mnT[:, k, 128 * a:128 * a + 128], in_=pb[:, 0:128]), [("ps", b)], ["hT"])
            for half in range(2):
                ws = wslot()
                wv = WS[ws][:].rearrange("p (k c) -> p k c", k=8)
                S.op("sp", lambda e, l=l, half=half, wv=wv: e.dma_start(
                    out=wv, in_=wb_kv[l].rearrange("(k p) c -> p k c", p=128)[:, :, 512 * half:512 * half + 512]),
                    reads=[("wb_kv", l)], writes=wk(ws), dma=("ws", ws))
                for a in range(2):
                    b = bank()
                    for k in range(8):
                        S.op("pe", lambda e, a=a, k=k, b=b, wv=wv: e.matmul(PS[b][:], lhsT=mnT[:, k, 128 * a:128 * a + 128], rhs=wv[:, k, :],
                                                                          start=(k == 0), stop=(k == 7)),
                             reads=["hT", ("ws", ws)], writes=[("ps", b)])
                    if half == 0:
                        kk = t1
                        dve(lambda e: e.memset(small[:, 16:20], 0.0), [], ["small"])
                        for h in range(4):
                            act(lambda e, b=b, h=h: e.activation(out=t2[:, 128 * h:128 * h + 128], in_=PS[b][:, 128 * h:128 * h + 128],
                                                                 func=AF.Square, accum_out=small[:, 16 + h:17 + h]),
                                [("ps", b)], ["t2", "small"])
                        act(lambda e: e.activation(out=small[:, 20:24], in_=small[:, 16:20], func=AF.Sqrt, scale=1.0 / 128, bias=small[:, 1:2]),
                            ["small"], ["small"])
                        dve(lambda e: e.reciprocal(out=small[:, 24:28], in_=small[:, 20:24]), ["small"], ["small"])
                        for h in range(4):
                            dve(lambda e, b=b, h=h: e.scalar_tensor_tensor(
                                out=kk[:, 128 * h:128 * h + 128], in0=PS[b][:, 128 * h:128 * h + 128], scalar=small[:, 24 + h:25 + h],
                                in1=gmk_b[:], op0=ALU.mult, op1=ALU.mult), [("ps", b), "small", "gmk_b"], ["t1"])
                        out_toks.append(S.op("sp", lambda e, l=l, a=a: e.dma_start(out=mkp[l, 128 * a:128 * a + 128, :], in_=kk[:]),
                                             reads=["t1"], dma="o_mkp"))
                        act(lambda e: e.copy(out=sqb[:], in_=kk[:]), ["t1"], ["sqb"])
                        for h in range(4):
                            b2 = bank()
                            pb = PS[b2][:].bitcast(BF16)
                            S.op("pe", lambda e, h=h, pb=pb: e.transpose(out=pb[:, 0:128], in_=sqb[:, 128 * h:128 * h + 128], identity=ident_b[:]),
                                 reads=["sqb", "ident_b"], writes=[("ps", b2)])
                            act(lambda e, l=l, a=a, h=h, pb=pb: e.copy(out=MKT[:, l, h, 128 * a:128 * a + 128], in_=pb[:, 0:128]),
                                [("ps", b2)], ["MKT"])
                    else:
                        vv = t2
                        act(lambda e, b=b: e.copy(out=vv[:], in_=PS[b][:]), [("ps", b)], ["t2"])
                        out_toks.append(S.op("sp", lambda e, l=l, a=a: e.dma_start(out=mvp[l, 128 * a:128 * a + 128, :], in_=vv[:]),
                                             reads=["t2"], dma="o_mvp"))
                        dve(lambda e, l=l, a=a: e.tensor_copy(out=MV[:, l, a, :], in_=vv[:]), ["t2"], ["MV"])

        def load_w(dst_view, src_ap, srckey):
            ws = wslot()
            dv = dst_view(ws)
            S.op("sp", lambda e: e.dma_start(out=dv, in_=src_ap), reads=[srckey], writes=wk(ws), dma=("ws", ws))
            return ws

        def rms_rstd(N, ssb, inv_n, R):
            act(lambda e: e.activation(out=sdv[:, 0:N], in_=PS[ssb][:, 0:N], func=AF.Sqrt, scale=inv_n, bias=small[:, 1:2]),
                [("ps", ssb), "small"], ["sdv"])
            dve(lambda e: e.reciprocal(out=rstd[:, 0:N], in_=sdv[:, 0:N]), ["sdv"], ["rstd"])

        def norm_block(N, gtab):
            act(lambda e: e.activation(out=sqT[:, :, 0:N], in_=xT[:, :, 0:N], func=AF.Square), ["xT"], ["sqT"])
            b = bank()
            for k in range(8):
                S.op("pe", lambda e, k=k, b=b: e.matmul(PS[b][:, 0:N], lhsT=ones_b[:], rhs=sqT[:, k, 0:N], start=(k == 0), stop=(k == 7)),
                     reads=["sqT", "ones_b"], writes=[("ps", b)])
            rms_rstd(N, b, 1.0 / D, None)
            for k in range(8):
                dve(lambda e, k=k: e.scalar_tensor_tensor(out=hT[:, k, 0:N], in0=xT[:, k, 0:N], scalar=gtab[:, k:k + 1], in1=rstd[:, 0:N],
                                                          op0=ALU.mult, op1=ALU.mult), ["xT", "rstd", "gA", "gF"], ["hT"])

        def proj_tile(N, wv, c0, ws, b=None):
            if b is None:
                b = bank()
            for k in range(8):
                S.op("pe", lambda e, k=k, b=b: e.matmul(PS[b][:, 0:N], lhsT=wv[:, k, c0:c0 + 128], rhs=hT[:, k, 0:N],
                                                        start=(k == 0), stop=(k == 7)),
                     reads=["hT", ("ws", ws)], writes=[("ps", b)])
            return b

        def headnorm_rope(N, b, l, gcol, onesm, inv_n, rope, out_bf, out_keys, out32=None, out32_keys=()):
            H = HN[hn_i[0] % 2]
            hn_i[0] += 1
            x = H["sfx"]
            qf_, sqb_, sdv_, rstd_, qn_, t1_, t2_ = H["qf"], H["sqb"], H["sdv"], H["rstd"], H["qn"], H["t1"], H["t2"]
            act(lambda e: e.copy(out=qf_[:, 0:N], in_=PS[b][:, 0:N]), [("ps", b)], ["qf" + x])
            act(lambda e: e.activation(out=sqb_[:, 0:N], in_=qf_[:, 0:N], func=AF.Square), ["qf" + x], ["sqb" + x])
            b2 = bank()
            S.op("pe", lambda e: e.matmul(PS[b2][:, 0:N], lhsT=onesm[:], rhs=sqb_[:, 0:N], start=True, stop=True),
                 reads=["sqb" + x, "blk_b", "ones_b"], writes=[("ps", b2)])
            act(lambda e: e.activation(out=sdv_[:, 0:N], in_=PS[b2][:, 0:N], func=AF.Sqrt, scale=inv_n, bias=small[:, 1:2]),
                [("ps", b2), "small"], ["sdv"])
            dve(lambda e: e.reciprocal(out=rstd_[:, 0:N], in_=sdv_[:, 0:N]), ["sdv"], ["rstd" + x])
            if not rope:
                S.op("dve", lambda e: e.scalar_tensor_tensor(out=out_bf, in0=qf_[:, 0:N], scalar=gcol, in1=rstd_[:, 0:N],
                                                             op0=ALU.mult, op1=ALU.mult),
                     reads=["qf" + x, "rstd" + x, "gmq"], writes=out_keys)
                return
            S.op("dve", lambda e: e.scalar_tensor_tensor(out=qn_[:, 0:N], in0=qf_[:, 0:N], scalar=gcol, in1=rstd_[:, 0:N],
                                                         op0=ALU.mult, op1=ALU.mult),
                 reads=["qf" + x, "rstd" + x, "gq", "gk"], writes=["qn" + x])
            b3 = bank()
            S.op("pe", lambda e: e.matmul(PS[b3][:, 0:N], lhsT=perm_b[:], rhs=qn_[:, 0:N], start=True, stop=True),
                 reads=["qn" + x, "perm_b"], writes=[("ps", b3)])
            S.op("pool", lambda e: e.tensor_tensor(out=t1_[:, 0:N], in0=qn_[:, 0:N], in1=cosb[:, 0:N], op=ALU.mult),
                 reads=["qn" + x, "cosb"], writes=["t1"])
            dve(lambda e: e.tensor_tensor(out=t2_[:, 0:N], in0=PS[b3][:, 0:N], in1=sinb[:, 0:N], op=ALU.mult), [("ps", b3), "sinb"], ["t2"])
            if out32 is not None:
                dve(lambda e: e.tensor_tensor(out=out32, in0=t1_[:, 0:N], in1=t2_[:, 0:N], op=ALU.add), ["t1", "t2"], list(out32_keys))
                act(lambda e: e.copy(out=out_bf, in_=out32), list(out32_keys), out_keys)
            else:
                dve(lambda e: e.tensor_tensor(out=out_bf, in0=t1_[:, 0:N], in1=t2_[:, 0:N], op=ALU.add), ["t1", "t2"], out_keys)

        def cmul_add(eng, dr, di, sr, si, lr, li, lin, kdr, kdi, ksr, ksi, T, kT):
            o = lambda fn, R, W: S.op(eng, fn, reads=R, writes=W)
            o(lambda e: e.tensor_tensor(out=T[0], in0=sr, in1=lr, op=ALU.mult), ksr + ["LR"], kT[0])
            o(lambda e: e.tensor_tensor(out=T[1], in0=si, in1=lin, op=ALU.mult), ksi + ["LIn"], kT[1])
            o(lambda e: e.tensor_tensor(out=T[2], in0=si, in1=lr, op=ALU.mult), ksi + ["LR"], kT[2])
            o(lambda e: e.tensor_tensor(out=T[3], in0=sr, in1=li, op=ALU.mult), ksr + ["LI"], kT[3])
            o(lambda e: e.tensor_tensor(out=dr, in0=dr, in1=T[0], op=ALU.add), kdr + kT[0], kdr)
            o(lambda e: e.tensor_tensor(out=di, in0=di, in1=T[2], op=ALU.add), kdi + kT[2], kdi)
            o(lambda e: e.tensor_tensor(out=dr, in0=dr, in1=T[1], op=ALU.add), kdr + kT[1], kdr)
            o(lambda e: e.tensor_tensor(out=di, in0=di, in1=T[3], op=ALU.add), kdi + kT[3], kdi)

        kXS = kXSr + kXSi

        def lam_b(tab, l, lev, tp0, ntp, shape):
            return tab[:, l, lev, tp0:tp0 + ntp].rearrange("p (t o) -> p t o", o=1).to_broadcast(shape)

        ybank = []

        def ssm_group_prompt(l, ct, N, first_block):
            for i in range(4):
                tp = 4 * ct + i
                for ri, X, kX in ((0, XSr, kXSr), (1, XSi, kXSi)):
                    b = bank()
                    S.op("pe", lambda e, tp=tp, ri=ri, b=b: e.matmul(PS[b][:, 0:N], lhsT=W2[:, l, tp, ri, :], rhs=uT[:, ct, 0:N],
                                                                    start=True, stop=True), reads=["W2", "uT"], writes=[("ps", b)])
                    act(lambda e, X=X, i=i, b=b: e.copy(out=X[:, i * TB:i * TB + N], in_=PS[b][:, 0:N]), [("ps", b)], kX[2 * i:2 * i + 2])
            Xr3 = XSr.rearrange("p (t n) -> p t n", t=4)
            Xi3 = XSi.rearrange("p (t n) -> p t n", t=4)
            parts = [("dve", 0, 4, TD)]
            nlev = int(math.log2(N))
            steps = []
            if not first_block:
                steps.append(("carry", 0))
            for lev in range(nlev):
                steps.append(("up", lev))
            for lev in range(nlev - 2, -1, -1):
                steps.append(("down", lev))
            for kind, lev in steps:
                for eng, a0, na, TT in parts:
                    kr = kXSr[2 * a0:2 * (a0 + na)]; ki = kXSi[2 * a0:2 * (a0 + na)]
                    Xr = Xr3[:, a0:a0 + na, :]; Xi = Xi3[:, a0:a0 + na, :]
                    if kind == "carry":
                        m = 1
                        dr, di = Xr[:, :, 0:1], Xi[:, :, 0:1]
                        sr = car_r[:, l, 4 * ct + a0:4 * ct + a0 + na].rearrange("p (t o) -> p t o", o=1)
                        si = car_i[:, l, 4 * ct + a0:4 * ct + a0 + na].rearrange("p (t o) -> p t o", o=1)
                        ksr = ksi = ["car"]
                    else:
                        d = 1 << lev
                        if kind == "up":
                            m = N // (2 * d)
                            Xr4 = Xr.rearrange("p t (m s) -> p t m s", s=2 * d)
                            Xi4 = Xi.rearrange("p t (m s) -> p t m s", s=2 * d)
                        else:
                            m = N // (2 * d) - 1
                            Xr4 = Xr[:, :, d:N - d].rearrange("p t (m s) -> p t m s", s=2 * d)
                            Xi4 = Xi[:, :, d:N - d].rearrange("p t (m s) -> p t m s", s=2 * d)
                        dr, di = Xr4[:, :, :, 2 * d - 1], Xi4[:, :, :, 2 * d - 1]
                        sr, si = Xr4[:, :, :, d - 1], Xi4[:, :, :, d - 1]
                        ksr, ksi = kr, ki
                    sh = [128, na, m]
                    T = [t_[0].rearrange("p (t n) -> p t n", t=na)[:, :, 0:m] for t_ in TT]
                    kT = [t_[1] for t_ in TT]
                    cmul_add(eng, dr, di, sr, si, lam_b(LR, l, lev, 4 * ct + a0, na, sh), lam_b(LI, l, lev, 4 * ct + a0, na, sh),
                             lam_b(LIn, l, lev, 4 * ct + a0, na, sh), kr, ki, ksr, ksi, T, kT)
                yield
            yield "PRE_Y"
            dve(lambda e: e.tensor_copy(out=car_r[:, l, 4 * ct:4 * ct + 4], in_=Xr3[:, :, N - 1]), kXSr, ["car"])
            dve(lambda e: e.tensor_copy(out=car_i[:, l, 4 * ct:4 * ct + 4], in_=Xi3[:, :, N - 1]), kXSi, ["car"])
            act(lambda e: e.copy(out=xbr[:], in_=XSr), kXSr, kxbr)
            act(lambda e: e.copy(out=xbi[:], in_=XSi), kXSi, kxbi)
            ybank.append(ssm_y(l, ct, N))

        def ssm_y(l, ct, N):
            b = bank()
            n = 0
            for i in range(4):
                tp = 4 * ct + i
                for ri, xb_, kx in ((0, xbr, kxbr), (1, xbi, kxbi)):
                    S.op("pe", lambda e, tp=tp, ri=ri, xb_=xb_, i=i, b=b, n=n: e.matmul(
                        PS[b][:, 0:N], lhsT=CP[:, l, tp, ri, :], rhs=xb_[:, i * TB:i * TB + N], start=(n == 0), stop=False),
                        reads=["CP"] + kx, writes=[("ps", b)])
                    n += 1
            S.op("pe", lambda e, b=b: e.matmul(PS[b][:, 0:N], lhsT=Dd[:, l, ct, :], rhs=uT[:, ct, 0:N], start=False, stop=True),
                 reads=["Dd", "uT"], writes=[("ps", b)])
            return b

        def ssm_group_sample(l, ct):
            N = NS
            for i in range(4):
                tp = 4 * ct + i
                for ri, X in ((0, XSr), (1, XSi)):
                    b = bank()
                    S.op("pe", lambda e, tp=tp, ri=ri, b=b: e.matmul(PS[b][:, 0:N], lhsT=W2[:, l, tp, ri, :], rhs=uT[:, ct, 0:N],
                                                                    start=True, stop=True), reads=["W2", "uT"], writes=[("ps", b)])
                    act(lambda e, X=X, i=i, b=b: e.copy(out=X[:, i * TB:i * TB + N], in_=PS[b][:, 0:N]), [("ps", b)], kXS)
            Xr4 = XSr.rearrange("p (t n) -> p t n", t=4)[:, :, 0:NS].rearrange("p t (b i) -> p t b i", i=4)
            Xi4 = XSi.rearrange("p (t n) -> p t n", t=4)[:, :, 0:NS].rearrange("p t (b i) -> p t b i", i=4)
            sh = [128, 4, NSB]
            T = [t_[0][:, 0:4 * NSB].rearrange("p (t n) -> p t n", t=4) for t_ in TD]
            kT = [t_[1] for t_ in TD]
            lr, li, lin = lam_b(LR, l, 0, 4 * ct, 4, sh), lam_b(LI, l, 0, 4 * ct, 4, sh), lam_b(LIn, l, 0, 4 * ct, 4, sh)
            for i in range(4):
                if i == 0:
                    sr, si, ksr, ksi = hs_r[:, 4 * ct:4 * ct + 4, :], hs_i[:, 4 * ct:4 * ct + 4, :], ["hs"], ["hs"]
                else:
                    sr, si, ksr, ksi = Xr4[:, :, :, i - 1], Xi4[:, :, :, i - 1], kXSr, kXSi
                cmul_add("dve", Xr4[:, :, :, i], Xi4[:, :, :, i], sr, si, lr, li, lin, kXSr, kXSi, ksr, ksi, T, kT)
            dve(lambda e: e.tensor_copy(out=hs_r[:, 4 * ct:4 * ct + 4, :], in_=Xr4[:, :, :, 3]), kXS, ["hs"])
            dve(lambda e: e.tensor_copy(out=hs_i[:, 4 * ct:4 * ct + 4, :], in_=Xi4[:, :, :, 3]), kXS, ["hs"])
            act(lambda e: e.copy(out=xbr[:], in_=XSr), kXS, kxbr)
            act(lambda e: e.copy(out=xbi[:], in_=XSi), kXS, kxbi)
            return ssm_y(l, ct, N)

        def layer_block(l, N, sample, blk):
            first_block = (blk == 0)
            last_block = (blk == NBLK - 1)
            wvin = wb_in[l].rearrange("(k p) c -> p k c", p=128)
            v8 = lambda ws: WS[ws][:].rearrange("p (k c) -> p k c", k=8)
            norm_block(N, gA[:, l, :])
            ws = wslot()
            wq = WS[ws][:].rearrange("p (k t two d) -> p k t two d", k=8, t=4, two=2)
            for two in range(2):
                for k in range(8):
                    S.op("sp", lambda e, two=two, k=k: e.dma_start(
                        out=wq[:, k, :, two, :], in_=wvin[:, k, 256 * two:256 * two + 256].rearrange("p (t d) -> p t d", d=64)),
                        reads=[("wb_in", l)], writes=wk(ws), dma=("ws", ws))
            wqv = v8(ws)
            for t in range(4):
                b = proj_tile(N, wqv, 128 * t, ws)
                headnorm_rope(N, b, l, gq[:, l:l + 1], blk_b, 1.0 / 64, True, qr[:, t, 0:N], ["qr"])
            ws = load_w(lambda w: v8(w)[:, :, 0:256], wvin[:, :, K_OFF:K_OFF + 256], ("wb_in", l))
            wkv_ = v8(ws)
            b = proj_tile(N, wkv_, 0, ws)
            kdst = kTc[:, l, 128:128 + N] if not sample else kTc[:, l, 0:N]
            headnorm_rope(N, b, l, gk[:, l:l + 1], blk_b, 1.0 / 64, True, kdst, ["kTc"], out32=k32[:, 0:N], out32_keys=["k32"])
            nsub = max(1, N // 128)
            pn = min(N, 128)
            b = bank()
            for s in range(nsub):
                for k in range(8):
                    S.op("pe", lambda e, s=s, k=k, b=b: e.matmul(PS[b][0:pn, 128 * s:128 * s + 128], lhsT=hT[:, k, 128 * s:128 * s + pn],
                                                                rhs=wkv_[:, k, 128:256], start=(k == 0), stop=(k == 7)),
                         reads=["hT", ("ws", ws)], writes=[("ps", b)])
            act(lambda e, b=b: e.copy(out=v32[0:pn, 0:nsub, :], in_=PS[b][0:pn, 0:128 * nsub].rearrange("p (s c) -> p s c", c=128)),
                [("ps", b)], ["v32"])
            vdst = vtc[0:pn, l, 1:1 + nsub, :] if not sample else vtc[0:pn, l, 0:1, :]
            dve(lambda e: e.tensor_copy(out=vdst, in_=v32[0:pn, 0:nsub, :]), ["v32"], ["vtc"])
            def swa_part():
                if not sample:
                    for s in range(nsub):
                        for h in range(2):
                            hs = slice(64 * h, 64 * h + 64)
                            pt = PT[(2 * s + h) % 2]; kpt = "PT%d" % ((2 * s + h) % 2)
                            use_prev = not (first_block and s == 0)
                            parts = ([0] if use_prev else []) + [1]
                            sbk = {}
                            for part in parts:
                                b = bank(); sbk[part] = b
                                c0 = 128 * s + 128 * part
                                S.op("pe", lambda e, b=b, c0=c0, hs=hs, s=s: e.matmul(
                                    PS[b][:].rearrange("p (t c) -> p t c", t=4), lhsT=kTc[hs, l, c0:c0 + 128],
                                    rhs=qr[hs, :, 128 * s:128 * s + 128], start=True, stop=True),
                                    reads=["kTc", "qr"], writes=[("ps", b)])
                                act(lambda e, b=b, part=part, pt=pt: e.activation(out=pt[:, part, :], in_=PS[b][:], func=AF.Exp, scale=0.125),
                                    [("ps", b)], [kpt])
                                mk_ = mprev_b if part == 0 else mcur_b
                                S.op("pool", lambda e, part=part, pt=pt, mk_=mk_: e.tensor_tensor(
                                    out=pt[:, part, :].rearrange("p (t c) -> p t c", t=4), in0=pt[:, part, :].rearrange("p (t c) -> p t c", t=4),
                                    in1=mk_[:].rearrange("p (o c) -> p o c", o=1).to_broadcast([128, 4, 128]), op=ALU.mult),
                                    reads=[kpt, "mprev_b", "mcur_b"], writes=[kpt])
                            bo = bank(); bd = bank()
                            for n_, part in enumerate(parts):
                                S.op("pe", lambda e, part=part, n_=n_, bo=bo, pt=pt, s=s, hs=hs: e.matmul(
                                    PS[bo][hs, :], lhsT=vtc[:, l, s + part, hs], rhs=pt[:, part, :], start=(n_ == 0), stop=(n_ == len(parts) - 1)),
                                    reads=["vtc", kpt], writes=[("ps", bo)])
                            for n_, part in enumerate(parts):
                                S.op("pe", lambda e, part=part, n_=n_, bd=bd, pt=pt, hs=hs: e.matmul(
                                    PS[bd][hs, :], lhsT=ones_b[:, hs], rhs=pt[:, part, :], start=(n_ == 0), stop=(n_ == len(parts) - 1)),
                                    reads=["ones_b", kpt], writes=[("ps", bd)])
                            dd = dn[h]; kd = "dn%d" % h
                            dve(lambda e, bd=bd, dd=dd, hs=hs: e.tensor_tensor(
                                out=dd[hs, :].rearrange("p (t c) -> p t c", t=4), in0=PS[bd][hs, :].rearrange("p (t c) -> p t c", t=4),
                                in1=esink[hs, l, :].rearrange("p (t o) -> p t o", o=1).to_broadcast([64, 4, 128]), op=ALU.add),
                                [("ps", bd), "esink"], [kd])
                            dve(lambda e, dd=dd, hs=hs: e.reciprocal(out=dd[hs, :], in_=dd[hs, :]), [kd], [kd])
                            dve(lambda e, bo=bo, dd=dd, hs=hs, s=s: e.tensor_tensor(
                                out=oa[hs, :, 128 * s:128 * s + 128], in0=PS[bo][hs, :].rearrange("p (t c) -> p t c", t=4),
                                in1=dd[hs, :].rearrange("p (t c) -> p t c", t=4), op=ALU.mult), [("ps", bo), kd], ["oa"])
                        yield
                    if last_block:
                        b = bank()
                        S.op("pe", lambda e, b=b: e.transpose(out=PS[b][:, 0:128], in_=k32[:, N - 128:N], identity=ident_f[:]),
                             reads=["k32", "ident_f"], writes=[("ps", b)])
                        act(lambda e, b=b: e.copy(out=t1[:, 0:128], in_=PS[b][:, 0:128]), [("ps", b)], ["t1"])
                        out_toks.append(S.op("sp", lambda e: e.dma_start(out=kp[l], in_=t1[:, 0:128]), reads=["t1"], dma="o_kp"))
                        out_toks.append(S.op("sp", lambda e: e.dma_start(out=vp[l], in_=v32[:, 3, :]), reads=["v32"], dma="o_vp"))
                    else:
                        act(lambda e: e.copy(out=kTc[:, l, 0:128], in_=kTc[:, l, N:N + 128]), ["kTc"], ["kTc"])
                        act(lambda e: e.copy(out=vtc[:, l, 0, :], in_=vtc[:, l, 4, :]), ["vtc"], ["vtc"])
                else:
                    b = bank()
                    S.op("pe", lambda e, b=b: e.transpose(out=PS[b][0:NS, 0:128], in_=k32[:, 0:NS], identity=ident_f[:]),
                         reads=["k32", "ident_f"], writes=[("ps", b)])
                    act(lambda e, b=b: e.copy(out=t1[0:NS, 0:128], in_=PS[b][0:NS, 0:128]), [("ps", b)], ["t1"])
                    for bb in range(NSB):
                        out_toks.append(S.op("sp", lambda e, bb=bb: e.dma_start(out=ks[l, bb, 124:128, :], in_=t1[4 * bb:4 * bb + 4, 0:128]),
                                             reads=["t1"], dma="o_ks"))
                        out_toks.append(S.op("sp", lambda e, bb=bb: e.dma_start(out=vs[l, bb, 124:128, :], in_=v32[4 * bb:4 * bb + 4, 0, :]),
                                             reads=["v32"], dma="o_vs"))
                    out_toks.append(S.op("sp", lambda e: e.dma_start(out=ks[l, :, 0:124, :], in_=csk[l, :, 4:128, :]), dma="o_ks"))
                    out_toks.append(S.op("sp", lambda e: e.dma_start(out=vs[l, :, 0:124, :], in_=csv[l, :, 4:128, :]), dma="o_vs"))
                    ptn = PT[0]; ptc = PT[1]
                    for h in range(2):
                        hs = slice(64 * h, 64 * h + 64)
                        b = bank()
                        S.op("pe", lambda e, b=b, hs=hs: e.matmul(PS[b][0:NS, 0:4 * NS].rearrange("p (t c) -> p t c", t=4),
                                                                 lhsT=kTc[hs, l, 0:NS], rhs=qr[hs, :, 0:NS], start=True, stop=True),
                             reads=["kTc", "qr"], writes=[("ps", b)])
                        act(lambda e, b=b, h=h: e.activation(out=ptn[0:NS, h, 0:4 * NS], in_=PS[b][0:NS, 0:4 * NS], func=AF.Exp, scale=0.125),
                            [("ps", b)], ["PT0"])
                        dve(lambda e, h=h: e.tensor_tensor(
                            out=ptn[0:NS, h, 0:4 * NS].rearrange("p (t c) -> p t c", t=4), in0=ptn[0:NS, h, 0:4 * NS].rearrange("p (t c) -> p t c", t=4),
                            in1=mnew_b[:].rearrange("p (o c) -> p o c", o=1).to_broadcast([NS, 4, NS]), op=ALU.mult), ["PT0", "mnew_b"], ["PT0"])
                    bsc = bank(); held.add(bsc)
                    for bb in range(NSB):
                        kst = sg[bb % 2]; kk_ = "sg%d" % (bb % 2)
                        S.op("pool", lambda e, bb=bb, kst=kst: e.dma_start(out=kst[:, 0:128], in_=csk[l, bb]), writes=[kk_], dma=kk_)
                        S.op("pool", lambda e, bb=bb, kst=kst: e.dma_start(out=kst[:, 128:256], in_=csv[l, bb]), writes=[kk_], dma=kk_)
                        b = bank()
                        pb = PS[b][:].bitcast(BF16)
                        S.op("pe", lambda e, kst=kst, pb=pb: e.transpose(out=pb[:, 0:128], in_=kst[:, 0:128], identity=ident_b[:]),
                             reads=[kk_, "ident_b"], writes=[("ps", b)])
                        act(lambda e, pb=pb, bb=bb: e.copy(out=kcT[:, bb % 2, :], in_=pb[:, 0:128]), [("ps", b)], ["kcT%d" % (bb % 2)])
                        for h in range(2):
                            hs = slice(64 * h, 64 * h + 64)
                            c0 = (bb * 2 + h) * 16
                            S.op("pe", lambda e, bb=bb, hs=hs, c0=c0: e.matmul(
                                PS[bsc][:, c0:c0 + 16].rearrange("p (t i) -> p t i", t=4), lhsT=kcT[hs, bb % 2, :],
                                rhs=qr[hs, :, 4 * bb:4 * bb + 4], start=True, stop=True),
                                reads=["kcT%d" % (bb % 2), "qr"], writes=[("ps", bsc)])
                        dve(lambda e, bb=bb, kst=kst: e.tensor_copy(out=RA[:, 128 * bb:128 * bb + 128], in_=kst[:, 128:256]), [kk_], RAK[0:4])
                    act(lambda e: e.activation(out=ptc[:, 0, :], in_=PS[bsc][:], func=AF.Exp, scale=0.125), [("ps", bsc)], ["PT1"])
                    held.discard(bsc)
                    dve(lambda e: e.tensor_tensor(
                        out=ptc[:, 0, :].rearrange("p (a i) -> p a i", i=4), in0=ptc[:, 0, :].rearrange("p (a i) -> p a i", i=4),
                        in1=mc_b[:].rearrange("p (o i) -> p o i", o=1).to_broadcast([128, 128, 4]), op=ALU.mult), ["PT1", "mc_b"], ["PT1"])
                    for h in range(2):
                        hs = slice(64 * h, 64 * h + 64)
                        for which, lw in ((0, None), (1, None)):
                            bo = bank()
                            lhs_new = vtc[0:NS, l, 0, hs] if which == 0 else ones_b[0:NS, hs]
                            S.op("pe", lambda e, bo=bo, hs=hs, h=h, lhs_new=lhs_new: e.matmul(
                                PS[bo][hs, 0:4 * NS], lhsT=lhs_new, rhs=ptn[0:NS, h, 0:4 * NS], start=True, stop=False),
                                reads=["vtc", "ones_b", "PT0"], writes=[("ps", bo)])
                            for bb in range(NSB):
                                c0 = (bb * 2 + h) * 16
                                lhs_c = RA[:, 128 * bb + 64 * h:128 * bb + 64 * h + 64] if which == 0 else ones_b[:, hs]
                                S.op("pe", lambda e, bo=bo, hs=hs, bb=bb, c0=c0, lhs_c=lhs_c: e.matmul(
                                    PS[bo][hs, 0:4 * NS].rearrange("p (t c) -> p t c", t=4)[:, :, 4 * bb:4 * bb + 4], lhsT=lhs_c,
                                    rhs=ptc[:, 0, c0:c0 + 16].rearrange("p (t i) -> p t i", t=4), start=False, stop=(bb == NSB - 1)),
                                    reads=RAK[0:4] + ["ones_b", "PT1"], writes=[("ps", bo)])
                            if which == 0:
                                bnum = bo
                            else:
                                bden = bo
                        dd = dn[h]; kd = "dn%d" % h
                        dve(lambda e, bden=bden, dd=dd, hs=hs: e.tensor_tensor(
                            out=dd[hs, 0:4 * NS].rearrange("p (t c) -> p t c", t=4), in0=PS[bden][hs, 0:4 * NS].rearrange("p (t c) -> p t c", t=4),
                            in1=esink[hs, l, :].rearrange("p (t o) -> p t o", o=1).to_broadcast([64, 4, NS]), op=ALU.add),
                            [("ps", bden), "esink"], [kd])
                        dve(lambda e, dd=dd, hs=hs: e.reciprocal(out=dd[hs, 0:4 * NS], in_=dd[hs, 0:4 * NS]), [kd], [kd])
                        dve(lambda e, bnum=bnum, dd=dd, hs=hs: e.tensor_tensor(
                            out=oa[hs, :, 0:NS], in0=PS[bnum][hs, 0:4 * NS].rearrange("p (t c) -> p t c", t=4),
                            in1=dd[hs, 0:4 * NS].rearrange("p (t c) -> p t c", t=4), op=ALU.mult), [("ps", bnum), kd], ["oa"])
                yield

            def mem_part():
                ws = load_w(lambda w: v8(w), wvin[:, :, MQ_OFF:MQ_OFF + 512], ("wb_in", l))
                for t in range(4):
                    b = proj_tile(N, v8(ws), 128 * t, ws)
                    headnorm_rope(N, b, l, gmq[:, l:l + 1], ones_b, 1.0 / 128, False, qmn[:, t, 0:N], ["qmn"])
                    yield
                sc_m = 1.0 / math.sqrt(128.0)
                if not sample:
                    for h in range(4):
                        pt = PT[h % 2]; kpt = "PT%d" % (h % 2)
                        for kt in range(2):
                            b = bank()
                            S.op("pe", lambda e, b=b, h=h, kt=kt: e.matmul(PS[b][:, 0:N], lhsT=MKT[:, l, h, 128 * kt:128 * kt + 128], rhs=qmn[:, h, 0:N],
                                                                          start=True, stop=True), reads=["MKT", "qmn"], writes=[("ps", b)])
                            act(lambda e, b=b, kt=kt, pt=pt: e.activation(out=pt[:, kt, 0:N], in_=PS[b][:, 0:N], func=AF.Exp, scale=sc_m),
                                [("ps", b)], [kpt])
                        bo = bank(); bd = bank()
                        for kt in range(2):
                            S.op("pe", lambda e, bo=bo, h=h, kt=kt, pt=pt: e.matmul(PS[bo][:, 0:N], lhsT=MV[:, l, kt, 128 * h:128 * h + 128],
                                                                                   rhs=pt[:, kt, 0:N], start=(kt == 0), stop=(kt == 1)),
                                 reads=["MV", kpt], writes=[("ps", bo)])
                        for kt in range(2):
                            S.op("pe", lambda e, bd=bd, kt=kt, pt=pt: e.matmul(PS[bd][:, 0:N], lhsT=ones_b[:], rhs=pt[:, kt, 0:N],
                                                                              start=(kt == 0), stop=(kt == 1)),
                                 reads=["ones_b", kpt], writes=[("ps", bd)])
                        dd = dn[h % 2]; kd = "dn%d" % (h % 2)
                        dve(lambda e, bd=bd, dd=dd: e.reciprocal(out=dd[:, 0:N], in_=PS[bd][:, 0:N]), [("ps", bd)], [kd])
                        dve(lambda e, bo=bo, dd=dd, h=h: e.tensor_tensor(out=oc[:, h, 0:N], in0=PS[bo][:, 0:N], in1=dd[:, 0:N], op=ALU.mult),
                            [("ps", bo), kd], ["oc"])
                        yield
                else:
                    bsc = bank(); held.add(bsc)
                    bo = bank(); held.add(bo)
                    ptc = PT[1]
                    for bb in range(NSB):
                        wsk = wslot()
                        kcb = WS[wsk][:, 0:1024].rearrange("p (a c) -> p a c", a=2)
                        vcb = WS[wsk][:, 1024:2048].rearrange("p (a c) -> p a c", a=2)
                        S.op("pool", lambda e, bb=bb, kcb=kcb: e.dma_start(out=kcb, in_=cmk[l, bb].rearrange("(a p) c -> p a c", p=128)),
                             writes=wk(wsk), dma=("ws", wsk))
                        S.op("pool", lambda e, bb=bb, vcb=vcb: e.dma_start(out=vcb, in_=cmv[l, bb].rearrange("(a p) c -> p a c", p=128)),
                             writes=wk(wsk), dma=("ws", wsk))
                        for h in range(4):
                            for kt in range(2):
                                b = bank()
                                pb = PS[b][:].bitcast(BF16)
                                o0 = 2048 + 256 * h + 128 * kt
                                S.op("pe", lambda e, pb=pb, kcb=kcb, h=h, kt=kt: e.transpose(out=pb[:, 0:128], in_=kcb[:, kt, 128 * h:128 * h + 128],
                                                                                            identity=ident_b[:]),
                                     reads=[("ws", wsk), "ident_b"], writes=[("ps", b)])
                                act(lambda e, pb=pb, wsk=wsk, o0=o0: e.copy(out=WS[wsk][:, o0:o0 + 128], in_=pb[:, 0:128]),
                                    [("ps", b)], [("wsT", wsk)])
                        for h in range(4):
                            for kt in range(2):
                                c0 = ((bb * 4 + h) * 2 + kt) * 4
                                o0 = 2048 + 256 * h + 128 * kt
                                S.op("pe", lambda e, h=h, c0=c0, bb=bb, wsk=wsk, o0=o0: e.matmul(
                                    PS[bsc][:, c0:c0 + 4], lhsT=WS[wsk][:, o0:o0 + 128],
                                    rhs=qmn[:, h, 4 * bb:4 * bb + 4], start=True, stop=True),
                                    reads=[("wsT", wsk), "qmn"], writes=[("ps", bsc)])
                        c0 = bb * 32
                        act(lambda e, c0=c0: e.activation(out=ptc[:, 0, c0:c0 + 32], in_=PS[bsc][:, c0:c0 + 32], func=AF.Exp, scale=sc_m),
                            [("ps", bsc)], ["PT1"])
                        for h in range(4):
                            for kt in range(2):
                                c1 = ((bb * 4 + h) * 2 + kt) * 4
                                S.op("pe", lambda e, h=h, kt=kt, c1=c1, bb=bb, vcb=vcb: e.matmul(
                                    PS[bo][:, h * NS + 4 * bb:h * NS + 4 * bb + 4], lhsT=vcb[:, kt, 128 * h:128 * h + 128],
                                    rhs=ptc[:, 0, c1:c1 + 4], start=(kt == 0), stop=(kt == 1)),
                                    reads=[("ws", wsk), "PT1"], writes=[("ps", bo)])
                    bd = bank()
                    pv = ptc[:, 0, :].rearrange("p (b h kt i) -> p h kt b i", b=NSB, h=4, kt=2)
                    for h in range(4):
                        for kt in range(2):
                            S.op("pe", lambda e, bd=bd, kt=kt, h=h: e.matmul(PS[bd][:, h * NS:(h + 1) * NS].rearrange("p (b i) -> p b i", i=4),
                                                                            lhsT=ones_b[:], rhs=pv[:, h, kt, :, :], start=(kt == 0), stop=(kt == 1)),
                                 reads=["ones_b", "PT1"], writes=[("ps", bd)])
                    dd = dn[0]
                    dve(lambda e, bd=bd: e.reciprocal(out=dd[:, 0:4 * NS], in_=PS[bd][:, 0:4 * NS]), [("ps", bd)], ["dn0"])
                    dve(lambda e, bo=bo: e.tensor_tensor(out=oc[:, :, 0:NS], in0=PS[bo][:, 0:4 * NS].rearrange("p (h c) -> p h c", h=4),
                                                         in1=dd[:, 0:4 * NS].rearrange("p (h c) -> p h c", h=4), op=ALU.mult),
                        [("ps", bo), "dn0"], ["oc"])
                    held.discard(bsc); held.discard(bo)
                yield

            if sample:
                for _ in swa_part():
                    pass
            else:
                def thread_o():
                    yield from swa_part()
                    yield "SWA_DONE"
                    yield from mem_part()
                O = thread_o()
                o_state = {"done": False, "swa": False}

                def adv_o():
                    if o_state["done"]:
                        return
                    try:
                        r = next(O)
                        if r == "SWA_DONE":
                            o_state["swa"] = True
                    except StopIteration:
                        o_state["done"] = True
            ws = load_w(lambda w: v8(w), wvin[:, :, U_OFF:U_OFF + 512], ("wb_in", l))
            for t in range(4):
                b = proj_tile(N, v8(ws), 128 * t, ws)
                act(lambda e, b=b, t=t: e.copy(out=uT[:, t, 0:N], in_=PS[b][:, 0:N]), [("ps", b)], ["uT"])
            if sample:
                for gl in range(2):
                    sl = slice(64 * gl, 64 * gl + 64)
                    for bb in range(NSB):
                        S.op("sp", lambda e, gl=gl, sl=sl, bb=bb: e.dma_start(
                            out=hs_r[sl, :, bb], in_=sre[l, bb].rearrange("(tp gl) p -> gl p tp", gl=2)[gl], allow_slow_non_contiguous=True),
                            writes=["hs"], dma="hs")
                        S.op("sp", lambda e, gl=gl, sl=sl, bb=bb: e.dma_start(
                            out=hs_i[sl, :, bb], in_=sim[l, bb].rearrange("(tp gl) p -> gl p tp", gl=2)[gl], allow_slow_non_contiguous=True),
                            writes=["hs"], dma="hs")
            for ct in range(4):
                if sample:
                    by = ssm_group_sample(l, ct)
                else:
                    klev = 0
                    for r_ in ssm_group_prompt(l, ct, N, first_block):
                        if r_ == "PRE_Y":
                            if ct == 0:
                                while not (o_state["swa"] or o_state["done"]):
                                    adv_o()
                            continue
                        klev += 1
                        if ct == 0 or klev % 2 == 0:
                            adv_o()
                    by = ybank.pop()
                act(lambda e, by=by, ct=ct: e.activation(out=zT[:, ct, 0:N], in_=PS[by][:, 0:N], func=AF.Gelu), [("ps", by)], ["zT"])
            if not sample:
                while not o_state["done"]:
                    adv_o()
            if sample:
                for gl in range(2):
                    sl = slice(64 * gl, 64 * gl + 64)
                    for bb in range(NSB):
                        out_toks.append(S.op("sp", lambda e, gl=gl, sl=sl, bb=bb: e.dma_start(
                            out=hrs[l, bb].rearrange("(tp gl) p -> gl p tp", gl=2)[gl], in_=hs_r[sl, :, bb], allow_slow_non_contiguous=True),
                            reads=["hs"], dma="o_hs"))
                        out_toks.append(S.op("sp", lambda e, gl=gl, sl=sl, bb=bb: e.dma_start(
                            out=his[l, bb].rearrange("(tp gl) p -> gl p tp", gl=2)[gl], in_=hs_i[sl, :, bb], allow_slow_non_contiguous=True),
                            reads=["hs"], dma="o_hs"))
            elif last_block:
                for gl in range(2):
                    sl = slice(64 * gl, 64 * gl + 64)
                    out_toks.append(S.op("sp", lambda e, gl=gl, sl=sl: e.dma_start(
                        out=hrp[l].rearrange("(tp gl) p -> gl p tp", gl=2)[gl], in_=car_r[sl, l, :], allow_slow_non_contiguous=True),
                        reads=["car"], dma="o_hp"))
                    out_toks.append(S.op("sp", lambda e, gl=gl, sl=sl: e.dma_start(
                        out=hip[l].rearrange("(tp gl) p -> gl p tp", gl=2)[gl], in_=car_i[sl, l, :], allow_slow_non_contiguous=True),
                        reads=["car"], dma="o_hp"))
            ws = load_w(lambda w: WS[w][:, 0:2048].rearrange("p (k c) -> p k c", k=4), wb_glu[l].rearrange("(k p) c -> p k c", p=128),
                        ("wb_glu", l))
            wg = WS[ws][:, 0:2048].rearrange("p (k c) -> p k c", k=4)
            for t in range(4):
                b = bank()
                for k in range(4):
                    S.op("pe", lambda e, b=b, k=k, t=t: e.matmul(PS[b][:, 0:N], lhsT=wg[:, k, 128 * t:128 * t + 128], rhs=zT[:, k, 0:N],
                                                                start=(k == 0), stop=(k == 3)), reads=["zT", ("ws", ws)], writes=[("ps", b)])
                s_ = sg[t % 2]; ks_ = "sg%d" % (t % 2)
                act(lambda e, b=b, s_=s_: e.activation(out=s_[:, 0:N], in_=PS[b][:, 0:N], func=AF.Sigmoid), [("ps", b)], [ks_])
                dve(lambda e, t=t, s_=s_: e.tensor_tensor(out=ob[:, t, 0:N], in0=zT[:, t, 0:N], in1=s_[:, 0:N], op=ALU.mult), ["zT", ks_], ["ob"])
            if sample:
                for _ in mem_part():
                    pass
            macc3 = macc.rearrange("p (m n) -> p m n", m=8)
            mgT3 = mgT.rearrange("p (m n) -> p m n", m=8)
            for n_, on_ in enumerate((oa, ob, oc)):
                okey = ("oa", "ob", "oc")[n_]
                wsb = wslot()
                wbv = WS[wsb][:].rearrange("p (k c) -> p k c", k=4)
                if n_ == 0:
                    for t in range(4):
                        for two in range(2):
                            r0 = (two * 4 + t) * 64
                            S.op("sp", lambda e, t=t, two=two, r0=r0: e.dma_start(out=wbv[64 * two:64 * two + 64, t, :], in_=wb_br[l, 0, r0:r0 + 64, :]),
                                 reads=[("wb_br", l)], writes=wk(wsb), dma=("ws", wsb))
                else:
                    S.op("sp", lambda e, n_=n_: e.dma_start(out=wbv, in_=wb_br[l, n_].rearrange("(k p) c -> p k c", p=128)),
                         reads=[("wb_br", l)], writes=wk(wsb), dma=("ws", wsb))
                for half in range(2):
                    wsg = load_w(lambda w: v8(w), wvin[:, :, G_OFF + n_ * 1024 + 512 * half:G_OFF + n_ * 1024 + 512 * half + 512], ("wb_in", l))
                    for mm_ in range(4):
                        m = 4 * half + mm_
                        bg = proj_tile(N, v8(wsg), 128 * mm_, wsg)
                        s_ = sg[m % 2]; ks_ = "sg%d" % (m % 2)
                        act(lambda e, bg=bg, s_=s_: e.activation(out=s_[:, 0:N], in_=PS[bg][:, 0:N], func=AF.Sigmoid), [("ps", bg)], [ks_])
                        bp = bank()
                        for k in range(4):
                            S.op("pe", lambda e, bp=bp, k=k, m=m, on_=on_: e.matmul(PS[bp][:, 0:N], lhsT=wbv[:, k, 128 * m:128 * m + 128],
                                                                                  rhs=on_[:, k, 0:N], start=(k == 0), stop=(k == 3)),
                                 reads=[okey, ("ws", wsb)], writes=[("ps", bp)])
                        if n_ == 0:
                            dve(lambda e, bp=bp, s_=s_, m=m: e.tensor_tensor(out=macc3[:, m, 0:N], in0=PS[bp][:, 0:N], in1=s_[:, 0:N], op=ALU.mult),
                                [("ps", bp), ks_], kmacc)
                        else:
                            dve(lambda e, bp=bp, s_=s_: e.tensor_tensor(out=t1[:, 0:N], in0=PS[bp][:, 0:N], in1=s_[:, 0:N], op=ALU.mult),
                                [("ps", bp), ks_], ["t1"])
                            if n_ == 1:
                                dve(lambda e, m=m: e.tensor_tensor(out=macc3[:, m, 0:N], in0=macc3[:, m, 0:N], in1=t1[:, 0:N], op=ALU.add),
                                    kmacc + ["t1"], kmacc)
                            else:
                                dve(lambda e, m=m: e.tensor_tensor(out=mgT3[:, m, 0:N], in0=macc3[:, m, 0:N], in1=t1[:, 0:N], op=ALU.add),
                                    kmacc + ["t1"], kmgT)
            for half in range(2):
                wso = load_w(lambda w: v8(w), wb_out[l].rearrange("(k p) c -> p k c", p=128)[:, :, 512 * half:512 * half + 512], ("wb_out", l))
                for mm_ in range(4):
                    m = 4 * half + mm_
                    b = bank()
                    for k in range(8):
                        S.op("pe", lambda e, b=b, k=k, mm_=mm_, wso=wso: e.matmul(PS[b][:, 0:N], lhsT=v8(wso)[:, k, 128 * mm_:128 * mm_ + 128],
                                                                                 rhs=mgT3[:, k, 0:N], start=(k == 0), stop=(k == 7)),
                             reads=kmgT + [("ws", wso)], writes=[("ps", b)])
                    dve(lambda e, b=b, m=m: e.tensor_tensor(out=xT[:, m, 0:N], in0=xT[:, m, 0:N], in1=PS[b][:, 0:N], op=ALU.add),
                        [("ps", b), "xT"], ["xT"])
            norm_block(N, gF[:, l, :])
            wvup = wb_up[l].rearrange("(k p) c -> p k c", p=128)

            def actT(j):
                return RA[:, j * TB:(j + 1) * TB], [RAK[j]]

            for grp in range(6):
                nt = 4 if grp < 5 else 2
                wsg = load_w(lambda w: v8(w)[:, :, 0:128 * nt], wvup[:, :, 512 * grp:512 * grp + 128 * nt], ("wb_up", l))
                wsu = load_w(lambda w: v8(w)[:, :, 0:128 * nt], wvup[:, :, DFF + 512 * grp:DFF + 512 * grp + 128 * nt], ("wb_up", l))
                for jj in range(nt):
                    j = 4 * grp + jj
                    bg = proj_tile(N, v8(wsg), 128 * jj, wsg)
                    bu = proj_tile(N, v8(wsu), 128 * jj, wsu)
                    s_ = sg[j % 2]; ks_ = "sg%d" % (j % 2)
                    act(lambda e, bg=bg, s_=s_: e.activation(out=s_[:, 0:N], in_=PS[bg][:, 0:N], func=AF.Silu), [("ps", bg)], [ks_])
                    av, ak = actT(j)
                    dve(lambda e, bu=bu, s_=s_, av=av: e.tensor_tensor(out=av[:, 0:N], in0=PS[bu][:, 0:N], in1=s_[:, 0:N], op=ALU.mult),
                        [("ps", bu), ks_], ak)
            wvdn = wb_dn[l].rearrange("(k p) c -> p k c", p=128)
            for q4 in range(4):
                wsl = []
                for hh in range(2):
                    w_ = wslot()
                    S.op("sp", lambda e, w_=w_, hh=hh, q4=q4: e.dma_start(
                        out=WS[w_][:, 0:11 * 256].rearrange("p (k c) -> p k c", k=11), in_=wvdn[:, 11 * hh:11 * hh + 11, 256 * q4:256 * q4 + 256]),
                        reads=[("wb_dn", l)], writes=wk(w_), dma=("ws", w_))
                    wsl.append(w_)
                for mm_ in range(2):
                    m = 2 * q4 + mm_
                    b = bank()
                    for j in range(22):
                        w_ = wsl[j // 11]
                        wv_ = WS[w_][:, 0:11 * 256].rearrange("p (k c) -> p k c", k=11)
                        av, ak = actT(j)
                        S.op("pe", lambda e, b=b, j=j, mm_=mm_, wv_=wv_, av=av: e.matmul(
                            PS[b][:, 0:N], lhsT=wv_[:, j % 11, 128 * mm_:128 * mm_ + 128], rhs=av[:, 0:N], start=(j == 0), stop=(j == 21)),
                            reads=ak + [("ws", w_)], writes=[("ps", b)])
                    dve(lambda e, b=b, m=m: e.tensor_tensor(out=xT[:, m, 0:N], in0=xT[:, m, 0:N], in1=PS[b][:, 0:N], op=ALU.add),
                        [("ps", b), "xT"], ["xT"])


        def run_block(blk, sample):
            N = NS if sample else TB
            nsub = max(1, N // 128)
            pn = min(N, 128)
            if sample:
                S.op("sp", lambda e: e.dma_start(out=xtok[0:NS, 0, :], in_=xs), writes=["xtok"], dma="xtok")
                S.op("sp", lambda e: e.dma_start(out=cosb[:, 0:NS], in_=c_cos_s), writes=["cosb"], dma="cosb")
                S.op("sp", lambda e: e.dma_start(out=sinb[:, 0:NS], in_=c_sin_s), writes=["sinb"], dma="sinb")
            else:
                t0 = blk * TB
                S.op("sp", lambda e: e.dma_start(out=xtok[:], in_=xp[t0:t0 + TB, :].rearrange("(s p) d -> p s d", p=128)),
                     writes=["xtok"], dma="xtok")
                S.op("sp", lambda e: e.dma_start(out=cosb[:], in_=c_cos[:, t0:t0 + TB]), writes=["cosb"], dma="cosb")
                S.op("sp", lambda e: e.dma_start(out=sinb[:], in_=c_sin[:, t0:t0 + TB]), writes=["sinb"], dma="sinb")
            for s in range(nsub):
                for k in range(8):
                    b = bank()
                    S.op("pe", lambda e, s=s, k=k, b=b: e.transpose(out=PS[b][:, 0:pn], in_=xtok[0:pn, s, 128 * k:128 * k + 128],
                                                                    identity=ident_f[0:pn, 0:pn]),
                         reads=["xtok", "ident_f"], writes=[("ps", b)])
                    act(lambda e, s=s, k=k, b=b: e.copy(out=xT[:, k, 128 * s:128 * s + pn], in_=PS[b][:, 0:pn]), [("ps", b)], ["xT"])
            for l in range(NL):
                layer_block(l, N, sample, blk)
            for s in range(nsub):
                for k in range(8):
                    b = bank()
                    S.op("pe", lambda e, s=s, k=k, b=b: e.transpose(out=PS[b][0:pn, 0:128], in_=xT[:, k, 128 * s:128 * s + pn],
                                                                    identity=ident_f[:]),
                         reads=["xT", "ident_f"], writes=[("ps", b)])
                    act(lambda e, s=s, k=k, b=b: e.copy(out=xtok[0:pn, s, 128 * k:128 * k + 128], in_=PS[b][0:pn, 0:128]),
                        [("ps", b)], ["xtok"])
            if sample:
                out_toks.append(S.op("sp", lambda e: e.dma_start(out=ys, in_=xtok[0:NS, 0, :]), reads=["xtok"], dma="o_y"))
            else:
                t0 = blk * TB
                out_toks.append(S.op("sp", lambda e: e.dma_start(out=yp[t0:t0 + TB, :].rearrange("(s p) d -> p s d", p=128), in_=xtok[:]),
                                     reads=["xtok"], dma="o_y"))

        for blk in range(NBLK):
            run_block(blk, False)
        run_block(0, True)
        S.wait_all("sp", out_toks)
        with nc.allow_non_contiguous_dma(reason="small strided parameter / state transfers"):
            S.emit()
    return nc


_NC_CACHE = {}


def _consts():
    c = {}
    c["c_ident"] = np.eye(128, dtype=np.float32)
    blk = np.zeros((128, 128), np.float32)
    blk[:64, :64] = 1.0
    blk[64:, 64:] = 1.0
    c["c_blk"] = blk
    p = np.arange(128)
    lo = (p % 64) < 32
    partner = np.where(lo, p + 32, p - 32)
    perm = np.zeros((128, 128), np.float32)
    perm[partner, p] = 1.0
    c["c_perm"] = perm
    j = np.arange(128)[:, None]
    i = np.arange(128)[None, :]
    c["c_mprev"] = (j > i).astype(np.float32)
    c["c_mcur"] = (j <= i).astype(np.float32)
    half = 32
    inv = (np.float32(10000.0) ** (-(np.arange(half, dtype=np.float32) / np.float32(half)))).astype(np.float32)
    invp = inv[p % 32]
    sign = np.where(lo, -1.0, 1.0).astype(np.float32)

    def tables(pos):
        ang = (pos[None, :].astype(np.float32) * invp[:, None]).astype(np.float32)
        return np.cos(ang).astype(np.float32), (np.sin(ang) * sign[:, None]).astype(np.float32)

    c["c_cos"], c["c_sin"] = tables(np.arange(SEQ, dtype=np.float32))
    pos_s = np.float32(PAST) + np.tile(np.arange(4, dtype=np.float32), NSB)
    c["c_cos_s"], c["c_sin_s"] = tables(pos_s)
    r = np.arange(128)[:, None]
    ii = np.arange(4)[None, :]
    c["c_mc"] = (r > ii).astype(np.float32)
    kb, kj = np.divmod(np.arange(NS), 4)
    c["c_mnew"] = ((kb[:, None] == kb[None, :]) & (kj[:, None] <= kj[None, :])).astype(np.float32)
    c["c_rowm"] = (np.arange(128)[:, None] // 32 == np.arange(4)[None, :]).astype(np.float32)
    return c


def kernel(**inputs):
    f = lambda a: np.ascontiguousarray(np.asarray(a, dtype=np.float32))
    inp = {k: f(v) for k, v in inputs.items()}
    if "nc" not in _NC_CACHE:
        _NC_CACHE["nc"] = build_program()
    nc = _NC_CACHE["nc"]
    consts = _consts()
    wnames = ["attn_norm", "w_in", "q_norm", "k_norm", "attn_sinks", "ssm_a_re", "ssm_a_im", "ssm_log_dt", "ssm_b_re", "ssm_b_im",
              "ssm_c_re", "ssm_c_im", "ssm_d", "ssm_w_glu", "mem_norm", "w_mem_kv", "mem_q_norm", "mem_k_norm", "w_branch", "w_out",
              "ffn_norm", "w_ffn_up", "w_ffn_down"]
    in_maps = []
    for c in range(8):
        b0 = NSB * c
        m = {
            "xp": inp["x_prompt"][c % 4],
            "xs": inp["x_sample"][b0:b0 + NSB].reshape(NS, D),
            "csk": inp["cache_swa_k"][:, b0:b0 + NSB].reshape(NL, NSB, 128, 128),
            "csv": inp["cache_swa_v"][:, b0:b0 + NSB].reshape(NL, NSB, 128, 128),
            "sre": inp["state_ssm_re"][:, b0:b0 + NSB],
            "sim": inp["state_ssm_im"][:, b0:b0 + NSB],
            "cmk": inp["cache_mem_k"][:, b0:b0 + NSB].reshape(NL, NSB, 256, 512),
            "cmv": inp["cache_mem_v"][:, b0:b0 + NSB].reshape(NL, NSB, 256, 512),
            "memp": inp["mem_prompt"][c % 4],
        }
        for w in wnames:
            m[w] = inp[w]
        m.update(consts)
        in_maps.append({k: np.ascontiguousarray(v) for k, v in m.items()})
    res = run_bass_kernel_spmd(nc, in_maps, core_ids=list(range(8)))
    R = res.results
    cat = lambda name, cores: np.stack([R[c][name] for c in cores])
    y_p = cat("yp", range(4))
    y_s = np.concatenate([R[c]["ys"].reshape(NSB, 4, D) for c in range(8)], axis=0)
    per_l = lambda name, shape: np.stack([R[c][name] for c in range(4)], axis=1).reshape(shape)
    swa_k_p = per_l("kp", (NL, 4, 128, 2, 64))
    swa_v_p = per_l("vp", (NL, 4, 128, 2, 64))
    ssm_re_p = per_l("hrp", (NL, 4, 32, 64))
    ssm_im_p = per_l("hip", (NL, 4, 32, 64))
    mem_k_p = per_l("mkp", (NL, 4, 256, 4, 128))
    mem_v_p = per_l("mvp", (NL, 4, 256, 4, 128))
    cat_s = lambda name, shape: np.concatenate([R[c][name] for c in range(8)], axis=1).reshape(shape)
    swa_k_s = cat_s("ks", (NL, 128, 128, 2, 64))
    swa_v_s = cat_s("vs", (NL, 128, 128, 2, 64))
    ssm_re_s = cat_s("hrs", (NL, 128, 32, 64))
    ssm_im_s = cat_s("his", (NL, 128, 32, 64))
    outs = (y_p, y_s, swa_k_p, swa_v_p, ssm_re_p, ssm_im_p, mem_k_p, mem_v_p, swa_k_s, swa_v_s, ssm_re_s, ssm_im_s)
    return tuple(np.ascontiguousarray(o, dtype=np.float32) for o in outs)
```

```python
import contextlib
import math
import types
import numpy as np
import concourse.bass as bass
import concourse.mybir as mybir
from concourse.bass_utils import run_bass_kernel_spmd

F32 = mybir.dt.float32
BF16 = mybir.dt.bfloat16
I32 = mybir.dt.int32
AF = mybir.ActivationFunctionType
ALU = mybir.AluOpType

D = 1024
SEQ = 4096
NL = 2
INW = 4864
DFF = 2816
K_OFF, V_OFF, U_OFF, MQ_OFF, G_OFF = 512, 640, 768, 1280, 1792
PAST = 16384
TB = 512
NBLK = SEQ // TB
NSB = 16
NS = NSB * 4
NLEV = 9
EPS = 1e-6
NWS = 4
ENGS = ("pe", "act", "dve", "pool", "sp")


def _freeze(fn):
    if fn is None or fn.__closure__ is None:
        return fn
    cells = []
    for c in fn.__closure__:
        try:
            cells.append(types.CellType(c.cell_contents))
        except ValueError:
            cells.append(c)
    return types.FunctionType(fn.__code__, fn.__globals__, fn.__name__, fn.__defaults__, tuple(cells))


class Sched:
    def __init__(self, nc, stack):
        self.nc = nc
        self.stack = stack
        self.q = {e: [] for e in ENGS}
        self.cnt = {e: 0 for e in ENGS}
        self.esem = {e: stack.enter_context(nc.semaphore("sem_" + e)) for e in ENGS}
        self.dsem = {}
        self.dcnt = {}
        self.last_w = {}
        self.readers = {}
        self.seen = {e: {} for e in ENGS}
        self.alias = {}

    def _exp(self, keys):
        out = []
        for k in keys:
            out.append(k)
            out.extend(self.alias.get(k, ()))
        return out

    def op(self, eng, fn, reads=(), writes=(), dma=None):
        reads = self._exp(reads)
        writes = self._exp(writes)
        fn = _freeze(fn)
        deps = []
        for r in reads:
            t = self.last_w.get(r)
            if t is not None:
                deps.append((t, True))
        for w in writes:
            t = self.last_w.get(w)
            if t is not None:
                deps.append((t, False))
            for t in self.readers.get(w, ()):
                deps.append((t, False))
        waits = {}
        for (kind, key, val), raw in deps:
            if kind == "eng" and key == eng and (eng == "pe" or not raw):
                continue
            sk = (kind, key)
            if val > self.seen[eng].get(sk, 0):
                waits[sk] = max(waits.get(sk, 0), val)
        for sk, val in waits.items():
            self.seen[eng][sk] = val
        if dma is not None:
            if dma not in self.dsem:
                self.dsem[dma] = self.stack.enter_context(self.nc.semaphore("dq%d" % len(self.dsem)))
                self.dcnt[dma] = 0
            self.dcnt[dma] += 16
            tok = ("dma", dma, self.dcnt[dma])
        else:
            self.cnt[eng] += 1
            tok = ("eng", eng, self.cnt[eng])
        self.q[eng].append((fn, list(waits.items()), tok))
        for w in writes:
            self.last_w[w] = tok
            self.readers[w] = []
        for r in reads:
            self.readers.setdefault(r, []).append(tok)
        return tok

    def wait_all(self, eng, toks):
        waits = {}
        for kind, key, val in toks:
            sk = (kind, key)
            if val > self.seen[eng].get(sk, 0):
                waits[sk] = max(waits.get(sk, 0), val)
        for sk, val in waits.items():
            self.seen[eng][sk] = val
        self.q[eng].append((None, list(waits.items()), None))

    def emit(self):
        nc = self.nc
        sem = lambda sk: self.esem[sk[1]] if sk[0] == "eng" else self.dsem[sk[1]]
        with nc.Block() as block:
            def run(e, engobj):
                for fn, waits, tok in self.q[e]:
                    for sk, val in waits:
                        engobj.wait_ge(sem(sk), val)
                    if fn is None:
                        continue
                    ins = fn(engobj)
                    if tok[0] == "dma":
                        ins.then_inc(self.dsem[tok[1]], 16)
                    else:
                        ins.then_inc(self.esem[e], 1)

            @block.tensor
            def _(t):
                run("pe", t)

            @block.scalar
            def _(t):
                run("act", t)

            @block.vector
            def _(t):
                run("dve", t)

            @block.gpsimd
            def _(t):
                run("pool", t)

            @block.sync
            def _(t):
                run("sp", t)


def build_program():
    nc = bass.Bass("TRN2", target_bir_lowering=False)

    def din(name, shape):
        return nc.dram_tensor(name, list(shape), F32, kind="ExternalInput").ap()

    def dout(name, shape):
        return nc.dram_tensor(name, list(shape), F32, kind="ExternalOutput").ap()

    def dscr(name, shape, dt=BF16):
        return nc.dram_tensor(name, list(shape), dt).ap()

    xp = din("xp", [SEQ, D]); xs = din("xs", [NS, D])
    csk = din("csk", [NL, NSB, 128, 128]); csv = din("csv", [NL, NSB, 128, 128])
    sre = din("sre", [NL, NSB, 32, 64]); sim = din("sim", [NL, NSB, 32, 64])
    cmk = din("cmk", [NL, NSB, 256, 512]); cmv = din("cmv", [NL, NSB, 256, 512])
    memp = din("memp", [256, D])
    attn_norm = din("attn_norm", [NL, D]); w_in = din("w_in", [NL, D, INW])
    q_norm = din("q_norm", [NL, 64]); k_norm = din("k_norm", [NL, 64]); attn_sinks = din("attn_sinks", [NL, 8])
    a_re = din("ssm_a_re", [NL, 32, 64]); a_im = din("ssm_a_im", [NL, 32, 64]); log_dt = din("ssm_log_dt", [NL, 32])
    b_re = din("ssm_b_re", [NL, 32, 64, 16]); b_im = din("ssm_b_im", [NL, 32, 64, 16])
    c_re = din("ssm_c_re", [NL, 32, 16, 64]); c_im = din("ssm_c_im", [NL, 32, 16, 64])
    ssm_d = din("ssm_d", [NL, 512]); w_glu = din("ssm_w_glu", [NL, 512, 512])
    mem_norm = din("mem_norm", [NL, D]); w_kv = din("w_mem_kv", [NL, D, D])
    mq_norm = din("mem_q_norm", [NL, 128]); mk_norm = din("mem_k_norm", [NL, 128])
    w_br = din("w_branch", [NL, 3, 512, D]); w_out = din("w_out", [NL, D, D])
    ffn_norm = din("ffn_norm", [NL, D]); w_up = din("w_ffn_up", [NL, D, 2 * DFF]); w_dn = din("w_ffn_down", [NL, DFF, D])
    c_ident = din("c_ident", [128, 128]); c_blk = din("c_blk", [128, 128]); c_perm = din("c_perm", [128, 128])
    c_mprev = din("c_mprev", [128, 128]); c_mcur = din("c_mcur", [128, 128])
    c_cos = din("c_cos", [128, SEQ]); c_sin = din("c_sin", [128, SEQ])
    c_cos_s = din("c_cos_s", [128, NS]); c_sin_s = din("c_sin_s", [128, NS])
    c_mc = din("c_mc", [128, 4]); c_mnew = din("c_mnew", [NS, NS]); c_rowm = din("c_rowm", [128, 4])

    yp = dout("yp", [SEQ, D]); ys = dout("ys", [NS, D])
    kp = dout("kp", [NL, 128, 128]); vp = dout("vp", [NL, 128, 128])
    hrp = dout("hrp", [NL, 32, 64]); hip = dout("hip", [NL, 32, 64])
    mkp = dout("mkp", [NL, 256, 512]); mvp = dout("mvp", [NL, 256, 512])
    ks = dout("ks", [NL, NSB, 128, 128]); vs = dout("vs", [NL, NSB, 128, 128])
    hrs = dout("hrs", [NL, NSB, 32, 64]); his = dout("his", [NL, NSB, 32, 64])

    wb_in = dscr("wb_in", [NL, D, INW]); wb_glu = dscr("wb_glu", [NL, 512, 512]); wb_kv = dscr("wb_kv", [NL, D, D])
    wb_br = dscr("wb_br", [NL, 3, 512, D]); wb_out = dscr("wb_out", [NL, D, D])
    wb_up = dscr("wb_up", [NL, D, 2 * DFF]); wb_dn = dscr("wb_dn", [NL, DFF, D])

    out_toks = []

    with contextlib.ExitStack() as st:
        S = Sched(nc, st)

        def sb(name, shape, dt):
            return st.enter_context(nc.sbuf_tensor(name, list(shape), dt))

        PS = [st.enter_context(nc.psum_tensor("ps%d" % i, [128, 512], F32)) for i in range(8)]
        psn = [0]

        held = set()

        def bank():
            while True:
                i = psn[0] % 8
                psn[0] += 1
                if i not in held:
                    return i

        ident_f = sb("ident_f", [128, 128], F32); ident_b = sb("ident_b", [128, 128], BF16)
        ones_b = sb("ones_b", [128, 128], BF16); blk_b = sb("blk_b", [128, 128], BF16); perm_b = sb("perm_b", [128, 128], BF16)
        mprev_b = sb("mprev_b", [128, 128], BF16); mcur_b = sb("mcur_b", [128, 128], BF16)
        mc_b = sb("mc_b", [128, 4], BF16); mnew_b = sb("mnew_b", [NS, NS], BF16); rowm = sb("rowm", [128, 4], F32)
        cosb = sb("cosb", [128, TB], F32); sinb = sb("sinb", [128, TB], F32)
        gA = sb("gA", [128, NL, 8], F32); gF = sb("gF", [128, NL, 8], F32)
        gq = sb("gq", [128, NL], F32); gk = sb("gk", [128, NL], F32); gmq = sb("gmq", [128, NL], F32)
        esink = sb("esink", [128, NL, 4], F32); dcol = sb("dcol", [128, NL, 4], F32)
        W2 = sb("W2", [128, NL, 16, 2, 128], BF16); CP = sb("CP", [128, NL, 16, 2, 128], BF16)
        Dd = sb("Dd", [128, NL, 4, 128], BF16)
        LR = sb("LR", [128, NL, NLEV, 16], F32); LI = sb("LI", [128, NL, NLEV, 16], F32); LIn = sb("LIn", [128, NL, NLEV, 16], F32)
        MKT = sb("MKT", [128, NL, 4, 256], BF16); MV = sb("MV", [128, NL, 2, 512], BF16)
        kTc = sb("kTc", [128, NL, 128 + TB], BF16); vtc = sb("vtc", [128, NL, 5, 128], BF16)
        car_r = sb("car_r", [128, NL, 16], F32); car_i = sb("car_i", [128, NL, 16], F32)
        xT = sb("xT", [128, 8, TB], F32)
        hT = sb("hT", [128, 8, TB], BF16)
        qf = sb("qf", [128, TB], F32); sqb = sb("sqb", [128, TB], BF16); sdv = sb("sdv", [128, TB], F32); rstd = sb("rstd", [128, TB], F32)
        qn = sb("qn", [128, TB], BF16); t1 = sb("t1", [128, TB], F32); t2 = sb("t2", [128, TB], F32)
        HN = [dict(qf=qf, sqb=sqb, sdv=sdv, rstd=rstd, qn=qn, t1=t1, t2=t2, sfx=""),
              dict(qf=sb("qfB", [128, TB], F32), sqb=sb("sqbB", [128, TB], BF16), sdv=sdv,
                   rstd=sb("rstdB", [128, TB], F32), qn=sb("qnB", [128, TB], BF16), t1=t1, t2=t2, sfx="B")]
        hn_i = [0]
        S.alias["zT"] = ["qr"]
        qr = sb("qr", [128, 4, TB], BF16); k32 = sb("k32", [128, TB], F32); v32 = sb("v32", [128, 4, 128], F32)
        uT = sb("uT", [128, 4, TB], BF16); qmn = sb("qmn", [128, 4, TB], BF16)
        oa = sb("oa", [128, 4, TB], BF16); ob = sb("ob", [128, 4, TB], BF16); oc = sb("oc", [128, 4, TB], BF16)
        PT = [sb("PT%d" % i, [128, 2, TB], BF16) for i in range(2)]
        dn = [sb("dn%d" % i, [128, TB], F32) for i in range(2)]
        zT = qr; sg = [sb("sg%d" % i, [128, TB], BF16) for i in range(2)]
        WS = [sb("ws%d" % i, [128, 4096], BF16) for i in range(NWS)]
        RA = sb("RA", [128, 16384], BF16)
        small = sb("small", [128, 64], F32)
        smi = sb("smi", [128, 16], I32)
        hs_r = sb("hs_r", [128, 16, NSB], F32); hs_i = sb("hs_i", [128, 16, NSB], F32)
        kcT = sb("kcT", [128, 2, 128], BF16)

        RAK = [("RA", i) for i in range(32)]

        def ra(off_b, nbytes, dt):
            a = RA[:, off_b // 2:(off_b + nbytes) // 2]
            keys = RAK[off_b // 1024:(off_b + nbytes + 1023) // 1024]
            return (a.bitcast(F32) if dt == F32 else a), keys

        xtok = ra(0, 16384, F32)[0].rearrange("p (s d) -> p s d", s=4)
        S.alias["xtok"] = RAK[0:16]
        sqT = ra(24576, 8192, BF16)[0].rearrange("p (k n) -> p k n", k=8)
        S.alias["sqT"] = RAK[24:32]
        XSr, kXSr = ra(0, 8192, F32); XSi, kXSi = ra(8192, 8192, F32)
        TD = [ra(16384 + 4096 * i, 4096, F32) for i in range(4)]
        xbr, kxbr = ra(16384, 4096, BF16); xbi, kxbi = ra(20480, 4096, BF16)
        macc, kmacc = ra(0, 16384, F32); mgT, kmgT = ra(16384, 8192, BF16)
        wsn = [0]

        def wk(i):
            return [("ws", i), ("wsT", i)]

        def wslot():
            i = wsn[0] % NWS
            wsn[0] += 1
            return i

        S.op("sp", lambda e: e.dma_start(out=ident_f[:], in_=c_ident), writes=["ident_f"], dma="c0")
        for (dst, src, nm) in [(ident_b, c_ident, "ident_b"), (blk_b, c_blk, "blk_b"), (perm_b, c_perm, "perm_b"),
                               (mprev_b, c_mprev, "mprev_b"), (mcur_b, c_mcur, "mcur_b"), (mc_b, c_mc, "mc_b"),
                               (mnew_b, c_mnew, "mnew_b")]:
            S.op("pool", lambda e, dst=dst, src=src: e.dma_start(out=dst[:], in_=src), writes=[nm], dma=nm)
        S.op("sp", lambda e: e.dma_start(out=rowm[:], in_=c_rowm), writes=["rowm"], dma="rowm")
        S.op("dve", lambda e: e.memset(ones_b[:], 1.0), writes=["ones_b"])
        S.op("dve", lambda e: e.memset(small[:], 0.0), writes=["small"])
        S.op("dve", lambda e: e.memset(small[:, 0:1], math.pi / 2), writes=["small"])
        S.op("dve", lambda e: e.memset(small[:, 1:2], EPS), writes=["small"])

        def conv(dst, src, key, rows):
            r0 = 0
            while r0 < rows:
                r1 = min(rows, r0 + 256)
                S.op("pool", lambda e, r0=r0, r1=r1: e.dma_start(out=dst[r0:r1, :], in_=src[r0:r1, :]),
                     writes=[key], dma=key)
                r0 = r1

        for l in range(NL):
            conv(wb_kv[l], w_kv[l], ("wb_kv", l), D)
        for l in range(NL):
            conv(wb_in[l], w_in[l], ("wb_in", l), D)
            conv(wb_glu[l], w_glu[l], ("wb_glu", l), 512)
            for n in range(3):
                conv(wb_br[l, n], w_br[l, n], ("wb_br", l), 512)
            conv(wb_out[l], w_out[l], ("wb_out", l), D)
            conv(wb_up[l], w_up[l], ("wb_up", l), D)
            conv(wb_dn[l], w_dn[l], ("wb_dn", l), DFF)

        for l in range(NL):
            S.op("sp", lambda e, l=l: e.dma_start(out=gA[:, l, :], in_=attn_norm[l].rearrange("(k p) -> p k", p=128),
                                                  allow_slow_non_contiguous=True), writes=["gA"], dma="gA")
            S.op("sp", lambda e, l=l: e.dma_start(out=gF[:, l, :], in_=ffn_norm[l].rearrange("(k p) -> p k", p=128),
                                                  allow_slow_non_contiguous=True), writes=["gF"], dma="gF")
            S.op("sp", lambda e, l=l: e.dma_start(out=dcol[:, l, :], in_=ssm_d[l].rearrange("(k p) -> p k", p=128),
                                                  allow_slow_non_contiguous=True), writes=["dcol"], dma="dcol")
            for two in range(2):
                sl = slice(64 * two, 64 * two + 64)
                S.op("sp", lambda e, l=l, sl=sl: e.dma_start(out=gq[sl, l:l + 1], in_=q_norm[l].rearrange("(p o) -> p o", o=1)),
                     writes=["gq"], dma="gq")
                S.op("sp", lambda e, l=l, sl=sl: e.dma_start(out=gk[sl, l:l + 1], in_=k_norm[l].rearrange("(p o) -> p o", o=1)),
                     writes=["gk"], dma="gk")
                S.op("sp", lambda e, l=l, sl=sl, two=two: e.dma_start(out=esink[sl, l, :],
                                                                      in_=attn_sinks[l, 4 * two:4 * two + 4].partition_broadcast(64)),
                     writes=["esink"], dma="esink")
            S.op("sp", lambda e, l=l: e.dma_start(out=gmq[:, l:l + 1], in_=mq_norm[l].rearrange("(p o) -> p o", o=1)),
                 writes=["gmq"], dma="gmq")
        S.op("act", lambda e: e.activation(out=esink[:], in_=esink[:], func=AF.Exp), reads=["esink"], writes=["esink"])

        are_t = sb("are_t", [128, 16], F32); aim_t = sb("aim_t", [128, 16], F32); dt_t = sb("dt_t", [128, 16], F32)
        sA = [sb("sA%d" % i, [128, 16], F32) for i in range(8)]
        def ra3(idx, nm):
            v, kk = ra(16384 + 2048 * idx, 2048, F32)
            S.alias[nm] = kk
            return v.rearrange("p (t c) -> p t c", t=16)
        Bb = [ra3(i, "Bb%d" % i) for i in range(2)]
        Cb = [ra3(2 + i, "Cb%d" % i) for i in range(2)]
        GB = [ra3(4 + i, "GB%d" % i) for i in range(2)]
        tG = [ra3(6 + i, "tG%d" % i) for i in range(2)]
        tT = sb("tT", [128, 128], F32)

        def dve(fn, R, W):
            return S.op("dve", fn, reads=R, writes=W)

        def act(fn, R, W):
            return S.op("act", fn, reads=R, writes=W)

        TWO_PI = 2.0 * math.pi
        for l in range(NL):
            for gl in range(2):
                sl = slice(64 * gl, 64 * gl + 64)
                S.op("sp", lambda e, l=l, gl=gl, sl=sl: e.dma_start(
                    out=are_t[sl, :], in_=a_re[l].rearrange("(tp gl) p -> gl p tp", gl=2)[gl], allow_slow_non_contiguous=True),
                    writes=["are_t"], dma="are_t")
                S.op("sp", lambda e, l=l, gl=gl, sl=sl: e.dma_start(
                    out=aim_t[sl, :], in_=a_im[l].rearrange("(tp gl) p -> gl p tp", gl=2)[gl], allow_slow_non_contiguous=True),
                    writes=["aim_t"], dma="aim_t")
                S.op("sp", lambda e, l=l, gl=gl, sl=sl: e.dma_start(
                    out=dt_t[sl, :], in_=log_dt[l].rearrange("(tp gl) -> gl tp", gl=2)[gl].partition_broadcast(64)),
                    writes=["dt_t"], dma="dt_t")
            for ri, (bsrc, csrc) in enumerate([(b_re, c_re), (b_im, c_im)]):
                S.op("pool", lambda e, ri=ri: e.memset(Bb[ri][:], 0.0), writes=["Bb%d" % ri])
                S.op("pool", lambda e, ri=ri: e.memset(Cb[ri][:], 0.0), writes=["Cb%d" % ri])
                for gl in range(2):
                    sl = slice(64 * gl, 64 * gl + 64)
                    cs = slice(16 * gl, 16 * gl + 16)
                    S.op("sp", lambda e, l=l, gl=gl, sl=sl, cs=cs, ri=ri, bsrc=bsrc: e.dma_start(
                        out=Bb[ri][sl, :, cs], in_=bsrc[l].rearrange("(tp gl) p c -> gl p tp c", gl=2)[gl]),
                        writes=["Bb%d" % ri], dma="Bb%d" % ri)
                cst = t1[:].rearrange("p (a q) -> p a q", a=4)
                csrc2 = csrc[l].rearrange("g c p -> (g c) p").rearrange("(a q) p -> q a p", q=128)
                for dup in range(2):
                    S.op("sp", lambda e, dup=dup: e.dma_start(out=cst[:, :, 64 * dup:64 * dup + 64], in_=csrc2), writes=["t1"], dma="t1")
                for a in range(4):
                    b = bank()
                    S.op("pe", lambda e, a=a, b=b: e.transpose(out=PS[b][:, 0:128], in_=cst[:, a, :], identity=ident_f[:]),
                         reads=["t1", "ident_f"], writes=[("ps", b)])
                    for gl in range(2):
                        act(lambda e, a=a, b=b, gl=gl, ri=ri: e.copy(
                            out=Cb[ri][64 * gl:64 * gl + 64, 4 * a:4 * a + 4, 16 * gl:16 * gl + 16],
                            in_=PS[b][64 * gl:64 * gl + 64, 0:128].rearrange("p (tp gl c) -> p gl tp c", gl=2, c=16)[:, gl]),
                            [("ps", b)], ["Cb%d" % ri])
            dtv, ard, mag, th, kf, s_, c_, tmp = sA
            act(lambda e: e.activation(out=dtv[:], in_=dt_t[:], func=AF.Exp), ["dt_t"], ["sA0"])
            dve(lambda e: e.tensor_tensor(out=ard[:], in0=are_t[:], in1=dtv[:], op=ALU.mult), ["are_t", "sA0"], ["sA1"])
            act(lambda e: e.activation(out=mag[:], in_=ard[:], func=AF.Exp), ["sA1"], ["sA2"])
            dve(lambda e: e.tensor_tensor(out=th[:], in0=aim_t[:], in1=dtv[:], op=ALU.mult), ["aim_t", "sA0"], ["sA3"])
            dve(lambda e: e.tensor_scalar(out=kf[:], in0=th[:], scalar1=1.0 / TWO_PI, scalar2=None, op0=ALU.mult), ["sA3"], ["sA4"])
            dve(lambda e: e.tensor_copy(out=smi[:], in_=kf[:]), ["sA4"], ["smi"])
            dve(lambda e: e.tensor_copy(out=kf[:], in_=smi[:]), ["smi"], ["sA4"])
            dve(lambda e: e.scalar_tensor_tensor(out=th[:], in0=kf[:], scalar=-TWO_PI, in1=th[:], op0=ALU.mult, op1=ALU.add),
                ["sA4", "sA3"], ["sA3"])
            act(lambda e: e.activation(out=s_[:], in_=th[:], func=AF.Sin, scale=0.5), ["sA3"], ["sA5"])
            act(lambda e: e.activation(out=c_[:], in_=th[:], func=AF.Sin, scale=0.5, bias=small[:, 0:1]), ["sA3", "small"], ["sA6"])
            lr0 = LR[:, l, 0, :]; li0 = LI[:, l, 0, :]
            dve(lambda e: e.tensor_tensor(out=tmp[:], in0=s_[:], in1=c_[:], op=ALU.mult), ["sA5", "sA6"], ["sA7"])
            dve(lambda e: e.scalar_tensor_tensor(out=li0, in0=tmp[:], scalar=2.0, in1=mag[:], op0=ALU.mult, op1=ALU.mult),
                ["sA7", "sA2"], ["LI"])
            dve(lambda e: e.tensor_tensor(out=tmp[:], in0=s_[:], in1=s_[:], op=ALU.mult), ["sA5"], ["sA7"])
            dve(lambda e: e.tensor_scalar(out=tmp[:], in0=tmp[:], scalar1=-2.0, scalar2=1.0, op0=ALU.mult, op1=ALU.add), ["sA7"], ["sA7"])
            dve(lambda e: e.tensor_tensor(out=lr0, in0=tmp[:], in1=mag[:], op=ALU.mult), ["sA7", "sA2"], ["LR"])
            den_, nr_, gr_, gi_, rd_ = sA[0], sA[1], sA[2], sA[3], sA[4]
            dve(lambda e: e.tensor_tensor(out=den_[:], in0=are_t[:], in1=are_t[:], op=ALU.mult), ["are_t"], ["sA0"])
            dve(lambda e: e.tensor_tensor(out=tmp[:], in0=aim_t[:], in1=aim_t[:], op=ALU.mult), ["aim_t"], ["sA7"])
            dve(lambda e: e.tensor_tensor(out=den_[:], in0=den_[:], in1=tmp[:], op=ALU.add), ["sA0", "sA7"], ["sA0"])
            dve(lambda e: e.reciprocal(out=rd_[:], in_=den_[:]), ["sA0"], ["sA4"])
            dve(lambda e: e.tensor_scalar(out=nr_[:], in0=lr0, scalar1=-1.0, scalar2=None, op0=ALU.add), ["LR"], ["sA1"])
            dve(lambda e: e.tensor_tensor(out=gr_[:], in0=nr_[:], in1=are_t[:], op=ALU.mult), ["sA1", "are_t"], ["sA2"])
            dve(lambda e: e.tensor_tensor(out=tmp[:], in0=li0, in1=aim_t[:], op=ALU.mult), ["LI", "aim_t"], ["sA7"])
            dve(lambda e: e.tensor_tensor(out=gr_[:], in0=gr_[:], in1=tmp[:], op=ALU.add), ["sA2", "sA7"], ["sA2"])
            dve(lambda e: e.tensor_tensor(out=gr_[:], in0=gr_[:], in1=rd_[:], op=ALU.mult), ["sA2", "sA4"], ["sA2"])
            dve(lambda e: e.tensor_tensor(out=gi_[:], in0=li0, in1=are_t[:], op=ALU.mult), ["LI", "are_t"], ["sA3"])
            dve(lambda e: e.tensor_tensor(out=tmp[:], in0=nr_[:], in1=aim_t[:], op=ALU.mult), ["sA1", "aim_t"], ["sA7"])
            dve(lambda e: e.tensor_tensor(out=gi_[:], in0=gi_[:], in1=tmp[:], op=ALU.subtract), ["sA3", "sA7"], ["sA3"])
            dve(lambda e: e.tensor_tensor(out=gi_[:], in0=gi_[:], in1=rd_[:], op=ALU.mult), ["sA3", "sA4"], ["sA3"])
            for i in range(NLEV - 1):
                a, b = LR[:, l, i, :], LI[:, l, i, :]
                a2, b2 = LR[:, l, i + 1, :], LI[:, l, i + 1, :]
                dve(lambda e, a=a, b=b: e.tensor_tensor(out=tmp[:], in0=b, in1=b, op=ALU.mult), ["LI"], ["sA7"])
                dve(lambda e, a=a, a2=a2: e.tensor_tensor(out=a2, in0=a, in1=a, op=ALU.mult), ["LR"], ["LR"])
                dve(lambda e, a2=a2: e.tensor_tensor(out=a2, in0=a2, in1=tmp[:], op=ALU.subtract), ["LR", "sA7"], ["LR"])
                dve(lambda e, a=a, b=b, b2=b2: e.scalar_tensor_tensor(out=b2, in0=a, scalar=2.0, in1=b, op0=ALU.mult, op1=ALU.mult),
                    ["LR", "LI"], ["LI"])
            dve(lambda e, l=l: e.tensor_scalar(out=LIn[:, l], in0=LI[:, l], scalar1=-1.0, scalar2=None, op0=ALU.mult), ["LI"], ["LIn"])
            grb = gr_[:].rearrange("p (t o) -> p t o", o=1).to_broadcast([128, 16, 32])
            gib = gi_[:].rearrange("p (t o) -> p t o", o=1).to_broadcast([128, 16, 32])
            dve(lambda e: e.tensor_tensor(out=GB[0][:], in0=Bb[0][:], in1=grb, op=ALU.mult), ["Bb0", "sA2"], ["GB0"])
            dve(lambda e: e.tensor_tensor(out=tG[0][:], in0=Bb[1][:], in1=gib, op=ALU.mult), ["Bb1", "sA3"], ["tG0"])
            dve(lambda e: e.tensor_tensor(out=GB[0][:], in0=GB[0][:], in1=tG[0][:], op=ALU.subtract), ["GB0", "tG0"], ["GB0"])
            dve(lambda e: e.tensor_tensor(out=GB[1][:], in0=Bb[1][:], in1=grb, op=ALU.mult), ["Bb1", "sA2"], ["GB1"])
            dve(lambda e: e.tensor_tensor(out=tG[1][:], in0=Bb[0][:], in1=gib, op=ALU.mult), ["Bb0", "sA3"], ["tG1"])
            dve(lambda e: e.tensor_tensor(out=GB[1][:], in0=GB[1][:], in1=tG[1][:], op=ALU.add), ["GB1", "tG1"], ["GB1"])
            for ri in range(2):
                for ct in range(4):
                    b = bank()
                    S.op("pe", lambda e, b=b, ri=ri, ct=ct: e.transpose(
                        out=PS[b][:, 0:128], in_=GB[ri][:, 4 * ct:4 * ct + 4, :].rearrange("p a b -> p (a b)"), identity=ident_f[:]),
                        reads=["GB%d" % ri, "ident_f"], writes=[("ps", b)])
                    act(lambda e, b=b: e.copy(out=tT[:], in_=PS[b][:, 0:128]), [("ps", b)], ["tT"])
                    for i in range(4):
                        dve(lambda e, l=l, ri=ri, ct=ct, i=i: e.tensor_scalar(
                            out=W2[:, l, 4 * ct + i, ri, :], in0=tT[:], scalar1=rowm[:, i:i + 1], scalar2=None, op0=ALU.mult),
                            ["tT", "rowm"], ["W2"])
            S.op("pool", lambda e, l=l: e.memset(CP[:, l], 0.0), writes=["CP"])
            for i in range(4):
                act(lambda e, l=l, i=i: e.copy(out=CP[:, l, i::4, 0, 32 * i:32 * i + 32], in_=Cb[0][:, i::4, :]), ["Cb0", "CP"], ["CP"])
                act(lambda e, l=l, i=i: e.mul(out=CP[:, l, i::4, 1, 32 * i:32 * i + 32], in_=Cb[1][:, i::4, :], mul=-1.0), ["Cb1", "CP"], ["CP"])
            for ct in range(4):
                dve(lambda e, l=l, ct=ct: e.tensor_scalar(out=Dd[:, l, ct, :], in0=ident_f[:], scalar1=dcol[:, l, ct:ct + 1], scalar2=None,
                                                          op0=ALU.mult), ["ident_f", "dcol"], ["Dd"])

        memt = xtok
        mnb = qr[:].rearrange("p t n -> p (t n)").rearrange("p (a d) -> p a d", a=2)
        S.alias["mnb"] = ["qr"]
        mnT = hT
        gmk_b = sb("gmk_b", [128, 128], F32)
        S.op("sp", lambda e: e.dma_start(out=memt[:, 0:2, :], in_=memp.rearrange("(a p) d -> p a d", p=128)), writes=["xtok"], dma="xtok")
        for l in range(NL):
            S.op("sp", lambda e, l=l: e.dma_start(out=memt[:, 2, :], in_=mem_norm[l].partition_broadcast(128)), writes=["xtok"], dma="xtok")
            S.op("sp", lambda e, l=l: e.dma_start(out=gmk_b[:], in_=mk_norm[l].partition_broadcast(128)), writes=["gmk_b"], dma="gmk_b")
            dve(lambda e: e.memset(small[:, 8:10], 0.0), [], ["small"])
            for a in range(2):
                act(lambda e, a=a: e.activation(out=memt[:, 3, :], in_=memt[:, a, :], func=AF.Square, accum_out=small[:, 8 + a:9 + a]),
                    ["xtok"], ["xtok", "small"])
            act(lambda e: e.activation(out=small[:, 10:12], in_=small[:, 8:10], func=AF.Sqrt, scale=1.0 / D, bias=small[:, 1:2]),
                ["small"], ["small"])
            dve(lambda e: e.reciprocal(out=small[:, 12:14], in_=small[:, 10:12]), ["small"], ["small"])
            for a in range(2):
                dve(lambda e, a=a: e.scalar_tensor_tensor(out=mnb[:, a, :], in0=memt[:, a, :], scalar=small[:, 12 + a:13 + a],
                                                          in1=memt[:, 2, :], op0=ALU.mult, op1=ALU.mult), ["xtok", "small"], ["mnb"])
            for a in range(2):
                for k in range(8):
                    b = bank()
                    pb = PS[b][:].bitcast(BF16)
                    S.op("pe", lambda e, a=a, k=k, pb=pb: e.transpose(out=pb[:, 0:128], in_=mnb[:, a, 128 * k:128 * k + 128],
                                                                      identity=ident_b[:]),
                         reads=["mnb", "ident_b"], writes=[("ps", b)])
                    act(lambda e, a=a, k=k, pb=pb: e.copy(out=mnT[:, k, 128 * a:128 * a + 128], in_=pb[:, 0:128]), [("ps", b)], ["hT"])
            for half in range(2):
                ws = wslot()
                wv = WS[ws][:].rearrange("p (k c) -> p k c", k=8)
                S.op("sp", lambda e, l=l, half=half, wv=wv: e.dma_start(
                    out=wv, in_=wb_kv[l].rearrange("(k p) c -> p k c", p=128)[:, :, 512 * half:512 * half + 512]),
                    reads=[("wb_kv", l)], writes=wk(ws), dma=("ws", ws))
                for a in range(2):
                    b = bank()
                    for k in range(8):
                        S.op("pe", lambda e, a=a, k=k, b=b, wv=wv: e.matmul(PS[b][:], lhsT=mnT[:, k, 128 * a:128 * a + 128], rhs=wv[:, k, :],
                                                                          start=(k == 0), stop=(k == 7)),
                             reads=["hT", ("ws", ws)], writes=[("ps", b)])
                    if half == 0:
                        kk = t1
                        dve(lambda e: e.memset(small[:, 16:20], 0.0), [], ["small"])
                        for h in range(4):
                            act(lambda e, b=b, h=h: e.activation(out=t2[:, 128 * h:128 * h + 128], in_=PS[b][:, 128 * h:128 * h + 128],
                                                                 func=AF.Square, accum_out=small[:, 16 + h:17 + h]),
                                [("ps", b)], ["t2", "small"])
                        act(lambda e: e.activation(out=small[:, 20:24], in_=small[:, 16:20], func=AF.Sqrt, scale=1.0 / 128, bias=small[:, 1:2]),
                            ["small"], ["small"])
                        dve(lambda e: e.reciprocal(out=small[:, 24:28], in_=small[:, 20:24]), ["small"], ["small"])
                        for h in range(4):
                            dve(lambda e, b=b, h=h: e.scalar_tensor_tensor(
                                out=kk[:, 128 * h:128 * h + 128], in0=PS[b][:, 128 * h:128 * h + 128], scalar=small[:, 24 + h:25 + h],
                                in1=gmk_b[:], op0=ALU.mult, op1=ALU.mult), [("ps", b), "small", "gmk_b"], ["t1"])
                        out_toks.append(S.op("sp", lambda e, l=l, a=a: e.dma_start(out=mkp[l, 128 * a:128 * a + 128, :], in_=kk[:]),
                                             reads=["t1"], dma="o_mkp"))
                        act(lambda e: e.copy(out=sqb[:], in_=kk[:]), ["t1"], ["sqb"])
                        for h in range(4):
                            b2 = bank()
                            pb = PS[b2][:].bitcast(BF16)
                            S.op("pe", lambda e, h=h, pb=pb: e.transpose(out=pb[:, 0:128], in_=sqb[:, 128 * h:128 * h + 128], identity=ident_b[:]),
                                 reads=["sqb", "ident_b"], writes=[("ps", b2)])
                            act(lambda e, l=l, a=a, h=h, pb=pb: e.copy(out=MKT[:, l, h, 128 * a:128 * a + 128], in_=pb[:, 0:128]),
                                [("ps", b2)], ["MKT"])
                    else:
                        vv = t2
                        act(lambda e, b=b: e.copy(out=vv[:], in_=PS[b][:]), [("ps", b)], ["t2"])
                        out_toks.append(S.op("sp", lambda e, l=l, a=a: e.dma_start(out=mvp[l, 128 * a:128 * a + 128, :], in_=vv[:]),
                                             reads=["t2"], dma="o_mvp"))
                        dve(lambda e, l=l, a=a: e.tensor_copy(out=MV[:, l, a, :], in_=vv[:]), ["t2"], ["MV"])

        def load_w(dst_view, src_ap, srckey):
            ws = wslot()
            dv = dst_view(ws)
            S.op("sp", lambda e: e.dma_start(out=dv, in_=src_ap), reads=[srckey], writes=wk(ws), dma=("ws", ws))
            return ws

        def rms_rstd(N, ssb, inv_n, R):
            act(lambda e: e.activation(out=sdv[:, 0:N], in_=PS[ssb][:, 0:N], func=AF.Sqrt, scale=inv_n, bias=small[:, 1:2]),
                [("ps", ssb), "small"], ["sdv"])
            dve(lambda e: e.reciprocal(out=rstd[:, 0:N], in_=sdv[:, 0:N]), ["sdv"], ["rstd"])

        def norm_block(N, gtab):
            act(lambda e: e.activation(out=sqT[:, :, 0:N], in_=xT[:, :, 0:N], func=AF.Square), ["xT"], ["sqT"])
            b = bank()
            for k in range(8):
                S.op("pe", lambda e, k=k, b=b: e.matmul(PS[b][:, 0:N], lhsT=ones_b[:], rhs=sqT[:, k, 0:N], start=(k == 0), stop=(k == 7)),
                     reads=["sqT", "ones_b"], writes=[("ps", b)])
            rms_rstd(N, b, 1.0 / D, None)
            for k in range(8):
                dve(lambda e, k=k: e.scalar_tensor_tensor(out=hT[:, k, 0:N], in0=xT[:, k, 0:N], scalar=gtab[:, k:k + 1], in1=rstd[:, 0:N],
                                                          op0=ALU.mult, op1=ALU.mult), ["xT", "rstd", "gA", "gF"], ["hT"])

        def proj_tile(N, wv, c0, ws, b=None):
            if b is None:
                b = bank()
            for k in range(8):
                S.op("pe", lambda e, k=k, b=b: e.matmul(PS[b][:, 0:N], lhsT=wv[:, k, c0:c0 + 128], rhs=hT[:, k, 0:N],
                                                        start=(k == 0), stop=(k == 7)),
                     reads=["hT", ("ws", ws)], writes=[("ps", b)])
            return b

        def headnorm_rope(N, b, l, gcol, onesm, inv_n, rope, out_bf, out_keys, out32=None, out32_keys=()):
            H = HN[hn_i[0] % 2]
            hn_i[0] += 1
            x = H["sfx"]
            qf_, sqb_, sdv_, rstd_, qn_, t1_, t2_ = H["qf"], H["sqb"], H["sdv"], H["rstd"], H["qn"], H["t1"], H["t2"]
            act(lambda e: e.copy(out=qf_[:, 0:N], in_=PS[b][:, 0:N]), [("ps", b)], ["qf" + x])
            act(lambda e: e.activation(out=sqb_[:, 0:N], in_=qf_[:, 0:N], func=AF.Square), ["qf" + x], ["sqb" + x])
            b2 = bank()
            S.op("pe", lambda e: e.matmul(PS[b2][:, 0:N], lhsT=onesm[:], rhs=sqb_[:, 0:N], start=True, stop=True),
                 reads=["sqb" + x, "blk_b", "ones_b"], writes=[("ps", b2)])
            act(lambda e: e.activation(out=sdv_[:, 0:N], in_=PS[b2][:, 0:N], func=AF.Sqrt, scale=inv_n, bias=small[:, 1:2]),
                [("ps", b2), "small"], ["sdv"])
            dve(lambda e: e.reciprocal(out=rstd_[:, 0:N], in_=sdv_[:, 0:N]), ["sdv"], ["rstd" + x])
            if not rope:
                S.op("dve", lambda e: e.scalar_tensor_tensor(out=out_bf, in0=qf_[:, 0:N], scalar=gcol, in1=rstd_[:, 0:N],
                                                             op0=ALU.mult, op1=ALU.mult),
                     reads=["qf" + x, "rstd" + x, "gmq"], writes=out_keys)
                return
            S.op("dve", lambda e: e.scalar_tensor_tensor(out=qn_[:, 0:N], in0=qf_[:, 0:N], scalar=gcol, in1=rstd_[:, 0:N],
                                                         op0=ALU.mult, op1=ALU.mult),
                 reads=["qf" + x, "rstd" + x, "gq", "gk"], writes=["qn" + x])
            b3 = bank()
            S.op("pe", lambda e: e.matmul(PS[b3][:, 0:N], lhsT=perm_b[:], rhs=qn_[:, 0:N], start=True, stop=True),
                 reads=["qn" + x, "perm_b"], writes=[("ps", b3)])
            S.op("pool", lambda e: e.tensor_tensor(out=t1_[:, 0:N], in0=qn_[:, 0:N], in1=cosb[:, 0:N], op=ALU.mult),
                 reads=["qn" + x, "cosb"], writes=["t1"])
            dve(lambda e: e.tensor_tensor(out=t2_[:, 0:N], in0=PS[b3][:, 0:N], in1=sinb[:, 0:N], op=ALU.mult), [("ps", b3), "sinb"], ["t2"])
            if out32 is not None:
                dve(lambda e: e.tensor_tensor(out=out32, in0=t1_[:, 0:N], in1=t2_[:, 0:N], op=ALU.add), ["t1", "t2"], list(out32_keys))
                act(lambda e: e.copy(out=out_bf, in_=out32), list(out32_keys), out_keys)
            else:
                dve(lambda e: e.tensor_tensor(out=out_bf, in0=t1_[:, 0:N], in1=t2_[:, 0:N], op=ALU.add), ["t1", "t2"], out_keys)

        def cmul_add(eng, dr, di, sr, si, lr, li, lin, kdr, kdi, ksr, ksi, T, kT):
            o = lambda fn, R, W: S.op(eng, fn, reads=R, writes=W)
            o(lambda e: e.tensor_tensor(out=T[0], in0=sr, in1=lr, op=ALU.mult), ksr + ["LR"], kT[0])
            o(lambda e: e.tensor_tensor(out=T[1], in0=si, in1=lin, op=ALU.mult), ksi + ["LIn"], kT[1])
            o(lambda e: e.tensor_tensor(out=T[2], in0=si, in1=lr, op=ALU.mult), ksi + ["LR"], kT[2])
            o(lambda e: e.tensor_tensor(out=T[3], in0=sr, in1=li, op=ALU.mult), ksr + ["LI"], kT[3])
            o(lambda e: e.tensor_tensor(out=dr, in0=dr, in1=T[0], op=ALU.add), kdr + kT[0], kdr)
            o(lambda e: e.tensor_tensor(out=di, in0=di, in1=T[2], op=ALU.add), kdi + kT[2], kdi)
            o(lambda e: e.tensor_tensor(out=dr, in0=dr, in1=T[1], op=ALU.add), kdr + kT[1], kdr)
            o(lambda e: e.tensor_tensor(out=di, in0=di, in1=T[3], op=ALU.add), kdi + kT[3], kdi)

        kXS = kXSr + kXSi

        def lam_b(tab, l, lev, tp0, ntp, shape):
            return tab[:, l, lev, tp0:tp0 + ntp].rearrange("p (t o) -> p t o", o=1).to_broadcast(shape)

        def ssm_group_prompt(l, ct, N, first_block):
            for i in range(4):
                tp = 4 * ct + i
                for ri, X, kX in ((0, XSr, kXSr), (1, XSi, kXSi)):
                    b = bank()
                    S.op("pe", lambda e, tp=tp, ri=ri, b=b: e.matmul(PS[b][:, 0:N], lhsT=W2[:, l, tp, ri, :], rhs=uT[:, ct, 0:N],
                                                                    start=True, stop=True), reads=["W2", "uT"], writes=[("ps", b)])
                    act(lambda e, X=X, i=i, b=b: e.copy(out=X[:, i * TB:i * TB + N], in_=PS[b][:, 0:N]), [("ps", b)], kX[2 * i:2 * i + 2])
            Xr3 = XSr.rearrange("p (t n) -> p t n", t=4)
            Xi3 = XSi.rearrange("p (t n) -> p t n", t=4)
            parts = [("dve", 0, 4, TD)]
            nlev = int(math.log2(N))
            steps = []
            if not first_block:
                steps.append(("carry", 0))
            for lev in range(nlev):
                steps.append(("up", lev))
            for lev in range(nlev - 2, -1, -1):
                steps.append(("down", lev))
            for kind, lev in steps:
                for eng, a0, na, TT in parts:
                    kr = kXSr[2 * a0:2 * (a0 + na)]; ki = kXSi[2 * a0:2 * (a0 + na)]
                    Xr = Xr3[:, a0:a0 + na, :]; Xi = Xi3[:, a0:a0 + na, :]
                    if kind == "carry":
                        m = 1
                        dr, di = Xr[:, :, 0:1], Xi[:, :, 0:1]
                        sr = car_r[:, l, 4 * ct + a0:4 * ct + a0 + na].rearrange("p (t o) -> p t o", o=1)
                        si = car_i[:, l, 4 * ct + a0:4 * ct + a0 + na].rearrange("p (t o) -> p t o", o=1)
                        ksr = ksi = ["car"]
                    else:
                        d = 1 << lev
                        if kind == "up":
                            m = N // (2 * d)
                            Xr4 = Xr.rearrange("p t (m s) -> p t m s", s=2 * d)
                            Xi4 = Xi.rearrange("p t (m s) -> p t m s", s=2 * d)
                        else:
                            m = N // (2 * d) - 1
                            Xr4 = Xr[:, :, d:N - d].rearrange("p t (m s) -> p t m s", s=2 * d)
                            Xi4 = Xi[:, :, d:N - d].rearrange("p t (m s) -> p t m s", s=2 * d)
                        dr, di = Xr4[:, :, :, 2 * d - 1], Xi4[:, :, :, 2 * d - 1]
                        sr, si = Xr4[:, :, :, d - 1], Xi4[:, :, :, d - 1]
                        ksr, ksi = kr, ki
                    sh = [128, na, m]
                    T = [t_[0].rearrange("p (t n) -> p t n", t=na)[:, :, 0:m] for t_ in TT]
                    kT = [t_[1] for t_ in TT]
                    cmul_add(eng, dr, di, sr, si, lam_b(LR, l, lev, 4 * ct + a0, na, sh), lam_b(LI, l, lev, 4 * ct + a0, na, sh),
                             lam_b(LIn, l, lev, 4 * ct + a0, na, sh), kr, ki, ksr, ksi, T, kT)
            dve(lambda e: e.tensor_copy(out=car_r[:, l, 4 * ct:4 * ct + 4], in_=Xr3[:, :, N - 1]), kXSr, ["car"])
            dve(lambda e: e.tensor_copy(out=car_i[:, l, 4 * ct:4 * ct + 4], in_=Xi3[:, :, N - 1]), kXSi, ["car"])
            act(lambda e: e.copy(out=xbr[:], in_=XSr), kXSr, kxbr)
            act(lambda e: e.copy(out=xbi[:], in_=XSi), kXSi, kxbi)
            return ssm_y(l, ct, N)

        def ssm_y(l, ct, N):
            b = bank()
            n = 0
            for i in range(4):
                tp = 4 * ct + i
                for ri, xb_, kx in ((0, xbr, kxbr), (1, xbi, kxbi)):
                    S.op("pe", lambda e, tp=tp, ri=ri, xb_=xb_, i=i, b=b, n=n: e.matmul(
                        PS[b][:, 0:N], lhsT=CP[:, l, tp, ri, :], rhs=xb_[:, i * TB:i * TB + N], start=(n == 0), stop=False),
                        reads=["CP"] + kx, writes=[("ps", b)])
                    n += 1
            S.op("pe", lambda e, b=b: e.matmul(PS[b][:, 0:N], lhsT=Dd[:, l, ct, :], rhs=uT[:, ct, 0:N], start=False, stop=True),
                 reads=["Dd", "uT"], writes=[("ps", b)])
            return b

        def ssm_group_sample(l, ct):
            N = NS
            for i in range(4):
                tp = 4 * ct + i
                for ri, X in ((0, XSr), (1, XSi)):
                    b = bank()
                    S.op("pe", lambda e, tp=tp, ri=ri, b=b: e.matmul(PS[b][:, 0:N], lhsT=W2[:, l, tp, ri, :], rhs=uT[:, ct, 0:N],
                                                                    start=True, stop=True), reads=["W2", "uT"], writes=[("ps", b)])
                    act(lambda e, X=X, i=i, b=b: e.copy(out=X[:, i * TB:i * TB + N], in_=PS[b][:, 0:N]), [("ps", b)], kXS)
            Xr4 = XSr.rearrange("p (t n) -> p t n", t=4)[:, :, 0:NS].rearrange("p t (b i) -> p t b i", i=4)
            Xi4 = XSi.rearrange("p (t n) -> p t n", t=4)[:, :, 0:NS].rearrange("p t (b i) -> p t b i", i=4)
            sh = [128, 4, NSB]
            T = [t_[0][:, 0:4 * NSB].rearrange("p (t n) -> p t n", t=4) for t_ in TD]
            kT = [t_[1] for t_ in TD]
            lr, li, lin = lam_b(LR, l, 0, 4 * ct, 4, sh), lam_b(LI, l, 0, 4 * ct, 4, sh), lam_b(LIn, l, 0, 4 * ct, 4, sh)
            for i in range(4):
                if i == 0:
                    sr, si, ksr, ksi = hs_r[:, 4 * ct:4 * ct + 4, :], hs_i[:, 4 * ct:4 * ct + 4, :], ["hs"], ["hs"]
                else:
                    sr, si, ksr, ksi = Xr4[:, :, :, i - 1], Xi4[:, :, :, i - 1], kXSr, kXSi
                cmul_add("dve", Xr4[:, :, :, i], Xi4[:, :, :, i], sr, si, lr, li, lin, kXSr, kXSi, ksr, ksi, T, kT)
            dve(lambda e: e.tensor_copy(out=hs_r[:, 4 * ct:4 * ct + 4, :], in_=Xr4[:, :, :, 3]), kXS, ["hs"])
            dve(lambda e: e.tensor_copy(out=hs_i[:, 4 * ct:4 * ct + 4, :], in_=Xi4[:, :, :, 3]), kXS, ["hs"])
            act(lambda e: e.copy(out=xbr[:], in_=XSr), kXS, kxbr)
            act(lambda e: e.copy(out=xbi[:], in_=XSi), kXS, kxbi)
            return ssm_y(l, ct, N)

        def layer_block(l, N, sample, blk):
            first_block = (blk == 0)
            last_block = (blk == NBLK - 1)
            wvin = wb_in[l].rearrange("(k p) c -> p k c", p=128)
            v8 = lambda ws: WS[ws][:].rearrange("p (k c) -> p k c", k=8)
            norm_block(N, gA[:, l, :])
            ws = wslot()
            wq = WS[ws][:].rearrange("p (k t two d) -> p k t two d", k=8, t=4, two=2)
            for two in range(2):
                for k in range(8):
                    S.op("sp", lambda e, two=two, k=k: e.dma_start(
                        out=wq[:, k, :, two, :], in_=wvin[:, k, 256 * two:256 * two + 256].rearrange("p (t d) -> p t d", d=64)),
                        reads=[("wb_in", l)], writes=wk(ws), dma=("ws", ws))
            wqv = v8(ws)
            for t in range(4):
                b = proj_tile(N, wqv, 128 * t, ws)
                headnorm_rope(N, b, l, gq[:, l:l + 1], blk_b, 1.0 / 64, True, qr[:, t, 0:N], ["qr"])
            ws = load_w(lambda w: v8(w)[:, :, 0:256], wvin[:, :, K_OFF:K_OFF + 256], ("wb_in", l))
            wkv_ = v8(ws)
            b = proj_tile(N, wkv_, 0, ws)
            kdst = kTc[:, l, 128:128 + N] if not sample else kTc[:, l, 0:N]
            headnorm_rope(N, b, l, gk[:, l:l + 1], blk_b, 1.0 / 64, True, kdst, ["kTc"], out32=k32[:, 0:N], out32_keys=["k32"])
            nsub = max(1, N // 128)
            pn = min(N, 128)
            b = bank()
            for s in range(nsub):
                for k in range(8):
                    S.op("pe", lambda e, s=s, k=k, b=b: e.matmul(PS[b][0:pn, 128 * s:128 * s + 128], lhsT=hT[:, k, 128 * s:128 * s + pn],
                                                                rhs=wkv_[:, k, 128:256], start=(k == 0), stop=(k == 7)),
                         reads=["hT", ("ws", ws)], writes=[("ps", b)])
            act(lambda e, b=b: e.copy(out=v32[0:pn, 0:nsub, :], in_=PS[b][0:pn, 0:128 * nsub].rearrange("p (s c) -> p s c", c=128)),
                [("ps", b)], ["v32"])
            vdst = vtc[0:pn, l, 1:1 + nsub, :] if not sample else vtc[0:pn, l, 0:1, :]
            dve(lambda e: e.tensor_copy(out=vdst, in_=v32[0:pn, 0:nsub, :]), ["v32"], ["vtc"])
            if not sample:
                for s in range(nsub):
                    for h in range(2):
                        hs = slice(64 * h, 64 * h + 64)
                        pt = PT[(2 * s + h) % 2]; kpt = "PT%d" % ((2 * s + h) % 2)
                        use_prev = not (first_block and s == 0)
                        parts = ([0] if use_prev else []) + [1]
                        sbk = {}
                        for part in parts:
                            b = bank(); sbk[part] = b
                            c0 = 128 * s + 128 * part
                            S.op("pe", lambda e, b=b, c0=c0, hs=hs, s=s: e.matmul(
                                PS[b][:].rearrange("p (t c) -> p t c", t=4), lhsT=kTc[hs, l, c0:c0 + 128],
                                rhs=qr[hs, :, 128 * s:128 * s + 128], start=True, stop=True),
                                reads=["kTc", "qr"], writes=[("ps", b)])
                            act(lambda e, b=b, part=part, pt=pt: e.activation(out=pt[:, part, :], in_=PS[b][:], func=AF.Exp, scale=0.125),
                                [("ps", b)], [kpt])
                            mk_ = mprev_b if part == 0 else mcur_b
                            S.op("pool", lambda e, part=part, pt=pt, mk_=mk_: e.tensor_tensor(
                                out=pt[:, part, :].rearrange("p (t c) -> p t c", t=4), in0=pt[:, part, :].rearrange("p (t c) -> p t c", t=4),
                                in1=mk_[:].rearrange("p (o c) -> p o c", o=1).to_broadcast([128, 4, 128]), op=ALU.mult),
                                reads=[kpt, "mprev_b", "mcur_b"], writes=[kpt])
                        bo = bank(); bd = bank()
                        for n_, part in enumerate(parts):
                            S.op("pe", lambda e, part=part, n_=n_, bo=bo, pt=pt, s=s, hs=hs: e.matmul(
                                PS[bo][hs, :], lhsT=vtc[:, l, s + part, hs], rhs=pt[:, part, :], start=(n_ == 0), stop=(n_ == len(parts) - 1)),
                                reads=["vtc", kpt], writes=[("ps", bo)])
                        for n_, part in enumerate(parts):
                            S.op("pe", lambda e, part=part, n_=n_, bd=bd, pt=pt, hs=hs: e.matmul(
                                PS[bd][hs, :], lhsT=ones_b[:, hs], rhs=pt[:, part, :], start=(n_ == 0), stop=(n_ == len(parts) - 1)),
                                reads=["ones_b", kpt], writes=[("ps", bd)])
                        dd = dn[h]; kd = "dn%d" % h
                        dve(lambda e, bd=bd, dd=dd, hs=hs: e.tensor_tensor(
                            out=dd[hs, :].rearrange("p (t c) -> p t c", t=4), in0=PS[bd][hs, :].rearrange("p (t c) -> p t c", t=4),
                            in1=esink[hs, l, :].rearrange("p (t o) -> p t o", o=1).to_broadcast([64, 4, 128]), op=ALU.add),
                            [("ps", bd), "esink"], [kd])
                        dve(lambda e, dd=dd, hs=hs: e.reciprocal(out=dd[hs, :], in_=dd[hs, :]), [kd], [kd])
                        dve(lambda e, bo=bo, dd=dd, hs=hs, s=s: e.tensor_tensor(
                            out=oa[hs, :, 128 * s:128 * s + 128], in0=PS[bo][hs, :].rearrange("p (t c) -> p t c", t=4),
                            in1=dd[hs, :].rearrange("p (t c) -> p t c", t=4), op=ALU.mult), [("ps", bo), kd], ["oa"])
                if last_block:
                    b = bank()
                    S.op("pe", lambda e, b=b: e.transpose(out=PS[b][:, 0:128], in_=k32[:, N - 128:N], identity=ident_f[:]),
                         reads=["k32", "ident_f"], writes=[("ps", b)])
                    act(lambda e, b=b: e.copy(out=t1[:, 0:128], in_=PS[b][:, 0:128]), [("ps", b)], ["t1"])
                    out_toks.append(S.op("sp", lambda e: e.dma_start(out=kp[l], in_=t1[:, 0:128]), reads=["t1"], dma="o_kp"))
                    out_toks.append(S.op("sp", lambda e: e.dma_start(out=vp[l], in_=v32[:, 3, :]), reads=["v32"], dma="o_vp"))
                else:
                    act(lambda e: e.copy(out=kTc[:, l, 0:128], in_=kTc[:, l, N:N + 128]), ["kTc"], ["kTc"])
                    act(lambda e: e.copy(out=vtc[:, l, 0, :], in_=vtc[:, l, 4, :]), ["vtc"], ["vtc"])
            else:
                b = bank()
                S.op("pe", lambda e, b=b: e.transpose(out=PS[b][0:NS, 0:128], in_=k32[:, 0:NS], identity=ident_f[:]),
                     reads=["k32", "ident_f"], writes=[("ps", b)])
                act(lambda e, b=b: e.copy(out=t1[0:NS, 0:128], in_=PS[b][0:NS, 0:128]), [("ps", b)], ["t1"])
                out_toks.append(S.op("sp", lambda e: e.dma_start(out=ks[l, :, 124:128, :], in_=t1[0:NS, 0:128]), reads=["t1"], dma="o_ks"))
                out_toks.append(S.op("sp", lambda e: e.dma_start(out=vs[l, :, 124:128, :], in_=v32[0:NS, 0, :]), reads=["v32"], dma="o_vs"))
                out_toks.append(S.op("sp", lambda e: e.dma_start(out=ks[l, :, 0:124, :], in_=csk[l, :, 4:128, :]), dma="o_ks"))
                out_toks.append(S.op("sp", lambda e: e.dma_start(out=vs[l, :, 0:124, :], in_=csv[l, :, 4:128, :]), dma="o_vs"))
                ptn = PT[0]; ptc = PT[1]
                for h in range(2):
                    hs = slice(64 * h, 64 * h + 64)
                    b = bank()
                    S.op("pe", lambda e, b=b, hs=hs: e.matmul(PS[b][0:NS, 0:4 * NS].rearrange("p (t c) -> p t c", t=4),
                                                             lhsT=kTc[hs, l, 0:NS], rhs=qr[hs, :, 0:NS], start=True, stop=True),
                         reads=["kTc", "qr"], writes=[("ps", b)])
                    act(lambda e, b=b, h=h: e.activation(out=ptn[0:NS, h, 0:4 * NS], in_=PS[b][0:NS, 0:4 * NS], func=AF.Exp, scale=0.125),
                        [("ps", b)], ["PT0"])
                    dve(lambda e, h=h: e.tensor_tensor(
                        out=ptn[0:NS, h, 0:4 * NS].rearrange("p (t c) -> p t c", t=4), in0=ptn[0:NS, h, 0:4 * NS].rearrange("p (t c) -> p t c", t=4),
                        in1=mnew_b[:].rearrange("p (o c) -> p o c", o=1).to_broadcast([NS, 4, NS]), op=ALU.mult), ["PT0", "mnew_b"], ["PT0"])
                bsc = bank(); held.add(bsc)
                for bb in range(NSB):
                    kst = sg[bb % 2]; kk_ = "sg%d" % (bb % 2)
                    S.op("pool", lambda e, bb=bb, kst=kst: e.dma_start(out=kst[:, 0:128], in_=csk[l, bb]), writes=[kk_], dma=kk_)
                    S.op("pool", lambda e, bb=bb, kst=kst: e.dma_start(out=kst[:, 128:256], in_=csv[l, bb]), writes=[kk_], dma=kk_)
                    b = bank()
                    pb = PS[b][:].bitcast(BF16)
                    S.op("pe", lambda e, kst=kst, pb=pb: e.transpose(out=pb[:, 0:128], in_=kst[:, 0:128], identity=ident_b[:]),
                         reads=[kk_, "ident_b"], writes=[("ps", b)])
                    act(lambda e, pb=pb, bb=bb: e.copy(out=kcT[:, bb % 2, :], in_=pb[:, 0:128]), [("ps", b)], ["kcT%d" % (bb % 2)])
                    for h in range(2):
                        hs = slice(64 * h, 64 * h + 64)
                        c0 = (bb * 2 + h) * 16
                        S.op("pe", lambda e, bb=bb, hs=hs, c0=c0: e.matmul(
                            PS[bsc][:, c0:c0 + 16].rearrange("p (t i) -> p t i", t=4), lhsT=kcT[hs, bb % 2, :],
                            rhs=qr[hs, :, 4 * bb:4 * bb + 4], start=True, stop=True),
                            reads=["kcT%d" % (bb % 2), "qr"], writes=[("ps", bsc)])
                    dve(lambda e, bb=bb, kst=kst: e.tensor_copy(out=RA[:, 128 * bb:128 * bb + 128], in_=kst[:, 128:256]), [kk_], RAK[0:4])
                act(lambda e: e.activation(out=ptc[:, 0, :], in_=PS[bsc][:], func=AF.Exp, scale=0.125), [("ps", bsc)], ["PT1"])
                held.discard(bsc)
                dve(lambda e: e.tensor_tensor(
                    out=ptc[:, 0, :].rearrange("p (a i) -> p a i", i=4), in0=ptc[:, 0, :].rearrange("p (a i) -> p a i", i=4),
                    in1=mc_b[:].rearrange("p (o i) -> p o i", o=1).to_broadcast([128, 128, 4]), op=ALU.mult), ["PT1", "mc_b"], ["PT1"])
                for h in range(2):
                    hs = slice(64 * h, 64 * h + 64)
                    for which, lw in ((0, None), (1, None)):
                        bo = bank()
                        lhs_new = vtc[0:NS, l, 0, hs] if which == 0 else ones_b[0:NS, hs]
                        S.op("pe", lambda e, bo=bo, hs=hs, h=h, lhs_new=lhs_new: e.matmul(
                            PS[bo][hs, 0:4 * NS], lhsT=lhs_new, rhs=ptn[0:NS, h, 0:4 * NS], start=True, stop=False),
                            reads=["vtc", "ones_b", "PT0"], writes=[("ps", bo)])
                        for bb in range(NSB):
                            c0 = (bb * 2 + h) * 16
                            lhs_c = RA[:, 128 * bb + 64 * h:128 * bb + 64 * h + 64] if which == 0 else ones_b[:, hs]
                            S.op("pe", lambda e, bo=bo, hs=hs, bb=bb, c0=c0, lhs_c=lhs_c: e.matmul(
                                PS[bo][hs, 0:4 * NS].rearrange("p (t c) -> p t c", t=4)[:, :, 4 * bb:4 * bb + 4], lhsT=lhs_c,
                                rhs=ptc[:, 0, c0:c0 + 16].rearrange("p (t i) -> p t i", t=4), start=False, stop=(bb == NSB - 1)),
                                reads=RAK[0:4] + ["ones_b", "PT1"], writes=[("ps", bo)])
                        if which == 0:
                            bnum = bo
                        else:
                            bden = bo
                    dd = dn[h]; kd = "dn%d" % h
                    dve(lambda e, bden=bden, dd=dd, hs=hs: e.tensor_tensor(
                        out=dd[hs, 0:4 * NS].rearrange("p (t c) -> p t c", t=4), in0=PS[bden][hs, 0:4 * NS].rearrange("p (t c) -> p t c", t=4),
                        in1=esink[hs, l, :].rearrange("p (t o) -> p t o", o=1).to_broadcast([64, 4, NS]), op=ALU.add),
                        [("ps", bden), "esink"], [kd])
                    dve(lambda e, dd=dd, hs=hs: e.reciprocal(out=dd[hs, 0:4 * NS], in_=dd[hs, 0:4 * NS]), [kd], [kd])
                    dve(lambda e, bnum=bnum, dd=dd, hs=hs: e.tensor_tensor(
                        out=oa[hs, :, 0:NS], in0=PS[bnum][hs, 0:4 * NS].rearrange("p (t c) -> p t c", t=4),
                        in1=dd[hs, 0:4 * NS].rearrange("p (t c) -> p t c", t=4), op=ALU.mult), [("ps", bnum), kd], ["oa"])
            ws = load_w(lambda w: v8(w), wvin[:, :, U_OFF:U_OFF + 512], ("wb_in", l))
            for t in range(4):
                b = proj_tile(N, v8(ws), 128 * t, ws)
                act(lambda e, b=b, t=t: e.copy(out=uT[:, t, 0:N], in_=PS[b][:, 0:N]), [("ps", b)], ["uT"])
            if sample:
                for (src_, dst_, stg, kst) in ((sre, hs_r, t1, "t1"), (sim, hs_i, t2, "t2")):
                    stv = stg[:].rearrange("p (a q) -> p a q", a=4)
                    s2 = src_[l].rearrange("b g p -> (b g) p").rearrange("(a q) p -> q a p", q=128)
                    for dup in range(2):
                        S.op("sp", lambda e, dup=dup, stv=stv, s2=s2: e.dma_start(out=stv[:, :, 64 * dup:64 * dup + 64], in_=s2),
                             writes=[kst], dma=kst)
                    for a in range(4):
                        b = bank()
                        S.op("pe", lambda e, a=a, b=b, stv=stv: e.transpose(out=PS[b][:, 0:128], in_=stv[:, a, :], identity=ident_f[:]),
                             reads=[kst, "ident_f"], writes=[("ps", b)])
                        for gl in range(2):
                            act(lambda e, a=a, b=b, gl=gl, dst_=dst_: e.copy(
                                out=dst_[64 * gl:64 * gl + 64, :, 4 * a:4 * a + 4],
                                in_=PS[b][64 * gl:64 * gl + 64, 0:128].rearrange("p (b tp gl) -> p gl tp b", gl=2, tp=16)[:, gl]),
                                [("ps", b)], ["hs"])
            for ct in range(4):
                by = ssm_group_sample(l, ct) if sample else ssm_group_prompt(l, ct, N, first_block)
                act(lambda e, by=by, ct=ct: e.activation(out=zT[:, ct, 0:N], in_=PS[by][:, 0:N], func=AF.Gelu), [("ps", by)], ["zT"])
            if sample:
                for (dram_, buf_, stg, kst) in ((hrs, hs_r, t1, "t1"), (his, hs_i, t2, "t2")):
                    for a in range(4):
                        act(lambda e, a=a, buf_=buf_: e.copy(out=qf[:, 0:64].rearrange("p (b tp) -> p b tp", b=4),
                                                             in_=buf_[:, :, 4 * a:4 * a + 4].rearrange("p tp b -> p b tp")), ["hs"], ["qf"])
                        b = bank()
                        S.op("pe", lambda e, b=b: e.transpose(out=PS[b][0:64, 0:128], in_=qf[:, 0:64], identity=ident_f[:]),
                             reads=["qf", "ident_f"], writes=[("ps", b)])
                        act(lambda e, a=a, b=b, stg=stg: e.copy(out=stg[0:64, 128 * a:128 * a + 128], in_=PS[b][0:64, 0:128]), [("ps", b)], [kst])
                    out_toks.append(S.op("sp", lambda e, dram_=dram_, stg=stg: e.dma_start(
                        out=dram_[l].rearrange("b (tp gl) p -> (b tp) (gl p)", gl=2).rearrange("(a r) c -> r a c", a=4),
                        in_=stg[0:64, :].rearrange("p (a c) -> p a c", a=4)), reads=[kst], dma="o_hs"))
            elif last_block:
                for gl in range(2):
                    sl = slice(64 * gl, 64 * gl + 64)
                    out_toks.append(S.op("sp", lambda e, gl=gl, sl=sl: e.dma_start(
                        out=hrp[l].rearrange("(tp gl) p -> gl p tp", gl=2)[gl], in_=car_r[sl, l, :], allow_slow_non_contiguous=True),
                        reads=["car"], dma="o_hp"))
                    out_toks.append(S.op("sp", lambda e, gl=gl, sl=sl: e.dma_start(
                        out=hip[l].rearrange("(tp gl) p -> gl p tp", gl=2)[gl], in_=car_i[sl, l, :], allow_slow_non_contiguous=True),
                        reads=["car"], dma="o_hp"))
            ws = load_w(lambda w: WS[w][:, 0:2048].rearrange("p (k c) -> p k c", k=4), wb_glu[l].rearrange("(k p) c -> p k c", p=128),
                        ("wb_glu", l))
            wg = WS[ws][:, 0:2048].rearrange("p (k c) -> p k c", k=4)
            for t in range(4):
                b = bank()
                for k in range(4):
                    S.op("pe", lambda e, b=b, k=k, t=t: e.matmul(PS[b][:, 0:N], lhsT=wg[:, k, 128 * t:128 * t + 128], rhs=zT[:, k, 0:N],
                                                                start=(k == 0), stop=(k == 3)), reads=["zT", ("ws", ws)], writes=[("ps", b)])
                s_ = sg[t % 2]; ks_ = "sg%d" % (t % 2)
                act(lambda e, b=b, s_=s_: e.activation(out=s_[:, 0:N], in_=PS[b][:, 0:N], func=AF.Sigmoid), [("ps", b)], [ks_])
                dve(lambda e, t=t, s_=s_: e.tensor_tensor(out=ob[:, t, 0:N], in0=zT[:, t, 0:N], in1=s_[:, 0:N], op=ALU.mult), ["zT", ks_], ["ob"])
            ws = load_w(lambda w: v8(w), wvin[:, :, MQ_OFF:MQ_OFF + 512], ("wb_in", l))
            for t in range(4):
                b = proj_tile(N, v8(ws), 128 * t, ws)
                headnorm_rope(N, b, l, gmq[:, l:l + 1], ones_b, 1.0 / 128, False, qmn[:, t, 0:N], ["qmn"])
            sc_m = 1.0 / math.sqrt(128.0)
            if not sample:
                for h in range(4):
                    pt = PT[h % 2]; kpt = "PT%d" % (h % 2)
                    for kt in range(2):
                        b = bank()
                        S.op("pe", lambda e, b=b, h=h, kt=kt: e.matmul(PS[b][:, 0:N], lhsT=MKT[:, l, h, 128 * kt:128 * kt + 128], rhs=qmn[:, h, 0:N],
                                                                      start=True, stop=True), reads=["MKT", "qmn"], writes=[("ps", b)])
                        act(lambda e, b=b, kt=kt, pt=pt: e.activation(out=pt[:, kt, 0:N], in_=PS[b][:, 0:N], func=AF.Exp, scale=sc_m),
                            [("ps", b)], [kpt])
                    bo = bank(); bd = bank()
                    for kt in range(2):
                        S.op("pe", lambda e, bo=bo, h=h, kt=kt, pt=pt: e.matmul(PS[bo][:, 0:N], lhsT=MV[:, l, kt, 128 * h:128 * h + 128],
                                                                               rhs=pt[:, kt, 0:N], start=(kt == 0), stop=(kt == 1)),
                             reads=["MV", kpt], writes=[("ps", bo)])
                    for kt in range(2):
                        S.op("pe", lambda e, bd=bd, kt=kt, pt=pt: e.matmul(PS[bd][:, 0:N], lhsT=ones_b[:], rhs=pt[:, kt, 0:N],
                                                                          start=(kt == 0), stop=(kt == 1)),
                             reads=["ones_b", kpt], writes=[("ps", bd)])
                    dd = dn[h % 2]; kd = "dn%d" % (h % 2)
                    dve(lambda e, bd=bd, dd=dd: e.reciprocal(out=dd[:, 0:N], in_=PS[bd][:, 0:N]), [("ps", bd)], [kd])
                    dve(lambda e, bo=bo, dd=dd, h=h: e.tensor_tensor(out=oc[:, h, 0:N], in0=PS[bo][:, 0:N], in1=dd[:, 0:N], op=ALU.mult),
                        [("ps", bo), kd], ["oc"])
            else:
                bsc = bank(); held.add(bsc)
                bo = bank(); held.add(bo)
                ptc = PT[1]
                for bb in range(NSB):
                    wsk = wslot()
                    kcb = WS[wsk][:, 0:1024].rearrange("p (a c) -> p a c", a=2)
                    vcb = WS[wsk][:, 1024:2048].rearrange("p (a c) -> p a c", a=2)
                    S.op("pool", lambda e, bb=bb, kcb=kcb: e.dma_start(out=kcb, in_=cmk[l, bb].rearrange("(a p) c -> p a c", p=128)),
                         writes=wk(wsk), dma=("ws", wsk))
                    S.op("pool", lambda e, bb=bb, vcb=vcb: e.dma_start(out=vcb, in_=cmv[l, bb].rearrange("(a p) c -> p a c", p=128)),
                         writes=wk(wsk), dma=("ws", wsk))
                    for h in range(4):
                        for kt in range(2):
                            b = bank()
                            pb = PS[b][:].bitcast(BF16)
                            o0 = 2048 + 256 * h + 128 * kt
                            S.op("pe", lambda e, pb=pb, kcb=kcb, h=h, kt=kt: e.transpose(out=pb[:, 0:128], in_=kcb[:, kt, 128 * h:128 * h + 128],
                                                                                        identity=ident_b[:]),
                                 reads=[("ws", wsk), "ident_b"], writes=[("ps", b)])
                            act(lambda e, pb=pb, wsk=wsk, o0=o0: e.copy(out=WS[wsk][:, o0:o0 + 128], in_=pb[:, 0:128]),
                                [("ps", b)], [("wsT", wsk)])
                    for h in range(4):
                        for kt in range(2):
                            c0 = ((bb * 4 + h) * 2 + kt) * 4
                            o0 = 2048 + 256 * h + 128 * kt
                            S.op("pe", lambda e, h=h, c0=c0, bb=bb, wsk=wsk, o0=o0: e.matmul(
                                PS[bsc][:, c0:c0 + 4], lhsT=WS[wsk][:, o0:o0 + 128],
                                rhs=qmn[:, h, 4 * bb:4 * bb + 4], start=True, stop=True),
                                reads=[("wsT", wsk), "qmn"], writes=[("ps", bsc)])
                    c0 = bb * 32
                    act(lambda e, c0=c0: e.activation(out=ptc[:, 0, c0:c0 + 32], in_=PS[bsc][:, c0:c0 + 32], func=AF.Exp, scale=sc_m),
                        [("ps", bsc)], ["PT1"])
                    for h in range(4):
                        for kt in range(2):
                            c1 = ((bb * 4 + h) * 2 + kt) * 4
                            S.op("pe", lambda e, h=h, kt=kt, c1=c1, bb=bb, vcb=vcb: e.matmul(
                                PS[bo][:, h * NS + 4 * bb:h * NS + 4 * bb + 4], lhsT=vcb[:, kt, 128 * h:128 * h + 128],
                                rhs=ptc[:, 0, c1:c1 + 4], start=(kt == 0), stop=(kt == 1)),
                                reads=[("ws", wsk), "PT1"], writes=[("ps", bo)])
                bd = bank()
                pv = ptc[:, 0, :].rearrange("p (b h kt i) -> p h kt b i", b=NSB, h=4, kt=2)
                for h in range(4):
                    for kt in range(2):
                        S.op("pe", lambda e, bd=bd, kt=kt, h=h: e.matmul(PS[bd][:, h * NS:(h + 1) * NS].rearrange("p (b i) -> p b i", i=4),
                                                                        lhsT=ones_b[:], rhs=pv[:, h, kt, :, :], start=(kt == 0), stop=(kt == 1)),
                             reads=["ones_b", "PT1"], writes=[("ps", bd)])
                dd = dn[0]
                dve(lambda e, bd=bd: e.reciprocal(out=dd[:, 0:4 * NS], in_=PS[bd][:, 0:4 * NS]), [("ps", bd)], ["dn0"])
                dve(lambda e, bo=bo: e.tensor_tensor(out=oc[:, :, 0:NS], in0=PS[bo][:, 0:4 * NS].rearrange("p (h c) -> p h c", h=4),
                                                     in1=dd[:, 0:4 * NS].rearrange("p (h c) -> p h c", h=4), op=ALU.mult),
                    [("ps", bo), "dn0"], ["oc"])
                held.discard(bsc); held.discard(bo)
            macc3 = macc.rearrange("p (m n) -> p m n", m=8)
            mgT3 = mgT.rearrange("p (m n) -> p m n", m=8)
            for n_, on_ in enumerate((oa, ob, oc)):
                okey = ("oa", "ob", "oc")[n_]
                wsb = wslot()
                wbv = WS[wsb][:].rearrange("p (k c) -> p k c", k=4)
                if n_ == 0:
                    for t in range(4):
                        for two in range(2):
                            r0 = (two * 4 + t) * 64
                            S.op("sp", lambda e, t=t, two=two, r0=r0: e.dma_start(out=wbv[64 * two:64 * two + 64, t, :], in_=wb_br[l, 0, r0:r0 + 64, :]),
                                 reads=[("wb_br", l)], writes=wk(wsb), dma=("ws", wsb))
                else:
                    S.op("sp", lambda e, n_=n_: e.dma_start(out=wbv, in_=wb_br[l, n_].rearrange("(k p) c -> p k c", p=128)),
                         reads=[("wb_br", l)], writes=wk(wsb), dma=("ws", wsb))
                for half in range(2):
                    wsg = load_w(lambda w: v8(w), wvin[:, :, G_OFF + n_ * 1024 + 512 * half:G_OFF + n_ * 1024 + 512 * half + 512], ("wb_in", l))
                    for mm_ in range(4):
                        m = 4 * half + mm_
                        bg = proj_tile(N, v8(wsg), 128 * mm_, wsg)
                        s_ = sg[m % 2]; ks_ = "sg%d" % (m % 2)
                        act(lambda e, bg=bg, s_=s_: e.activation(out=s_[:, 0:N], in_=PS[bg][:, 0:N], func=AF.Sigmoid), [("ps", bg)], [ks_])
                        bp = bank()
                        for k in range(4):
                            S.op("pe", lambda e, bp=bp, k=k, m=m, on_=on_: e.matmul(PS[bp][:, 0:N], lhsT=wbv[:, k, 128 * m:128 * m + 128],
                                                                                  rhs=on_[:, k, 0:N], start=(k == 0), stop=(k == 3)),
                                 reads=[okey, ("ws", wsb)], writes=[("ps", bp)])
                        if n_ == 0:
                            dve(lambda e, bp=bp, s_=s_, m=m: e.tensor_tensor(out=macc3[:, m, 0:N], in0=PS[bp][:, 0:N], in1=s_[:, 0:N], op=ALU.mult),
                                [("ps", bp), ks_], kmacc)
                        else:
                            dve(lambda e, bp=bp, s_=s_: e.tensor_tensor(out=t1[:, 0:N], in0=PS[bp][:, 0:N], in1=s_[:, 0:N], op=ALU.mult),
                                [("ps", bp), ks_], ["t1"])
                            if n_ == 1:
                                dve(lambda e, m=m: e.tensor_tensor(out=macc3[:, m, 0:N], in0=macc3[:, m, 0:N], in1=t1[:, 0:N], op=ALU.add),
                                    kmacc + ["t1"], kmacc)
                            else:
                                dve(lambda e, m=m: e.tensor_tensor(out=mgT3[:, m, 0:N], in0=macc3[:, m, 0:N], in1=t1[:, 0:N], op=ALU.add),
                                    kmacc + ["t1"], kmgT)
            for half in range(2):
                wso = load_w(lambda w: v8(w), wb_out[l].rearrange("(k p) c -> p k c", p=128)[:, :, 512 * half:512 * half + 512], ("wb_out", l))
                for mm_ in range(4):
                    m = 4 * half + mm_
                    b = bank()
                    for k in range(8):
                        S.op("pe", lambda e, b=b, k=k, mm_=mm_, wso=wso: e.matmul(PS[b][:, 0:N], lhsT=v8(wso)[:, k, 128 * mm_:128 * mm_ + 128],
                                                                                 rhs=mgT3[:, k, 0:N], start=(k == 0), stop=(k == 7)),
                             reads=kmgT + [("ws", wso)], writes=[("ps", b)])
                    dve(lambda e, b=b, m=m: e.tensor_tensor(out=xT[:, m, 0:N], in0=xT[:, m, 0:N], in1=PS[b][:, 0:N], op=ALU.add),
                        [("ps", b), "xT"], ["xT"])
            norm_block(N, gF[:, l, :])
            wvup = wb_up[l].rearrange("(k p) c -> p k c", p=128)

            def actT(j):
                return RA[:, j * TB:(j + 1) * TB], [RAK[j]]

            for grp in range(6):
                nt = 4 if grp < 5 else 2
                wsg = load_w(lambda w: v8(w)[:, :, 0:128 * nt], wvup[:, :, 512 * grp:512 * grp + 128 * nt], ("wb_up", l))
                wsu = load_w(lambda w: v8(w)[:, :, 0:128 * nt], wvup[:, :, DFF + 512 * grp:DFF + 512 * grp + 128 * nt], ("wb_up", l))
                for jj in range(nt):
                    j = 4 * grp + jj
                    bg = proj_tile(N, v8(wsg), 128 * jj, wsg)
                    bu = proj_tile(N, v8(wsu), 128 * jj, wsu)
                    s_ = sg[j % 2]; ks_ = "sg%d" % (j % 2)
                    act(lambda e, bg=bg, s_=s_: e.activation(out=s_[:, 0:N], in_=PS[bg][:, 0:N], func=AF.Silu), [("ps", bg)], [ks_])
                    av, ak = actT(j)
                    dve(lambda e, bu=bu, s_=s_, av=av: e.tensor_tensor(out=av[:, 0:N], in0=PS[bu][:, 0:N], in1=s_[:, 0:N], op=ALU.mult),
                        [("ps", bu), ks_], ak)
            wvdn = wb_dn[l].rearrange("(k p) c -> p k c", p=128)
            for q4 in range(4):
                wsl = []
                for hh in range(2):
                    w_ = wslot()
                    S.op("sp", lambda e, w_=w_, hh=hh, q4=q4: e.dma_start(
                        out=WS[w_][:, 0:11 * 256].rearrange("p (k c) -> p k c", k=11), in_=wvdn[:, 11 * hh:11 * hh + 11, 256 * q4:256 * q4 + 256]),
                        reads=[("wb_dn", l)], writes=wk(w_), dma=("ws", w_))
                    wsl.append(w_)
                for mm_ in range(2):
                    m = 2 * q4 + mm_
                    b = bank()
                    for j in range(22):
                        w_ = wsl[j // 11]
                        wv_ = WS[w_][:, 0:11 * 256].rearrange("p (k c) -> p k c", k=11)
                        av, ak = actT(j)
                        S.op("pe", lambda e, b=b, j=j, mm_=mm_, wv_=wv_, av=av: e.matmul(
                            PS[b][:, 0:N], lhsT=wv_[:, j % 11, 128 * mm_:128 * mm_ + 128], rhs=av[:, 0:N], start=(j == 0), stop=(j == 21)),
                            reads=ak + [("ws", w_)], writes=[("ps", b)])
                    dve(lambda e, b=b, m=m: e.tensor_tensor(out=xT[:, m, 0:N], in0=xT[:, m, 0:N], in1=PS[b][:, 0:N], op=ALU.add),
                        [("ps", b), "xT"], ["xT"])


        def run_block(blk, sample):
            N = NS if sample else TB
            nsub = max(1, N // 128)
            pn = min(N, 128)
            if sample:
                S.op("sp", lambda e: e.dma_start(out=xtok[0:NS, 0, :], in_=xs), writes=["xtok"], dma="xtok")
                S.op("sp", lambda e: e.dma_start(out=cosb[:, 0:NS], in_=c_cos_s), writes=["cosb"], dma="cosb")
                S.op("sp", lambda e: e.dma_start(out=sinb[:, 0:NS], in_=c_sin_s), writes=["sinb"], dma="sinb")
            else:
                t0 = blk * TB
                S.op("sp", lambda e: e.dma_start(out=xtok[:], in_=xp[t0:t0 + TB, :].rearrange("(s p) d -> p s d", p=128)),
                     writes=["xtok"], dma="xtok")
                S.op("sp", lambda e: e.dma_start(out=cosb[:], in_=c_cos[:, t0:t0 + TB]), writes=["cosb"], dma="cosb")
                S.op("sp", lambda e: e.dma_start(out=sinb[:], in_=c_sin[:, t0:t0 + TB]), writes=["sinb"], dma="sinb")
            for s in range(nsub):
                for k in range(8):
                    b = bank()
                    S.op("pe", lambda e, s=s, k=k, b=b: e.transpose(out=PS[b][:, 0:pn], in_=xtok[0:pn, s, 128 * k:128 * k + 128],
                                                                    identity=ident_f[0:pn, 0:pn]),
                         reads=["xtok", "ident_f"], writes=[("ps", b)])
                    act(lambda e, s=s, k=k, b=b: e.copy(out=xT[:, k, 128 * s:128 * s + pn], in_=PS[b][:, 0:pn]), [("ps", b)], ["xT"])
            for l in range(NL):
                layer_block(l, N, sample, blk)
            for s in range(nsub):
                for k in range(8):
                    b = bank()
                    S.op("pe", lambda e, s=s, k=k, b=b: e.transpose(out=PS[b][0:pn, 0:128], in_=xT[:, k, 128 * s:128 * s + pn],
                                                                    identity=ident_f[:]),
                         reads=["xT", "ident_f"], writes=[("ps", b)])
                    act(lambda e, s=s, k=k, b=b: e.copy(out=xtok[0:pn, s, 128 * k:128 * k + 128], in_=PS[b][0:pn, 0:128]),
                        [("ps", b)], ["xtok"])
            if sample:
                out_toks.append(S.op("sp", lambda e: e.dma_start(out=ys, in_=xtok[0:NS, 0, :]), reads=["xtok"], dma="o_y"))
            else:
                t0 = blk * TB
                out_toks.append(S.op("sp", lambda e: e.dma_start(out=yp[t0:t0 + TB, :].rearrange("(s p) d -> p s d", p=128), in_=xtok[:]),
                                     reads=["xtok"], dma="o_y"))

        for blk in range(NBLK):
            run_block(blk, False)
        run_block(0, True)
        S.wait_all("sp", out_toks)
        with nc.allow_non_contiguous_dma(reason="small strided parameter / state transfers"):
            S.emit()
    return nc


_NC_CACHE = {}


def _consts():
    c = {}
    c["c_ident"] = np.eye(128, dtype=np.float32)
    blk = np.zeros((128, 128), np.float32)
    blk[:64, :64] = 1.0
    blk[64:, 64:] = 1.0
    c["c_blk"] = blk
    p = np.arange(128)
    lo = (p % 64) < 32
    partner = np.where(lo, p + 32, p - 32)
    perm = np.zeros((128, 128), np.float32)
    perm[partner, p] = 1.0
    c["c_perm"] = perm
    j = np.arange(128)[:, None]
    i = np.arange(128)[None, :]
    c["c_mprev"] = (j > i).astype(np.float32)
    c["c_mcur"] = (j <= i).astype(np.float32)
    half = 32
    inv = (np.float32(10000.0) ** (-(np.arange(half, dtype=np.float32) / np.float32(half)))).astype(np.float32)
    invp = inv[p % 32]
    sign = np.where(lo, -1.0, 1.0).astype(np.float32)

    def tables(pos):
        ang = (pos[None, :].astype(np.float32) * invp[:, None]).astype(np.float32)
        return np.cos(ang).astype(np.float32), (np.sin(ang) * sign[:, None]).astype(np.float32)

    c["c_cos"], c["c_sin"] = tables(np.arange(SEQ, dtype=np.float32))
    pos_s = np.float32(PAST) + np.tile(np.arange(4, dtype=np.float32), NSB)
    c["c_cos_s"], c["c_sin_s"] = tables(pos_s)
    r = np.arange(128)[:, None]
    ii = np.arange(4)[None, :]
    c["c_mc"] = (r > ii).astype(np.float32)
    kb, kj = np.divmod(np.arange(NS), 4)
    c["c_mnew"] = ((kb[:, None] == kb[None, :]) & (kj[:, None] <= kj[None, :])).astype(np.float32)
    c["c_rowm"] = (np.arange(128)[:, None] // 32 == np.arange(4)[None, :]).astype(np.float32)
    return c


def kernel(**inputs):
    f = lambda a: np.ascontiguousarray(np.asarray(a, dtype=np.float32))
    inp = {k: f(v) for k, v in inputs.items()}
    if "nc" not in _NC_CACHE:
        _NC_CACHE["nc"] = build_program()
    nc = _NC_CACHE["nc"]
    consts = _consts()
    wnames = ["attn_norm", "w_in", "q_norm", "k_norm", "attn_sinks", "ssm_a_re", "ssm_a_im", "ssm_log_dt", "ssm_b_re", "ssm_b_im",
              "ssm_c_re", "ssm_c_im", "ssm_d", "ssm_w_glu", "mem_norm", "w_mem_kv", "mem_q_norm", "mem_k_norm", "w_branch", "w_out",
              "ffn_norm", "w_ffn_up", "w_ffn_down"]
    in_maps = []
    for c in range(8):
        b0 = NSB * c
        m = {
            "xp": inp["x_prompt"][c % 4],
            "xs": inp["x_sample"][b0:b0 + NSB].reshape(NS, D),
            "csk": inp["cache_swa_k"][:, b0:b0 + NSB].reshape(NL, NSB, 128, 128),
            "csv": inp["cache_swa_v"][:, b0:b0 + NSB].reshape(NL, NSB, 128, 128),
            "sre": inp["state_ssm_re"][:, b0:b0 + NSB],
            "sim": inp["state_ssm_im"][:, b0:b0 + NSB],
            "cmk": inp["cache_mem_k"][:, b0:b0 + NSB].reshape(NL, NSB, 256, 512),
            "cmv": inp["cache_mem_v"][:, b0:b0 + NSB].reshape(NL, NSB, 256, 512),
            "memp": inp["mem_prompt"][c % 4],
        }
        for w in wnames:
            m[w] = inp[w]
        m.update(consts)
        in_maps.append({k: np.ascontiguousarray(v) for k, v in m.items()})
    res = run_bass_kernel_spmd(nc, in_maps, core_ids=list(range(8)))
    R = res.results
    cat = lambda name, cores: np.stack([R[c][name] for c in cores])
    y_p = cat("yp", range(4))
    y_s = np.concatenate([R[c]["ys"].reshape(NSB, 4, D) for c in range(8)], axis=0)
    per_l = lambda name, shape: np.stack([R[c][name] for c in range(4)], axis=1).reshape(shape)
    swa_k_p = per_l("kp", (NL, 4, 128, 2, 64))
    swa_v_p = per_l("vp", (NL, 4, 128, 2, 64))
    ssm_re_p = per_l("hrp", (NL, 4, 32, 64))
    ssm_im_p = per_l("hip", (NL, 4, 32, 64))
    mem_k_p = per_l("mkp", (NL, 4, 256, 4, 128))
    mem_v_p = per_l("mvp", (NL, 4, 256, 4, 128))
    cat_s = lambda name, shape: np.concatenate([R[c][name] for c in range(8)], axis=1).reshape(shape)
    swa_k_s = cat_s("ks", (NL, 128, 128, 2, 64))
    swa_v_s = cat_s("vs", (NL, 128, 128, 2, 64))
    ssm_re_s = cat_s("hrs", (NL, 128, 32, 64))
    ssm_im_s = cat_s("his", (NL, 128, 32, 64))
    outs = (y_p, y_s, swa_k_p, swa_v_p, ssm_re_p, ssm_im_p, mem_k_p, mem_v_p, swa_k_s, swa_v_s, ssm_re_s, ssm_im_s)
    return tuple(np.ascontiguousarray(o, dtype=np.float32) for o in outs)
```

```python
import contextlib
import math
import types
import numpy as np
import concourse.bass as bass
import concourse.mybir as mybir
from concourse.bass_utils import run_bass_kernel_spmd

F32 = mybir.dt.float32
BF16 = mybir.dt.bfloat16
I32 = mybir.dt.int32
AF = mybir.ActivationFunctionType
ALU = mybir.AluOpType

D = 1024
SEQ = 4096
NL = 2
INW = 4864
DFF = 2816
K_OFF, V_OFF, U_OFF, MQ_OFF, G_OFF = 512, 640, 768, 1280, 1792
PAST = 16384
TB = 512
NBLK = SEQ // TB
NSB = 16
NS = NSB * 4
NLEV = 9
EPS = 1e-6
NWS = 4
ENGS = ("pe", "act", "dve", "pool", "sp")


def _freeze(fn):
    if fn is None or fn.__closure__ is None:
        return fn
    cells = []
    for c in fn.__closure__:
        try:
            cells.append(types.CellType(c.cell_contents))
        except ValueError:
            cells.append(c)
    return types.FunctionType(fn.__code__, fn.__globals__, fn.__name__, fn.__defaults__, tuple(cells))


class Sched:
    def __init__(self, nc, stack):
        self.nc = nc
        self.stack = stack
        self.q = {e: [] for e in ENGS}
        self.cnt = {e: 0 for e in ENGS}
        self.esem = {e: stack.enter_context(nc.semaphore("sem_" + e)) for e in ENGS}
        self.dsem = {}
        self.dcnt = {}
        self.last_w = {}
        self.readers = {}
        self.seen = {e: {} for e in ENGS}
        self.alias = {}

    def _exp(self, keys):
        out = []
        for k in keys:
            out.append(k)
            out.extend(self.alias.get(k, ()))
        return out

    def op(self, eng, fn, reads=(), writes=(), dma=None):
        reads = self._exp(reads)
        writes = self._exp(writes)
        fn = _freeze(fn)
        deps = []
        for r in reads:
            t = self.last_w.get(r)
            if t is not None:
                deps.append((t, True))
        for w in writes:
            t = self.last_w.get(w)
            if t is not None:
                deps.append((t, False))
            for t in self.readers.get(w, ()):
                deps.append((t, False))
        waits = {}
        for (kind, key, val), raw in deps:
            if kind == "eng" and key == eng and (eng == "pe" or not raw):
                continue
            sk = (kind, key)
            if val > self.seen[eng].get(sk, 0):
                waits[sk] = max(waits.get(sk, 0), val)
        for sk, val in waits.items():
            self.seen[eng][sk] = val
        if dma is not None:
            if dma not in self.dsem:
                self.dsem[dma] = self.stack.enter_context(self.nc.semaphore("dq%d" % len(self.dsem)))
                self.dcnt[dma] = 0
            self.dcnt[dma] += 16
            tok = ("dma", dma, self.dcnt[dma])
        else:
            self.cnt[eng] += 1
            tok = ("eng", eng, self.cnt[eng])
        self.q[eng].append((fn, list(waits.items()), tok))
        for w in writes:
            self.last_w[w] = tok
            self.readers[w] = []
        for r in reads:
            self.readers.setdefault(r, []).append(tok)
        return tok

    def wait_all(self, eng, toks):
        waits = {}
        for kind, key, val in toks:
            sk = (kind, key)
            if val > self.seen[eng].get(sk, 0):
                waits[sk] = max(waits.get(sk, 0), val)
        for sk, val in waits.items():
            self.seen[eng][sk] = val
        self.q[eng].append((None, list(waits.items()), None))

    def emit(self):
        nc = self.nc
        sem = lambda sk: self.esem[sk[1]] if sk[0] == "eng" else self.dsem[sk[1]]
        with nc.Block() as block:
            def run(e, engobj):
                for fn, waits, tok in self.q[e]:
                    for sk, val in waits:
                        engobj.wait_ge(sem(sk), val)
                    if fn is None:
                        continue
                    ins = fn(engobj)
                    if tok[0] == "dma":
                        ins.then_inc(self.dsem[tok[1]], 16)
                    else:
                        ins.then_inc(self.esem[e], 1)

            @block.tensor
            def _(t):
                run("pe", t)

            @block.scalar
            def _(t):
                run("act", t)

            @block.vector
            def _(t):
                run("dve", t)

            @block.gpsimd
            def _(t):
                run("pool", t)

            @block.sync
            def _(t):
                run("sp", t)


def build_program():
    nc = bass.Bass("TRN2", target_bir_lowering=False)

    def din(name, shape):
        return nc.dram_tensor(name, list(shape), F32, kind="ExternalInput").ap()

    def dout(name, shape):
        return nc.dram_tensor(name, list(shape), F32, kind="ExternalOutput").ap()

    def dscr(name, shape, dt=BF16):
        return nc.dram_tensor(name, list(shape), dt).ap()

    xp = din("xp", [SEQ, D]); xs = din("xs", [NS, D])
    csk = din("csk", [NL, NSB, 128, 128]); csv = din("csv", [NL, NSB, 128, 128])
    sre = din("sre", [NL, NSB, 32, 64]); sim = din("sim", [NL, NSB, 32, 64])
    cmk = din("cmk", [NL, NSB, 256, 512]); cmv = din("cmv", [NL, NSB, 256, 512])
    memp = din("memp", [256, D])
    attn_norm = din("attn_norm", [NL, D]); w_in = din("w_in", [NL, D, INW])
    q_norm = din("q_norm", [NL, 64]); k_norm = din("k_norm", [NL, 64]); attn_sinks = din("attn_sinks", [NL, 8])
    a_re = din("ssm_a_re", [NL, 32, 64]); a_im = din("ssm_a_im", [NL, 32, 64]); log_dt = din("ssm_log_dt", [NL, 32])
    b_re = din("ssm_b_re", [NL, 32, 64, 16]); b_im = din("ssm_b_im", [NL, 32, 64, 16])
    c_re = din("ssm_c_re", [NL, 32, 16, 64]); c_im = din("ssm_c_im", [NL, 32, 16, 64])
    ssm_d = din("ssm_d", [NL, 512]); w_glu = din("ssm_w_glu", [NL, 512, 512])
    mem_norm = din("mem_norm", [NL, D]); w_kv = din("w_mem_kv", [NL, D, D])
    mq_norm = din("mem_q_norm", [NL, 128]); mk_norm = din("mem_k_norm", [NL, 128])
    w_br = din("w_branch", [NL, 3, 512, D]); w_out = din("w_out", [NL, D, D])
    ffn_norm = din("ffn_norm", [NL, D]); w_up = din("w_ffn_up", [NL, D, 2 * DFF]); w_dn = din("w_ffn_down", [NL, DFF, D])
    c_ident = din("c_ident", [128, 128]); c_blk = din("c_blk", [128, 128]); c_perm = din("c_perm", [128, 128])
    c_mprev = din("c_mprev", [128, 128]); c_mcur = din("c_mcur", [128, 128])
    c_cos = din("c_cos", [128, SEQ]); c_sin = din("c_sin", [128, SEQ])
    c_cos_s = din("c_cos_s", [128, NS]); c_sin_s = din("c_sin_s", [128, NS])
    c_mc = din("c_mc", [128, 4]); c_mnew = din("c_mnew", [NS, NS]); c_rowm = din("c_rowm", [128, 4])

    yp = dout("yp", [SEQ, D]); ys = dout("ys", [NS, D])
    kp = dout("kp", [NL, 128, 128]); vp = dout("vp", [NL, 128, 128])
    hrp = dout("hrp", [NL, 32, 64]); hip = dout("hip", [NL, 32, 64])
    mkp = dout("mkp", [NL, 256, 512]); mvp = dout("mvp", [NL, 256, 512])
    ks = dout("ks", [NL, NSB, 128, 128]); vs = dout("vs", [NL, NSB, 128, 128])
    hrs = dout("hrs", [NL, NSB, 32, 64]); his = dout("his", [NL, NSB, 32, 64])

    wb_in = dscr("wb_in", [NL, D, INW]); wb_glu = dscr("wb_glu", [NL, 512, 512]); wb_kv = dscr("wb_kv", [NL, D, D])
    wb_br = dscr("wb_br", [NL, 3, 512, D]); wb_out = dscr("wb_out", [NL, D, D])
    wb_up = dscr("wb_up", [NL, D, 2 * DFF]); wb_dn = dscr("wb_dn", [NL, DFF, D])

    out_toks = []

    with contextlib.ExitStack() as st:
        S = Sched(nc, st)

        def sb(name, shape, dt):
            return st.enter_context(nc.sbuf_tensor(name, list(shape), dt))

        PS = [st.enter_context(nc.psum_tensor("ps%d" % i, [128, 512], F32)) for i in range(8)]
        psn = [0]

        held = set()

        def bank():
            while True:
                i = psn[0] % 8
                psn[0] += 1
                if i not in held:
                    return i

        ident_f = sb("ident_f", [128, 128], F32); ident_b = sb("ident_b", [128, 128], BF16)
        ones_b = sb("ones_b", [128, 128], BF16); blk_b = sb("blk_b", [128, 128], BF16); perm_b = sb("perm_b", [128, 128], BF16)
        mprev_b = sb("mprev_b", [128, 128], BF16); mcur_b = sb("mcur_b", [128, 128], BF16)
        mc_b = sb("mc_b", [128, 4], BF16); mnew_b = sb("mnew_b", [NS, NS], BF16); rowm = sb("rowm", [128, 4], F32)
        cosb = sb("cosb", [128, TB], F32); sinb = sb("sinb", [128, TB], F32)
        gA = sb("gA", [128, NL, 8], F32); gF = sb("gF", [128, NL, 8], F32)
        gq = sb("gq", [128, NL], F32); gk = sb("gk", [128, NL], F32); gmq = sb("gmq", [128, NL], F32)
        esink = sb("esink", [128, NL, 4], F32); dcol = sb("dcol", [128, NL, 4], F32)
        W2 = sb("W2", [128, NL, 16, 2, 128], BF16); CP = sb("CP", [128, NL, 16, 2, 128], BF16)
        Dd = sb("Dd", [128, NL, 4, 128], BF16)
        LR = sb("LR", [128, NL, NLEV, 16], F32); LI = sb("LI", [128, NL, NLEV, 16], F32); LIn = sb("LIn", [128, NL, NLEV, 16], F32)
        MKT = sb("MKT", [128, NL, 4, 256], BF16); MV = sb("MV", [128, NL, 2, 512], BF16)
        kTc = sb("kTc", [128, NL, 128 + TB], BF16); vtc = sb("vtc", [128, NL, 5, 128], BF16)
        car_r = sb("car_r", [128, NL, 16], F32); car_i = sb("car_i", [128, NL, 16], F32)
        xT = sb("xT", [128, 8, TB], F32)
        hT = sb("hT", [128, 8, TB], BF16)
        qf = sb("qf", [128, TB], F32); sqb = sb("sqb", [128, TB], BF16); sdv = sb("sdv", [128, TB], F32); rstd = sb("rstd", [128, TB], F32)
        qn = sb("qn", [128, TB], BF16); t1 = sb("t1", [128, TB], F32); t2 = sb("t2", [128, TB], F32)
        HN = [dict(qf=qf, sqb=sqb, sdv=sdv, rstd=rstd, qn=qn, t1=t1, t2=t2, sfx=""),
              dict(qf=sb("qfB", [128, TB], F32), sqb=sb("sqbB", [128, TB], BF16), sdv=sdv,
                   rstd=sb("rstdB", [128, TB], F32), qn=sb("qnB", [128, TB], BF16), t1=t1, t2=t2, sfx="B")]
        hn_i = [0]
        S.alias["zT"] = ["qr"]
        qr = sb("qr", [128, 4, TB], BF16); k32 = sb("k32", [128, TB], F32); v32 = sb("v32", [128, 4, 128], F32)
        uT = sb("uT", [128, 4, TB], BF16); qmn = sb("qmn", [128, 4, TB], BF16)
        oa = sb("oa", [128, 4, TB], BF16); ob = sb("ob", [128, 4, TB], BF16); oc = sb("oc", [128, 4, TB], BF16)
        PT = [sb("PT%d" % i, [128, 2, TB], BF16) for i in range(2)]
        dn = [sb("dn%d" % i, [128, TB], F32) for i in range(2)]
        zT = qr; sg = [sb("sg%d" % i, [128, TB], BF16) for i in range(2)]
        WS = [sb("ws%d" % i, [128, 4096], BF16) for i in range(NWS)]
        RA = sb("RA", [128, 16384], BF16)
        small = sb("small", [128, 64], F32)
        smi = sb("smi", [128, 16], I32)
        hs_r = sb("hs_r", [128, 16, NSB], F32); hs_i = sb("hs_i", [128, 16, NSB], F32)
        kcT = sb("kcT", [128, 2, 128], BF16)

        RAK = [("RA", i) for i in range(32)]

        def ra(off_b, nbytes, dt):
            a = RA[:, off_b // 2:(off_b + nbytes) // 2]
            keys = RAK[off_b // 1024:(off_b + nbytes + 1023) // 1024]
            return (a.bitcast(F32) if dt == F32 else a), keys

        xtok = ra(0, 16384, F32)[0].rearrange("p (s d) -> p s d", s=4)
        S.alias["xtok"] = RAK[0:16]
        sqT = ra(24576, 8192, BF16)[0].rearrange("p (k n) -> p k n", k=8)
        S.alias["sqT"] = RAK[24:32]
        XSr, kXSr = ra(0, 8192, F32); XSi, kXSi = ra(8192, 8192, F32)
        TD = [ra(16384 + 4096 * i, 4096, F32) for i in range(4)]
        xbr, kxbr = ra(16384, 4096, BF16); xbi, kxbi = ra(20480, 4096, BF16)
        macc, kmacc = ra(0, 16384, F32); mgT, kmgT = ra(16384, 8192, BF16)
        wsn = [0]

        def wk(i):
            return [("ws", i), ("wsT", i)]

        def wslot():
            i = wsn[0] % NWS
            wsn[0] += 1
            return i

        S.op("sp", lambda e: e.dma_start(out=ident_f[:], in_=c_ident), writes=["ident_f"], dma="c0")
        for (dst, src, nm) in [(ident_b, c_ident, "ident_b"), (blk_b, c_blk, "blk_b"), (perm_b, c_perm, "perm_b"),
                               (mprev_b, c_mprev, "mprev_b"), (mcur_b, c_mcur, "mcur_b"), (mc_b, c_mc, "mc_b"),
                               (mnew_b, c_mnew, "mnew_b")]:
            S.op("pool", lambda e, dst=dst, src=src: e.dma_start(out=dst[:], in_=src), writes=[nm], dma=nm)
        S.op("sp", lambda e: e.dma_start(out=rowm[:], in_=c_rowm), writes=["rowm"], dma="rowm")
        S.op("dve", lambda e: e.memset(ones_b[:], 1.0), writes=["ones_b"])
        S.op("dve", lambda e: e.memset(small[:], 0.0), writes=["small"])
        S.op("dve", lambda e: e.memset(small[:, 0:1], math.pi / 2), writes=["small"])
        S.op("dve", lambda e: e.memset(small[:, 1:2], EPS), writes=["small"])

        for l in range(NL):
            S.op("sp", lambda e, l=l: e.dma_start(out=gA[:, l, :], in_=attn_norm[l].rearrange("(k p) -> p k", p=128),
                                                  allow_slow_non_contiguous=True), writes=["gA"], dma="gA")
            S.op("sp", lambda e, l=l: e.dma_start(out=gF[:, l, :], in_=ffn_norm[l].rearrange("(k p) -> p k", p=128),
                                                  allow_slow_non_contiguous=True), writes=["gF"], dma="gF")
            S.op("sp", lambda e, l=l: e.dma_start(out=dcol[:, l, :], in_=ssm_d[l].rearrange("(k p) -> p k", p=128),
                                                  allow_slow_non_contiguous=True), writes=["dcol"], dma="dcol")
            for two in range(2):
                sl = slice(64 * two, 64 * two + 64)
                S.op("sp", lambda e, l=l, sl=sl: e.dma_start(out=gq[sl, l:l + 1], in_=q_norm[l].rearrange("(p o) -> p o", o=1)),
                     writes=["gq"], dma="gq")
                S.op("sp", lambda e, l=l, sl=sl: e.dma_start(out=gk[sl, l:l + 1], in_=k_norm[l].rearrange("(p o) -> p o", o=1)),
                     writes=["gk"], dma="gk")
                S.op("sp", lambda e, l=l, sl=sl, two=two: e.dma_start(out=esink[sl, l, :],
                                                                      in_=attn_sinks[l, 4 * two:4 * two + 4].partition_broadcast(64)),
                     writes=["esink"], dma="esink")
            S.op("sp", lambda e, l=l: e.dma_start(out=gmq[:, l:l + 1], in_=mq_norm[l].rearrange("(p o) -> p o", o=1)),
                 writes=["gmq"], dma="gmq")
        S.op("act", lambda e: e.activation(out=esink[:], in_=esink[:], func=AF.Exp), reads=["esink"], writes=["esink"])

        are_t = sb("are_t", [128, 16], F32); aim_t = sb("aim_t", [128, 16], F32); dt_t = sb("dt_t", [128, 16], F32)
        sA = [sb("sA%d" % i, [128, 16], F32) for i in range(8)]
        def ra3(idx, nm):
            v, kk = ra(16384 + 2048 * idx, 2048, F32)
            S.alias[nm] = kk
            return v.rearrange("p (t c) -> p t c", t=16)
        Bb = [ra3(i, "Bb%d" % i) for i in range(2)]
        Cb = [ra3(2 + i, "Cb%d" % i) for i in range(2)]
        GB = [ra3(4 + i, "GB%d" % i) for i in range(2)]
        tG = [ra3(6 + i, "tG%d" % i) for i in range(2)]
        tT = sb("tT", [128, 128], F32)

        def dve(fn, R, W):
            return S.op("dve", fn, reads=R, writes=W)

        def act(fn, R, W):
            return S.op("act", fn, reads=R, writes=W)

        TWO_PI = 2.0 * math.pi
        for l in range(NL):
            for gl in range(2):
                sl = slice(64 * gl, 64 * gl + 64)
                S.op("sp", lambda e, l=l, gl=gl, sl=sl: e.dma_start(
                    out=are_t[sl, :], in_=a_re[l].rearrange("(tp gl) p -> gl p tp", gl=2)[gl], allow_slow_non_contiguous=True),
                    writes=["are_t"], dma="are_t")
                S.op("sp", lambda e, l=l, gl=gl, sl=sl: e.dma_start(
                    out=aim_t[sl, :], in_=a_im[l].rearrange("(tp gl) p -> gl p tp", gl=2)[gl], allow_slow_non_contiguous=True),
                    writes=["aim_t"], dma="aim_t")
                S.op("sp", lambda e, l=l, gl=gl, sl=sl: e.dma_start(
                    out=dt_t[sl, :], in_=log_dt[l].rearrange("(tp gl) -> gl tp", gl=2)[gl].partition_broadcast(64)),
                    writes=["dt_t"], dma="dt_t")
            for ri, (bsrc, csrc) in enumerate([(b_re, c_re), (b_im, c_im)]):
                S.op("pool", lambda e, ri=ri: e.memset(Bb[ri][:], 0.0), writes=["Bb%d" % ri])
                S.op("pool", lambda e, ri=ri: e.memset(Cb[ri][:], 0.0), writes=["Cb%d" % ri])
                for gl in range(2):
                    sl = slice(64 * gl, 64 * gl + 64)
                    cs = slice(16 * gl, 16 * gl + 16)
                    S.op("sp", lambda e, l=l, gl=gl, sl=sl, cs=cs, ri=ri, bsrc=bsrc: e.dma_start(
                        out=Bb[ri][sl, :, cs], in_=bsrc[l].rearrange("(tp gl) p c -> gl p tp c", gl=2)[gl]),
                        writes=["Bb%d" % ri], dma="Bb%d" % ri)
                cst = t1[:].rearrange("p (a q) -> p a q", a=4)
                csrc2 = csrc[l].rearrange("g c p -> (g c) p").rearrange("(a q) p -> q a p", q=128)
                for dup in range(2):
                    S.op("sp", lambda e, dup=dup: e.dma_start(out=cst[:, :, 64 * dup:64 * dup + 64], in_=csrc2), writes=["t1"], dma="t1")
                for a in range(4):
                    b = bank()
                    S.op("pe", lambda e, a=a, b=b: e.transpose(out=PS[b][:, 0:128], in_=cst[:, a, :], identity=ident_f[:]),
                         reads=["t1", "ident_f"], writes=[("ps", b)])
                    for gl in range(2):
                        act(lambda e, a=a, b=b, gl=gl, ri=ri: e.copy(
                            out=Cb[ri][64 * gl:64 * gl + 64, 4 * a:4 * a + 4, 16 * gl:16 * gl + 16],
                            in_=PS[b][64 * gl:64 * gl + 64, 0:128].rearrange("p (tp gl c) -> p gl tp c", gl=2, c=16)[:, gl]),
                            [("ps", b)], ["Cb%d" % ri])
            dtv, ard, mag, th, kf, s_, c_, tmp = sA
            act(lambda e: e.activation(out=dtv[:], in_=dt_t[:], func=AF.Exp), ["dt_t"], ["sA0"])
            dve(lambda e: e.tensor_tensor(out=ard[:], in0=are_t[:], in1=dtv[:], op=ALU.mult), ["are_t", "sA0"], ["sA1"])
            act(lambda e: e.activation(out=mag[:], in_=ard[:], func=AF.Exp), ["sA1"], ["sA2"])
            dve(lambda e: e.tensor_tensor(out=th[:], in0=aim_t[:], in1=dtv[:], op=ALU.mult), ["aim_t", "sA0"], ["sA3"])
            dve(lambda e: e.tensor_scalar(out=kf[:], in0=th[:], scalar1=1.0 / TWO_PI, scalar2=None, op0=ALU.mult), ["sA3"], ["sA4"])
            dve(lambda e: e.tensor_copy(out=smi[:], in_=kf[:]), ["sA4"], ["smi"])
            dve(lambda e: e.tensor_copy(out=kf[:], in_=smi[:]), ["smi"], ["sA4"])
            dve(lambda e: e.scalar_tensor_tensor(out=th[:], in0=kf[:], scalar=-TWO_PI, in1=th[:], op0=ALU.mult, op1=ALU.add),
                ["sA4", "sA3"], ["sA3"])
            act(lambda e: e.activation(out=s_[:], in_=th[:], func=AF.Sin, scale=0.5), ["sA3"], ["sA5"])
            act(lambda e: e.activation(out=c_[:], in_=th[:], func=AF.Sin, scale=0.5, bias=small[:, 0:1]), ["sA3", "small"], ["sA6"])
            lr0 = LR[:, l, 0, :]; li0 = LI[:, l, 0, :]
            dve(lambda e: e.tensor_tensor(out=tmp[:], in0=s_[:], in1=c_[:], op=ALU.mult), ["sA5", "sA6"], ["sA7"])
            dve(lambda e: e.scalar_tensor_tensor(out=li0, in0=tmp[:], scalar=2.0, in1=mag[:], op0=ALU.mult, op1=ALU.mult),
                ["sA7", "sA2"], ["LI"])
            dve(lambda e: e.tensor_tensor(out=tmp[:], in0=s_[:], in1=s_[:], op=ALU.mult), ["sA5"], ["sA7"])
            dve(lambda e: e.tensor_scalar(out=tmp[:], in0=tmp[:], scalar1=-2.0, scalar2=1.0, op0=ALU.mult, op1=ALU.add), ["sA7"], ["sA7"])
            dve(lambda e: e.tensor_tensor(out=lr0, in0=tmp[:], in1=mag[:], op=ALU.mult), ["sA7", "sA2"], ["LR"])
            den_, nr_, gr_, gi_, rd_ = sA[0], sA[1], sA[2], sA[3], sA[4]
            dve(lambda e: e.tensor_tensor(out=den_[:], in0=are_t[:], in1=are_t[:], op=ALU.mult), ["are_t"], ["sA0"])
            dve(lambda e: e.tensor_tensor(out=tmp[:], in0=aim_t[:], in1=aim_t[:], op=ALU.mult), ["aim_t"], ["sA7"])
            dve(lambda e: e.tensor_tensor(out=den_[:], in0=den_[:], in1=tmp[:], op=ALU.add), ["sA0", "sA7"], ["sA0"])
            dve(lambda e: e.reciprocal(out=rd_[:], in_=den_[:]), ["sA0"], ["sA4"])
            dve(lambda e: e.tensor_scalar(out=nr_[:], in0=lr0, scalar1=-1.0, scalar2=None, op0=ALU.add), ["LR"], ["sA1"])
            dve(lambda e: e.tensor_tensor(out=gr_[:], in0=nr_[:], in1=are_t[:], op=ALU.mult), ["sA1", "are_t"], ["sA2"])
            dve(lambda e: e.tensor_tensor(out=tmp[:], in0=li0, in1=aim_t[:], op=ALU.mult), ["LI", "aim_t"], ["sA7"])
            dve(lambda e: e.tensor_tensor(out=gr_[:], in0=gr_[:], in1=tmp[:], op=ALU.add), ["sA2", "sA7"], ["sA2"])
            dve(lambda e: e.tensor_tensor(out=gr_[:], in0=gr_[:], in1=rd_[:], op=ALU.mult), ["sA2", "sA4"], ["sA2"])
            dve(lambda e: e.tensor_tensor(out=gi_[:], in0=li0, in1=are_t[:], op=ALU.mult), ["LI", "are_t"], ["sA3"])
            dve(lambda e: e.tensor_tensor(out=tmp[:], in0=nr_[:], in1=aim_t[:], op=ALU.mult), ["sA1", "aim_t"], ["sA7"])
            dve(lambda e: e.tensor_tensor(out=gi_[:], in0=gi_[:], in1=tmp[:], op=ALU.subtract), ["sA3", "sA7"], ["sA3"])
            dve(lambda e: e.tensor_tensor(out=gi_[:], in0=gi_[:], in1=rd_[:], op=ALU.mult), ["sA3", "sA4"], ["sA3"])
            for i in range(NLEV - 1):
                a, b = LR[:, l, i, :], LI[:, l, i, :]
                a2, b2 = LR[:, l, i + 1, :], LI[:, l, i + 1, :]
                dve(lambda e, a=a, b=b: e.tensor_tensor(out=tmp[:], in0=b, in1=b, op=ALU.mult), ["LI"], ["sA7"])
                dve(lambda e, a=a, a2=a2: e.tensor_tensor(out=a2, in0=a, in1=a, op=ALU.mult), ["LR"], ["LR"])
                dve(lambda e, a2=a2: e.tensor_tensor(out=a2, in0=a2, in1=tmp[:], op=ALU.subtract), ["LR", "sA7"], ["LR"])
                dve(lambda e, a=a, b=b, b2=b2: e.scalar_tensor_tensor(out=b2, in0=a, scalar=2.0, in1=b, op0=ALU.mult, op1=ALU.mult),
                    ["LR", "LI"], ["LI"])
            dve(lambda e, l=l: e.tensor_scalar(out=LIn[:, l], in0=LI[:, l], scalar1=-1.0, scalar2=None, op0=ALU.mult), ["LI"], ["LIn"])
            grb = gr_[:].rearrange("p (t o) -> p t o", o=1).to_broadcast([128, 16, 32])
            gib = gi_[:].rearrange("p (t o) -> p t o", o=1).to_broadcast([128, 16, 32])
            dve(lambda e: e.tensor_tensor(out=GB[0][:], in0=Bb[0][:], in1=grb, op=ALU.mult), ["Bb0", "sA2"], ["GB0"])
            dve(lambda e: e.tensor_tensor(out=tG[0][:], in0=Bb[1][:], in1=gib, op=ALU.mult), ["Bb1", "sA3"], ["tG0"])
            dve(lambda e: e.tensor_tensor(out=GB[0][:], in0=GB[0][:], in1=tG[0][:], op=ALU.subtract), ["GB0", "tG0"], ["GB0"])
            dve(lambda e: e.tensor_tensor(out=GB[1][:], in0=Bb[1][:], in1=grb, op=ALU.mult), ["Bb1", "sA2"], ["GB1"])
            dve(lambda e: e.tensor_tensor(out=tG[1][:], in0=Bb[0][:], in1=gib, op=ALU.mult), ["Bb0", "sA3"], ["tG1"])
            dve(lambda e: e.tensor_tensor(out=GB[1][:], in0=GB[1][:], in1=tG[1][:], op=ALU.add), ["GB1", "tG1"], ["GB1"])
            for ri in range(2):
                for ct in range(4):
                    b = bank()
                    S.op("pe", lambda e, b=b, ri=ri, ct=ct: e.transpose(
                        out=PS[b][:, 0:128], in_=GB[ri][:, 4 * ct:4 * ct + 4, :].rearrange("p a b -> p (a b)"), identity=ident_f[:]),
                        reads=["GB%d" % ri, "ident_f"], writes=[("ps", b)])
                    act(lambda e, b=b: e.copy(out=tT[:], in_=PS[b][:, 0:128]), [("ps", b)], ["tT"])
                    for i in range(4):
                        dve(lambda e, l=l, ri=ri, ct=ct, i=i: e.tensor_scalar(
                            out=W2[:, l, 4 * ct + i, ri, :], in0=tT[:], scalar1=rowm[:, i:i + 1], scalar2=None, op0=ALU.mult),
                            ["tT", "rowm"], ["W2"])
            S.op("pool", lambda e, l=l: e.memset(CP[:, l], 0.0), writes=["CP"])
            for i in range(4):
                act(lambda e, l=l, i=i: e.copy(out=CP[:, l, i::4, 0, 32 * i:32 * i + 32], in_=Cb[0][:, i::4, :]), ["Cb0", "CP"], ["CP"])
                act(lambda e, l=l, i=i: e.mul(out=CP[:, l, i::4, 1, 32 * i:32 * i + 32], in_=Cb[1][:, i::4, :], mul=-1.0), ["Cb1", "CP"], ["CP"])
            for ct in range(4):
                dve(lambda e, l=l, ct=ct: e.tensor_scalar(out=Dd[:, l, ct, :], in0=ident_f[:], scalar1=dcol[:, l, ct:ct + 1], scalar2=None,
                                                          op0=ALU.mult), ["ident_f", "dcol"], ["Dd"])

        def conv(dst, src, key, rows):
            r0 = 0
            while r0 < rows:
                r1 = min(rows, r0 + 256)
                S.op("pool", lambda e, r0=r0, r1=r1: e.dma_start(out=dst[r0:r1, :], in_=src[r0:r1, :]),
                     writes=[key], dma=key)
                r0 = r1

        for l in range(NL):
            conv(wb_kv[l], w_kv[l], ("wb_kv", l), D)
        for l in range(NL):
            conv(wb_in[l], w_in[l], ("wb_in", l), D)
            conv(wb_glu[l], w_glu[l], ("wb_glu", l), 512)
            for n in range(3):
                conv(wb_br[l, n], w_br[l, n], ("wb_br", l), 512)
            conv(wb_out[l], w_out[l], ("wb_out", l), D)
            conv(wb_up[l], w_up[l], ("wb_up", l), D)
            conv(wb_dn[l], w_dn[l], ("wb_dn", l), DFF)

        memt = xtok
        mnb = qr[:].rearrange("p t n -> p (t n)").rearrange("p (a d) -> p a d", a=2)
        S.alias["mnb"] = ["qr"]
        mnT = hT
        gmk_b = sb("gmk_b", [128, 128], F32)
        S.op("sp", lambda e: e.dma_start(out=memt[:, 0:2, :], in_=memp.rearrange("(a p) d -> p a d", p=128)), writes=["xtok"], dma="xtok")
        for l in range(NL):
            S.op("sp", lambda e, l=l: e.dma_start(out=memt[:, 2, :], in_=mem_norm[l].partition_broadcast(128)), writes=["xtok"], dma="xtok")
            S.op("sp", lambda e, l=l: e.dma_start(out=gmk_b[:], in_=mk_norm[l].partition_broadcast(128)), writes=["gmk_b"], dma="gmk_b")
            dve(lambda e: e.memset(small[:, 8:10], 0.0), [], ["small"])
            for a in range(2):
                act(lambda e, a=a: e.activation(out=memt[:, 3, :], in_=memt[:, a, :], func=AF.Square, accum_out=small[:, 8 + a:9 + a]),
                    ["xtok"], ["xtok", "small"])
            act(lambda e: e.activation(out=small[:, 10:12], in_=small[:, 8:10], func=AF.Sqrt, scale=1.0 / D, bias=small[:, 1:2]),
                ["small"], ["small"])
            dve(lambda e: e.reciprocal(out=small[:, 12:14], in_=small[:, 10:12]), ["small"], ["small"])
            for a in range(2):
                dve(lambda e, a=a: e.scalar_tensor_tensor(out=mnb[:, a, :], in0=memt[:, a, :], scalar=small[:, 12 + a:13 + a],
                                                          in1=memt[:, 2, :], op0=ALU.mult, op1=ALU.mult), ["xtok", "small"], ["mnb"])
            for a in range(2):
                for k in range(8):
                    b = bank()
                    pb = PS[b][:].bitcast(BF16)
                    S.op("pe", lambda e, a=a, k=k, pb=pb: e.transpose(out=pb[:, 0:128], in_=mnb[:, a, 128 * k:128 * k + 128],
                                                                      identity=ident_b[:]),
                         reads=["mnb", "ident_b"], writes=[("ps", b)])
                    act(lambda e, a=a, k=k, pb=pb: e.copy(out=mnT[:, k, 128 * a:128 * a + 128], in_=pb[:, 0:128]), [("ps", b)], ["hT"])
            for half in range(2):
                ws = wslot()
                wv = WS[ws][:].rearrange("p (k c) -> p k c", k=8)
                S.op("sp", lambda e, l=l, half=half, wv=wv: e.dma_start(
                    out=wv, in_=wb_kv[l].rearrange("(k p) c -> p k c", p=128)[:, :, 512 * half:512 * half + 512]),
                    reads=[("wb_kv", l)], writes=wk(ws), dma=("ws", ws))
                for a in range(2):
                    b = bank()
                    for k in range(8):
                        S.op("pe", lambda e, a=a, k=k, b=b, wv=wv: e.matmul(PS[b][:], lhsT=mnT[:, k, 128 * a:128 * a + 128], rhs=wv[:, k, :],
                                                                          start=(k == 0), stop=(k == 7)),
                             reads=["hT", ("ws", ws)], writes=[("ps", b)])
                    if half == 0:
                        kk = t1
                        dve(lambda e: e.memset(small[:, 16:20], 0.0), [], ["small"])
                        for h in range(4):
                            act(lambda e, b=b, h=h: e.activation(out=t2[:, 128 * h:128 * h + 128], in_=PS[b][:, 128 * h:128 * h + 128],
                                                                 func=AF.Square, accum_out=small[:, 16 + h:17 + h]),
                                [("ps", b)], ["t2", "small"])
                        act(lambda e: e.activation(out=small[:, 20:24], in_=small[:, 16:20], func=AF.Sqrt, scale=1.0 / 128, bias=small[:, 1:2]),
                            ["small"], ["small"])
                        dve(lambda e: e.reciprocal(out=small[:, 24:28], in_=small[:, 20:24]), ["small"], ["small"])
                        for h in range(4):
                            dve(lambda e, b=b, h=h: e.scalar_tensor_tensor(
                                out=kk[:, 128 * h:128 * h + 128], in0=PS[b][:, 128 * h:128 * h + 128], scalar=small[:, 24 + h:25 + h],
                                in1=gmk_b[:], op0=ALU.mult, op1=ALU.mult), [("ps", b), "small", "gmk_b"], ["t1"])
                        out_toks.append(S.op("sp", lambda e, l=l, a=a: e.dma_start(out=mkp[l, 128 * a:128 * a + 128, :], in_=kk[:]),
                                             reads=["t1"], dma="o_mkp"))
                        act(lambda e: e.copy(out=sqb[:], in_=kk[:]), ["t1"], ["sqb"])
                        for h in range(4):
                            b2 = bank()
                            pb = PS[b2][:].bitcast(BF16)
                            S.op("pe", lambda e, h=h, pb=pb: e.transpose(out=pb[:, 0:128], in_=sqb[:, 128 * h:128 * h + 128], identity=ident_b[:]),
                                 reads=["sqb", "ident_b"], writes=[("ps", b2)])
                            act(lambda e, l=l, a=a, h=h, pb=pb: e.copy(out=MKT[:, l, h, 128 * a:128 * a + 128], in_=pb[:, 0:128]),
                                [("ps", b2)], ["MKT"])
                    else:
                        vv = t2
                        act(lambda e, b=b: e.copy(out=vv[:], in_=PS[b][:]), [("ps", b)], ["t2"])
                        out_toks.append(S.op("sp", lambda e, l=l, a=a: e.dma_start(out=mvp[l, 128 * a:128 * a + 128, :], in_=vv[:]),
                                             reads=["t2"], dma="o_mvp"))
                        dve(lambda e, l=l, a=a: e.tensor_copy(out=MV[:, l, a, :], in_=vv[:]), ["t2"], ["MV"])

        def load_w(dst_view, src_ap, srckey):
            ws = wslot()
            dv = dst_view(ws)
            S.op("sp", lambda e: e.dma_start(out=dv, in_=src_ap), reads=[srckey], writes=wk(ws), dma=("ws", ws))
            return ws

        def rms_rstd(N, ssb, inv_n, R):
            act(lambda e: e.activation(out=sdv[:, 0:N], in_=PS[ssb][:, 0:N], func=AF.Sqrt, scale=inv_n, bias=small[:, 1:2]),
                [("ps", ssb), "small"], ["sdv"])
            dve(lambda e: e.reciprocal(out=rstd[:, 0:N], in_=sdv[:, 0:N]), ["sdv"], ["rstd"])

        def norm_block(N, gtab):
            act(lambda e: e.activation(out=sqT[:, :, 0:N], in_=xT[:, :, 0:N], func=AF.Square), ["xT"], ["sqT"])
            b = bank()
            for k in range(8):
                S.op("pe", lambda e, k=k, b=b: e.matmul(PS[b][:, 0:N], lhsT=ones_b[:], rhs=sqT[:, k, 0:N], start=(k == 0), stop=(k == 7)),
                     reads=["sqT", "ones_b"], writes=[("ps", b)])
            rms_rstd(N, b, 1.0 / D, None)
            for k in range(8):
                dve(lambda e, k=k: e.scalar_tensor_tensor(out=hT[:, k, 0:N], in0=xT[:, k, 0:N], scalar=gtab[:, k:k + 1], in1=rstd[:, 0:N],
                                                          op0=ALU.mult, op1=ALU.mult), ["xT", "rstd", "gA", "gF"], ["hT"])

        def proj_tile(N, wv, c0, ws, b=None):
            if b is None:
                b = bank()
            for k in range(8):
                S.op("pe", lambda e, k=k, b=b: e.matmul(PS[b][:, 0:N], lhsT=wv[:, k, c0:c0 + 128], rhs=hT[:, k, 0:N],
                                                        start=(k == 0), stop=(k == 7)),
                     reads=["hT", ("ws", ws)], writes=[("ps", b)])
            return b

        def headnorm_rope(N, b, l, gcol, onesm, inv_n, rope, out_bf, out_keys, out32=None, out32_keys=()):
            H = HN[hn_i[0] % 2]
            hn_i[0] += 1
            x = H["sfx"]
            qf_, sqb_, sdv_, rstd_, qn_, t1_, t2_ = H["qf"], H["sqb"], H["sdv"], H["rstd"], H["qn"], H["t1"], H["t2"]
            act(lambda e: e.copy(out=qf_[:, 0:N], in_=PS[b][:, 0:N]), [("ps", b)], ["qf" + x])
            act(lambda e: e.activation(out=sqb_[:, 0:N], in_=qf_[:, 0:N], func=AF.Square), ["qf" + x], ["sqb" + x])
            b2 = bank()
            S.op("pe", lambda e: e.matmul(PS[b2][:, 0:N], lhsT=onesm[:], rhs=sqb_[:, 0:N], start=True, stop=True),
                 reads=["sqb" + x, "blk_b", "ones_b"], writes=[("ps", b2)])
            act(lambda e: e.activation(out=sdv_[:, 0:N], in_=PS[b2][:, 0:N], func=AF.Sqrt, scale=inv_n, bias=small[:, 1:2]),
                [("ps", b2), "small"], ["sdv"])
            dve(lambda e: e.reciprocal(out=rstd_[:, 0:N], in_=sdv_[:, 0:N]), ["sdv"], ["rstd" + x])
            if not rope:
                S.op("dve", lambda e: e.scalar_tensor_tensor(out=out_bf, in0=qf_[:, 0:N], scalar=gcol, in1=rstd_[:, 0:N],
                                                             op0=ALU.mult, op1=ALU.mult),
                     reads=["qf" + x, "rstd" + x, "gmq"], writes=out_keys)
                return
            S.op("dve", lambda e: e.scalar_tensor_tensor(out=qn_[:, 0:N], in0=qf_[:, 0:N], scalar=gcol, in1=rstd_[:, 0:N],
                                                         op0=ALU.mult, op1=ALU.mult),
                 reads=["qf" + x, "rstd" + x, "gq", "gk"], writes=["qn" + x])
            b3 = bank()
            S.op("pe", lambda e: e.matmul(PS[b3][:, 0:N], lhsT=perm_b[:], rhs=qn_[:, 0:N], start=True, stop=True),
                 reads=["qn" + x, "perm_b"], writes=[("ps", b3)])
            S.op("pool", lambda e: e.tensor_tensor(out=t1_[:, 0:N], in0=qn_[:, 0:N], in1=cosb[:, 0:N], op=ALU.mult),
                 reads=["qn" + x, "cosb"], writes=["t1"])
            dve(lambda e: e.tensor_tensor(out=t2_[:, 0:N], in0=PS[b3][:, 0:N], in1=sinb[:, 0:N], op=ALU.mult), [("ps", b3), "sinb"], ["t2"])
            if out32 is not None:
                dve(lambda e: e.tensor_tensor(out=out32, in0=t1_[:, 0:N], in1=t2_[:, 0:N], op=ALU.add), ["t1", "t2"], list(out32_keys))
                act(lambda e: e.copy(out=out_bf, in_=out32), list(out32_keys), out_keys)
            else:
                dve(lambda e: e.tensor_tensor(out=out_bf, in0=t1_[:, 0:N], in1=t2_[:, 0:N], op=ALU.add), ["t1", "t2"], out_keys)

        def cmul_add(eng, dr, di, sr, si, lr, li, lin, kdr, kdi, ksr, ksi, T, kT):
            o = lambda fn, R, W: S.op(eng, fn, reads=R, writes=W)
            o(lambda e: e.tensor_tensor(out=T[0], in0=sr, in1=lr, op=ALU.mult), ksr + ["LR"], kT[0])
            o(lambda e: e.tensor_tensor(out=T[1], in0=si, in1=lin, op=ALU.mult), ksi + ["LIn"], kT[1])
            o(lambda e: e.tensor_tensor(out=T[2], in0=si, in1=lr, op=ALU.mult), ksi + ["LR"], kT[2])
            o(lambda e: e.tensor_tensor(out=T[3], in0=sr, in1=li, op=ALU.mult), ksr + ["LI"], kT[3])
            o(lambda e: e.tensor_tensor(out=dr, in0=dr, in1=T[0], op=ALU.add), kdr + kT[0], kdr)
            o(lambda e: e.tensor_tensor(out=di, in0=di, in1=T[2], op=ALU.add), kdi + kT[2], kdi)
            o(lambda e: e.tensor_tensor(out=dr, in0=dr, in1=T[1], op=ALU.add), kdr + kT[1], kdr)
            o(lambda e: e.tensor_tensor(out=di, in0=di, in1=T[3], op=ALU.add), kdi + kT[3], kdi)

        kXS = kXSr + kXSi

        def lam_b(tab, l, lev, tp0, ntp, shape):
            return tab[:, l, lev, tp0:tp0 + ntp].rearrange("p (t o) -> p t o", o=1).to_broadcast(shape)

        def ssm_group_prompt(l, ct, N, first_block):
            for i in range(4):
                tp = 4 * ct + i
                for ri, X, kX in ((0, XSr, kXSr), (1, XSi, kXSi)):
                    b = bank()
                    S.op("pe", lambda e, tp=tp, ri=ri, b=b: e.matmul(PS[b][:, 0:N], lhsT=W2[:, l, tp, ri, :], rhs=uT[:, ct, 0:N],
                                                                    start=True, stop=True), reads=["W2", "uT"], writes=[("ps", b)])
                    act(lambda e, X=X, i=i, b=b: e.copy(out=X[:, i * TB:i * TB + N], in_=PS[b][:, 0:N]), [("ps", b)], kX[2 * i:2 * i + 2])
            Xr3 = XSr.rearrange("p (t n) -> p t n", t=4)
            Xi3 = XSi.rearrange("p (t n) -> p t n", t=4)
            parts = [("dve", 0, 4, TD)]
            nlev = int(math.log2(N))
            steps = []
            if not first_block:
                steps.append(("carry", 0))
            for lev in range(nlev):
                steps.append(("up", lev))
            for lev in range(nlev - 2, -1, -1):
                steps.append(("down", lev))
            for kind, lev in steps:
                for eng, a0, na, TT in parts:
                    kr = kXSr[2 * a0:2 * (a0 + na)]; ki = kXSi[2 * a0:2 * (a0 + na)]
                    Xr = Xr3[:, a0:a0 + na, :]; Xi = Xi3[:, a0:a0 + na, :]
                    if kind == "carry":
                        m = 1
                        dr, di = Xr[:, :, 0:1], Xi[:, :, 0:1]
                        sr = car_r[:, l, 4 * ct + a0:4 * ct + a0 + na].rearrange("p (t o) -> p t o", o=1)
                        si = car_i[:, l, 4 * ct + a0:4 * ct + a0 + na].rearrange("p (t o) -> p t o", o=1)
                        ksr = ksi = ["car"]
                    else:
                        d = 1 << lev
                        if kind == "up":
                            m = N // (2 * d)
                            Xr4 = Xr.rearrange("p t (m s) -> p t m s", s=2 * d)
                            Xi4 = Xi.rearrange("p t (m s) -> p t m s", s=2 * d)
                        else:
                            m = N // (2 * d) - 1
                            Xr4 = Xr[:, :, d:N - d].rearrange("p t (m s) -> p t m s", s=2 * d)
                            Xi4 = Xi[:, :, d:N - d].rearrange("p t (m s) -> p t m s", s=2 * d)
                        dr, di = Xr4[:, :, :, 2 * d - 1], Xi4[:, :, :, 2 * d - 1]
                        sr, si = Xr4[:, :, :, d - 1], Xi4[:, :, :, d - 1]
                        ksr, ksi = kr, ki
                    sh = [128, na, m]
                    if kind != "carry" and m >= 63:
                        for (dst, src, tab, kd_, ks_) in ((dr, sr, LR, kr, kr), (di, si, LR, ki, ki), (dr, si, LIn, kr, ki), (di, sr, LI, ki, kr)):
                            for tpi in range(na):
                                tg = 4 * ct + a0 + tpi
                                dve(lambda e, dst=dst, src=src, tab=tab, tpi=tpi, tg=tg: e.scalar_tensor_tensor(
                                    out=dst[:, tpi, :], in0=src[:, tpi, :], scalar=tab[:, l, lev, tg:tg + 1], in1=dst[:, tpi, :],
                                    op0=ALU.mult, op1=ALU.add),
                                    ks_[2 * tpi:2 * tpi + 2] + kd_[2 * tpi:2 * tpi + 2] + ["LR", "LI", "LIn"], kd_[2 * tpi:2 * tpi + 2])
                        continue
                    T = [t_[0].rearrange("p (t n) -> p t n", t=na)[:, :, 0:m] for t_ in TT]
                    kT = [t_[1] for t_ in TT]
                    cmul_add(eng, dr, di, sr, si, lam_b(LR, l, lev, 4 * ct + a0, na, sh), lam_b(LI, l, lev, 4 * ct + a0, na, sh),
                             lam_b(LIn, l, lev, 4 * ct + a0, na, sh), kr, ki, ksr, ksi, T, kT)
            dve(lambda e: e.tensor_copy(out=car_r[:, l, 4 * ct:4 * ct + 4], in_=Xr3[:, :, N - 1]), kXSr, ["car"])
            dve(lambda e: e.tensor_copy(out=car_i[:, l, 4 * ct:4 * ct + 4], in_=Xi3[:, :, N - 1]), kXSi, ["car"])
            act(lambda e: e.copy(out=xbr[:], in_=XSr), kXSr, kxbr)
            act(lambda e: e.copy(out=xbi[:], in_=XSi), kXSi, kxbi)
            return ssm_y(l, ct, N)

        def ssm_y(l, ct, N):
            b = bank()
            n = 0
            for i in range(4):
                tp = 4 * ct + i
                for ri, xb_, kx in ((0, xbr, kxbr), (1, xbi, kxbi)):
                    S.op("pe", lambda e, tp=tp, ri=ri, xb_=xb_, i=i, b=b, n=n: e.matmul(
                        PS[b][:, 0:N], lhsT=CP[:, l, tp, ri, :], rhs=xb_[:, i * TB:i * TB + N], start=(n == 0), stop=False),
                        reads=["CP"] + kx, writes=[("ps", b)])
                    n += 1
            S.op("pe", lambda e, b=b: e.matmul(PS[b][:, 0:N], lhsT=Dd[:, l, ct, :], rhs=uT[:, ct, 0:N], start=False, stop=True),
                 reads=["Dd", "uT"], writes=[("ps", b)])
            return b

        def ssm_group_sample(l, ct):
            N = NS
            for i in range(4):
                tp = 4 * ct + i
                for ri, X in ((0, XSr), (1, XSi)):
                    b = bank()
                    S.op("pe", lambda e, tp=tp, ri=ri, b=b: e.matmul(PS[b][:, 0:N], lhsT=W2[:, l, tp, ri, :], rhs=uT[:, ct, 0:N],
                                                                    start=True, stop=True), reads=["W2", "uT"], writes=[("ps", b)])
                    act(lambda e, X=X, i=i, b=b: e.copy(out=X[:, i * TB:i * TB + N], in_=PS[b][:, 0:N]), [("ps", b)], kXS)
            Xr4 = XSr.rearrange("p (t n) -> p t n", t=4)[:, :, 0:NS].rearrange("p t (b i) -> p t b i", i=4)
            Xi4 = XSi.rearrange("p (t n) -> p t n", t=4)[:, :, 0:NS].rearrange("p t (b i) -> p t b i", i=4)
            sh = [128, 4, NSB]
            T = [t_[0][:, 0:4 * NSB].rearrange("p (t n) -> p t n", t=4) for t_ in TD]
            kT = [t_[1] for t_ in TD]
            lr, li, lin = lam_b(LR, l, 0, 4 * ct, 4, sh), lam_b(LI, l, 0, 4 * ct, 4, sh), lam_b(LIn, l, 0, 4 * ct, 4, sh)
            for i in range(4):
                if i == 0:
                    sr, si, ksr, ksi = hs_r[:, 4 * ct:4 * ct + 4, :], hs_i[:, 4 * ct:4 * ct + 4, :], ["hs"], ["hs"]
                else:
                    sr, si, ksr, ksi = Xr4[:, :, :, i - 1], Xi4[:, :, :, i - 1], kXSr, kXSi
                cmul_add("dve", Xr4[:, :, :, i], Xi4[:, :, :, i], sr, si, lr, li, lin, kXSr, kXSi, ksr, ksi, T, kT)
            dve(lambda e: e.tensor_copy(out=hs_r[:, 4 * ct:4 * ct + 4, :], in_=Xr4[:, :, :, 3]), kXS, ["hs"])
            dve(lambda e: e.tensor_copy(out=hs_i[:, 4 * ct:4 * ct + 4, :], in_=Xi4[:, :, :, 3]), kXS, ["hs"])
            act(lambda e: e.copy(out=xbr[:], in_=XSr), kXS, kxbr)
            act(lambda e: e.copy(out=xbi[:], in_=XSi), kXS, kxbi)
            return ssm_y(l, ct, N)

        def layer_block(l, N, sample, blk):
            first_block = (blk == 0)
            last_block = (blk == NBLK - 1)
            wvin = wb_in[l].rearrange("(k p) c -> p k c", p=128)
            v8 = lambda ws: WS[ws][:].rearrange("p (k c) -> p k c", k=8)
            norm_block(N, gA[:, l, :])
            ws = wslot()
            wq = WS[ws][:].rearrange("p (k t two d) -> p k t two d", k=8, t=4, two=2)
            for two in range(2):
                for k in range(8):
                    S.op("sp", lambda e, two=two, k=k: e.dma_start(
                        out=wq[:, k, :, two, :], in_=wvin[:, k, 256 * two:256 * two + 256].rearrange("p (t d) -> p t d", d=64)),
                        reads=[("wb_in", l)], writes=wk(ws), dma=("ws", ws))
            wqv = v8(ws)
            for t in range(4):
                b = proj_tile(N, wqv, 128 * t, ws)
                headnorm_rope(N, b, l, gq[:, l:l + 1], blk_b, 1.0 / 64, True, qr[:, t, 0:N], ["qr"])
            ws = load_w(lambda w: v8(w)[:, :, 0:256], wvin[:, :, K_OFF:K_OFF + 256], ("wb_in", l))
            wkv_ = v8(ws)
            b = proj_tile(N, wkv_, 0, ws)
            kdst = kTc[:, l, 128:128 + N] if not sample else kTc[:, l, 0:N]
            headnorm_rope(N, b, l, gk[:, l:l + 1], blk_b, 1.0 / 64, True, kdst, ["kTc"], out32=k32[:, 0:N], out32_keys=["k32"])
            nsub = max(1, N // 128)
            pn = min(N, 128)
            b = bank()
            for s in range(nsub):
                for k in range(8):
                    S.op("pe", lambda e, s=s, k=k, b=b: e.matmul(PS[b][0:pn, 128 * s:128 * s + 128], lhsT=hT[:, k, 128 * s:128 * s + pn],
                                                                rhs=wkv_[:, k, 128:256], start=(k == 0), stop=(k == 7)),
                         reads=["hT", ("ws", ws)], writes=[("ps", b)])
            act(lambda e, b=b: e.copy(out=v32[0:pn, 0:nsub, :], in_=PS[b][0:pn, 0:128 * nsub].rearrange("p (s c) -> p s c", c=128)),
                [("ps", b)], ["v32"])
            vdst = vtc[0:pn, l, 1:1 + nsub, :] if not sample else vtc[0:pn, l, 0:1, :]
            dve(lambda e: e.tensor_copy(out=vdst, in_=v32[0:pn, 0:nsub, :]), ["v32"], ["vtc"])
            if not sample:
                for s in range(nsub):
                    for h in range(2):
                        hs = slice(64 * h, 64 * h + 64)
                        pt = PT[(2 * s + h) % 2]; kpt = "PT%d" % ((2 * s + h) % 2)
                        use_prev = not (first_block and s == 0)
                        parts = ([0] if use_prev else []) + [1]
                        sbk = {}
                        for part in parts:
                            b = bank(); sbk[part] = b
                            c0 = 128 * s + 128 * part
                            S.op("pe", lambda e, b=b, c0=c0, hs=hs, s=s: e.matmul(
                                PS[b][:].rearrange("p (t c) -> p t c", t=4), lhsT=kTc[hs, l, c0:c0 + 128],
                                rhs=qr[hs, :, 128 * s:128 * s + 128], start=True, stop=True),
                                reads=["kTc", "qr"], writes=[("ps", b)])
                            act(lambda e, b=b, part=part, pt=pt: e.activation(out=pt[:, part, :], in_=PS[b][:], func=AF.Exp, scale=0.125),
                                [("ps", b)], [kpt])
                            mk_ = mprev_b if part == 0 else mcur_b
                            S.op("pool", lambda e, part=part, pt=pt, mk_=mk_: e.tensor_tensor(
                                out=pt[:, part, :].rearrange("p (t c) -> p t c", t=4), in0=pt[:, part, :].rearrange("p (t c) -> p t c", t=4),
                                in1=mk_[:].rearrange("p (o c) -> p o c", o=1).to_broadcast([128, 4, 128]), op=ALU.mult),
                                reads=[kpt, "mprev_b", "mcur_b"], writes=[kpt])
                        bo = bank(); bd = bank()
                        for n_, part in enumerate(parts):
                            S.op("pe", lambda e, part=part, n_=n_, bo=bo, pt=pt, s=s, hs=hs: e.matmul(
                                PS[bo][hs, :], lhsT=vtc[:, l, s + part, hs], rhs=pt[:, part, :], start=(n_ == 0), stop=(n_ == len(parts) - 1)),
                                reads=["vtc", kpt], writes=[("ps", bo)])
                        for n_, part in enumerate(parts):
                            S.op("pe", lambda e, part=part, n_=n_, bd=bd, pt=pt, hs=hs: e.matmul(
                                PS[bd][hs, :], lhsT=ones_b[:, hs], rhs=pt[:, part, :], start=(n_ == 0), stop=(n_ == len(parts) - 1)),
                                reads=["ones_b", kpt], writes=[("ps", bd)])
                        dd = dn[h]; kd = "dn%d" % h
                        dve(lambda e, bd=bd, dd=dd, hs=hs: e.tensor_tensor(
                            out=dd[hs, :].rearrange("p (t c) -> p t c", t=4), in0=PS[bd][hs, :].rearrange("p (t c) -> p t c", t=4),
                            in1=esink[hs, l, :].rearrange("p (t o) -> p t o", o=1).to_broadcast([64, 4, 128]), op=ALU.add),
                            [("ps", bd), "esink"], [kd])
                        dve(lambda e, dd=dd, hs=hs: e.reciprocal(out=dd[hs, :], in_=dd[hs, :]), [kd], [kd])
                        dve(lambda e, bo=bo, dd=dd, hs=hs, s=s: e.tensor_tensor(
                            out=oa[hs, :, 128 * s:128 * s + 128], in0=PS[bo][hs, :].rearrange("p (t c) -> p t c", t=4),
                            in1=dd[hs, :].rearrange("p (t c) -> p t c", t=4), op=ALU.mult), [("ps", bo), kd], ["oa"])
                if last_block:
                    b = bank()
                    S.op("pe", lambda e, b=b: e.transpose(out=PS[b][:, 0:128], in_=k32[:, N - 128:N], identity=ident_f[:]),
                         reads=["k32", "ident_f"], writes=[("ps", b)])
                    act(lambda e, b=b: e.copy(out=t1[:, 0:128], in_=PS[b][:, 0:128]), [("ps", b)], ["t1"])
                    out_toks.append(S.op("sp", lambda e: e.dma_start(out=kp[l], in_=t1[:, 0:128]), reads=["t1"], dma="o_kp"))
                    out_toks.append(S.op("sp", lambda e: e.dma_start(out=vp[l], in_=v32[:, 3, :]), reads=["v32"], dma="o_vp"))
                else:
                    act(lambda e: e.copy(out=kTc[:, l, 0:128], in_=kTc[:, l, N:N + 128]), ["kTc"], ["kTc"])
                    act(lambda e: e.copy(out=vtc[:, l, 0, :], in_=vtc[:, l, 4, :]), ["vtc"], ["vtc"])
            else:
                b = bank()
                S.op("pe", lambda e, b=b: e.transpose(out=PS[b][0:NS, 0:128], in_=k32[:, 0:NS], identity=ident_f[:]),
                     reads=["k32", "ident_f"], writes=[("ps", b)])
                act(lambda e, b=b: e.copy(out=t1[0:NS, 0:128], in_=PS[b][0:NS, 0:128]), [("ps", b)], ["t1"])
                out_toks.append(S.op("sp", lambda e: e.dma_start(out=ks[l, :, 124:128, :], in_=t1[0:NS, 0:128]), reads=["t1"], dma="o_ks"))
                out_toks.append(S.op("sp", lambda e: e.dma_start(out=vs[l, :, 124:128, :], in_=v32[0:NS, 0, :]), reads=["v32"], dma="o_vs"))
                out_toks.append(S.op("sp", lambda e: e.dma_start(out=ks[l, :, 0:124, :], in_=csk[l, :, 4:128, :]), dma="o_ks"))
                out_toks.append(S.op("sp", lambda e: e.dma_start(out=vs[l, :, 0:124, :], in_=csv[l, :, 4:128, :]), dma="o_vs"))
                ptn = PT[0]; ptc = PT[1]
                for h in range(2):
                    hs = slice(64 * h, 64 * h + 64)
                    b = bank()
                    S.op("pe", lambda e, b=b, hs=hs: e.matmul(PS[b][0:NS, 0:4 * NS].rearrange("p (t c) -> p t c", t=4),
                                                             lhsT=kTc[hs, l, 0:NS], rhs=qr[hs, :, 0:NS], start=True, stop=True),
                         reads=["kTc", "qr"], writes=[("ps", b)])
                    act(lambda e, b=b, h=h: e.activation(out=ptn[0:NS, h, 0:4 * NS], in_=PS[b][0:NS, 0:4 * NS], func=AF.Exp, scale=0.125),
                        [("ps", b)], ["PT0"])
                    dve(lambda e, h=h: e.tensor_tensor(
                        out=ptn[0:NS, h, 0:4 * NS].rearrange("p (t c) -> p t c", t=4), in0=ptn[0:NS, h, 0:4 * NS].rearrange("p (t c) -> p t c", t=4),
                        in1=mnew_b[:].rearrange("p (o c) -> p o c", o=1).to_broadcast([NS, 4, NS]), op=ALU.mult), ["PT0", "mnew_b"], ["PT0"])
                bsc = bank(); held.add(bsc)
                for bb in range(NSB):
                    kst = sg[bb % 2]; kk_ = "sg%d" % (bb % 2)
                    S.op("pool", lambda e, bb=bb, kst=kst: e.dma_start(out=kst[:, 0:128], in_=csk[l, bb]), writes=[kk_], dma=kk_)
                    S.op("pool", lambda e, bb=bb, kst=kst: e.dma_start(out=kst[:, 128:256], in_=csv[l, bb]), writes=[kk_], dma=kk_)
                    b = bank()
                    pb = PS[b][:].bitcast(BF16)
                    S.op("pe", lambda e, kst=kst, pb=pb: e.transpose(out=pb[:, 0:128], in_=kst[:, 0:128], identity=ident_b[:]),
                         reads=[kk_, "ident_b"], writes=[("ps", b)])
                    act(lambda e, pb=pb, bb=bb: e.copy(out=kcT[:, bb % 2, :], in_=pb[:, 0:128]), [("ps", b)], ["kcT%d" % (bb % 2)])
                    for h in range(2):
                        hs = slice(64 * h, 64 * h + 64)
                        c0 = (bb * 2 + h) * 16
                        S.op("pe", lambda e, bb=bb, hs=hs, c0=c0: e.matmul(
                            PS[bsc][:, c0:c0 + 16].rearrange("p (t i) -> p t i", t=4), lhsT=kcT[hs, bb % 2, :],
                            rhs=qr[hs, :, 4 * bb:4 * bb + 4], start=True, stop=True),
                            reads=["kcT%d" % (bb % 2), "qr"], writes=[("ps", bsc)])
                    dve(lambda e, bb=bb, kst=kst: e.tensor_copy(out=RA[:, 128 * bb:128 * bb + 128], in_=kst[:, 128:256]), [kk_], RAK[0:4])
                act(lambda e: e.activation(out=ptc[:, 0, :], in_=PS[bsc][:], func=AF.Exp, scale=0.125), [("ps", bsc)], ["PT1"])
                held.discard(bsc)
                dve(lambda e: e.tensor_tensor(
                    out=ptc[:, 0, :].rearrange("p (a i) -> p a i", i=4), in0=ptc[:, 0, :].rearrange("p (a i) -> p a i", i=4),
                    in1=mc_b[:].rearrange("p (o i) -> p o i", o=1).to_broadcast([128, 128, 4]), op=ALU.mult), ["PT1", "mc_b"], ["PT1"])
                for h in range(2):
                    hs = slice(64 * h, 64 * h + 64)
                    for which, lw in ((0, None), (1, None)):
                        bo = bank()
                        lhs_new = vtc[0:NS, l, 0, hs] if which == 0 else ones_b[0:NS, hs]
                        S.op("pe", lambda e, bo=bo, hs=hs, h=h, lhs_new=lhs_new: e.matmul(
                            PS[bo][hs, 0:4 * NS], lhsT=lhs_new, rhs=ptn[0:NS, h, 0:4 * NS], start=True, stop=False),
                            reads=["vtc", "ones_b", "PT0"], writes=[("ps", bo)])
                        for bb in range(NSB):
                            c0 = (bb * 2 + h) * 16
                            lhs_c = RA[:, 128 * bb + 64 * h:128 * bb + 64 * h + 64] if which == 0 else ones_b[:, hs]
                            S.op("pe", lambda e, bo=bo, hs=hs, bb=bb, c0=c0, lhs_c=lhs_c: e.matmul(
                                PS[bo][hs, 0:4 * NS].rearrange("p (t c) -> p t c", t=4)[:, :, 4 * bb:4 * bb + 4], lhsT=lhs_c,
                                rhs=ptc[:, 0, c0:c0 + 16].rearrange("p (t i) -> p t i", t=4), start=False, stop=(bb == NSB - 1)),
                                reads=RAK[0:4] + ["ones_b", "PT1"], writes=[("ps", bo)])
                        if which == 0:
                            bnum = bo
                        else:
                            bden = bo
                    dd = dn[h]; kd = "dn%d" % h
                    dve(lambda e, bden=bden, dd=dd, hs=hs: e.tensor_tensor(
                        out=dd[hs, 0:4 * NS].rearrange("p (t c) -> p t c", t=4), in0=PS[bden][hs, 0:4 * NS].rearrange("p (t c) -> p t c", t=4),
                        in1=esink[hs, l, :].rearrange("p (t o) -> p t o", o=1).to_broadcast([64, 4, NS]), op=ALU.add),
                        [("ps", bden), "esink"], [kd])
                    dve(lambda e, dd=dd, hs=hs: e.reciprocal(out=dd[hs, 0:4 * NS], in_=dd[hs, 0:4 * NS]), [kd], [kd])
                    dve(lambda e, bnum=bnum, dd=dd, hs=hs: e.tensor_tensor(
                        out=oa[hs, :, 0:NS], in0=PS[bnum][hs, 0:4 * NS].rearrange("p (t c) -> p t c", t=4),
                        in1=dd[hs, 0:4 * NS].rearrange("p (t c) -> p t c", t=4), op=ALU.mult), [("ps", bnum), kd], ["oa"])
            ws = load_w(lambda w: v8(w), wvin[:, :, U_OFF:U_OFF + 512], ("wb_in", l))
            for t in range(4):
                b = proj_tile(N, v8(ws), 128 * t, ws)
                act(lambda e, b=b, t=t: e.copy(out=uT[:, t, 0:N], in_=PS[b][:, 0:N]), [("ps", b)], ["uT"])
            if sample:
                for (src_, dst_, stg, kst) in ((sre, hs_r, t1, "t1"), (sim, hs_i, t2, "t2")):
                    stv = stg[:].rearrange("p (a q) -> p a q", a=4)
                    s2 = src_[l].rearrange("b g p -> (b g) p").rearrange("(a q) p -> q a p", q=128)
                    for dup in range(2):
                        S.op("sp", lambda e, dup=dup, stv=stv, s2=s2: e.dma_start(out=stv[:, :, 64 * dup:64 * dup + 64], in_=s2),
                             writes=[kst], dma=kst)
                    for a in range(4):
                        b = bank()
                        S.op("pe", lambda e, a=a, b=b, stv=stv: e.transpose(out=PS[b][:, 0:128], in_=stv[:, a, :], identity=ident_f[:]),
                             reads=[kst, "ident_f"], writes=[("ps", b)])
                        for gl in range(2):
                            act(lambda e, a=a, b=b, gl=gl, dst_=dst_: e.copy(
                                out=dst_[64 * gl:64 * gl + 64, :, 4 * a:4 * a + 4],
                                in_=PS[b][64 * gl:64 * gl + 64, 0:128].rearrange("p (b tp gl) -> p gl tp b", gl=2, tp=16)[:, gl]),
                                [("ps", b)], ["hs"])
            for ct in range(4):
                by = ssm_group_sample(l, ct) if sample else ssm_group_prompt(l, ct, N, first_block)
                act(lambda e, by=by, ct=ct: e.activation(out=zT[:, ct, 0:N], in_=PS[by][:, 0:N], func=AF.Gelu), [("ps", by)], ["zT"])
            if sample:
                for (dram_, buf_, stg, kst) in ((hrs, hs_r, t1, "t1"), (his, hs_i, t2, "t2")):
                    for a in range(4):
                        act(lambda e, a=a, buf_=buf_: e.copy(out=qf[:, 0:64].rearrange("p (b tp) -> p b tp", b=4),
                                                             in_=buf_[:, :, 4 * a:4 * a + 4].rearrange("p tp b -> p b tp")), ["hs"], ["qf"])
                        b = bank()
                        S.op("pe", lambda e, b=b: e.transpose(out=PS[b][0:64, 0:128], in_=qf[:, 0:64], identity=ident_f[:]),
                             reads=["qf", "ident_f"], writes=[("ps", b)])
                        act(lambda e, a=a, b=b, stg=stg: e.copy(out=stg[0:64, 128 * a:128 * a + 128], in_=PS[b][0:64, 0:128]), [("ps", b)], [kst])
                    out_toks.append(S.op("sp", lambda e, dram_=dram_, stg=stg: e.dma_start(
                        out=dram_[l].rearrange("b (tp gl) p -> (b tp) (gl p)", gl=2).rearrange("(a r) c -> r a c", a=4),
                        in_=stg[0:64, :].rearrange("p (a c) -> p a c", a=4)), reads=[kst], dma="o_hs"))
            elif last_block:
                for gl in range(2):
                    sl = slice(64 * gl, 64 * gl + 64)
                    out_toks.append(S.op("sp", lambda e, gl=gl, sl=sl: e.dma_start(
                        out=hrp[l].rearrange("(tp gl) p -> gl p tp", gl=2)[gl], in_=car_r[sl, l, :], allow_slow_non_contiguous=True),
                        reads=["car"], dma="o_hp"))
                    out_toks.append(S.op("sp", lambda e, gl=gl, sl=sl: e.dma_start(
                        out=hip[l].rearrange("(tp gl) p -> gl p tp", gl=2)[gl], in_=car_i[sl, l, :], allow_slow_non_contiguous=True),
                        reads=["car"], dma="o_hp"))
            ws = load_w(lambda w: WS[w][:, 0:2048].rearrange("p (k c) -> p k c", k=4), wb_glu[l].rearrange("(k p) c -> p k c", p=128),
                        ("wb_glu", l))
            wg = WS[ws][:, 0:2048].rearrange("p (k c) -> p k c", k=4)
            for t in range(4):
                b = bank()
                for k in range(4):
                    S.op("pe", lambda e, b=b, k=k, t=t: e.matmul(PS[b][:, 0:N], lhsT=wg[:, k, 128 * t:128 * t + 128], rhs=zT[:, k, 0:N],
                                                                start=(k == 0), stop=(k == 3)), reads=["zT", ("ws", ws)], writes=[("ps", b)])
                s_ = sg[t % 2]; ks_ = "sg%d" % (t % 2)
                act(lambda e, b=b, s_=s_: e.activation(out=s_[:, 0:N], in_=PS[b][:, 0:N], func=AF.Sigmoid), [("ps", b)], [ks_])
                dve(lambda e, t=t, s_=s_: e.tensor_tensor(out=ob[:, t, 0:N], in0=zT[:, t, 0:N], in1=s_[:, 0:N], op=ALU.mult), ["zT", ks_], ["ob"])
            ws = load_w(lambda w: v8(w), wvin[:, :, MQ_OFF:MQ_OFF + 512], ("wb_in", l))
            for t in range(4):
                b = proj_tile(N, v8(ws), 128 * t, ws)
                headnorm_rope(N, b, l, gmq[:, l:l + 1], ones_b, 1.0 / 128, False, qmn[:, t, 0:N], ["qmn"])
            sc_m = 1.0 / math.sqrt(128.0)
            if not sample:
                for h in range(4):
                    pt = PT[h % 2]; kpt = "PT%d" % (h % 2)
                    for kt in range(2):
                        b = bank()
                        S.op("pe", lambda e, b=b, h=h, kt=kt: e.matmul(PS[b][:, 0:N], lhsT=MKT[:, l, h, 128 * kt:128 * kt + 128], rhs=qmn[:, h, 0:N],
                                                                      start=True, stop=True), reads=["MKT", "qmn"], writes=[("ps", b)])
                        act(lambda e, b=b, kt=kt, pt=pt: e.activation(out=pt[:, kt, 0:N], in_=PS[b][:, 0:N], func=AF.Exp, scale=sc_m),
                            [("ps", b)], [kpt])
                    bo = bank(); bd = bank()
                    for kt in range(2):
                        S.op("pe", lambda e, bo=bo, h=h, kt=kt, pt=pt: e.matmul(PS[bo][:, 0:N], lhsT=MV[:, l, kt, 128 * h:128 * h + 128],
                                                                               rhs=pt[:, kt, 0:N], start=(kt == 0), stop=(kt == 1)),
                             reads=["MV", kpt], writes=[("ps", bo)])
                    for kt in range(2):
                        S.op("pe", lambda e, bd=bd, kt=kt, pt=pt: e.matmul(PS[bd][:, 0:N], lhsT=ones_b[:], rhs=pt[:, kt, 0:N],
                                                                          start=(kt == 0), stop=(kt == 1)),
                             reads=["ones_b", kpt], writes=[("ps", bd)])
                    dd = dn[h % 2]; kd = "dn%d" % (h % 2)
                    dve(lambda e, bd=bd, dd=dd: e.reciprocal(out=dd[:, 0:N], in_=PS[bd][:, 0:N]), [("ps", bd)], [kd])
                    dve(lambda e, bo=bo, dd=dd, h=h: e.tensor_tensor(out=oc[:, h, 0:N], in0=PS[bo][:, 0:N], in1=dd[:, 0:N], op=ALU.mult),
                        [("ps", bo), kd], ["oc"])
            else:
                bsc = bank(); held.add(bsc)
                bo = bank(); held.add(bo)
                ptc = PT[1]
                for bb in range(NSB):
                    wsk = wslot()
                    kcb = WS[wsk][:, 0:1024].rearrange("p (a c) -> p a c", a=2)
                    vcb = WS[wsk][:, 1024:2048].rearrange("p (a c) -> p a c", a=2)
                    S.op("pool", lambda e, bb=bb, kcb=kcb: e.dma_start(out=kcb, in_=cmk[l, bb].rearrange("(a p) c -> p a c", p=128)),
                         writes=wk(wsk), dma=("ws", wsk))
                    S.op("pool", lambda e, bb=bb, vcb=vcb: e.dma_start(out=vcb, in_=cmv[l, bb].rearrange("(a p) c -> p a c", p=128)),
                         writes=wk(wsk), dma=("ws", wsk))
                    for h in range(4):
                        for kt in range(2):
                            b = bank()
                            pb = PS[b][:].bitcast(BF16)
                            o0 = 2048 + 256 * h + 128 * kt
                            S.op("pe", lambda e, pb=pb, kcb=kcb, h=h, kt=kt: e.transpose(out=pb[:, 0:128], in_=kcb[:, kt, 128 * h:128 * h + 128],
                                                                                        identity=ident_b[:]),
                                 reads=[("ws", wsk), "ident_b"], writes=[("ps", b)])
                            act(lambda e, pb=pb, wsk=wsk, o0=o0: e.copy(out=WS[wsk][:, o0:o0 + 128], in_=pb[:, 0:128]),
                                [("ps", b)], [("wsT", wsk)])
                    for h in range(4):
                        for kt in range(2):
                            c0 = ((bb * 4 + h) * 2 + kt) * 4
                            o0 = 2048 + 256 * h + 128 * kt
                            S.op("pe", lambda e, h=h, c0=c0, bb=bb, wsk=wsk, o0=o0: e.matmul(
                                PS[bsc][:, c0:c0 + 4], lhsT=WS[wsk][:, o0:o0 + 128],
                                rhs=qmn[:, h, 4 * bb:4 * bb + 4], start=True, stop=True),
                                reads=[("wsT", wsk), "qmn"], writes=[("ps", bsc)])
                    c0 = bb * 32
                    act(lambda e, c0=c0: e.activation(out=ptc[:, 0, c0:c0 + 32], in_=PS[bsc][:, c0:c0 + 32], func=AF.Exp, scale=sc_m),
                        [("ps", bsc)], ["PT1"])
                    for h in range(4):
                        for kt in range(2):
                            c1 = ((bb * 4 + h) * 2 + kt) * 4
                            S.op("pe", lambda e, h=h, kt=kt, c1=c1, bb=bb, vcb=vcb: e.matmul(
                                PS[bo][:, h * NS + 4 * bb:h * NS + 4 * bb + 4], lhsT=vcb[:, kt, 128 * h:128 * h + 128],
                                rhs=ptc[:, 0, c1:c1 + 4], start=(kt == 0), stop=(kt == 1)),
                                reads=[("ws", wsk), "PT1"], writes=[("ps", bo)])
                bd = bank()
                pv = ptc[:, 0, :].rearrange("p (b h kt i) -> p h kt b i", b=NSB, h=4, kt=2)
                for h in range(4):
                    for kt in range(2):
                        S.op("pe", lambda e, bd=bd, kt=kt, h=h: e.matmul(PS[bd][:, h * NS:(h + 1) * NS].rearrange("p (b i) -> p b i", i=4),
                                                                        lhsT=ones_b[:], rhs=pv[:, h, kt, :, :], start=(kt == 0), stop=(kt == 1)),
                             reads=["ones_b", "PT1"], writes=[("ps", bd)])
                dd = dn[0]
                dve(lambda e, bd=bd: e.reciprocal(out=dd[:, 0:4 * NS], in_=PS[bd][:, 0:4 * NS]), [("ps", bd)], ["dn0"])
                dve(lambda e, bo=bo: e.tensor_tensor(out=oc[:, :, 0:NS], in0=PS[bo][:, 0:4 * NS].rearrange("p (h c) -> p h c", h=4),
                                                     in1=dd[:, 0:4 * NS].rearrange("p (h c) -> p h c", h=4), op=ALU.mult),
                    [("ps", bo), "dn0"], ["oc"])
                held.discard(bsc); held.discard(bo)
            macc3 = macc.rearrange("p (m n) -> p m n", m=8)
            mgT3 = mgT.rearrange("p (m n) -> p m n", m=8)
            for n_, on_ in enumerate((oa, ob, oc)):
                okey = ("oa", "ob", "oc")[n_]
                wsb = wslot()
                wbv = WS[wsb][:].rearrange("p (k c) -> p k c", k=4)
                if n_ == 0:
                    for t in range(4):
                        for two in range(2):
                            r0 = (two * 4 + t) * 64
                            S.op("sp", lambda e, t=t, two=two, r0=r0: e.dma_start(out=wbv[64 * two:64 * two + 64, t, :], in_=wb_br[l, 0, r0:r0 + 64, :]),
                                 reads=[("wb_br", l)], writes=wk(wsb), dma=("ws", wsb))
                else:
                    S.op("sp", lambda e, n_=n_: e.dma_start(out=wbv, in_=wb_br[l, n_].rearrange("(k p) c -> p k c", p=128)),
                         reads=[("wb_br", l)], writes=wk(wsb), dma=("ws", wsb))
                for half in range(2):
                    wsg = load_w(lambda w: v8(w), wvin[:, :, G_OFF + n_ * 1024 + 512 * half:G_OFF + n_ * 1024 + 512 * half + 512], ("wb_in", l))
                    for mm_ in range(4):
                        m = 4 * half + mm_
                        bg = proj_tile(N, v8(wsg), 128 * mm_, wsg)
                        s_ = sg[m % 2]; ks_ = "sg%d" % (m % 2)
                        act(lambda e, bg=bg, s_=s_: e.activation(out=s_[:, 0:N], in_=PS[bg][:, 0:N], func=AF.Sigmoid), [("ps", bg)], [ks_])
                        bp = bank()
                        for k in range(4):
                            S.op("pe", lambda e, bp=bp, k=k, m=m, on_=on_: e.matmul(PS[bp][:, 0:N], lhsT=wbv[:, k, 128 * m:128 * m + 128],
                                                                                  rhs=on_[:, k, 0:N], start=(k == 0), stop=(k == 3)),
                                 reads=[okey, ("ws", wsb)], writes=[("ps", bp)])
                        if n_ == 0:
                            dve(lambda e, bp=bp, s_=s_, m=m: e.tensor_tensor(out=macc3[:, m, 0:N], in0=PS[bp][:, 0:N], in1=s_[:, 0:N], op=ALU.mult),
                                [("ps", bp), ks_], kmacc)
                        else:
                            dve(lambda e, bp=bp, s_=s_: e.tensor_tensor(out=t1[:, 0:N], in0=PS[bp][:, 0:N], in1=s_[:, 0:N], op=ALU.mult),
                                [("ps", bp), ks_], ["t1"])
                            if n_ == 1:
                                dve(lambda e, m=m: e.tensor_tensor(out=macc3[:, m, 0:N], in0=macc3[:, m, 0:N], in1=t1[:, 0:N], op=ALU.add),
                                    kmacc + ["t1"], kmacc)
                            else:
                                dve(lambda e, m=m: e.tensor_tensor(out=mgT3[:, m, 0:N], in0=macc3[:, m, 0:N], in1=t1[:, 0:N], op=ALU.add),
                                    kmacc + ["t1"], kmgT)
            for half in range(2):
                wso = load_w(lambda w: v8(w), wb_out[l].rearrange("(k p) c -> p k c", p=128)[:, :, 512 * half:512 * half + 512], ("wb_out", l))
                for mm_ in range(4):
                    m = 4 * half + mm_
                    b = bank()
                    for k in range(8):
                        S.op("pe", lambda e, b=b, k=k, mm_=mm_, wso=wso: e.matmul(PS[b][:, 0:N], lhsT=v8(wso)[:, k, 128 * mm_:128 * mm_ + 128],
                                                                                 rhs=mgT3[:, k, 0:N], start=(k == 0), stop=(k == 7)),
                             reads=kmgT + [("ws", wso)], writes=[("ps", b)])
                    dve(lambda e, b=b, m=m: e.tensor_tensor(out=xT[:, m, 0:N], in0=xT[:, m, 0:N], in1=PS[b][:, 0:N], op=ALU.add),
                        [("ps", b), "xT"], ["xT"])
            norm_block(N, gF[:, l, :])
            wvup = wb_up[l].rearrange("(k p) c -> p k c", p=128)

            def actT(j):
                return RA[:, j * TB:(j + 1) * TB], [RAK[j]]

            for grp in range(6):
                nt = 4 if grp < 5 else 2
                wsg = load_w(lambda w: v8(w)[:, :, 0:128 * nt], wvup[:, :, 512 * grp:512 * grp + 128 * nt], ("wb_up", l))
                wsu = load_w(lambda w: v8(w)[:, :, 0:128 * nt], wvup[:, :, DFF + 512 * grp:DFF + 512 * grp + 128 * nt], ("wb_up", l))
                for jj in range(nt):
                    j = 4 * grp + jj
                    bg = proj_tile(N, v8(wsg), 128 * jj, wsg)
                    bu = proj_tile(N, v8(wsu), 128 * jj, wsu)
                    s_ = sg[j % 2]; ks_ = "sg%d" % (j % 2)
                    act(lambda e, bg=bg, s_=s_: e.activation(out=s_[:, 0:N], in_=PS[bg][:, 0:N], func=AF.Silu), [("ps", bg)], [ks_])
                    av, ak = actT(j)
                    dve(lambda e, bu=bu, s_=s_, av=av: e.tensor_tensor(out=av[:, 0:N], in0=PS[bu][:, 0:N], in1=s_[:, 0:N], op=ALU.mult),
                        [("ps", bu), ks_], ak)
            wvdn = wb_dn[l].rearrange("(k p) c -> p k c", p=128)
            for q4 in range(4):
                wsl = []
                for hh in range(2):
                    w_ = wslot()
                    S.op("sp", lambda e, w_=w_, hh=hh, q4=q4: e.dma_start(
                        out=WS[w_][:, 0:11 * 256].rearrange("p (k c) -> p k c", k=11), in_=wvdn[:, 11 * hh:11 * hh + 11, 256 * q4:256 * q4 + 256]),
                        reads=[("wb_dn", l)], writes=wk(w_), dma=("ws", w_))
                    wsl.append(w_)
                for mm_ in range(2):
                    m = 2 * q4 + mm_
                    b = bank()
                    for j in range(22):
                        w_ = wsl[j // 11]
                        wv_ = WS[w_][:, 0:11 * 256].rearrange("p (k c) -> p k c", k=11)
                        av, ak = actT(j)
                        S.op("pe", lambda e, b=b, j=j, mm_=mm_, wv_=wv_, av=av: e.matmul(
                            PS[b][:, 0:N], lhsT=wv_[:, j % 11, 128 * mm_:128 * mm_ + 128], rhs=av[:, 0:N], start=(j == 0), stop=(j == 21)),
                            reads=ak + [("ws", w_)], writes=[("ps", b)])
                    dve(lambda e, b=b, m=m: e.tensor_tensor(out=xT[:, m, 0:N], in0=xT[:, m, 0:N], in1=PS[b][:, 0:N], op=ALU.add),
                        [("ps", b), "xT"], ["xT"])


        def run_block(blk, sample):
            N = NS if sample else TB
            nsub = max(1, N // 128)
            pn = min(N, 128)
            if sample:
                S.op("sp", lambda e: e.dma_start(out=xtok[0:NS, 0, :], in_=xs), writes=["xtok"], dma="xtok")
                S.op("sp", lambda e: e.dma_start(out=cosb[:, 0:NS], in_=c_cos_s), writes=["cosb"], dma="cosb")
                S.op("sp", lambda e: e.dma_start(out=sinb[:, 0:NS], in_=c_sin_s), writes=["sinb"], dma="sinb")
            else:
                t0 = blk * TB
                S.op("sp", lambda e: e.dma_start(out=xtok[:], in_=xp[t0:t0 + TB, :].rearrange("(s p) d -> p s d", p=128)),
                     writes=["xtok"], dma="xtok")
                S.op("sp", lambda e: e.dma_start(out=cosb[:], in_=c_cos[:, t0:t0 + TB]), writes=["cosb"], dma="cosb")
                S.op("sp", lambda e: e.dma_start(out=sinb[:], in_=c_sin[:, t0:t0 + TB]), writes=["sinb"], dma="sinb")
            for s in range(nsub):
                for k in range(8):
                    b = bank()
                    S.op("pe", lambda e, s=s, k=k, b=b: e.transpose(out=PS[b][:, 0:pn], in_=xtok[0:pn, s, 128 * k:128 * k + 128],
                                                                    identity=ident_f[0:pn, 0:pn]),
                         reads=["xtok", "ident_f"], writes=[("ps", b)])
                    act(lambda e, s=s, k=k, b=b: e.copy(out=xT[:, k, 128 * s:128 * s + pn], in_=PS[b][:, 0:pn]), [("ps", b)], ["xT"])
            for l in range(NL):
                layer_block(l, N, sample, blk)
            for s in range(nsub):
                for k in range(8):
                    b = bank()
                    S.op("pe", lambda e, s=s, k=k, b=b: e.transpose(out=PS[b][0:pn, 0:128], in_=xT[:, k, 128 * s:128 * s + pn],
                                                                    identity=ident_f[:]),
                         reads=["xT", "ident_f"], writes=[("ps", b)])
                    act(lambda e, s=s, k=k, b=b: e.copy(out=xtok[0:pn, s, 128 * k:128 * k + 128], in_=PS[b][0:pn, 0:128]),
                        [("ps", b)], ["xtok"])
            if sample:
                out_toks.append(S.op("sp", lambda e: e.dma_start(out=ys, in_=xtok[0:NS, 0, :]), reads=["xtok"], dma="o_y"))
            else:
                t0 = blk * TB
                out_toks.append(S.op("sp", lambda e: e.dma_start(out=yp[t0:t0 + TB, :].rearrange("(s p) d -> p s d", p=128), in_=xtok[:]),
                                     reads=["xtok"], dma="o_y"))

        for blk in range(NBLK):
            run_block(blk, False)
        run_block(0, True)
        S.wait_all("sp", out_toks)
        with nc.allow_non_contiguous_dma(reason="small strided parameter / state transfers"):
            S.emit()
    return nc


_NC_CACHE = {}


def _consts():
    c = {}
    c["c_ident"] = np.eye(128, dtype=np.float32)
    blk = np.zeros((128, 128), np.float32)
    blk[:64, :64] = 1.0
    blk[64:, 64:] = 1.0
    c["c_blk"] = blk
    p = np.arange(128)
    lo = (p % 64) < 32
    partner = np.where(lo, p + 32, p - 32)
    perm = np.zeros((128, 128), np.float32)
    perm[partner, p] = 1.0
    c["c_perm"] = perm
    j = np.arange(128)[:, None]
    i = np.arange(128)[None, :]
    c["c_mprev"] = (j > i).astype(np.float32)
    c["c_mcur"] = (j <= i).astype(np.float32)
    half = 32
    inv = (np.float32(10000.0) ** (-(np.arange(half, dtype=np.float32) / np.float32(half)))).astype(np.float32)
    invp = inv[p % 32]
    sign = np.where(lo, -1.0, 1.0).astype(np.float32)

    def tables(pos):
        ang = (pos[None, :].astype(np.float32) * invp[:, None]).astype(np.float32)
        return np.cos(ang).astype(np.float32), (np.sin(ang) * sign[:, None]).astype(np.float32)

    c["c_cos"], c["c_sin"] = tables(np.arange(SEQ, dtype=np.float32))
    pos_s = np.float32(PAST) + np.tile(np.arange(4, dtype=np.float32), NSB)
    c["c_cos_s"], c["c_sin_s"] = tables(pos_s)
    r = np.arange(128)[:, None]
    ii = np.arange(4)[None, :]
    c["c_mc"] = (r > ii).astype(np.float32)
    kb, kj = np.divmod(np.arange(NS), 4)
    c["c_mnew"] = ((kb[:, None] == kb[None, :]) & (kj[:, None] <= kj[None, :])).astype(np.float32)
    c["c_rowm"] = (np.arange(128)[:, None] // 32 == np.arange(4)[None, :]).astype(np.float32)
    return c


def kernel(**inputs):
    f = lambda a: np.ascontiguousarray(np.asarray(a, dtype=np.float32))
    inp = {k: f(v) for k, v in inputs.items()}
    if "nc" not in _NC_CACHE:
        _NC_CACHE["nc"] = build_program()
    nc = _NC_CACHE["nc"]
    consts = _consts()
    wnames = ["attn_norm", "w_in", "q_norm", "k_norm", "attn_sinks", "ssm_a_re", "ssm_a_im", "ssm_log_dt", "ssm_b_re", "ssm_b_im",
              "ssm_c_re", "ssm_c_im", "ssm_d", "ssm_w_glu", "mem_norm", "w_mem_kv", "mem_q_norm", "mem_k_norm", "w_branch", "w_out",
              "ffn_norm", "w_ffn_up", "w_ffn_down"]
    in_maps = []
    for c in range(8):
        b0 = NSB * c
        m = {
            "xp": inp["x_prompt"][c % 4],
            "xs": inp["x_sample"][b0:b0 + NSB].reshape(NS, D),
            "csk": inp["cache_swa_k"][:, b0:b0 + NSB].reshape(NL, NSB, 128, 128),
            "csv": inp["cache_swa_v"][:, b0:b0 + NSB].reshape(NL, NSB, 128, 128),
            "sre": inp["state_ssm_re"][:, b0:b0 + NSB],
            "sim": inp["state_ssm_im"][:, b0:b0 + NSB],
            "cmk": inp["cache_mem_k"][:, b0:b0 + NSB].reshape(NL, NSB, 256, 512),
            "cmv": inp["cache_mem_v"][:, b0:b0 + NSB].reshape(NL, NSB, 256, 512),
            "memp": inp["mem_prompt"][c % 4],
        }
        for w in wnames:
            m[w] = inp[w]
        m.update(consts)
        in_maps.append({k: np.ascontiguousarray(v) for k, v in m.items()})
    res = run_bass_kernel_spmd(nc, in_maps, core_ids=list(range(8)))
    R = res.results
    cat = lambda name, cores: np.stack([R[c][name] for c in cores])
    y_p = cat("yp", range(4))
    y_s = np.concatenate([R[c]["ys"].reshape(NSB, 4, D) for c in range(8)], axis=0)
    per_l = lambda name, shape: np.stack([R[c][name] for c in range(4)], axis=1).reshape(shape)
    swa_k_p = per_l("kp", (NL, 4, 128, 2, 64))
    swa_v_p = per_l("vp", (NL, 4, 128, 2, 64))
    ssm_re_p = per_l("hrp", (NL, 4, 32, 64))
    ssm_im_p = per_l("hip", (NL, 4, 32, 64))
    mem_k_p = per_l("mkp", (NL, 4, 256, 4, 128))
    mem_v_p = per_l("mvp", (NL, 4, 256, 4, 128))
    cat_s = lambda name, shape: np.concatenate([R[c][name] for c in range(8)], axis=1).reshape(shape)
    swa_k_s = cat_s("ks", (NL, 128, 128, 2, 64))
    swa_v_s = cat_s("vs", (NL, 128, 128, 2, 64))
    ssm_re_s = cat_s("hrs", (NL, 128, 32, 64))
    ssm_im_s = cat_s("his", (NL, 128, 32, 64))
    outs = (y_p, y_s, swa_k_p, swa_v_p, ssm_re_p, ssm_im_p, mem_k_p, mem_v_p, swa_k_s, swa_v_s, ssm_re_s, ssm_im_s)
    return tuple(np.ascontiguousarray(o, dtype=np.float32) for o in outs)
```

```python
import contextlib
import math
import types
import numpy as np
import concourse.bass as bass
import concourse.mybir as mybir
from concourse.bass_utils import run_bass_kernel_spmd

F32 = mybir.dt.float32
BF16 = mybir.dt.bfloat16
I32 = mybir.dt.int32
AF = mybir.ActivationFunctionType
ALU = mybir.AluOpType

D = 1024
SEQ = 4096
NL = 2
INW = 4864
DFF = 2816
K_OFF, V_OFF, U_OFF, MQ_OFF, G_OFF = 512, 640, 768, 1280, 1792
PAST = 16384
TB = 512
NBLK = SEQ // TB
NSB = 16
NS = NSB * 4
NLEV = 9
EPS = 1e-6
NWS = 4
ENGS = ("pe", "act", "dve", "pool", "sp")


def _freeze(fn):
    if fn is None or fn.__closure__ is None:
        return fn
    cells = []
    for c in fn.__closure__:
        try:
            cells.append(types.CellType(c.cell_contents))
        except ValueError:
            cells.append(c)
    return types.FunctionType(fn.__code__, fn.__globals__, fn.__name__, fn.__defaults__, tuple(cells))


class Sched:
    def __init__(self, nc, stack):
        self.nc = nc
        self.stack = stack
        self.q = {e: [] for e in ENGS}
        self.cnt = {e: 0 for e in ENGS}
        self.esem = {e: stack.enter_context(nc.semaphore("sem_" + e)) for e in ENGS}
        self.dsem = {}
        self.dcnt = {}
        self.last_w = {}
        self.readers = {}
        self.seen = {e: {} for e in ENGS}
        self.alias = {}

    def _exp(self, keys):
        out = []
        for k in keys:
            out.append(k)
            out.extend(self.alias.get(k, ()))
        return out

    def op(self, eng, fn, reads=(), writes=(), dma=None):
        reads = self._exp(reads)
        writes = self._exp(writes)
        fn = _freeze(fn)
        deps = []
        for r in reads:
            t = self.last_w.get(r)
            if t is not None:
                deps.append((t, True))
        for w in writes:
            t = self.last_w.get(w)
            if t is not None:
                deps.append((t, False))
            for t in self.readers.get(w, ()):
                deps.append((t, False))
        waits = {}
        for (kind, key, val), raw in deps:
            if kind == "eng" and key == eng and (eng == "pe" or not raw):
                continue
            sk = (kind, key)
            if val > self.seen[eng].get(sk, 0):
                waits[sk] = max(waits.get(sk, 0), val)
        for sk, val in waits.items():
            self.seen[eng][sk] = val
        if dma is not None:
            if dma not in self.dsem:
                self.dsem[dma] = self.stack.enter_context(self.nc.semaphore("dq%d" % len(self.dsem)))
                self.dcnt[dma] = 0
            self.dcnt[dma] += 16
            tok = ("dma", dma, self.dcnt[dma])
        else:
            self.cnt[eng] += 1
            tok = ("eng", eng, self.cnt[eng])
        self.q[eng].append((fn, list(waits.items()), tok))
        for w in writes:
            self.last_w[w] = tok
            self.readers[w] = []
        for r in reads:
            self.readers.setdefault(r, []).append(tok)
        return tok

    def wait_all(self, eng, toks):
        waits = {}
        for kind, key, val in toks:
            sk = (kind, key)
            if val > self.seen[eng].get(sk, 0):
                waits[sk] = max(waits.get(sk, 0), val)
        for sk, val in waits.items():
            self.seen[eng][sk] = val
        self.q[eng].append((None, list(waits.items()), None))

    def emit(self):
        nc = self.nc
        sem = lambda sk: self.esem[sk[1]] if sk[0] == "eng" else self.dsem[sk[1]]
        with nc.Block() as block:
            def run(e, engobj):
                for fn, waits, tok in self.q[e]:
                    for sk, val in waits:
                        engobj.wait_ge(sem(sk), val)
                    if fn is None:
                        continue
                    ins = fn(engobj)
                    if tok[0] == "dma":
                        ins.then_inc(self.dsem[tok[1]], 16)
                    else:
                        ins.then_inc(self.esem[e], 1)

            @block.tensor
            def _(t):
                run("pe", t)

            @block.scalar
            def _(t):
                run("act", t)

            @block.vector
            def _(t):
                run("dve", t)

            @block.gpsimd
            def _(t):
                run("pool", t)

            @block.sync
            def _(t):
                run("sp", t)


def build_program():
    nc = bass.Bass("TRN2", target_bir_lowering=False)

    def din(name, shape):
        return nc.dram_tensor(name, list(shape), F32, kind="ExternalInput").ap()

    def dout(name, shape):
        return nc.dram_tensor(name, list(shape), F32, kind="ExternalOutput").ap()

    def dscr(name, shape, dt=BF16):
        return nc.dram_tensor(name, list(shape), dt).ap()

    xp = din("xp", [SEQ, D]); xs = din("xs", [NS, D])
    csk = din("csk", [NL, NSB, 128, 128]); csv = din("csv", [NL, NSB, 128, 128])
    sre = din("sre", [NL, NSB, 32, 64]); sim = din("sim", [NL, NSB, 32, 64])
    cmk = din("cmk", [NL, NSB, 256, 512]); cmv = din("cmv", [NL, NSB, 256, 512])
    memp = din("memp", [256, D])
    attn_norm = din("attn_norm", [NL, D]); w_in = din("w_in", [NL, D, INW])
    q_norm = din("q_norm", [NL, 64]); k_norm = din("k_norm", [NL, 64]); attn_sinks = din("attn_sinks", [NL, 8])
    a_re = din("ssm_a_re", [NL, 32, 64]); a_im = din("ssm_a_im", [NL, 32, 64]); log_dt = din("ssm_log_dt", [NL, 32])
    b_re = din("ssm_b_re", [NL, 32, 64, 16]); b_im = din("ssm_b_im", [NL, 32, 64, 16])
    c_re = din("ssm_c_re", [NL, 32, 16, 64]); c_im = din("ssm_c_im", [NL, 32, 16, 64])
    ssm_d = din("ssm_d", [NL, 512]); w_glu = din("ssm_w_glu", [NL, 512, 512])
    mem_norm = din("mem_norm", [NL, D]); w_kv = din("w_mem_kv", [NL, D, D])
    mq_norm = din("mem_q_norm", [NL, 128]); mk_norm = din("mem_k_norm", [NL, 128])
    w_br = din("w_branch", [NL, 3, 512, D]); w_out = din("w_out", [NL, D, D])
    ffn_norm = din("ffn_norm", [NL, D]); w_up = din("w_ffn_up", [NL, D, 2 * DFF]); w_dn = din("w_ffn_down", [NL, DFF, D])
    c_ident = din("c_ident", [128, 128]); c_blk = din("c_blk", [128, 128]); c_perm = din("c_perm", [128, 128])
    c_mprev = din("c_mprev", [128, 128]); c_mcur = din("c_mcur", [128, 128])
    c_cos = din("c_cos", [128, SEQ]); c_sin = din("c_sin", [128, SEQ])
    c_cos_s = din("c_cos_s", [128, NS]); c_sin_s = din("c_sin_s", [128, NS])
    c_mc = din("c_mc", [128, 4]); c_mnew = din("c_mnew", [NS, NS]); c_rowm = din("c_rowm", [128, 4])

    yp = dout("yp", [SEQ, D]); ys = dout("ys", [NS, D])
    kp = dout("kp", [NL, 128, 128]); vp = dout("vp", [NL, 128, 128])
    hrp = dout("hrp", [NL, 32, 64]); hip = dout("hip", [NL, 32, 64])
    mkp = dout("mkp", [NL, 256, 512]); mvp = dout("mvp", [NL, 256, 512])
    ks = dout("ks", [NL, NSB, 128, 128]); vs = dout("vs", [NL, NSB, 128, 128])
    hrs = dout("hrs", [NL, NSB, 32, 64]); his = dout("his", [NL, NSB, 32, 64])

    wb_in = dscr("wb_in", [NL, D, INW]); wb_glu = dscr("wb_glu", [NL, 512, 512]); wb_kv = dscr("wb_kv", [NL, D, D])
    wb_br = dscr("wb_br", [NL, 3, 512, D]); wb_out = dscr("wb_out", [NL, D, D])
    wb_up = dscr("wb_up", [NL, D, 2 * DFF]); wb_dn = dscr("wb_dn", [NL, DFF, D])

    out_toks = []

    with contextlib.ExitStack() as st:
        S = Sched(nc, st)

        def sb(name, shape, dt):
            return st.enter_context(nc.sbuf_tensor(name, list(shape), dt))

        PS = [st.enter_context(nc.psum_tensor("ps%d" % i, [128, 512], F32)) for i in range(8)]
        psn = [0]

        held = set()

        def bank():
            while True:
                i = psn[0] % 8
                psn[0] += 1
                if i not in held:
                    return i

        ident_f = sb("ident_f", [128, 128], F32); ident_b = sb("ident_b", [128, 128], BF16)
        ones_b = sb("ones_b", [128, 128], BF16); blk_b = sb("blk_b", [128, 128], BF16); perm_b = sb("perm_b", [128, 128], BF16)
        mprev_b = sb("mprev_b", [128, 128], BF16); mcur_b = sb("mcur_b", [128, 128], BF16)
        mc_b = sb("mc_b", [128, 4], BF16); mnew_b = sb("mnew_b", [NS, NS], BF16); rowm = sb("rowm", [128, 4], F32)
        cosb = sb("cosb", [128, TB], F32); sinb = sb("sinb", [128, TB], F32)
        gA = sb("gA", [128, NL, 8], F32); gF = sb("gF", [128, NL, 8], F32)
        gq = sb("gq", [128, NL], F32); gk = sb("gk", [128, NL], F32); gmq = sb("gmq", [128, NL], F32)
        esink = sb("esink", [128, NL, 4], F32); dcol = sb("dcol", [128, NL, 4], F32)
        W2 = sb("W2", [128, NL, 16, 2, 128], BF16); CP = sb("CP", [128, NL, 16, 2, 128], BF16)
        Dd = sb("Dd", [128, NL, 4, 128], BF16)
        LR = sb("LR", [128, NL, NLEV, 16], F32); LI = sb("LI", [128, NL, NLEV, 16], F32); LIn = sb("LIn", [128, NL, NLEV, 16], F32)
        MKT = sb("MKT", [128, NL, 4, 256], BF16); MV = sb("MV", [128, NL, 2, 512], BF16)
        kTc = sb("kTc", [128, NL, 128 + TB], BF16); vtc = sb("vtc", [128, NL, 5, 128], BF16)
        car_r = sb("car_r", [128, NL, 16], F32); car_i = sb("car_i", [128, NL, 16], F32)
        xT = sb("xT", [128, 8, TB], F32)
        hT = sb("hT", [128, 8, TB], BF16)
        qf = sb("qf", [128, TB], F32); sqb = sb("sqb", [128, TB], BF16); sdv = sb("sdv", [128, TB], F32); rstd = sb("rstd", [128, TB], F32)
        qn = sb("qn", [128, TB], BF16); t1 = sb("t1", [128, TB], F32); t2 = sb("t2", [128, TB], F32)
        HN = [dict(qf=qf, sqb=sqb, sdv=sdv, rstd=rstd, qn=qn, t1=t1, t2=t2, sfx=""),
              dict(qf=sb("qfB", [128, TB], F32), sqb=sb("sqbB", [128, TB], BF16), sdv=sdv,
                   rstd=sb("rstdB", [128, TB], F32), qn=sb("qnB", [128, TB], BF16), t1=t1, t2=t2, sfx="B")]
        hn_i = [0]
        S.alias["zT"] = ["qr"]
        qr = sb("qr", [128, 4, TB], BF16); k32 = sb("k32", [128, TB], F32); v32 = sb("v32", [128, 4, 128], F32)
        uT = sb("uT", [128, 4, TB], BF16); qmn = sb("qmn", [128, 4, TB], BF16)
        oa = sb("oa", [128, 4, TB], BF16); ob = sb("ob", [128, 4, TB], BF16); oc = sb("oc", [128, 4, TB], BF16)
        PT = [sb("PT%d" % i, [128, 2, TB], BF16) for i in range(2)]
        dn = [sb("dn%d" % i, [128, TB], F32) for i in range(2)]
        zT = qr; sg = [sb("sg%d" % i, [128, TB], BF16) for i in range(2)]
        WS = [sb("ws%d" % i, [128, 4096], BF16) for i in range(NWS)]
        RA = sb("RA", [128, 16384], BF16)
        small = sb("small", [128, 64], F32)
        smi = sb("smi", [128, 16], I32)
        hs_r = sb("hs_r", [128, 16, NSB], F32); hs_i = sb("hs_i", [128, 16, NSB], F32)
        kcT = sb("kcT", [128, 2, 128], BF16)

        RAK = [("RA", i) for i in range(32)]

        def ra(off_b, nbytes, dt):
            a = RA[:, off_b // 2:(off_b + nbytes) // 2]
            keys = RAK[off_b // 1024:(off_b + nbytes + 1023) // 1024]
            return (a.bitcast(F32) if dt == F32 else a), keys

        xtok = ra(0, 16384, F32)[0].rearrange("p (s d) -> p s d", s=4)
        S.alias["xtok"] = RAK[0:16]
        sqT = ra(24576, 8192, BF16)[0].rearrange("p (k n) -> p k n", k=8)
        S.alias["sqT"] = RAK[24:32]
        XSr, kXSr = ra(0, 8192, F32); XSi, kXSi = ra(8192, 8192, F32)
        TD = [ra(16384 + 4096 * i, 4096, F32) for i in range(4)]
        xbr, kxbr = ra(16384, 4096, BF16); xbi, kxbi = ra(20480, 4096, BF16)
        macc, kmacc = ra(0, 16384, F32); mgT, kmgT = ra(16384, 8192, BF16)
        wsn = [0]

        def wk(i):
            return [("ws", i), ("wsT", i)]

        def wslot():
            i = wsn[0] % NWS
            wsn[0] += 1
            return i

        S.op("sp", lambda e: e.dma_start(out=ident_f[:], in_=c_ident), writes=["ident_f"], dma="c0")
        for (dst, src, nm) in [(ident_b, c_ident, "ident_b"), (blk_b, c_blk, "blk_b"), (perm_b, c_perm, "perm_b"),
                               (mprev_b, c_mprev, "mprev_b"), (mcur_b, c_mcur, "mcur_b"), (mc_b, c_mc, "mc_b"),
                               (mnew_b, c_mnew, "mnew_b")]:
            S.op("pool", lambda e, dst=dst, src=src: e.dma_start(out=dst[:], in_=src), writes=[nm], dma=nm)
        S.op("sp", lambda e: e.dma_start(out=rowm[:], in_=c_rowm), writes=["rowm"], dma="rowm")
        S.op("dve", lambda e: e.memset(ones_b[:], 1.0), writes=["ones_b"])
        S.op("dve", lambda e: e.memset(small[:], 0.0), writes=["small"])
        S.op("dve", lambda e: e.memset(small[:, 0:1], math.pi / 2), writes=["small"])
        S.op("dve", lambda e: e.memset(small[:, 1:2], EPS), writes=["small"])

        for l in range(NL):
            S.op("sp", lambda e, l=l: e.dma_start(out=gA[:, l, :], in_=attn_norm[l].rearrange("(k p) -> p k", p=128),
                                                  allow_slow_non_contiguous=True), writes=["gA"], dma="gA")
            S.op("sp", lambda e, l=l: e.dma_start(out=gF[:, l, :], in_=ffn_norm[l].rearrange("(k p) -> p k", p=128),
                                                  allow_slow_non_contiguous=True), writes=["gF"], dma="gF")
            S.op("sp", lambda e, l=l: e.dma_start(out=dcol[:, l, :], in_=ssm_d[l].rearrange("(k p) -> p k", p=128),
                                                  allow_slow_non_contiguous=True), writes=["dcol"], dma="dcol")
            for two in range(2):
                sl = slice(64 * two, 64 * two + 64)
                S.op("sp", lambda e, l=l, sl=sl: e.dma_start(out=gq[sl, l:l + 1], in_=q_norm[l].rearrange("(p o) -> p o", o=1)),
                     writes=["gq"], dma="gq")
                S.op("sp", lambda e, l=l, sl=sl: e.dma_start(out=gk[sl, l:l + 1], in_=k_norm[l].rearrange("(p o) -> p o", o=1)),
                     writes=["gk"], dma="gk")
                S.op("sp", lambda e, l=l, sl=sl, two=two: e.dma_start(out=esink[sl, l, :],
                                                                      in_=attn_sinks[l, 4 * two:4 * two + 4].partition_broadcast(64)),
                     writes=["esink"], dma="esink")
            S.op("sp", lambda e, l=l: e.dma_start(out=gmq[:, l:l + 1], in_=mq_norm[l].rearrange("(p o) -> p o", o=1)),
                 writes=["gmq"], dma="gmq")
        S.op("act", lambda e: e.activation(out=esink[:], in_=esink[:], func=AF.Exp), reads=["esink"], writes=["esink"])

        are_t = sb("are_t", [128, 16], F32); aim_t = sb("aim_t", [128, 16], F32); dt_t = sb("dt_t", [128, 16], F32)
        sA = [sb("sA%d" % i, [128, 16], F32) for i in range(8)]
        def ra3(idx, nm):
            v, kk = ra(16384 + 2048 * idx, 2048, F32)
            S.alias[nm] = kk
            return v.rearrange("p (t c) -> p t c", t=16)
        Bb = [ra3(i, "Bb%d" % i) for i in range(2)]
        Cb = [ra3(2 + i, "Cb%d" % i) for i in range(2)]
        GB = [ra3(4 + i, "GB%d" % i) for i in range(2)]
        tG = [ra3(6 + i, "tG%d" % i) for i in range(2)]
        tT = sb("tT", [128, 128], F32)

        def dve(fn, R, W):
            return S.op("dve", fn, reads=R, writes=W)

        def act(fn, R, W):
            return S.op("act", fn, reads=R, writes=W)

        TWO_PI = 2.0 * math.pi
        for l in range(NL):
            for gl in range(2):
                sl = slice(64 * gl, 64 * gl + 64)
                S.op("sp", lambda e, l=l, gl=gl, sl=sl: e.dma_start(
                    out=are_t[sl, :], in_=a_re[l].rearrange("(tp gl) p -> gl p tp", gl=2)[gl], allow_slow_non_contiguous=True),
                    writes=["are_t"], dma="are_t")
                S.op("sp", lambda e, l=l, gl=gl, sl=sl: e.dma_start(
                    out=aim_t[sl, :], in_=a_im[l].rearrange("(tp gl) p -> gl p tp", gl=2)[gl], allow_slow_non_contiguous=True),
                    writes=["aim_t"], dma="aim_t")
                S.op("sp", lambda e, l=l, gl=gl, sl=sl: e.dma_start(
                    out=dt_t[sl, :], in_=log_dt[l].rearrange("(tp gl) -> gl tp", gl=2)[gl].partition_broadcast(64)),
                    writes=["dt_t"], dma="dt_t")
            for ri, (bsrc, csrc) in enumerate([(b_re, c_re), (b_im, c_im)]):
                S.op("pool", lambda e, ri=ri: e.memset(Bb[ri][:], 0.0), writes=["Bb%d" % ri])
                S.op("pool", lambda e, ri=ri: e.memset(Cb[ri][:], 0.0), writes=["Cb%d" % ri])
                for gl in range(2):
                    sl = slice(64 * gl, 64 * gl + 64)
                    cs = slice(16 * gl, 16 * gl + 16)
                    S.op("sp", lambda e, l=l, gl=gl, sl=sl, cs=cs, ri=ri, bsrc=bsrc: e.dma_start(
                        out=Bb[ri][sl, :, cs], in_=bsrc[l].rearrange("(tp gl) p c -> gl p tp c", gl=2)[gl]),
                        writes=["Bb%d" % ri], dma="Bb%d" % ri)
                cst = t1[:].rearrange("p (a q) -> p a q", a=4)
                csrc2 = csrc[l].rearrange("g c p -> (g c) p").rearrange("(a q) p -> q a p", q=128)
                for dup in range(2):
                    S.op("sp", lambda e, dup=dup: e.dma_start(out=cst[:, :, 64 * dup:64 * dup + 64], in_=csrc2), writes=["t1"], dma="t1")
                for a in range(4):
                    b = bank()
                    S.op("pe", lambda e, a=a, b=b: e.transpose(out=PS[b][:, 0:128], in_=cst[:, a, :], identity=ident_f[:]),
                         reads=["t1", "ident_f"], writes=[("ps", b)])
                    for gl in range(2):
                        act(lambda e, a=a, b=b, gl=gl, ri=ri: e.copy(
                            out=Cb[ri][64 * gl:64 * gl + 64, 4 * a:4 * a + 4, 16 * gl:16 * gl + 16],
                            in_=PS[b][64 * gl:64 * gl + 64, 0:128].rearrange("p (tp gl c) -> p gl tp c", gl=2, c=16)[:, gl]),
                            [("ps", b)], ["Cb%d" % ri])
            dtv, ard, mag, th, kf, s_, c_, tmp = sA
            act(lambda e: e.activation(out=dtv[:], in_=dt_t[:], func=AF.Exp), ["dt_t"], ["sA0"])
            dve(lambda e: e.tensor_tensor(out=ard[:], in0=are_t[:], in1=dtv[:], op=ALU.mult), ["are_t", "sA0"], ["sA1"])
            act(lambda e: e.activation(out=mag[:], in_=ard[:], func=AF.Exp), ["sA1"], ["sA2"])
            dve(lambda e: e.tensor_tensor(out=th[:], in0=aim_t[:], in1=dtv[:], op=ALU.mult), ["aim_t", "sA0"], ["sA3"])
            dve(lambda e: e.tensor_scalar(out=kf[:], in0=th[:], scalar1=1.0 / TWO_PI, scalar2=None, op0=ALU.mult), ["sA3"], ["sA4"])
            dve(lambda e: e.tensor_copy(out=smi[:], in_=kf[:]), ["sA4"], ["smi"])
            dve(lambda e: e.tensor_copy(out=kf[:], in_=smi[:]), ["smi"], ["sA4"])
            dve(lambda e: e.scalar_tensor_tensor(out=th[:], in0=kf[:], scalar=-TWO_PI, in1=th[:], op0=ALU.mult, op1=ALU.add),
                ["sA4", "sA3"], ["sA3"])
            act(lambda e: e.activation(out=s_[:], in_=th[:], func=AF.Sin, scale=0.5), ["sA3"], ["sA5"])
            act(lambda e: e.activation(out=c_[:], in_=th[:], func=AF.Sin, scale=0.5, bias=small[:, 0:1]), ["sA3", "small"], ["sA6"])
            lr0 = LR[:, l, 0, :]; li0 = LI[:, l, 0, :]
            dve(lambda e: e.tensor_tensor(out=tmp[:], in0=s_[:], in1=c_[:], op=ALU.mult), ["sA5", "sA6"], ["sA7"])
            dve(lambda e: e.scalar_tensor_tensor(out=li0, in0=tmp[:], scalar=2.0, in1=mag[:], op0=ALU.mult, op1=ALU.mult),
                ["sA7", "sA2"], ["LI"])
            dve(lambda e: e.tensor_tensor(out=tmp[:], in0=s_[:], in1=s_[:], op=ALU.mult), ["sA5"], ["sA7"])
            dve(lambda e: e.tensor_scalar(out=tmp[:], in0=tmp[:], scalar1=-2.0, scalar2=1.0, op0=ALU.mult, op1=ALU.add), ["sA7"], ["sA7"])
            dve(lambda e: e.tensor_tensor(out=lr0, in0=tmp[:], in1=mag[:], op=ALU.mult), ["sA7", "sA2"], ["LR"])
            den_, nr_, gr_, gi_, rd_ = sA[0], sA[1], sA[2], sA[3], sA[4]
            dve(lambda e: e.tensor_tensor(out=den_[:], in0=are_t[:], in1=are_t[:], op=ALU.mult), ["are_t"], ["sA0"])
            dve(lambda e: e.tensor_tensor(out=tmp[:], in0=aim_t[:], in1=aim_t[:], op=ALU.mult), ["aim_t"], ["sA7"])
            dve(lambda e: e.tensor_tensor(out=den_[:], in0=den_[:], in1=tmp[:], op=ALU.add), ["sA0", "sA7"], ["sA0"])
            dve(lambda e: e.reciprocal(out=rd_[:], in_=den_[:]), ["sA0"], ["sA4"])
            dve(lambda e: e.tensor_scalar(out=nr_[:], in0=lr0, scalar1=-1.0, scalar2=None, op0=ALU.add), ["LR"], ["sA1"])
            dve(lambda e: e.tensor_tensor(out=gr_[:], in0=nr_[:], in1=are_t[:], op=ALU.mult), ["sA1", "are_t"], ["sA2"])
            dve(lambda e: e.tensor_tensor(out=tmp[:], in0=li0, in1=aim_t[:], op=ALU.mult), ["LI", "aim_t"], ["sA7"])
            dve(lambda e: e.tensor_tensor(out=gr_[:], in0=gr_[:], in1=tmp[:], op=ALU.add), ["sA2", "sA7"], ["sA2"])
            dve(lambda e: e.tensor_tensor(out=gr_[:], in0=gr_[:], in1=rd_[:], op=ALU.mult), ["sA2", "sA4"], ["sA2"])
            dve(lambda e: e.tensor_tensor(out=gi_[:], in0=li0, in1=are_t[:], op=ALU.mult), ["LI", "are_t"], ["sA3"])
            dve(lambda e: e.tensor_tensor(out=tmp[:], in0=nr_[:], in1=aim_t[:], op=ALU.mult), ["sA1", "aim_t"], ["sA7"])
            dve(lambda e: e.tensor_tensor(out=gi_[:], in0=gi_[:], in1=tmp[:], op=ALU.subtract), ["sA3", "sA7"], ["sA3"])
            dve(lambda e: e.tensor_tensor(out=gi_[:], in0=gi_[:], in1=rd_[:], op=ALU.mult), ["sA3", "sA4"], ["sA3"])
            for i in range(NLEV - 1):
                a, b = LR[:, l, i, :], LI[:, l, i, :]
                a2, b2 = LR[:, l, i + 1, :], LI[:, l, i + 1, :]
                dve(lambda e, a=a, b=b: e.tensor_tensor(out=tmp[:], in0=b, in1=b, op=ALU.mult), ["LI"], ["sA7"])
                dve(lambda e, a=a, a2=a2: e.tensor_tensor(out=a2, in0=a, in1=a, op=ALU.mult), ["LR"], ["LR"])
                dve(lambda e, a2=a2: e.tensor_tensor(out=a2, in0=a2, in1=tmp[:], op=ALU.subtract), ["LR", "sA7"], ["LR"])
                dve(lambda e, a=a, b=b, b2=b2: e.scalar_tensor_tensor(out=b2, in0=a, scalar=2.0, in1=b, op0=ALU.mult, op1=ALU.mult),
                    ["LR", "LI"], ["LI"])
            dve(lambda e, l=l: e.tensor_scalar(out=LIn[:, l], in0=LI[:, l], scalar1=-1.0, scalar2=None, op0=ALU.mult), ["LI"], ["LIn"])
            grb = gr_[:].rearrange("p (t o) -> p t o", o=1).to_broadcast([128, 16, 32])
            gib = gi_[:].rearrange("p (t o) -> p t o", o=1).to_broadcast([128, 16, 32])
            dve(lambda e: e.tensor_tensor(out=GB[0][:], in0=Bb[0][:], in1=grb, op=ALU.mult), ["Bb0", "sA2"], ["GB0"])
            dve(lambda e: e.tensor_tensor(out=tG[0][:], in0=Bb[1][:], in1=gib, op=ALU.mult), ["Bb1", "sA3"], ["tG0"])
            dve(lambda e: e.tensor_tensor(out=GB[0][:], in0=GB[0][:], in1=tG[0][:], op=ALU.subtract), ["GB0", "tG0"], ["GB0"])
            dve(lambda e: e.tensor_tensor(out=GB[1][:], in0=Bb[1][:], in1=grb, op=ALU.mult), ["Bb1", "sA2"], ["GB1"])
            dve(lambda e: e.tensor_tensor(out=tG[1][:], in0=Bb[0][:], in1=gib, op=ALU.mult), ["Bb0", "sA3"], ["tG1"])
            dve(lambda e: e.tensor_tensor(out=GB[1][:], in0=GB[1][:], in1=tG[1][:], op=ALU.add), ["GB1", "tG1"], ["GB1"])
            for ri in range(2):
                for ct in range(4):
                    b = bank()
                    S.op("pe", lambda e, b=b, ri=ri, ct=ct: e.transpose(
                        out=PS[b][:, 0:128], in_=GB[ri][:, 4 * ct:4 * ct + 4, :].rearrange("p a b -> p (a b)"), identity=ident_f[:]),
                        reads=["GB%d" % ri, "ident_f"], writes=[("ps", b)])
                    act(lambda e, b=b: e.copy(out=tT[:], in_=PS[b][:, 0:128]), [("ps", b)], ["tT"])
                    for i in range(4):
                        dve(lambda e, l=l, ri=ri, ct=ct, i=i: e.tensor_scalar(
                            out=W2[:, l, 4 * ct + i, ri, :], in0=tT[:], scalar1=rowm[:, i:i + 1], scalar2=None, op0=ALU.mult),
                            ["tT", "rowm"], ["W2"])
            S.op("pool", lambda e, l=l: e.memset(CP[:, l], 0.0), writes=["CP"])
            for i in range(4):
                act(lambda e, l=l, i=i: e.copy(out=CP[:, l, i::4, 0, 32 * i:32 * i + 32], in_=Cb[0][:, i::4, :]), ["Cb0", "CP"], ["CP"])
                act(lambda e, l=l, i=i: e.mul(out=CP[:, l, i::4, 1, 32 * i:32 * i + 32], in_=Cb[1][:, i::4, :], mul=-1.0), ["Cb1", "CP"], ["CP"])
            for ct in range(4):
                dve(lambda e, l=l, ct=ct: e.tensor_scalar(out=Dd[:, l, ct, :], in0=ident_f[:], scalar1=dcol[:, l, ct:ct + 1], scalar2=None,
                                                          op0=ALU.mult), ["ident_f", "dcol"], ["Dd"])

        conv_pieces = []

        def conv(dst, src, key, rows):
            r0 = 0
            while r0 < rows:
                r1 = min(rows, r0 + 256)
                conv_pieces.append((dst, src, key, r0, r1))
                r0 = r1

        for l in range(NL):
            conv(wb_in[l], w_in[l], ("wb_in", l), D)
            conv(wb_glu[l], w_glu[l], ("wb_glu", l), 512)
            for n in range(3):
                conv(wb_br[l, n], w_br[l, n], ("wb_br", l), 512)
            conv(wb_out[l], w_out[l], ("wb_out", l), D)
            conv(wb_up[l], w_up[l], ("wb_up", l), D)
            conv(wb_dn[l], w_dn[l], ("wb_dn", l), DFF)

        def emit_conv(n):
            for _ in range(n):
                if not conv_pieces:
                    return
                dst, src, key, r0, r1 = conv_pieces.pop(0)
                S.op("pool", lambda e: e.dma_start(out=dst[r0:r1, :], in_=src[r0:r1, :]), writes=[key], dma=key)

        memt = xtok
        mnb = qr[:].rearrange("p t n -> p (t n)").rearrange("p (a d) -> p a d", a=2)
        S.alias["mnb"] = ["qr"]
        mnT = hT
        gmk_b = sb("gmk_b", [128, 128], F32)
        S.op("sp", lambda e: e.dma_start(out=memt[:, 0:2, :], in_=memp.rearrange("(a p) d -> p a d", p=128)), writes=["xtok"], dma="xtok")
        for l in range(NL):
            S.op("sp", lambda e, l=l: e.dma_start(out=memt[:, 2, :], in_=mem_norm[l].partition_broadcast(128)), writes=["xtok"], dma="xtok")
            S.op("sp", lambda e, l=l: e.dma_start(out=gmk_b[:], in_=mk_norm[l].partition_broadcast(128)), writes=["gmk_b"], dma="gmk_b")
            dve(lambda e: e.memset(small[:, 8:10], 0.0), [], ["small"])
            for a in range(2):
                act(lambda e, a=a: e.activation(out=memt[:, 3, :], in_=memt[:, a, :], func=AF.Square, accum_out=small[:, 8 + a:9 + a]),
                    ["xtok"], ["xtok", "small"])
            act(lambda e: e.activation(out=small[:, 10:12], in_=small[:, 8:10], func=AF.Sqrt, scale=1.0 / D, bias=small[:, 1:2]),
                ["small"], ["small"])
            dve(lambda e: e.reciprocal(out=small[:, 12:14], in_=small[:, 10:12]), ["small"], ["small"])
            for a in range(2):
                dve(lambda e, a=a: e.scalar_tensor_tensor(out=mnb[:, a, :], in0=memt[:, a, :], scalar=small[:, 12 + a:13 + a],
                                                          in1=memt[:, 2, :], op0=ALU.mult, op1=ALU.mult), ["xtok", "small"], ["mnb"])
            for a in range(2):
                for k in range(8):
                    b = bank()
                    pb = PS[b][:].bitcast(BF16)
                    S.op("pe", lambda e, a=a, k=k, pb=pb: e.transpose(out=pb[:, 0:128], in_=mnb[:, a, 128 * k:128 * k + 128],
                                                                      identity=ident_b[:]),
                         reads=["mnb", "ident_b"], writes=[("ps", b)])
                    act(lambda e, a=a, k=k, pb=pb: e.copy(out=mnT[:, k, 128 * a:128 * a + 128], in_=pb[:, 0:128]), [("ps", b)], ["hT"])
            for half in range(2):
                ws = wslot()
                wv = WS[ws][:].rearrange("p (k c) -> p k c", k=8)
                S.op("pool", lambda e, l=l, half=half, wv=wv: e.dma_start(
                    out=wv, in_=w_kv[l].rearrange("(k p) c -> p k c", p=128)[:, :, 512 * half:512 * half + 512]),
                    writes=wk(ws), dma=("ws", ws))
                for a in range(2):
                    b = bank()
                    for k in range(8):
                        S.op("pe", lambda e, a=a, k=k, b=b, wv=wv: e.matmul(PS[b][:], lhsT=mnT[:, k, 128 * a:128 * a + 128], rhs=wv[:, k, :],
                                                                          start=(k == 0), stop=(k == 7)),
                             reads=["hT", ("ws", ws)], writes=[("ps", b)])
                    if half == 0:
                        kk = t1
                        dve(lambda e: e.memset(small[:, 16:20], 0.0), [], ["small"])
                        for h in range(4):
                            act(lambda e, b=b, h=h: e.activation(out=t2[:, 128 * h:128 * h + 128], in_=PS[b][:, 128 * h:128 * h + 128],
                                                                 func=AF.Square, accum_out=small[:, 16 + h:17 + h]),
                                [("ps", b)], ["t2", "small"])
                        act(lambda e: e.activation(out=small[:, 20:24], in_=small[:, 16:20], func=AF.Sqrt, scale=1.0 / 128, bias=small[:, 1:2]),
                            ["small"], ["small"])
                        dve(lambda e: e.reciprocal(out=small[:, 24:28], in_=small[:, 20:24]), ["small"], ["small"])
                        for h in range(4):
                            dve(lambda e, b=b, h=h: e.scalar_tensor_tensor(
                                out=kk[:, 128 * h:128 * h + 128], in0=PS[b][:, 128 * h:128 * h + 128], scalar=small[:, 24 + h:25 + h],
                                in1=gmk_b[:], op0=ALU.mult, op1=ALU.mult), [("ps", b), "small", "gmk_b"], ["t1"])
                        out_toks.append(S.op("sp", lambda e, l=l, a=a: e.dma_start(out=mkp[l, 128 * a:128 * a + 128, :], in_=kk[:]),
                                             reads=["t1"], dma="o_mkp"))
                        act(lambda e: e.copy(out=sqb[:], in_=kk[:]), ["t1"], ["sqb"])
                        for h in range(4):
                            b2 = bank()
                            pb = PS[b2][:].bitcast(BF16)
                            S.op("pe", lambda e, h=h, pb=pb: e.transpose(out=pb[:, 0:128], in_=sqb[:, 128 * h:128 * h + 128], identity=ident_b[:]),
                                 reads=["sqb", "ident_b"], writes=[("ps", b2)])
                            act(lambda e, l=l, a=a, h=h, pb=pb: e.copy(out=MKT[:, l, h, 128 * a:128 * a + 128], in_=pb[:, 0:128]),
                                [("ps", b2)], ["MKT"])
                    else:
                        vv = t2
                        act(lambda e, b=b: e.copy(out=vv[:], in_=PS[b][:]), [("ps", b)], ["t2"])
                        out_toks.append(S.op("sp", lambda e, l=l, a=a: e.dma_start(out=mvp[l, 128 * a:128 * a + 128, :], in_=vv[:]),
                                             reads=["t2"], dma="o_mvp"))
                        dve(lambda e, l=l, a=a: e.tensor_copy(out=MV[:, l, a, :], in_=vv[:]), ["t2"], ["MV"])

        fpm = [False]

        def wq():
            return "pool" if fpm[0] else "sp"

        def wr(key):
            return [] if fpm[0] else [key]

        def load_w(dst_view, src_ap, srckey):
            ws = wslot()
            dv = dst_view(ws)
            S.op(wq(), lambda e: e.dma_start(out=dv, in_=src_ap), reads=wr(srckey), writes=wk(ws), dma=("ws", ws))
            if fpm[0]:
                emit_conv(1)
            return ws

        def rms_rstd(N, ssb, inv_n, R):
            act(lambda e: e.activation(out=sdv[:, 0:N], in_=PS[ssb][:, 0:N], func=AF.Sqrt, scale=inv_n, bias=small[:, 1:2]),
                [("ps", ssb), "small"], ["sdv"])
            dve(lambda e: e.reciprocal(out=rstd[:, 0:N], in_=sdv[:, 0:N]), ["sdv"], ["rstd"])

        def norm_block(N, gtab):
            act(lambda e: e.activation(out=sqT[:, :, 0:N], in_=xT[:, :, 0:N], func=AF.Square), ["xT"], ["sqT"])
            b = bank()
            for k in range(8):
                S.op("pe", lambda e, k=k, b=b: e.matmul(PS[b][:, 0:N], lhsT=ones_b[:], rhs=sqT[:, k, 0:N], start=(k == 0), stop=(k == 7)),
                     reads=["sqT", "ones_b"], writes=[("ps", b)])
            rms_rstd(N, b, 1.0 / D, None)
            for k in range(8):
                dve(lambda e, k=k: e.scalar_tensor_tensor(out=hT[:, k, 0:N], in0=xT[:, k, 0:N], scalar=gtab[:, k:k + 1], in1=rstd[:, 0:N],
                                                          op0=ALU.mult, op1=ALU.mult), ["xT", "rstd", "gA", "gF"], ["hT"])

        def proj_tile(N, wv, c0, ws, b=None):
            if b is None:
                b = bank()
            for k in range(8):
                S.op("pe", lambda e, k=k, b=b: e.matmul(PS[b][:, 0:N], lhsT=wv[:, k, c0:c0 + 128], rhs=hT[:, k, 0:N],
                                                        start=(k == 0), stop=(k == 7)),
                     reads=["hT", ("ws", ws)], writes=[("ps", b)])
            return b

        def headnorm_rope(N, b, l, gcol, onesm, inv_n, rope, out_bf, out_keys, out32=None, out32_keys=()):
            H = HN[hn_i[0] % 2]
            hn_i[0] += 1
            x = H["sfx"]
            qf_, sqb_, sdv_, rstd_, qn_, t1_, t2_ = H["qf"], H["sqb"], H["sdv"], H["rstd"], H["qn"], H["t1"], H["t2"]
            act(lambda e: e.copy(out=qf_[:, 0:N], in_=PS[b][:, 0:N]), [("ps", b)], ["qf" + x])
            act(lambda e: e.activation(out=sqb_[:, 0:N], in_=qf_[:, 0:N], func=AF.Square), ["qf" + x], ["sqb" + x])
            b2 = bank()
            S.op("pe", lambda e: e.matmul(PS[b2][:, 0:N], lhsT=onesm[:], rhs=sqb_[:, 0:N], start=True, stop=True),
                 reads=["sqb" + x, "blk_b", "ones_b"], writes=[("ps", b2)])
            act(lambda e: e.activation(out=sdv_[:, 0:N], in_=PS[b2][:, 0:N], func=AF.Sqrt, scale=inv_n, bias=small[:, 1:2]),
                [("ps", b2), "small"], ["sdv"])
            dve(lambda e: e.reciprocal(out=rstd_[:, 0:N], in_=sdv_[:, 0:N]), ["sdv"], ["rstd" + x])
            if not rope:
                S.op("dve", lambda e: e.scalar_tensor_tensor(out=out_bf, in0=qf_[:, 0:N], scalar=gcol, in1=rstd_[:, 0:N],
                                                             op0=ALU.mult, op1=ALU.mult),
                     reads=["qf" + x, "rstd" + x, "gmq"], writes=out_keys)
                return
            S.op("dve", lambda e: e.scalar_tensor_tensor(out=qn_[:, 0:N], in0=qf_[:, 0:N], scalar=gcol, in1=rstd_[:, 0:N],
                                                         op0=ALU.mult, op1=ALU.mult),
                 reads=["qf" + x, "rstd" + x, "gq", "gk"], writes=["qn" + x])
            b3 = bank()
            S.op("pe", lambda e: e.matmul(PS[b3][:, 0:N], lhsT=perm_b[:], rhs=qn_[:, 0:N], start=True, stop=True),
                 reads=["qn" + x, "perm_b"], writes=[("ps", b3)])
            S.op("pool", lambda e: e.tensor_tensor(out=t1_[:, 0:N], in0=qn_[:, 0:N], in1=cosb[:, 0:N], op=ALU.mult),
                 reads=["qn" + x, "cosb"], writes=["t1"])
            dve(lambda e: e.tensor_tensor(out=t2_[:, 0:N], in0=PS[b3][:, 0:N], in1=sinb[:, 0:N], op=ALU.mult), [("ps", b3), "sinb"], ["t2"])
            if out32 is not None:
                dve(lambda e: e.tensor_tensor(out=out32, in0=t1_[:, 0:N], in1=t2_[:, 0:N], op=ALU.add), ["t1", "t2"], list(out32_keys))
                act(lambda e: e.copy(out=out_bf, in_=out32), list(out32_keys), out_keys)
            else:
                dve(lambda e: e.tensor_tensor(out=out_bf, in0=t1_[:, 0:N], in1=t2_[:, 0:N], op=ALU.add), ["t1", "t2"], out_keys)

        def cmul_add(eng, dr, di, sr, si, lr, li, lin, kdr, kdi, ksr, ksi, T, kT):
            o = lambda fn, R, W: S.op(eng, fn, reads=R, writes=W)
            o(lambda e: e.tensor_tensor(out=T[0], in0=sr, in1=lr, op=ALU.mult), ksr + ["LR"], kT[0])
            o(lambda e: e.tensor_tensor(out=T[1], in0=si, in1=lin, op=ALU.mult), ksi + ["LIn"], kT[1])
            o(lambda e: e.tensor_tensor(out=T[2], in0=si, in1=lr, op=ALU.mult), ksi + ["LR"], kT[2])
            o(lambda e: e.tensor_tensor(out=T[3], in0=sr, in1=li, op=ALU.mult), ksr + ["LI"], kT[3])
            o(lambda e: e.tensor_tensor(out=dr, in0=dr, in1=T[0], op=ALU.add), kdr + kT[0], kdr)
            o(lambda e: e.tensor_tensor(out=di, in0=di, in1=T[2], op=ALU.add), kdi + kT[2], kdi)
            o(lambda e: e.tensor_tensor(out=dr, in0=dr, in1=T[1], op=ALU.add), kdr + kT[1], kdr)
            o(lambda e: e.tensor_tensor(out=di, in0=di, in1=T[3], op=ALU.add), kdi + kT[3], kdi)

        kXS = kXSr + kXSi

        def lam_b(tab, l, lev, tp0, ntp, shape):
            return tab[:, l, lev, tp0:tp0 + ntp].rearrange("p (t o) -> p t o", o=1).to_broadcast(shape)

        def ssm_group_prompt(l, ct, N, first_block):
            for i in range(4):
                tp = 4 * ct + i
                for ri, X, kX in ((0, XSr, kXSr), (1, XSi, kXSi)):
                    b = bank()
                    S.op("pe", lambda e, tp=tp, ri=ri, b=b: e.matmul(PS[b][:, 0:N], lhsT=W2[:, l, tp, ri, :], rhs=uT[:, ct, 0:N],
                                                                    start=True, stop=True), reads=["W2", "uT"], writes=[("ps", b)])
                    act(lambda e, X=X, i=i, b=b: e.copy(out=X[:, i * TB:i * TB + N], in_=PS[b][:, 0:N]), [("ps", b)], kX[2 * i:2 * i + 2])
            Xr3 = XSr.rearrange("p (t n) -> p t n", t=4)
            Xi3 = XSi.rearrange("p (t n) -> p t n", t=4)
            parts = [("dve", 0, 4, TD)]
            nlev = int(math.log2(N))
            steps = []
            if not first_block:
                steps.append(("carry", 0))
            for lev in range(nlev):
                steps.append(("up", lev))
            for lev in range(nlev - 2, -1, -1):
                steps.append(("down", lev))
            for kind, lev in steps:
                for eng, a0, na, TT in parts:
                    kr = kXSr[2 * a0:2 * (a0 + na)]; ki = kXSi[2 * a0:2 * (a0 + na)]
                    Xr = Xr3[:, a0:a0 + na, :]; Xi = Xi3[:, a0:a0 + na, :]
                    if kind == "carry":
                        m = 1
                        dr, di = Xr[:, :, 0:1], Xi[:, :, 0:1]
                        sr = car_r[:, l, 4 * ct + a0:4 * ct + a0 + na].rearrange("p (t o) -> p t o", o=1)
                        si = car_i[:, l, 4 * ct + a0:4 * ct + a0 + na].rearrange("p (t o) -> p t o", o=1)
                        ksr = ksi = ["car"]
                    else:
                        d = 1 << lev
                        if kind == "up":
                            m = N // (2 * d)
                            Xr4 = Xr.rearrange("p t (m s) -> p t m s", s=2 * d)
                            Xi4 = Xi.rearrange("p t (m s) -> p t m s", s=2 * d)
                        else:
                            m = N // (2 * d) - 1
                            Xr4 = Xr[:, :, d:N - d].rearrange("p t (m s) -> p t m s", s=2 * d)
                            Xi4 = Xi[:, :, d:N - d].rearrange("p t (m s) -> p t m s", s=2 * d)
                        dr, di = Xr4[:, :, :, 2 * d - 1], Xi4[:, :, :, 2 * d - 1]
                        sr, si = Xr4[:, :, :, d - 1], Xi4[:, :, :, d - 1]
                        ksr, ksi = kr, ki
                    sh = [128, na, m]
                    if kind != "carry" and m >= 63:
                        for (dst, src, tab, kd_, ks_) in ((dr, sr, LR, kr, kr), (di, si, LR, ki, ki), (dr, si, LIn, kr, ki), (di, sr, LI, ki, kr)):
                            for tpi in range(na):
                                tg = 4 * ct + a0 + tpi
                                dve(lambda e, dst=dst, src=src, tab=tab, tpi=tpi, tg=tg: e.scalar_tensor_tensor(
                                    out=dst[:, tpi, :], in0=src[:, tpi, :], scalar=tab[:, l, lev, tg:tg + 1], in1=dst[:, tpi, :],
                                    op0=ALU.mult, op1=ALU.add),
                                    ks_[2 * tpi:2 * tpi + 2] + kd_[2 * tpi:2 * tpi + 2] + ["LR", "LI", "LIn"], kd_[2 * tpi:2 * tpi + 2])
                        continue
                    T = [t_[0].rearrange("p (t n) -> p t n", t=na)[:, :, 0:m] for t_ in TT]
                    kT = [t_[1] for t_ in TT]
                    cmul_add(eng, dr, di, sr, si, lam_b(LR, l, lev, 4 * ct + a0, na, sh), lam_b(LI, l, lev, 4 * ct + a0, na, sh),
                             lam_b(LIn, l, lev, 4 * ct + a0, na, sh), kr, ki, ksr, ksi, T, kT)
            dve(lambda e: e.tensor_copy(out=car_r[:, l, 4 * ct:4 * ct + 4], in_=Xr3[:, :, N - 1]), kXSr, ["car"])
            dve(lambda e: e.tensor_copy(out=car_i[:, l, 4 * ct:4 * ct + 4], in_=Xi3[:, :, N - 1]), kXSi, ["car"])
            act(lambda e: e.copy(out=xbr[:], in_=XSr), kXSr, kxbr)
            act(lambda e: e.copy(out=xbi[:], in_=XSi), kXSi, kxbi)
            return ssm_y(l, ct, N)

        def ssm_y(l, ct, N):
            b = bank()
            n = 0
            for i in range(4):
                tp = 4 * ct + i
                for ri, xb_, kx in ((0, xbr, kxbr), (1, xbi, kxbi)):
                    S.op("pe", lambda e, tp=tp, ri=ri, xb_=xb_, i=i, b=b, n=n: e.matmul(
                        PS[b][:, 0:N], lhsT=CP[:, l, tp, ri, :], rhs=xb_[:, i * TB:i * TB + N], start=(n == 0), stop=False),
                        reads=["CP"] + kx, writes=[("ps", b)])
                    n += 1
            S.op("pe", lambda e, b=b: e.matmul(PS[b][:, 0:N], lhsT=Dd[:, l, ct, :], rhs=uT[:, ct, 0:N], start=False, stop=True),
                 reads=["Dd", "uT"], writes=[("ps", b)])
            return b

        def ssm_group_sample(l, ct):
            N = NS
            for i in range(4):
                tp = 4 * ct + i
                for ri, X in ((0, XSr), (1, XSi)):
                    b = bank()
                    S.op("pe", lambda e, tp=tp, ri=ri, b=b: e.matmul(PS[b][:, 0:N], lhsT=W2[:, l, tp, ri, :], rhs=uT[:, ct, 0:N],
                                                                    start=True, stop=True), reads=["W2", "uT"], writes=[("ps", b)])
                    act(lambda e, X=X, i=i, b=b: e.copy(out=X[:, i * TB:i * TB + N], in_=PS[b][:, 0:N]), [("ps", b)], kXS)
            Xr4 = XSr.rearrange("p (t n) -> p t n", t=4)[:, :, 0:NS].rearrange("p t (b i) -> p t b i", i=4)
            Xi4 = XSi.rearrange("p (t n) -> p t n", t=4)[:, :, 0:NS].rearrange("p t (b i) -> p t b i", i=4)
            sh = [128, 4, NSB]
            T = [t_[0][:, 0:4 * NSB].rearrange("p (t n) -> p t n", t=4) for t_ in TD]
            kT = [t_[1] for t_ in TD]
            lr, li, lin = lam_b(LR, l, 0, 4 * ct, 4, sh), lam_b(LI, l, 0, 4 * ct, 4, sh), lam_b(LIn, l, 0, 4 * ct, 4, sh)
            for i in range(4):
                if i == 0:
                    sr, si, ksr, ksi = hs_r[:, 4 * ct:4 * ct + 4, :], hs_i[:, 4 * ct:4 * ct + 4, :], ["hs"], ["hs"]
                else:
                    sr, si, ksr, ksi = Xr4[:, :, :, i - 1], Xi4[:, :, :, i - 1], kXSr, kXSi
                cmul_add("dve", Xr4[:, :, :, i], Xi4[:, :, :, i], sr, si, lr, li, lin, kXSr, kXSi, ksr, ksi, T, kT)
            dve(lambda e: e.tensor_copy(out=hs_r[:, 4 * ct:4 * ct + 4, :], in_=Xr4[:, :, :, 3]), kXS, ["hs"])
            dve(lambda e: e.tensor_copy(out=hs_i[:, 4 * ct:4 * ct + 4, :], in_=Xi4[:, :, :, 3]), kXS, ["hs"])
            act(lambda e: e.copy(out=xbr[:], in_=XSr), kXS, kxbr)
            act(lambda e: e.copy(out=xbi[:], in_=XSi), kXS, kxbi)
            return ssm_y(l, ct, N)

        def layer_block(l, N, sample, blk):
            first_block = (blk == 0)
            last_block = (blk == NBLK - 1)
            fpm[0] = (blk == 0 and not sample)
            fp = fpm[0]
            wvin = (w_in if fp else wb_in)[l].rearrange("(k p) c -> p k c", p=128)
            v8 = lambda ws: WS[ws][:].rearrange("p (k c) -> p k c", k=8)
            norm_block(N, gA[:, l, :])
            ws = wslot()
            wq_ = WS[ws][:].rearrange("p (k t two d) -> p k t two d", k=8, t=4, two=2)
            for two in range(2):
                for k in range(8):
                    S.op(wq(), lambda e, two=two, k=k: e.dma_start(
                        out=wq_[:, k, :, two, :], in_=wvin[:, k, 256 * two:256 * two + 256].rearrange("p (t d) -> p t d", d=64)),
                        reads=wr(("wb_in", l)), writes=wk(ws), dma=("ws", ws))
            wqv = v8(ws)
            for t in range(4):
                b = proj_tile(N, wqv, 128 * t, ws)
                headnorm_rope(N, b, l, gq[:, l:l + 1], blk_b, 1.0 / 64, True, qr[:, t, 0:N], ["qr"])
            ws = load_w(lambda w: v8(w)[:, :, 0:256], wvin[:, :, K_OFF:K_OFF + 256], ("wb_in", l))
            wkv_ = v8(ws)
            b = proj_tile(N, wkv_, 0, ws)
            kdst = kTc[:, l, 128:128 + N] if not sample else kTc[:, l, 0:N]
            headnorm_rope(N, b, l, gk[:, l:l + 1], blk_b, 1.0 / 64, True, kdst, ["kTc"], out32=k32[:, 0:N], out32_keys=["k32"])
            nsub = max(1, N // 128)
            pn = min(N, 128)
            b = bank()
            for s in range(nsub):
                for k in range(8):
                    S.op("pe", lambda e, s=s, k=k, b=b: e.matmul(PS[b][0:pn, 128 * s:128 * s + 128], lhsT=hT[:, k, 128 * s:128 * s + pn],
                                                                rhs=wkv_[:, k, 128:256], start=(k == 0), stop=(k == 7)),
                         reads=["hT", ("ws", ws)], writes=[("ps", b)])
            act(lambda e, b=b: e.copy(out=v32[0:pn, 0:nsub, :], in_=PS[b][0:pn, 0:128 * nsub].rearrange("p (s c) -> p s c", c=128)),
                [("ps", b)], ["v32"])
            vdst = vtc[0:pn, l, 1:1 + nsub, :] if not sample else vtc[0:pn, l, 0:1, :]
            dve(lambda e: e.tensor_copy(out=vdst, in_=v32[0:pn, 0:nsub, :]), ["v32"], ["vtc"])
            if not sample:
                for s in range(nsub):
                    for h in range(2):
                        hs = slice(64 * h, 64 * h + 64)
                        pt = PT[(2 * s + h) % 2]; kpt = "PT%d" % ((2 * s + h) % 2)
                        use_prev = not (first_block and s == 0)
                        parts = ([0] if use_prev else []) + [1]
                        sbk = {}
                        for part in parts:
                            b = bank(); sbk[part] = b
                            c0 = 128 * s + 128 * part
                            S.op("pe", lambda e, b=b, c0=c0, hs=hs, s=s: e.matmul(
                                PS[b][:].rearrange("p (t c) -> p t c", t=4), lhsT=kTc[hs, l, c0:c0 + 128],
                                rhs=qr[hs, :, 128 * s:128 * s + 128], start=True, stop=True),
                                reads=["kTc", "qr"], writes=[("ps", b)])
                            act(lambda e, b=b, part=part, pt=pt: e.activation(out=pt[:, part, :], in_=PS[b][:], func=AF.Exp, scale=0.125),
                                [("ps", b)], [kpt])
                            mk_ = mprev_b if part == 0 else mcur_b
                            S.op("pool", lambda e, part=part, pt=pt, mk_=mk_: e.tensor_tensor(
                                out=pt[:, part, :].rearrange("p (t c) -> p t c", t=4), in0=pt[:, part, :].rearrange("p (t c) -> p t c", t=4),
                                in1=mk_[:].rearrange("p (o c) -> p o c", o=1).to_broadcast([128, 4, 128]), op=ALU.mult),
                                reads=[kpt, "mprev_b", "mcur_b"], writes=[kpt])
                        bo = bank(); bd = bank()
                        for n_, part in enumerate(parts):
                            S.op("pe", lambda e, part=part, n_=n_, bo=bo, pt=pt, s=s, hs=hs: e.matmul(
                                PS[bo][hs, :], lhsT=vtc[:, l, s + part, hs], rhs=pt[:, part, :], start=(n_ == 0), stop=(n_ == len(parts) - 1)),
                                reads=["vtc", kpt], writes=[("ps", bo)])
                        for n_, part in enumerate(parts):
                            S.op("pe", lambda e, part=part, n_=n_, bd=bd, pt=pt, hs=hs: e.matmul(
                                PS[bd][hs, :], lhsT=ones_b[:, hs], rhs=pt[:, part, :], start=(n_ == 0), stop=(n_ == len(parts) - 1)),
                                reads=["ones_b", kpt], writes=[("ps", bd)])
                        dd = dn[h]; kd = "dn%d" % h
                        dve(lambda e, bd=bd, dd=dd, hs=hs: e.tensor_tensor(
                            out=dd[hs, :].rearrange("p (t c) -> p t c", t=4), in0=PS[bd][hs, :].rearrange("p (t c) -> p t c", t=4),
                            in1=esink[hs, l, :].rearrange("p (t o) -> p t o", o=1).to_broadcast([64, 4, 128]), op=ALU.add),
                            [("ps", bd), "esink"], [kd])
                        dve(lambda e, dd=dd, hs=hs: e.reciprocal(out=dd[hs, :], in_=dd[hs, :]), [kd], [kd])
                        dve(lambda e, bo=bo, dd=dd, hs=hs, s=s: e.tensor_tensor(
                            out=oa[hs, :, 128 * s:128 * s + 128], in0=PS[bo][hs, :].rearrange("p (t c) -> p t c", t=4),
                            in1=dd[hs, :].rearrange("p (t c) -> p t c", t=4), op=ALU.mult), [("ps", bo), kd], ["oa"])
                if last_block:
                    b = bank()
                    S.op("pe", lambda e, b=b: e.transpose(out=PS[b][:, 0:128], in_=k32[:, N - 128:N], identity=ident_f[:]),
                         reads=["k32", "ident_f"], writes=[("ps", b)])
                    act(lambda e, b=b: e.copy(out=t1[:, 0:128], in_=PS[b][:, 0:128]), [("ps", b)], ["t1"])
                    out_toks.append(S.op("sp", lambda e: e.dma_start(out=kp[l], in_=t1[:, 0:128]), reads=["t1"], dma="o_kp"))
                    out_toks.append(S.op("sp", lambda e: e.dma_start(out=vp[l], in_=v32[:, 3, :]), reads=["v32"], dma="o_vp"))
                else:
                    act(lambda e: e.copy(out=kTc[:, l, 0:128], in_=kTc[:, l, N:N + 128]), ["kTc"], ["kTc"])
                    act(lambda e: e.copy(out=vtc[:, l, 0, :], in_=vtc[:, l, 4, :]), ["vtc"], ["vtc"])
            else:
                b = bank()
                S.op("pe", lambda e, b=b: e.transpose(out=PS[b][0:NS, 0:128], in_=k32[:, 0:NS], identity=ident_f[:]),
                     reads=["k32", "ident_f"], writes=[("ps", b)])
                act(lambda e, b=b: e.copy(out=t1[0:NS, 0:128], in_=PS[b][0:NS, 0:128]), [("ps", b)], ["t1"])
                out_toks.append(S.op("sp", lambda e: e.dma_start(out=ks[l, :, 124:128, :], in_=t1[0:NS, 0:128]), reads=["t1"], dma="o_ks"))
                out_toks.append(S.op("sp", lambda e: e.dma_start(out=vs[l, :, 124:128, :], in_=v32[0:NS, 0, :]), reads=["v32"], dma="o_vs"))
                out_toks.append(S.op("sp", lambda e: e.dma_start(out=ks[l, :, 0:124, :], in_=csk[l, :, 4:128, :]), dma="o_ks"))
                out_toks.append(S.op("sp", lambda e: e.dma_start(out=vs[l, :, 0:124, :], in_=csv[l, :, 4:128, :]), dma="o_vs"))
                ptn = PT[0]; ptc = PT[1]
                for h in range(2):
                    hs = slice(64 * h, 64 * h + 64)
                    b = bank()
                    S.op("pe", lambda e, b=b, hs=hs: e.matmul(PS[b][0:NS, 0:4 * NS].rearrange("p (t c) -> p t c", t=4),
                                                             lhsT=kTc[hs, l, 0:NS], rhs=qr[hs, :, 0:NS], start=True, stop=True),
                         reads=["kTc", "qr"], writes=[("ps", b)])
                    act(lambda e, b=b, h=h: e.activation(out=ptn[0:NS, h, 0:4 * NS], in_=PS[b][0:NS, 0:4 * NS], func=AF.Exp, scale=0.125),
                        [("ps", b)], ["PT0"])
                    dve(lambda e, h=h: e.tensor_tensor(
                        out=ptn[0:NS, h, 0:4 * NS].rearrange("p (t c) -> p t c", t=4), in0=ptn[0:NS, h, 0:4 * NS].rearrange("p (t c) -> p t c", t=4),
                        in1=mnew_b[:].rearrange("p (o c) -> p o c", o=1).to_broadcast([NS, 4, NS]), op=ALU.mult), ["PT0", "mnew_b"], ["PT0"])
                bsc = bank(); held.add(bsc)
                for bb in range(NSB):
                    kst = sg[bb % 2]; kk_ = "sg%d" % (bb % 2)
                    S.op("pool", lambda e, bb=bb, kst=kst: e.dma_start(out=kst[:, 0:128], in_=csk[l, bb]), writes=[kk_], dma=kk_)
                    S.op("pool", lambda e, bb=bb, kst=kst: e.dma_start(out=kst[:, 128:256], in_=csv[l, bb]), writes=[kk_], dma=kk_)
                    b = bank()
                    pb = PS[b][:].bitcast(BF16)
                    S.op("pe", lambda e, kst=kst, pb=pb: e.transpose(out=pb[:, 0:128], in_=kst[:, 0:128], identity=ident_b[:]),
                         reads=[kk_, "ident_b"], writes=[("ps", b)])
                    act(lambda e, pb=pb, bb=bb: e.copy(out=kcT[:, bb % 2, :], in_=pb[:, 0:128]), [("ps", b)], ["kcT%d" % (bb % 2)])
                    for h in range(2):
                        hs = slice(64 * h, 64 * h + 64)
                        c0 = (bb * 2 + h) * 16
                        S.op("pe", lambda e, bb=bb, hs=hs, c0=c0: e.matmul(
                            PS[bsc][:, c0:c0 + 16].rearrange("p (t i) -> p t i", t=4), lhsT=kcT[hs, bb % 2, :],
                            rhs=qr[hs, :, 4 * bb:4 * bb + 4], start=True, stop=True),
                            reads=["kcT%d" % (bb % 2), "qr"], writes=[("ps", bsc)])
                    dve(lambda e, bb=bb, kst=kst: e.tensor_copy(out=RA[:, 128 * bb:128 * bb + 128], in_=kst[:, 128:256]), [kk_], RAK[0:4])
                act(lambda e: e.activation(out=ptc[:, 0, :], in_=PS[bsc][:], func=AF.Exp, scale=0.125), [("ps", bsc)], ["PT1"])
                held.discard(bsc)
                dve(lambda e: e.tensor_tensor(
                    out=ptc[:, 0, :].rearrange("p (a i) -> p a i", i=4), in0=ptc[:, 0, :].rearrange("p (a i) -> p a i", i=4),
                    in1=mc_b[:].rearrange("p (o i) -> p o i", o=1).to_broadcast([128, 128, 4]), op=ALU.mult), ["PT1", "mc_b"], ["PT1"])
                for h in range(2):
                    hs = slice(64 * h, 64 * h + 64)
                    for which, lw in ((0, None), (1, None)):
                        bo = bank()
                        lhs_new = vtc[0:NS, l, 0, hs] if which == 0 else ones_b[0:NS, hs]
                        S.op("pe", lambda e, bo=bo, hs=hs, h=h, lhs_new=lhs_new: e.matmul(
                            PS[bo][hs, 0:4 * NS], lhsT=lhs_new, rhs=ptn[0:NS, h, 0:4 * NS], start=True, stop=False),
                            reads=["vtc", "ones_b", "PT0"], writes=[("ps", bo)])
                        for bb in range(NSB):
                            c0 = (bb * 2 + h) * 16
                            lhs_c = RA[:, 128 * bb + 64 * h:128 * bb + 64 * h + 64] if which == 0 else ones_b[:, hs]
                            S.op("pe", lambda e, bo=bo, hs=hs, bb=bb, c0=c0, lhs_c=lhs_c: e.matmul(
                                PS[bo][hs, 0:4 * NS].rearrange("p (t c) -> p t c", t=4)[:, :, 4 * bb:4 * bb + 4], lhsT=lhs_c,
                                rhs=ptc[:, 0, c0:c0 + 16].rearrange("p (t i) -> p t i", t=4), start=False, stop=(bb == NSB - 1)),
                                reads=RAK[0:4] + ["ones_b", "PT1"], writes=[("ps", bo)])
                        if which == 0:
                            bnum = bo
                        else:
                            bden = bo
                    dd = dn[h]; kd = "dn%d" % h
                    dve(lambda e, bden=bden, dd=dd, hs=hs: e.tensor_tensor(
                        out=dd[hs, 0:4 * NS].rearrange("p (t c) -> p t c", t=4), in0=PS[bden][hs, 0:4 * NS].rearrange("p (t c) -> p t c", t=4),
                        in1=esink[hs, l, :].rearrange("p (t o) -> p t o", o=1).to_broadcast([64, 4, NS]), op=ALU.add),
                        [("ps", bden), "esink"], [kd])
                    dve(lambda e, dd=dd, hs=hs: e.reciprocal(out=dd[hs, 0:4 * NS], in_=dd[hs, 0:4 * NS]), [kd], [kd])
                    dve(lambda e, bnum=bnum, dd=dd, hs=hs: e.tensor_tensor(
                        out=oa[hs, :, 0:NS], in0=PS[bnum][hs, 0:4 * NS].rearrange("p (t c) -> p t c", t=4),
                        in1=dd[hs, 0:4 * NS].rearrange("p (t c) -> p t c", t=4), op=ALU.mult), [("ps", bnum), kd], ["oa"])
            ws = load_w(lambda w: v8(w), wvin[:, :, U_OFF:U_OFF + 512], ("wb_in", l))
            for t in range(4):
                b = proj_tile(N, v8(ws), 128 * t, ws)
                act(lambda e, b=b, t=t: e.copy(out=uT[:, t, 0:N], in_=PS[b][:, 0:N]), [("ps", b)], ["uT"])
            if sample:
                for (src_, dst_, stg, kst) in ((sre, hs_r, t1, "t1"), (sim, hs_i, t2, "t2")):
                    stv = stg[:].rearrange("p (a q) -> p a q", a=4)
                    s2 = src_[l].rearrange("b g p -> (b g) p").rearrange("(a q) p -> q a p", q=128)
                    for dup in range(2):
                        S.op("sp", lambda e, dup=dup, stv=stv, s2=s2: e.dma_start(out=stv[:, :, 64 * dup:64 * dup + 64], in_=s2),
                             writes=[kst], dma=kst)
                    for a in range(4):
                        b = bank()
                        S.op("pe", lambda e, a=a, b=b, stv=stv: e.transpose(out=PS[b][:, 0:128], in_=stv[:, a, :], identity=ident_f[:]),
                             reads=[kst, "ident_f"], writes=[("ps", b)])
                        for gl in range(2):
                            act(lambda e, a=a, b=b, gl=gl, dst_=dst_: e.copy(
                                out=dst_[64 * gl:64 * gl + 64, :, 4 * a:4 * a + 4],
                                in_=PS[b][64 * gl:64 * gl + 64, 0:128].rearrange("p (b tp gl) -> p gl tp b", gl=2, tp=16)[:, gl]),
                                [("ps", b)], ["hs"])
            for ct in range(4):
                by = ssm_group_sample(l, ct) if sample else ssm_group_prompt(l, ct, N, first_block)
                act(lambda e, by=by, ct=ct: e.activation(out=zT[:, ct, 0:N], in_=PS[by][:, 0:N], func=AF.Gelu), [("ps", by)], ["zT"])
            if sample:
                for (dram_, buf_, stg, kst) in ((hrs, hs_r, t1, "t1"), (his, hs_i, t2, "t2")):
                    for a in range(4):
                        act(lambda e, a=a, buf_=buf_: e.copy(out=qf[:, 0:64].rearrange("p (b tp) -> p b tp", b=4),
                                                             in_=buf_[:, :, 4 * a:4 * a + 4].rearrange("p tp b -> p b tp")), ["hs"], ["qf"])
                        b = bank()
                        S.op("pe", lambda e, b=b: e.transpose(out=PS[b][0:64, 0:128], in_=qf[:, 0:64], identity=ident_f[:]),
                             reads=["qf", "ident_f"], writes=[("ps", b)])
                        act(lambda e, a=a, b=b, stg=stg: e.copy(out=stg[0:64, 128 * a:128 * a + 128], in_=PS[b][0:64, 0:128]), [("ps", b)], [kst])
                    out_toks.append(S.op("sp", lambda e, dram_=dram_, stg=stg: e.dma_start(
                        out=dram_[l].rearrange("b (tp gl) p -> (b tp) (gl p)", gl=2).rearrange("(a r) c -> r a c", a=4),
                        in_=stg[0:64, :].rearrange("p (a c) -> p a c", a=4)), reads=[kst], dma="o_hs"))
            elif last_block:
                for gl in range(2):
                    sl = slice(64 * gl, 64 * gl + 64)
                    out_toks.append(S.op("sp", lambda e, gl=gl, sl=sl: e.dma_start(
                        out=hrp[l].rearrange("(tp gl) p -> gl p tp", gl=2)[gl], in_=car_r[sl, l, :], allow_slow_non_contiguous=True),
                        reads=["car"], dma="o_hp"))
                    out_toks.append(S.op("sp", lambda e, gl=gl, sl=sl: e.dma_start(
                        out=hip[l].rearrange("(tp gl) p -> gl p tp", gl=2)[gl], in_=car_i[sl, l, :], allow_slow_non_contiguous=True),
                        reads=["car"], dma="o_hp"))
            ws = load_w(lambda w: WS[w][:, 0:2048].rearrange("p (k c) -> p k c", k=4), (w_glu if fp else wb_glu)[l].rearrange("(k p) c -> p k c", p=128),
                        ("wb_glu", l))
            wg = WS[ws][:, 0:2048].rearrange("p (k c) -> p k c", k=4)
            for t in range(4):
                b = bank()
                for k in range(4):
                    S.op("pe", lambda e, b=b, k=k, t=t: e.matmul(PS[b][:, 0:N], lhsT=wg[:, k, 128 * t:128 * t + 128], rhs=zT[:, k, 0:N],
                                                                start=(k == 0), stop=(k == 3)), reads=["zT", ("ws", ws)], writes=[("ps", b)])
                s_ = sg[t % 2]; ks_ = "sg%d" % (t % 2)
                act(lambda e, b=b, s_=s_: e.activation(out=s_[:, 0:N], in_=PS[b][:, 0:N], func=AF.Sigmoid), [("ps", b)], [ks_])
                dve(lambda e, t=t, s_=s_: e.tensor_tensor(out=ob[:, t, 0:N], in0=zT[:, t, 0:N], in1=s_[:, 0:N], op=ALU.mult), ["zT", ks_], ["ob"])
            ws = load_w(lambda w: v8(w), wvin[:, :, MQ_OFF:MQ_OFF + 512], ("wb_in", l))
            for t in range(4):
                b = proj_tile(N, v8(ws), 128 * t, ws)
                headnorm_rope(N, b, l, gmq[:, l:l + 1], ones_b, 1.0 / 128, False, qmn[:, t, 0:N], ["qmn"])
            sc_m = 1.0 / math.sqrt(128.0)
            if not sample:
                for h in range(4):
                    pt = PT[h % 2]; kpt = "PT%d" % (h % 2)
                    for kt in range(2):
                        b = bank()
                        S.op("pe", lambda e, b=b, h=h, kt=kt: e.matmul(PS[b][:, 0:N], lhsT=MKT[:, l, h, 128 * kt:128 * kt + 128], rhs=qmn[:, h, 0:N],
                                                                      start=True, stop=True), reads=["MKT", "qmn"], writes=[("ps", b)])
                        act(lambda e, b=b, kt=kt, pt=pt: e.activation(out=pt[:, kt, 0:N], in_=PS[b][:, 0:N], func=AF.Exp, scale=sc_m),
                            [("ps", b)], [kpt])
                    bo = bank(); bd = bank()
                    for kt in range(2):
                        S.op("pe", lambda e, bo=bo, h=h, kt=kt, pt=pt: e.matmul(PS[bo][:, 0:N], lhsT=MV[:, l, kt, 128 * h:128 * h + 128],
                                                                               rhs=pt[:, kt, 0:N], start=(kt == 0), stop=(kt == 1)),
                             reads=["MV", kpt], writes=[("ps", bo)])
                    for kt in range(2):
                        S.op("pe", lambda e, bd=bd, kt=kt, pt=pt: e.matmul(PS[bd][:, 0:N], lhsT=ones_b[:], rhs=pt[:, kt, 0:N],
                                                                          start=(kt == 0), stop=(kt == 1)),
                             reads=["ones_b", kpt], writes=[("ps", bd)])
                    dd = dn[h % 2]; kd = "dn%d" % (h % 2)
                    dve(lambda e, bd=bd, dd=dd: e.reciprocal(out=dd[:, 0:N], in_=PS[bd][:, 0:N]), [("ps", bd)], [kd])
                    dve(lambda e, bo=bo, dd=dd, h=h: e.tensor_tensor(out=oc[:, h, 0:N], in0=PS[bo][:, 0:N], in1=dd[:, 0:N], op=ALU.mult),
                        [("ps", bo), kd], ["oc"])
            else:
                bsc = bank(); held.add(bsc)
                bo = bank(); held.add(bo)
                ptc = PT[1]
                for bb in range(NSB):
                    wsk = wslot()
                    kcb = WS[wsk][:, 0:1024].rearrange("p (a c) -> p a c", a=2)
                    vcb = WS[wsk][:, 1024:2048].rearrange("p (a c) -> p a c", a=2)
                    S.op("pool", lambda e, bb=bb, kcb=kcb: e.dma_start(out=kcb, in_=cmk[l, bb].rearrange("(a p) c -> p a c", p=128)),
                         writes=wk(wsk), dma=("ws", wsk))
                    S.op("pool", lambda e, bb=bb, vcb=vcb: e.dma_start(out=vcb, in_=cmv[l, bb].rearrange("(a p) c -> p a c", p=128)),
                         writes=wk(wsk), dma=("ws", wsk))
                    for h in range(4):
                        for kt in range(2):
                            b = bank()
                            pb = PS[b][:].bitcast(BF16)
                            o0 = 2048 + 256 * h + 128 * kt
                            S.op("pe", lambda e, pb=pb, kcb=kcb, h=h, kt=kt: e.transpose(out=pb[:, 0:128], in_=kcb[:, kt, 128 * h:128 * h + 128],
                                                                                        identity=ident_b[:]),
                                 reads=[("ws", wsk), "ident_b"], writes=[("ps", b)])
                            act(lambda e, pb=pb, wsk=wsk, o0=o0: e.copy(out=WS[wsk][:, o0:o0 + 128], in_=pb[:, 0:128]),
                                [("ps", b)], [("wsT", wsk)])
                    for h in range(4):
                        for kt in range(2):
                            c0 = ((bb * 4 + h) * 2 + kt) * 4
                            o0 = 2048 + 256 * h + 128 * kt
                            S.op("pe", lambda e, h=h, c0=c0, bb=bb, wsk=wsk, o0=o0: e.matmul(
                                PS[bsc][:, c0:c0 + 4], lhsT=WS[wsk][:, o0:o0 + 128],
                                rhs=qmn[:, h, 4 * bb:4 * bb + 4], start=True, stop=True),
                                reads=[("wsT", wsk), "qmn"], writes=[("ps", bsc)])
                    c0 = bb * 32
                    act(lambda e, c0=c0: e.activation(out=ptc[:, 0, c0:c0 + 32], in_=PS[bsc][:, c0:c0 + 32], func=AF.Exp, scale=sc_m),
                        [("ps", bsc)], ["PT1"])
                    for h in range(4):
                        for kt in range(2):
                            c1 = ((bb * 4 + h) * 2 + kt) * 4
                            S.op("pe", lambda e, h=h, kt=kt, c1=c1, bb=bb, vcb=vcb: e.matmul(
                                PS[bo][:, h * NS + 4 * bb:h * NS + 4 * bb + 4], lhsT=vcb[:, kt, 128 * h:128 * h + 128],
                                rhs=ptc[:, 0, c1:c1 + 4], start=(kt == 0), stop=(kt == 1)),
                                reads=[("ws", wsk), "PT1"], writes=[("ps", bo)])
                bd = bank()
                pv = ptc[:, 0, :].rearrange("p (b h kt i) -> p h kt b i", b=NSB, h=4, kt=2)
                for h in range(4):
                    for kt in range(2):
                        S.op("pe", lambda e, bd=bd, kt=kt, h=h: e.matmul(PS[bd][:, h * NS:(h + 1) * NS].rearrange("p (b i) -> p b i", i=4),
                                                                        lhsT=ones_b[:], rhs=pv[:, h, kt, :, :], start=(kt == 0), stop=(kt == 1)),
                             reads=["ones_b", "PT1"], writes=[("ps", bd)])
                dd = dn[0]
                dve(lambda e, bd=bd: e.reciprocal(out=dd[:, 0:4 * NS], in_=PS[bd][:, 0:4 * NS]), [("ps", bd)], ["dn0"])
                dve(lambda e, bo=bo: e.tensor_tensor(out=oc[:, :, 0:NS], in0=PS[bo][:, 0:4 * NS].rearrange("p (h c) -> p h c", h=4),
                                                     in1=dd[:, 0:4 * NS].rearrange("p (h c) -> p h c", h=4), op=ALU.mult),
                    [("ps", bo), "dn0"], ["oc"])
                held.discard(bsc); held.discard(bo)
            macc3 = macc.rearrange("p (m n) -> p m n", m=8)
            mgT3 = mgT.rearrange("p (m n) -> p m n", m=8)
            for n_, on_ in enumerate((oa, ob, oc)):
                okey = ("oa", "ob", "oc")[n_]
                wsb = wslot()
                wbv = WS[wsb][:].rearrange("p (k c) -> p k c", k=4)
                if n_ == 0:
                    for t in range(4):
                        for two in range(2):
                            r0 = (two * 4 + t) * 64
                            S.op(wq(), lambda e, t=t, two=two, r0=r0: e.dma_start(out=wbv[64 * two:64 * two + 64, t, :],
                                                                                 in_=(w_br if fp else wb_br)[l, 0, r0:r0 + 64, :]),
                                 reads=wr(("wb_br", l)), writes=wk(wsb), dma=("ws", wsb))
                else:
                    S.op(wq(), lambda e, n_=n_: e.dma_start(out=wbv, in_=(w_br if fp else wb_br)[l, n_].rearrange("(k p) c -> p k c", p=128)),
                         reads=wr(("wb_br", l)), writes=wk(wsb), dma=("ws", wsb))
                for half in range(2):
                    wsg = load_w(lambda w: v8(w), wvin[:, :, G_OFF + n_ * 1024 + 512 * half:G_OFF + n_ * 1024 + 512 * half + 512], ("wb_in", l))
                    for mm_ in range(4):
                        m = 4 * half + mm_
                        bg = proj_tile(N, v8(wsg), 128 * mm_, wsg)
                        s_ = sg[m % 2]; ks_ = "sg%d" % (m % 2)
                        act(lambda e, bg=bg, s_=s_: e.activation(out=s_[:, 0:N], in_=PS[bg][:, 0:N], func=AF.Sigmoid), [("ps", bg)], [ks_])
                        bp = bank()
                        for k in range(4):
                            S.op("pe", lambda e, bp=bp, k=k, m=m, on_=on_: e.matmul(PS[bp][:, 0:N], lhsT=wbv[:, k, 128 * m:128 * m + 128],
                                                                                  rhs=on_[:, k, 0:N], start=(k == 0), stop=(k == 3)),
                                 reads=[okey, ("ws", wsb)], writes=[("ps", bp)])
                        if n_ == 0:
                            dve(lambda e, bp=bp, s_=s_, m=m: e.tensor_tensor(out=macc3[:, m, 0:N], in0=PS[bp][:, 0:N], in1=s_[:, 0:N], op=ALU.mult),
                                [("ps", bp), ks_], kmacc)
                        else:
                            dve(lambda e, bp=bp, s_=s_: e.tensor_tensor(out=t1[:, 0:N], in0=PS[bp][:, 0:N], in1=s_[:, 0:N], op=ALU.mult),
                                [("ps", bp), ks_], ["t1"])
                            if n_ == 1:
                                dve(lambda e, m=m: e.tensor_tensor(out=macc3[:, m, 0:N], in0=macc3[:, m, 0:N], in1=t1[:, 0:N], op=ALU.add),
                                    kmacc + ["t1"], kmacc)
                            else:
                                dve(lambda e, m=m: e.tensor_tensor(out=mgT3[:, m, 0:N], in0=macc3[:, m, 0:N], in1=t1[:, 0:N], op=ALU.add),
                                    kmacc + ["t1"], kmgT)
            for half in range(2):
                wso = load_w(lambda w: v8(w), (w_out if fp else wb_out)[l].rearrange("(k p) c -> p k c", p=128)[:, :, 512 * half:512 * half + 512], ("wb_out", l))
                for mm_ in range(4):
                    m = 4 * half + mm_
                    b = bank()
                    for k in range(8):
                        S.op("pe", lambda e, b=b, k=k, mm_=mm_, wso=wso: e.matmul(PS[b][:, 0:N], lhsT=v8(wso)[:, k, 128 * mm_:128 * mm_ + 128],
                                                                                 rhs=mgT3[:, k, 0:N], start=(k == 0), stop=(k == 7)),
                             reads=kmgT + [("ws", wso)], writes=[("ps", b)])
                    dve(lambda e, b=b, m=m: e.tensor_tensor(out=xT[:, m, 0:N], in0=xT[:, m, 0:N], in1=PS[b][:, 0:N], op=ALU.add),
                        [("ps", b), "xT"], ["xT"])
            norm_block(N, gF[:, l, :])
            wvup = (w_up if fp else wb_up)[l].rearrange("(k p) c -> p k c", p=128)

            def actT(j):
                return RA[:, j * TB:(j + 1) * TB], [RAK[j]]

            for grp in range(6):
                nt = 4 if grp < 5 else 2
                wsg = load_w(lambda w: v8(w)[:, :, 0:128 * nt], wvup[:, :, 512 * grp:512 * grp + 128 * nt], ("wb_up", l))
                wsu = load_w(lambda w: v8(w)[:, :, 0:128 * nt], wvup[:, :, DFF + 512 * grp:DFF + 512 * grp + 128 * nt], ("wb_up", l))
                for jj in range(nt):
                    j = 4 * grp + jj
                    bg = proj_tile(N, v8(wsg), 128 * jj, wsg)
                    bu = proj_tile(N, v8(wsu), 128 * jj, wsu)
                    s_ = sg[j % 2]; ks_ = "sg%d" % (j % 2)
                    act(lambda e, bg=bg, s_=s_: e.activation(out=s_[:, 0:N], in_=PS[bg][:, 0:N], func=AF.Silu), [("ps", bg)], [ks_])
                    av, ak = actT(j)
                    dve(lambda e, bu=bu, s_=s_, av=av: e.tensor_tensor(out=av[:, 0:N], in0=PS[bu][:, 0:N], in1=s_[:, 0:N], op=ALU.mult),
                        [("ps", bu), ks_], ak)
            wvdn = (w_dn if fp else wb_dn)[l].rearrange("(k p) c -> p k c", p=128)
            for q4 in range(4):
                wsl = []
                for hh in range(2):
                    w_ = wslot()
                    S.op(wq(), lambda e, w_=w_, hh=hh, q4=q4: e.dma_start(
                        out=WS[w_][:, 0:11 * 256].rearrange("p (k c) -> p k c", k=11), in_=wvdn[:, 11 * hh:11 * hh + 11, 256 * q4:256 * q4 + 256]),
                        reads=wr(("wb_dn", l)), writes=wk(w_), dma=("ws", w_))
                    wsl.append(w_)
                for mm_ in range(2):
                    m = 2 * q4 + mm_
                    b = bank()
                    for j in range(22):
                        w_ = wsl[j // 11]
                        wv_ = WS[w_][:, 0:11 * 256].rearrange("p (k c) -> p k c", k=11)
                        av, ak = actT(j)
                        S.op("pe", lambda e, b=b, j=j, mm_=mm_, wv_=wv_, av=av: e.matmul(
                            PS[b][:, 0:N], lhsT=wv_[:, j % 11, 128 * mm_:128 * mm_ + 128], rhs=av[:, 0:N], start=(j == 0), stop=(j == 21)),
                            reads=ak + [("ws", w_)], writes=[("ps", b)])
                    dve(lambda e, b=b, m=m: e.tensor_tensor(out=xT[:, m, 0:N], in0=xT[:, m, 0:N], in1=PS[b][:, 0:N], op=ALU.add),
                        [("ps", b), "xT"], ["xT"])


        def run_block(blk, sample):
            N = NS if sample else TB
            nsub = max(1, N // 128)
            pn = min(N, 128)
            if sample:
                S.op("sp", lambda e: e.dma_start(out=xtok[0:NS, 0, :], in_=xs), writes=["xtok"], dma="xtok")
                S.op("sp", lambda e: e.dma_start(out=cosb[:, 0:NS], in_=c_cos_s), writes=["cosb"], dma="cosb")
                S.op("sp", lambda e: e.dma_start(out=sinb[:, 0:NS], in_=c_sin_s), writes=["sinb"], dma="sinb")
            else:
                t0 = blk * TB
                S.op("sp", lambda e: e.dma_start(out=xtok[:], in_=xp[t0:t0 + TB, :].rearrange("(s p) d -> p s d", p=128)),
                     writes=["xtok"], dma="xtok")
                S.op("sp", lambda e: e.dma_start(out=cosb[:], in_=c_cos[:, t0:t0 + TB]), writes=["cosb"], dma="cosb")
                S.op("sp", lambda e: e.dma_start(out=sinb[:], in_=c_sin[:, t0:t0 + TB]), writes=["sinb"], dma="sinb")
            for s in range(nsub):
                for k in range(8):
                    b = bank()
                    S.op("pe", lambda e, s=s, k=k, b=b: e.transpose(out=PS[b][:, 0:pn], in_=xtok[0:pn, s, 128 * k:128 * k + 128],
                                                                    identity=ident_f[0:pn, 0:pn]),
                         reads=["xtok", "ident_f"], writes=[("ps", b)])
                    act(lambda e, s=s, k=k, b=b: e.copy(out=xT[:, k, 128 * s:128 * s + pn], in_=PS[b][:, 0:pn]), [("ps", b)], ["xT"])
            for l in range(NL):
                layer_block(l, N, sample, blk)
            for s in range(nsub):
                for k in range(8):
                    b = bank()
                    S.op("pe", lambda e, s=s, k=k, b=b: e.transpose(out=PS[b][0:pn, 0:128], in_=xT[:, k, 128 * s:128 * s + pn],
                                                                    identity=ident_f[:]),
                         reads=["xT", "ident_f"], writes=[("ps", b)])
                    act(lambda e, s=s, k=k, b=b: e.copy(out=xtok[0:pn, s, 128 * k:128 * k + 128], in_=PS[b][0:pn, 0:128]),
                        [("ps", b)], ["xtok"])
            if sample:
                out_toks.append(S.op("sp", lambda e: e.dma_start(out=ys, in_=xtok[0:NS, 0, :]), reads=["xtok"], dma="o_y"))
            else:
                t0 = blk * TB
                out_toks.append(S.op("sp", lambda e: e.dma_start(out=yp[t0:t0 + TB, :].rearrange("(s p) d -> p s d", p=128), in_=xtok[:]),
                                     reads=["xtok"], dma="o_y"))

        for blk in range(NBLK):
            run_block(blk, False)
            if blk == 0:
                fpm[0] = False
                emit_conv(10 ** 6)
        run_block(0, True)
        S.wait_all("sp", out_toks)
        with nc.allow_non_contiguous_dma(reason="small strided parameter / state transfers"):
            S.emit()
    return nc


_NC_CACHE = {}


def _consts():
    c = {}
    c["c_ident"] = np.eye(128, dtype=np.float32)
    blk = np.zeros((128, 128), np.float32)
    blk[:64, :64] = 1.0
    blk[64:, 64:] = 1.0
    c["c_blk"] = blk
    p = np.arange(128)
    lo = (p % 64) < 32
    partner = np.where(lo, p + 32, p - 32)
    perm = np.zeros((128, 128), np.float32)
    perm[partner, p] = 1.0
    c["c_perm"] = perm
    j = np.arange(128)[:, None]
    i = np.arange(128)[None, :]
    c["c_mprev"] = (j > i).astype(np.float32)
    c["c_mcur"] = (j <= i).astype(np.float32)
    half = 32
    inv = (np.float32(10000.0) ** (-(np.arange(half, dtype=np.float32) / np.float32(half)))).astype(np.float32)
    invp = inv[p % 32]
    sign = np.where(lo, -1.0, 1.0).astype(np.float32)

    def tables(pos):
        ang = (pos[None, :].astype(np.float32) * invp[:, None]).astype(np.float32)
        return np.cos(ang).astype(np.float32), (np.sin(ang) * sign[:, None]).astype(np.float32)

    c["c_cos"], c["c_sin"] = tables(np.arange(SEQ, dtype=np.float32))
    pos_s = np.float32(PAST) + np.tile(np.arange(4, dtype=np.float32), NSB)
    c["c_cos_s"], c["c_sin_s"] = tables(pos_s)
    r = np.arange(128)[:, None]
    ii = np.arange(4)[None, :]
    c["c_mc"] = (r > ii).astype(np.float32)
    kb, kj = np.divmod(np.arange(NS), 4)
    c["c_mnew"] = ((kb[:, None] == kb[None, :]) & (kj[:, None] <= kj[None, :])).astype(np.float32)
    c["c_rowm"] = (np.arange(128)[:, None] // 32 == np.arange(4)[None, :]).astype(np.float32)
    return c


def kernel(**inputs):
    f = lambda a: np.ascontiguousarray(np.asarray(a, dtype=np.float32))
    inp = {k: f(v) for k, v in inputs.items()}
    if "nc" not in _NC_CACHE:
        _NC_CACHE["nc"] = build_program()
    nc = _NC_CACHE["nc"]
    consts = _consts()
    wnames = ["attn_norm", "w_in", "q_norm", "k_norm", "attn_sinks", "ssm_a_re", "ssm_a_im", "ssm_log_dt", "ssm_b_re", "ssm_b_im",
              "ssm_c_re", "ssm_c_im", "ssm_d", "ssm_w_glu", "mem_norm", "w_mem_kv", "mem_q_norm", "mem_k_norm", "w_branch", "w_out",
              "ffn_norm", "w_ffn_up", "w_ffn_down"]
    in_maps = []
    for c in range(8):
        b0 = NSB * c
        m = {
            "xp": inp["x_prompt"][c % 4],
            "xs": inp["x_sample"][b0:b0 + NSB].reshape(NS, D),
            "csk": inp["cache_swa_k"][:, b0:b0 + NSB].reshape(NL, NSB, 128, 128),
            "csv": inp["cache_swa_v"][:, b0:b0 + NSB].reshape(NL, NSB, 128, 128),
            "sre": inp["state_ssm_re"][:, b0:b0 + NSB],
            "sim": inp["state_ssm_im"][:, b0:b0 + NSB],
            "cmk": inp["cache_mem_k"][:, b0:b0 + NSB].reshape(NL, NSB, 256, 512),
            "cmv": inp["cache_mem_v"][:, b0:b0 + NSB].reshape(NL, NSB, 256, 512),
            "memp": inp["mem_prompt"][c % 4],
        }
        for w in wnames:
            m[w] = inp[w]
        m.update(consts)
        in_maps.append({k: np.ascontiguousarray(v) for k, v in m.items()})
    res = run_bass_kernel_spmd(nc, in_maps, core_ids=list(range(8)))
    R = res.results
    cat = lambda name, cores: np.stack([R[c][name] for c in cores])
    y_p = cat("yp", range(4))
    y_s = np.concatenate([R[c]["ys"].reshape(NSB, 4, D) for c in range(8)], axis=0)
    per_l = lambda name, shape: np.stack([R[c][name] for c in range(4)], axis=1).reshape(shape)
    swa_k_p = per_l("kp", (NL, 4, 128, 2, 64))
    swa_v_p = per_l("vp", (NL, 4, 128, 2, 64))
    ssm_re_p = per_l("hrp", (NL, 4, 32, 64))
    ssm_im_p = per_l("hip", (NL, 4, 32, 64))
    mem_k_p = per_l("mkp", (NL, 4, 256, 4, 128))
    mem_v_p = per_l("mvp", (NL, 4, 256, 4, 128))
    cat_s = lambda name, shape: np.concatenate([R[c][name] for c in range(8)], axis=1).reshape(shape)
    swa_k_s = cat_s("ks", (NL, 128, 128, 2, 64))
    swa_v_s = cat_s("vs", (NL, 128, 128, 2, 64))
    ssm_re_s = cat_s("hrs", (NL, 128, 32, 64))
    ssm_im_s = cat_s("his", (NL, 128, 32, 64))
    outs = (y_p, y_s, swa_k_p, swa_v_p, ssm_re_p, ssm_im_p, mem_k_p, mem_v_p, swa_k_s, swa_v_s, ssm_re_s, ssm_im_s)
    return tuple(np.ascontiguousarray(o, dtype=np.float32) for o in outs)
```

```python
import contextlib
import math
import types
import numpy as np
import concourse.bass as bass
import concourse.mybir as mybir
from concourse.bass_utils import run_bass_kernel_spmd

F32 = mybir.dt.float32
BF16 = mybir.dt.bfloat16
I32 = mybir.dt.int32
AF = mybir.ActivationFunctionType
ALU = mybir.AluOpType

D = 1024
SEQ = 4096
NL = 2
INW = 4864
DFF = 2816
K_OFF, V_OFF, U_OFF, MQ_OFF, G_OFF = 512, 640, 768, 1280, 1792
PAST = 16384
TB = 512
NBLK = SEQ // TB
NSB = 16
NS = NSB * 4
NLEV = 9
EPS = 1e-6
NWS = 4
ENGS = ("pe", "act", "dve", "pool", "sp")


def _freeze(fn):
    if fn is None or fn.__closure__ is None:
        return fn
    cells = []
    for c in fn.__closure__:
        try:
            cells.append(types.CellType(c.cell_contents))
        except ValueError:
            cells.append(c)
    return types.FunctionType(fn.__code__, fn.__globals__, fn.__name__, fn.__defaults__, tuple(cells))


class Sched:
    def __init__(self, nc, stack):
        self.nc = nc
        self.stack = stack
        self.q = {e: [] for e in ENGS}
        self.cnt = {e: 0 for e in ENGS}
        self.esem = {e: stack.enter_context(nc.semaphore("sem_" + e)) for e in ENGS}
        self.dsem = {}
        self.dcnt = {}
        self.last_w = {}
        self.readers = {}
        self.seen = {e: {} for e in ENGS}
        self.alias = {}

    def _exp(self, keys):
        out = []
        for k in keys:
            out.append(k)
            out.extend(self.alias.get(k, ()))
        return out

    def op(self, eng, fn, reads=(), writes=(), dma=None):
        reads = self._exp(reads)
        writes = self._exp(writes)
        fn = _freeze(fn)
        deps = []
        for r in reads:
            t = self.last_w.get(r)
            if t is not None:
                deps.append((t, True))
        for w in writes:
            t = self.last_w.get(w)
            if t is not None:
                deps.append((t, False))
            for t in self.readers.get(w, ()):
                deps.append((t, False))
        waits = {}
        for (kind, key, val), raw in deps:
            if kind == "eng" and key == eng and (eng == "pe" or not raw):
                continue
            sk = (kind, key)
            if val > self.seen[eng].get(sk, 0):
                waits[sk] = max(waits.get(sk, 0), val)
        for sk, val in waits.items():
            self.seen[eng][sk] = val
        if dma is not None:
            if dma not in self.dsem:
                self.dsem[dma] = self.stack.enter_context(self.nc.semaphore("dq%d" % len(self.dsem)))
                self.dcnt[dma] = 0
            self.dcnt[dma] += 16
            tok = ("dma", dma, self.dcnt[dma])
        else:
            self.cnt[eng] += 1
            tok = ("eng", eng, self.cnt[eng])
        self.q[eng].append((fn, list(waits.items()), tok))
        for w in writes:
            self.last_w[w] = tok
            self.readers[w] = []
        for r in reads:
            self.readers.setdefault(r, []).append(tok)
        return tok

    def wait_all(self, eng, toks):
        waits = {}
        for kind, key, val in toks:
            sk = (kind, key)
            if val > self.seen[eng].get(sk, 0):
                waits[sk] = max(waits.get(sk, 0), val)
        for sk, val in waits.items():
            self.seen[eng][sk] = val
        self.q[eng].append((None, list(waits.items()), None))

    def emit(self):
        nc = self.nc
        sem = lambda sk: self.esem[sk[1]] if sk[0] == "eng" else self.dsem[sk[1]]
        with nc.Block() as block:
            def run(e, engobj):
                for fn, waits, tok in self.q[e]:
                    for sk, val in waits:
                        engobj.wait_ge(sem(sk), val)
                    if fn is None:
                        continue
                    ins = fn(engobj)
                    if tok[0] == "dma":
                        ins.then_inc(self.dsem[tok[1]], 16)
                    else:
                        ins.then_inc(self.esem[e], 1)

            @block.tensor
            def _(t):
                run("pe", t)

            @block.scalar
            def _(t):
                run("act", t)

            @block.vector
            def _(t):
                run("dve", t)

            @block.gpsimd
            def _(t):
                run("pool", t)

            @block.sync
            def _(t):
                run("sp", t)


def build_program():
    nc = bass.Bass("TRN2", target_bir_lowering=False)

    def din(name, shape):
        return nc.dram_tensor(name, list(shape), F32, kind="ExternalInput").ap()

    def dout(name, shape):
        return nc.dram_tensor(name, list(shape), F32, kind="ExternalOutput").ap()

    def dscr(name, shape, dt=BF16):
        return nc.dram_tensor(name, list(shape), dt).ap()

    xp = din("xp", [SEQ, D]); xs = din("xs", [NS, D])
    csk = din("csk", [NL, NSB, 128, 128]); csv = din("csv", [NL, NSB, 128, 128])
    sre = din("sre", [NL, NSB, 32, 64]); sim = din("sim", [NL, NSB, 32, 64])
    cmk = din("cmk", [NL, NSB, 256, 512]); cmv = din("cmv", [NL, NSB, 256, 512])
    memp = din("memp", [256, D])
    attn_norm = din("attn_norm", [NL, D]); w_in = din("w_in", [NL, D, INW])
    q_norm = din("q_norm", [NL, 64]); k_norm = din("k_norm", [NL, 64]); attn_sinks = din("attn_sinks", [NL, 8])
    a_re = din("ssm_a_re", [NL, 32, 64]); a_im = din("ssm_a_im", [NL, 32, 64]); log_dt = din("ssm_log_dt", [NL, 32])
    b_re = din("ssm_b_re", [NL, 32, 64, 16]); b_im = din("ssm_b_im", [NL, 32, 64, 16])
    c_re = din("ssm_c_re", [NL, 32, 16, 64]); c_im = din("ssm_c_im", [NL, 32, 16, 64])
    ssm_d = din("ssm_d", [NL, 512]); w_glu = din("ssm_w_glu", [NL, 512, 512])
    mem_norm = din("mem_norm", [NL, D]); w_kv = din("w_mem_kv", [NL, D, D])
    mq_norm = din("mem_q_norm", [NL, 128]); mk_norm = din("mem_k_norm", [NL, 128])
    w_br = din("w_branch", [NL, 3, 512, D]); w_out = din("w_out", [NL, D, D])
    ffn_norm = din("ffn_norm", [NL, D]); w_up = din("w_ffn_up", [NL, D, 2 * DFF]); w_dn = din("w_ffn_down", [NL, DFF, D])
    c_ident = din("c_ident", [128, 128]); c_blk = din("c_blk", [128, 128]); c_perm = din("c_perm", [128, 128])
    c_mprev = din("c_mprev", [128, 128]); c_mcur = din("c_mcur", [128, 128])
    c_cos = din("c_cos", [128, SEQ]); c_sin = din("c_sin", [128, SEQ])
    c_cos_s = din("c_cos_s", [128, NS]); c_sin_s = din("c_sin_s", [128, NS])
    c_mc = din("c_mc", [128, 4]); c_mnew = din("c_mnew", [NS, NS]); c_rowm = din("c_rowm", [128, 4])

    yp = dout("yp", [SEQ, D]); ys = dout("ys", [NS, D])
    kp = dout("kp", [NL, 128, 128]); vp = dout("vp", [NL, 128, 128])
    hrp = dout("hrp", [NL, 32, 64]); hip = dout("hip", [NL, 32, 64])
    mkp = dout("mkp", [NL, 256, 512]); mvp = dout("mvp", [NL, 256, 512])
    ks = dout("ks", [NL, NSB, 128, 128]); vs = dout("vs", [NL, NSB, 128, 128])
    hrs = dout("hrs", [NL, NSB, 32, 64]); his = dout("his", [NL, NSB, 32, 64])

    wb_in = dscr("wb_in", [NL, D, INW]); wb_glu = dscr("wb_glu", [NL, 512, 512]); wb_kv = dscr("wb_kv", [NL, D, D])
    wb_br = dscr("wb_br", [NL, 3, 512, D]); wb_out = dscr("wb_out", [NL, D, D])
    wb_up = dscr("wb_up", [NL, D, 2 * DFF]); wb_dn = dscr("wb_dn", [NL, DFF, D])

    out_toks = []

    with contextlib.ExitStack() as st:
        S = Sched(nc, st)

        def sb(name, shape, dt):
            return st.enter_context(nc.sbuf_tensor(name, list(shape), dt))

        PS = [st.enter_context(nc.psum_tensor("ps%d" % i, [128, 512], F32)) for i in range(8)]
        psn = [0]

        held = set()

        def bank():
            while True:
                i = psn[0] % 8
                psn[0] += 1
                if i not in held:
                    return i

        ident_f = sb("ident_f", [128, 128], F32); ident_b = sb("ident_b", [128, 128], BF16)
        ones_b = sb("ones_b", [128, 128], BF16); blk_b = sb("blk_b", [128, 128], BF16); perm_b = sb("perm_b", [128, 128], BF16)
        mprev_b = sb("mprev_b", [128, 128], BF16); mcur_b = sb("mcur_b", [128, 128], BF16)
        mc_b = sb("mc_b", [128, 4], BF16); mnew_b = sb("mnew_b", [NS, NS], BF16); rowm = sb("rowm", [128, 4], F32)
        cosb = sb("cosb", [128, TB], F32); sinb = sb("sinb", [128, TB], F32)
        gA = sb("gA", [128, NL, 8], F32); gF = sb("gF", [128, NL, 8], F32)
        gq = sb("gq", [128, NL], F32); gk = sb("gk", [128, NL], F32); gmq = sb("gmq", [128, NL], F32)
        esink = sb("esink", [128, NL, 4], F32); dcol = sb("dcol", [128, NL, 4], F32)
        W2 = sb("W2", [128, NL, 16, 2, 128], BF16); CP = sb("CP", [128, NL, 16, 2, 128], BF16)
        Dd = sb("Dd", [128, NL, 4, 128], BF16)
        LR = sb("LR", [128, NL, NLEV, 16], F32); LI = sb("LI", [128, NL, NLEV, 16], F32); LIn = sb("LIn", [128, NL, NLEV, 16], F32)
        MKT = sb("MKT", [128, NL, 4, 256], BF16); MV = sb("MV", [128, NL, 2, 512], BF16)
        kTc = sb("kTc", [128, NL, 128 + TB], BF16); vtc = sb("vtc", [128, NL, 5, 128], BF16)
        car_r = sb("car_r", [128, NL, 16], F32); car_i = sb("car_i", [128, NL, 16], F32)
        xT = sb("xT", [128, 8, TB], F32)
        hT = sb("hT", [128, 8, TB], BF16)
        qf = sb("qf", [128, TB], F32); sqb = sb("sqb", [128, TB], BF16); sdv = sb("sdv", [128, TB], F32); rstd = sb("rstd", [128, TB], F32)
        qn = sb("qn", [128, TB], BF16); t1 = sb("t1", [128, TB], F32); t2 = sb("t2", [128, TB], F32)
        HN = [dict(qf=qf, sqb=sqb, sdv=sdv, rstd=rstd, qn=qn, t1=t1, t2=t2, sfx=""),
              dict(qf=sb("qfB", [128, TB], F32), sqb=sb("sqbB", [128, TB], BF16), sdv=sdv,
                   rstd=sb("rstdB", [128, TB], F32), qn=sb("qnB", [128, TB], BF16), t1=t1, t2=t2, sfx="B")]
        hn_i = [0]
        S.alias["zT"] = ["qr"]
        qr = sb("qr", [128, 4, TB], BF16); k32 = sb("k32", [128, TB], F32); v32 = sb("v32", [128, 4, 128], F32)
        uT = sb("uT", [128, 4, TB], BF16); qmn = sb("qmn", [128, 4, TB], BF16)
        oa = sb("oa", [128, 4, TB], BF16); ob = sb("ob", [128, 4, TB], BF16); oc = sb("oc", [128, 4, TB], BF16)
        PT = [sb("PT%d" % i, [128, 2, TB], BF16) for i in range(2)]
        dn = [sb("dn%d" % i, [128, TB], F32) for i in range(2)]
        zT = qr; sg = [sb("sg%d" % i, [128, TB], BF16) for i in range(2)]
        WS = [sb("ws%d" % i, [128, 4096], BF16) for i in range(NWS)]
        RA = sb("RA", [128, 16384], BF16)
        small = sb("small", [128, 64], F32)
        smi = sb("smi", [128, 16], I32)
        hs_r = sb("hs_r", [128, 16, NSB], F32); hs_i = sb("hs_i", [128, 16, NSB], F32)
        kcT = sb("kcT", [128, 2, 128], BF16)

        RAK = [("RA", i) for i in range(32)]

        def ra(off_b, nbytes, dt):
            a = RA[:, off_b // 2:(off_b + nbytes) // 2]
            keys = RAK[off_b // 1024:(off_b + nbytes + 1023) // 1024]
            return (a.bitcast(F32) if dt == F32 else a), keys

        xtok = ra(0, 16384, F32)[0].rearrange("p (s d) -> p s d", s=4)
        S.alias["xtok"] = RAK[0:16]
        sqT = ra(24576, 8192, BF16)[0].rearrange("p (k n) -> p k n", k=8)
        S.alias["sqT"] = RAK[24:32]
        XSr, kXSr = ra(0, 8192, F32); XSi, kXSi = ra(8192, 8192, F32)
        TD = [ra(16384 + 4096 * i, 4096, F32) for i in range(4)]
        xbr, kxbr = ra(16384, 4096, BF16); xbi, kxbi = ra(20480, 4096, BF16)
        macc, kmacc = ra(0, 16384, F32); mgT, kmgT = ra(16384, 8192, BF16)
        wsn = [0]

        def wk(i):
            return [("ws", i), ("wsT", i)]

        def wslot():
            i = wsn[0] % NWS
            wsn[0] += 1
            return i

        S.op("sp", lambda e: e.dma_start(out=ident_f[:], in_=c_ident), writes=["ident_f"], dma="c0")
        for (dst, src, nm) in [(ident_b, c_ident, "ident_b"), (blk_b, c_blk, "blk_b"), (perm_b, c_perm, "perm_b"),
                               (mprev_b, c_mprev, "mprev_b"), (mcur_b, c_mcur, "mcur_b"), (mc_b, c_mc, "mc_b"),
                               (mnew_b, c_mnew, "mnew_b")]:
            S.op("pool", lambda e, dst=dst, src=src: e.dma_start(out=dst[:], in_=src), writes=[nm], dma=nm)
        S.op("sp", lambda e: e.dma_start(out=rowm[:], in_=c_rowm), writes=["rowm"], dma="rowm")
        S.op("dve", lambda e: e.memset(ones_b[:], 1.0), writes=["ones_b"])
        S.op("dve", lambda e: e.memset(small[:], 0.0), writes=["small"])
        S.op("dve", lambda e: e.memset(small[:, 0:1], math.pi / 2), writes=["small"])
        S.op("dve", lambda e: e.memset(small[:, 1:2], EPS), writes=["small"])

        for l in range(NL):
            S.op("sp", lambda e, l=l: e.dma_start(out=gA[:, l, :], in_=attn_norm[l].rearrange("(k p) -> p k", p=128),
                                                  allow_slow_non_contiguous=True), writes=["gA"], dma="gA")
            S.op("sp", lambda e, l=l: e.dma_start(out=gF[:, l, :], in_=ffn_norm[l].rearrange("(k p) -> p k", p=128),
                                                  allow_slow_non_contiguous=True), writes=["gF"], dma="gF")
            S.op("sp", lambda e, l=l: e.dma_start(out=dcol[:, l, :], in_=ssm_d[l].rearrange("(k p) -> p k", p=128),
                                                  allow_slow_non_contiguous=True), writes=["dcol"], dma="dcol")
            for two in range(2):
                sl = slice(64 * two, 64 * two + 64)
                S.op("sp", lambda e, l=l, sl=sl: e.dma_start(out=gq[sl, l:l + 1], in_=q_norm[l].rearrange("(p o) -> p o", o=1)),
                     writes=["gq"], dma="gq")
                S.op("sp", lambda e, l=l, sl=sl: e.dma_start(out=gk[sl, l:l + 1], in_=k_norm[l].rearrange("(p o) -> p o", o=1)),
                     writes=["gk"], dma="gk")
                S.op("sp", lambda e, l=l, sl=sl, two=two: e.dma_start(out=esink[sl, l, :],
                                                                      in_=attn_sinks[l, 4 * two:4 * two + 4].partition_broadcast(64)),
                     writes=["esink"], dma="esink")
            S.op("sp", lambda e, l=l: e.dma_start(out=gmq[:, l:l + 1], in_=mq_norm[l].rearrange("(p o) -> p o", o=1)),
                 writes=["gmq"], dma="gmq")
        S.op("act", lambda e: e.activation(out=esink[:], in_=esink[:], func=AF.Exp), reads=["esink"], writes=["esink"])

        are_t = sb("are_t", [128, 16], F32); aim_t = sb("aim_t", [128, 16], F32); dt_t = sb("dt_t", [128, 16], F32)
        sA = [sb("sA%d" % i, [128, 16], F32) for i in range(8)]
        def ra3(idx, nm):
            v, kk = ra(16384 + 2048 * idx, 2048, F32)
            S.alias[nm] = kk
            return v.rearrange("p (t c) -> p t c", t=16)
        Bb = [ra3(i, "Bb%d" % i) for i in range(2)]
        Cb = [ra3(2 + i, "Cb%d" % i) for i in range(2)]
        GB = [ra3(4 + i, "GB%d" % i) for i in range(2)]
        tG = [ra3(6 + i, "tG%d" % i) for i in range(2)]
        tT = sb("tT", [128, 128], F32)

        def dve(fn, R, W):
            return S.op("dve", fn, reads=R, writes=W)

        def act(fn, R, W):
            return S.op("act", fn, reads=R, writes=W)

        TWO_PI = 2.0 * math.pi
        for l in range(NL):
            for gl in range(2):
                sl = slice(64 * gl, 64 * gl + 64)
                S.op("sp", lambda e, l=l, gl=gl, sl=sl: e.dma_start(
                    out=are_t[sl, :], in_=a_re[l].rearrange("(tp gl) p -> gl p tp", gl=2)[gl], allow_slow_non_contiguous=True),
                    writes=["are_t"], dma="are_t")
                S.op("sp", lambda e, l=l, gl=gl, sl=sl: e.dma_start(
                    out=aim_t[sl, :], in_=a_im[l].rearrange("(tp gl) p -> gl p tp", gl=2)[gl], allow_slow_non_contiguous=True),
                    writes=["aim_t"], dma="aim_t")
                S.op("sp", lambda e, l=l, gl=gl, sl=sl: e.dma_start(
                    out=dt_t[sl, :], in_=log_dt[l].rearrange("(tp gl) -> gl tp", gl=2)[gl].partition_broadcast(64)),
                    writes=["dt_t"], dma="dt_t")
            for ri, (bsrc, csrc) in enumerate([(b_re, c_re), (b_im, c_im)]):
                S.op("pool", lambda e, ri=ri: e.memset(Bb[ri][:], 0.0), writes=["Bb%d" % ri])
                S.op("pool", lambda e, ri=ri: e.memset(Cb[ri][:], 0.0), writes=["Cb%d" % ri])
                for gl in range(2):
                    sl = slice(64 * gl, 64 * gl + 64)
                    cs = slice(16 * gl, 16 * gl + 16)
                    S.op("sp", lambda e, l=l, gl=gl, sl=sl, cs=cs, ri=ri, bsrc=bsrc: e.dma_start(
                        out=Bb[ri][sl, :, cs], in_=bsrc[l].rearrange("(tp gl) p c -> gl p tp c", gl=2)[gl]),
                        writes=["Bb%d" % ri], dma="Bb%d" % ri)
                cst = t1[:].rearrange("p (a q) -> p a q", a=4)
                csrc2 = csrc[l].rearrange("g c p -> (g c) p").rearrange("(a q) p -> q a p", q=128)
                for dup in range(2):
                    S.op("sp", lambda e, dup=dup: e.dma_start(out=cst[:, :, 64 * dup:64 * dup + 64], in_=csrc2), writes=["t1"], dma="t1")
                for a in range(4):
                    b = bank()
                    S.op("pe", lambda e, a=a, b=b: e.transpose(out=PS[b][:, 0:128], in_=cst[:, a, :], identity=ident_f[:]),
                         reads=["t1", "ident_f"], writes=[("ps", b)])
                    for gl in range(2):
                        act(lambda e, a=a, b=b, gl=gl, ri=ri: e.copy(
                            out=Cb[ri][64 * gl:64 * gl + 64, 4 * a:4 * a + 4, 16 * gl:16 * gl + 16],
                            in_=PS[b][64 * gl:64 * gl + 64, 0:128].rearrange("p (tp gl c) -> p gl tp c", gl=2, c=16)[:, gl]),
                            [("ps", b)], ["Cb%d" % ri])
            dtv, ard, mag, th, kf, s_, c_, tmp = sA
            act(lambda e: e.activation(out=dtv[:], in_=dt_t[:], func=AF.Exp), ["dt_t"], ["sA0"])
            dve(lambda e: e.tensor_tensor(out=ard[:], in0=are_t[:], in1=dtv[:], op=ALU.mult), ["are_t", "sA0"], ["sA1"])
            act(lambda e: e.activation(out=mag[:], in_=ard[:], func=AF.Exp), ["sA1"], ["sA2"])
            dve(lambda e: e.tensor_tensor(out=th[:], in0=aim_t[:], in1=dtv[:], op=ALU.mult), ["aim_t", "sA0"], ["sA3"])
            dve(lambda e: e.tensor_scalar(out=kf[:], in0=th[:], scalar1=1.0 / TWO_PI, scalar2=None, op0=ALU.mult), ["sA3"], ["sA4"])
            dve(lambda e: e.tensor_copy(out=smi[:], in_=kf[:]), ["sA4"], ["smi"])
            dve(lambda e: e.tensor_copy(out=kf[:], in_=smi[:]), ["smi"], ["sA4"])
            dve(lambda e: e.scalar_tensor_tensor(out=th[:], in0=kf[:], scalar=-TWO_PI, in1=th[:], op0=ALU.mult, op1=ALU.add),
                ["sA4", "sA3"], ["sA3"])
            act(lambda e: e.activation(out=s_[:], in_=th[:], func=AF.Sin, scale=0.5), ["sA3"], ["sA5"])
            act(lambda e: e.activation(out=c_[:], in_=th[:], func=AF.Sin, scale=0.5, bias=small[:, 0:1]), ["sA3", "small"], ["sA6"])
            lr0 = LR[:, l, 0, :]; li0 = LI[:, l, 0, :]
            dve(lambda e: e.tensor_tensor(out=tmp[:], in0=s_[:], in1=c_[:], op=ALU.mult), ["sA5", "sA6"], ["sA7"])
            dve(lambda e: e.scalar_tensor_tensor(out=li0, in0=tmp[:], scalar=2.0, in1=mag[:], op0=ALU.mult, op1=ALU.mult),
                ["sA7", "sA2"], ["LI"])
            dve(lambda e: e.tensor_tensor(out=tmp[:], in0=s_[:], in1=s_[:], op=ALU.mult), ["sA5"], ["sA7"])
            dve(lambda e: e.tensor_scalar(out=tmp[:], in0=tmp[:], scalar1=-2.0, scalar2=1.0, op0=ALU.mult, op1=ALU.add), ["sA7"], ["sA7"])
            dve(lambda e: e.tensor_tensor(out=lr0, in0=tmp[:], in1=mag[:], op=ALU.mult), ["sA7", "sA2"], ["LR"])
            den_, nr_, gr_, gi_, rd_ = sA[0], sA[1], sA[2], sA[3], sA[4]
            dve(lambda e: e.tensor_tensor(out=den_[:], in0=are_t[:], in1=are_t[:], op=ALU.mult), ["are_t"], ["sA0"])
            dve(lambda e: e.tensor_tensor(out=tmp[:], in0=aim_t[:], in1=aim_t[:], op=ALU.mult), ["aim_t"], ["sA7"])
            dve(lambda e: e.tensor_tensor(out=den_[:], in0=den_[:], in1=tmp[:], op=ALU.add), ["sA0", "sA7"], ["sA0"])
            dve(lambda e: e.reciprocal(out=rd_[:], in_=den_[:]), ["sA0"], ["sA4"])
            dve(lambda e: e.tensor_scalar(out=nr_[:], in0=lr0, scalar1=-1.0, scalar2=None, op0=ALU.add), ["LR"], ["sA1"])
            dve(lambda e: e.tensor_tensor(out=gr_[:], in0=nr_[:], in1=are_t[:], op=ALU.mult), ["sA1", "are_t"], ["sA2"])
            dve(lambda e: e.tensor_tensor(out=tmp[:], in0=li0, in1=aim_t[:], op=ALU.mult), ["LI", "aim_t"], ["sA7"])
            dve(lambda e: e.tensor_tensor(out=gr_[:], in0=gr_[:], in1=tmp[:], op=ALU.add), ["sA2", "sA7"], ["sA2"])
            dve(lambda e: e.tensor_tensor(out=gr_[:], in0=gr_[:], in1=rd_[:], op=ALU.mult), ["sA2", "sA4"], ["sA2"])
            dve(lambda e: e.tensor_tensor(out=gi_[:], in0=li0, in1=are_t[:], op=ALU.mult), ["LI", "are_t"], ["sA3"])
            dve(lambda e: e.tensor_tensor(out=tmp[:], in0=nr_[:], in1=aim_t[:], op=ALU.mult), ["sA1", "aim_t"], ["sA7"])
            dve(lambda e: e.tensor_tensor(out=gi_[:], in0=gi_[:], in1=tmp[:], op=ALU.subtract), ["sA3", "sA7"], ["sA3"])
            dve(lambda e: e.tensor_tensor(out=gi_[:], in0=gi_[:], in1=rd_[:], op=ALU.mult), ["sA3", "sA4"], ["sA3"])
            for i in range(NLEV - 1):
                a, b = LR[:, l, i, :], LI[:, l, i, :]
                a2, b2 = LR[:, l, i + 1, :], LI[:, l, i + 1, :]
                dve(lambda e, a=a, b=b: e.tensor_tensor(out=tmp[:], in0=b, in1=b, op=ALU.mult), ["LI"], ["sA7"])
                dve(lambda e, a=a, a2=a2: e.tensor_tensor(out=a2, in0=a, in1=a, op=ALU.mult), ["LR"], ["LR"])
                dve(lambda e, a2=a2: e.tensor_tensor(out=a2, in0=a2, in1=tmp[:], op=ALU.subtract), ["LR", "sA7"], ["LR"])
                dve(lambda e, a=a, b=b, b2=b2: e.scalar_tensor_tensor(out=b2, in0=a, scalar=2.0, in1=b, op0=ALU.mult, op1=ALU.mult),
                    ["LR", "LI"], ["LI"])
            dve(lambda e, l=l: e.tensor_scalar(out=LIn[:, l], in0=LI[:, l], scalar1=-1.0, scalar2=None, op0=ALU.mult), ["LI"], ["LIn"])
            grb = gr_[:].rearrange("p (t o) -> p t o", o=1).to_broadcast([128, 16, 32])
            gib = gi_[:].rearrange("p (t o) -> p t o", o=1).to_broadcast([128, 16, 32])
            dve(lambda e: e.tensor_tensor(out=GB[0][:], in0=Bb[0][:], in1=grb, op=ALU.mult), ["Bb0", "sA2"], ["GB0"])
            dve(lambda e: e.tensor_tensor(out=tG[0][:], in0=Bb[1][:], in1=gib, op=ALU.mult), ["Bb1", "sA3"], ["tG0"])
            dve(lambda e: e.tensor_tensor(out=GB[0][:], in0=GB[0][:], in1=tG[0][:], op=ALU.subtract), ["GB0", "tG0"], ["GB0"])
            dve(lambda e: e.tensor_tensor(out=GB[1][:], in0=Bb[1][:], in1=grb, op=ALU.mult), ["Bb1", "sA2"], ["GB1"])
            dve(lambda e: e.tensor_tensor(out=tG[1][:], in0=Bb[0][:], in1=gib, op=ALU.mult), ["Bb0", "sA3"], ["tG1"])
            dve(lambda e: e.tensor_tensor(out=GB[1][:], in0=GB[1][:], in1=tG[1][:], op=ALU.add), ["GB1", "tG1"], ["GB1"])
            for ri in range(2):
                for ct in range(4):
                    b = bank()
                    S.op("pe", lambda e, b=b, ri=ri, ct=ct: e.transpose(
                        out=PS[b][:, 0:128], in_=GB[ri][:, 4 * ct:4 * ct + 4, :].rearrange("p a b -> p (a b)"), identity=ident_f[:]),
                        reads=["GB%d" % ri, "ident_f"], writes=[("ps", b)])
                    act(lambda e, b=b: e.copy(out=tT[:], in_=PS[b][:, 0:128]), [("ps", b)], ["tT"])
                    for i in range(4):
                        dve(lambda e, l=l, ri=ri, ct=ct, i=i: e.tensor_scalar(
                            out=W2[:, l, 4 * ct + i, ri, :], in0=tT[:], scalar1=rowm[:, i:i + 1], scalar2=None, op0=ALU.mult),
                            ["tT", "rowm"], ["W2"])
            S.op("pool", lambda e, l=l: e.memset(CP[:, l], 0.0), writes=["CP"])
            for i in range(4):
                act(lambda e, l=l, i=i: e.copy(out=CP[:, l, i::4, 0, 32 * i:32 * i + 32], in_=Cb[0][:, i::4, :]), ["Cb0", "CP"], ["CP"])
                act(lambda e, l=l, i=i: e.mul(out=CP[:, l, i::4, 1, 32 * i:32 * i + 32], in_=Cb[1][:, i::4, :], mul=-1.0), ["Cb1", "CP"], ["CP"])
            for ct in range(4):
                dve(lambda e, l=l, ct=ct: e.tensor_scalar(out=Dd[:, l, ct, :], in0=ident_f[:], scalar1=dcol[:, l, ct:ct + 1], scalar2=None,
                                                          op0=ALU.mult), ["ident_f", "dcol"], ["Dd"])

        conv_pieces = []

        def conv(dst, src, key, rows):
            r0 = 0
            while r0 < rows:
                r1 = min(rows, r0 + 256)
                conv_pieces.append((dst, src, key, r0, r1))
                r0 = r1

        for l in range(NL):
            conv(wb_in[l], w_in[l], ("wb_in", l), D)
            conv(wb_glu[l], w_glu[l], ("wb_glu", l), 512)
            for n in range(3):
                conv(wb_br[l, n], w_br[l, n], ("wb_br", l), 512)
            conv(wb_out[l], w_out[l], ("wb_out", l), D)
            conv(wb_up[l], w_up[l], ("wb_up", l), D)
            conv(wb_dn[l], w_dn[l], ("wb_dn", l), DFF)

        def emit_conv(n):
            for _ in range(n):
                if not conv_pieces:
                    return
                dst, src, key, r0, r1 = conv_pieces.pop(0)
                S.op("pool", lambda e: e.dma_start(out=dst[r0:r1, :], in_=src[r0:r1, :]), writes=[key], dma=key)

        memt = xtok
        mnb = qr[:].rearrange("p t n -> p (t n)").rearrange("p (a d) -> p a d", a=2)
        S.alias["mnb"] = ["qr"]
        mnT = hT
        gmk_b = sb("gmk_b", [128, 128], F32)
        S.op("sp", lambda e: e.dma_start(out=memt[:, 0:2, :], in_=memp.rearrange("(a p) d -> p a d", p=128)), writes=["xtok"], dma="xtok")
        for l in range(NL):
            S.op("sp", lambda e, l=l: e.dma_start(out=memt[:, 2, :], in_=mem_norm[l].partition_broadcast(128)), writes=["xtok"], dma="xtok")
            S.op("sp", lambda e, l=l: e.dma_start(out=gmk_b[:], in_=mk_norm[l].partition_broadcast(128)), writes=["gmk_b"], dma="gmk_b")
            dve(lambda e: e.memset(small[:, 8:10], 0.0), [], ["small"])
            for a in range(2):
                act(lambda e, a=a: e.activation(out=memt[:, 3, :], in_=memt[:, a, :], func=AF.Square, accum_out=small[:, 8 + a:9 + a]),
                    ["xtok"], ["xtok", "small"])
            act(lambda e: e.activation(out=small[:, 10:12], in_=small[:, 8:10], func=AF.Sqrt, scale=1.0 / D, bias=small[:, 1:2]),
                ["small"], ["small"])
            dve(lambda e: e.reciprocal(out=small[:, 12:14], in_=small[:, 10:12]), ["small"], ["small"])
            for a in range(2):
                dve(lambda e, a=a: e.scalar_tensor_tensor(out=mnb[:, a, :], in0=memt[:, a, :], scalar=small[:, 12 + a:13 + a],
                                                          in1=memt[:, 2, :], op0=ALU.mult, op1=ALU.mult), ["xtok", "small"], ["mnb"])
            for a in range(2):
                for k in range(8):
                    b = bank()
                    pb = PS[b][:].bitcast(BF16)
                    S.op("pe", lambda e, a=a, k=k, pb=pb: e.transpose(out=pb[:, 0:128], in_=mnb[:, a, 128 * k:128 * k + 128],
                                                                      identity=ident_b[:]),
                         reads=["mnb", "ident_b"], writes=[("ps", b)])
                    act(lambda e, a=a, k=k, pb=pb: e.copy(out=mnT[:, k, 128 * a:128 * a + 128], in_=pb[:, 0:128]), [("ps", b)], ["hT"])
            for half in range(2):
                ws = wslot()
                wv = WS[ws][:].rearrange("p (k c) -> p k c", k=8)
                S.op("pool", lambda e, l=l, half=half, wv=wv: e.dma_start(
                    out=wv, in_=w_kv[l].rearrange("(k p) c -> p k c", p=128)[:, :, 512 * half:512 * half + 512]),
                    writes=wk(ws), dma=("ws", ws))
                for a in range(2):
                    b = bank()
                    for k in range(8):
                        S.op("pe", lambda e, a=a, k=k, b=b, wv=wv: e.matmul(PS[b][:], lhsT=mnT[:, k, 128 * a:128 * a + 128], rhs=wv[:, k, :],
                                                                          start=(k == 0), stop=(k == 7)),
                             reads=["hT", ("ws", ws)], writes=[("ps", b)])
                    if half == 0:
                        kk = t1
                        dve(lambda e: e.memset(small[:, 16:20], 0.0), [], ["small"])
                        for h in range(4):
                            act(lambda e, b=b, h=h: e.activation(out=t2[:, 128 * h:128 * h + 128], in_=PS[b][:, 128 * h:128 * h + 128],
                                                                 func=AF.Square, accum_out=small[:, 16 + h:17 + h]),
                                [("ps", b)], ["t2", "small"])
                        act(lambda e: e.activation(out=small[:, 20:24], in_=small[:, 16:20], func=AF.Sqrt, scale=1.0 / 128, bias=small[:, 1:2]),
                            ["small"], ["small"])
                        dve(lambda e: e.reciprocal(out=small[:, 24:28], in_=small[:, 20:24]), ["small"], ["small"])
                        for h in range(4):
                            dve(lambda e, b=b, h=h: e.scalar_tensor_tensor(
                                out=kk[:, 128 * h:128 * h + 128], in0=PS[b][:, 128 * h:128 * h + 128], scalar=small[:, 24 + h:25 + h],
                                in1=gmk_b[:], op0=ALU.mult, op1=ALU.mult), [("ps", b), "small", "gmk_b"], ["t1"])
                        out_toks.append(S.op("sp", lambda e, l=l, a=a: e.dma_start(out=mkp[l, 128 * a:128 * a + 128, :], in_=kk[:]),
                                             reads=["t1"], dma="o_mkp"))
                        act(lambda e: e.copy(out=sqb[:], in_=kk[:]), ["t1"], ["sqb"])
                        for h in range(4):
                            b2 = bank()
                            pb = PS[b2][:].bitcast(BF16)
                            S.op("pe", lambda e, h=h, pb=pb: e.transpose(out=pb[:, 0:128], in_=sqb[:, 128 * h:128 * h + 128], identity=ident_b[:]),
                                 reads=["sqb", "ident_b"], writes=[("ps", b2)])
                            act(lambda e, l=l, a=a, h=h, pb=pb: e.copy(out=MKT[:, l, h, 128 * a:128 * a + 128], in_=pb[:, 0:128]),
                                [("ps", b2)], ["MKT"])
                    else:
                        vv = t2
                        act(lambda e, b=b: e.copy(out=vv[:], in_=PS[b][:]), [("ps", b)], ["t2"])
                        out_toks.append(S.op("sp", lambda e, l=l, a=a: e.dma_start(out=mvp[l, 128 * a:128 * a + 128, :], in_=vv[:]),
                                             reads=["t2"], dma="o_mvp"))
                        dve(lambda e, l=l, a=a: e.tensor_copy(out=MV[:, l, a, :], in_=vv[:]), ["t2"], ["MV"])

        fpm = [False]

        def wq():
            return "pool" if fpm[0] else "sp"

        def wr(key):
            return [] if fpm[0] else [key]

        def scr_store(fn, slot, key):
            tok = S.op("sp", fn, reads=[("ws", slot)], dma=key)
            S.last_w[key] = tok
            S.readers[key] = []

        def load_w(dst_view, src_ap, srckey, scr_ap=None):
            ws = wslot()
            dv = dst_view(ws)
            S.op(wq(), lambda e: e.dma_start(out=dv, in_=src_ap), reads=wr(srckey), writes=wk(ws), dma=("ws", ws))
            if fpm[0]:
                scr_store(lambda e: e.dma_start(out=scr_ap, in_=dv), ws, srckey)
            return ws

        def rms_rstd(N, ssb, inv_n, R):
            act(lambda e: e.activation(out=sdv[:, 0:N], in_=PS[ssb][:, 0:N], func=AF.Sqrt, scale=inv_n, bias=small[:, 1:2]),
                [("ps", ssb), "small"], ["sdv"])
            dve(lambda e: e.reciprocal(out=rstd[:, 0:N], in_=sdv[:, 0:N]), ["sdv"], ["rstd"])

        def norm_block(N, gtab):
            act(lambda e: e.activation(out=sqT[:, :, 0:N], in_=xT[:, :, 0:N], func=AF.Square), ["xT"], ["sqT"])
            b = bank()
            for k in range(8):
                S.op("pe", lambda e, k=k, b=b: e.matmul(PS[b][:, 0:N], lhsT=ones_b[:], rhs=sqT[:, k, 0:N], start=(k == 0), stop=(k == 7)),
                     reads=["sqT", "ones_b"], writes=[("ps", b)])
            rms_rstd(N, b, 1.0 / D, None)
            for k in range(8):
                dve(lambda e, k=k: e.scalar_tensor_tensor(out=hT[:, k, 0:N], in0=xT[:, k, 0:N], scalar=gtab[:, k:k + 1], in1=rstd[:, 0:N],
                                                          op0=ALU.mult, op1=ALU.mult), ["xT", "rstd", "gA", "gF"], ["hT"])

        def proj_tile(N, wv, c0, ws, b=None):
            if b is None:
                b = bank()
            for k in range(8):
                S.op("pe", lambda e, k=k, b=b: e.matmul(PS[b][:, 0:N], lhsT=wv[:, k, c0:c0 + 128], rhs=hT[:, k, 0:N],
                                                        start=(k == 0), stop=(k == 7)),
                     reads=["hT", ("ws", ws)], writes=[("ps", b)])
            return b

        def headnorm_rope(N, b, l, gcol, onesm, inv_n, rope, out_bf, out_keys, out32=None, out32_keys=()):
            H = HN[hn_i[0] % 2]
            hn_i[0] += 1
            x = H["sfx"]
            qf_, sqb_, sdv_, rstd_, qn_, t1_, t2_ = H["qf"], H["sqb"], H["sdv"], H["rstd"], H["qn"], H["t1"], H["t2"]
            act(lambda e: e.copy(out=qf_[:, 0:N], in_=PS[b][:, 0:N]), [("ps", b)], ["qf" + x])
            act(lambda e: e.activation(out=sqb_[:, 0:N], in_=qf_[:, 0:N], func=AF.Square), ["qf" + x], ["sqb" + x])
            b2 = bank()
            S.op("pe", lambda e: e.matmul(PS[b2][:, 0:N], lhsT=onesm[:], rhs=sqb_[:, 0:N], start=True, stop=True),
                 reads=["sqb" + x, "blk_b", "ones_b"], writes=[("ps", b2)])
            act(lambda e: e.activation(out=sdv_[:, 0:N], in_=PS[b2][:, 0:N], func=AF.Sqrt, scale=inv_n, bias=small[:, 1:2]),
                [("ps", b2), "small"], ["sdv"])
            dve(lambda e: e.reciprocal(out=rstd_[:, 0:N], in_=sdv_[:, 0:N]), ["sdv"], ["rstd" + x])
            if not rope:
                S.op("dve", lambda e: e.scalar_tensor_tensor(out=out_bf, in0=qf_[:, 0:N], scalar=gcol, in1=rstd_[:, 0:N],
                                                             op0=ALU.mult, op1=ALU.mult),
                     reads=["qf" + x, "rstd" + x, "gmq"], writes=out_keys)
                return
            S.op("dve", lambda e: e.scalar_tensor_tensor(out=qn_[:, 0:N], in0=qf_[:, 0:N], scalar=gcol, in1=rstd_[:, 0:N],
                                                         op0=ALU.mult, op1=ALU.mult),
                 reads=["qf" + x, "rstd" + x, "gq", "gk"], writes=["qn" + x])
            b3 = bank()
            S.op("pe", lambda e: e.matmul(PS[b3][:, 0:N], lhsT=perm_b[:], rhs=qn_[:, 0:N], start=True, stop=True),
                 reads=["qn" + x, "perm_b"], writes=[("ps", b3)])
            S.op("pool", lambda e: e.tensor_tensor(out=t1_[:, 0:N], in0=qn_[:, 0:N], in1=cosb[:, 0:N], op=ALU.mult),
                 reads=["qn" + x, "cosb"], writes=["t1"])
            dve(lambda e: e.tensor_tensor(out=t2_[:, 0:N], in0=PS[b3][:, 0:N], in1=sinb[:, 0:N], op=ALU.mult), [("ps", b3), "sinb"], ["t2"])
            if out32 is not None:
                dve(lambda e: e.tensor_tensor(out=out32, in0=t1_[:, 0:N], in1=t2_[:, 0:N], op=ALU.add), ["t1", "t2"], list(out32_keys))
                act(lambda e: e.copy(out=out_bf, in_=out32), list(out32_keys), out_keys)
            else:
                dve(lambda e: e.tensor_tensor(out=out_bf, in0=t1_[:, 0:N], in1=t2_[:, 0:N], op=ALU.add), ["t1", "t2"], out_keys)

        def cmul_add(eng, dr, di, sr, si, lr, li, lin, kdr, kdi, ksr, ksi, T, kT):
            o = lambda fn, R, W: S.op(eng, fn, reads=R, writes=W)
            o(lambda e: e.tensor_tensor(out=T[0], in0=sr, in1=lr, op=ALU.mult), ksr + ["LR"], kT[0])
            o(lambda e: e.tensor_tensor(out=T[1], in0=si, in1=lin, op=ALU.mult), ksi + ["LIn"], kT[1])
            o(lambda e: e.tensor_tensor(out=T[2], in0=si, in1=lr, op=ALU.mult), ksi + ["LR"], kT[2])
            o(lambda e: e.tensor_tensor(out=T[3], in0=sr, in1=li, op=ALU.mult), ksr + ["LI"], kT[3])
            o(lambda e: e.tensor_tensor(out=dr, in0=dr, in1=T[0], op=ALU.add), kdr + kT[0], kdr)
            o(lambda e: e.tensor_tensor(out=di, in0=di, in1=T[2], op=ALU.add), kdi + kT[2], kdi)
            o(lambda e: e.tensor_tensor(out=dr, in0=dr, in1=T[1], op=ALU.add), kdr + kT[1], kdr)
            o(lambda e: e.tensor_tensor(out=di, in0=di, in1=T[3], op=ALU.add), kdi + kT[3], kdi)

        kXS = kXSr + kXSi

        def lam_b(tab, l, lev, tp0, ntp, shape):
            return tab[:, l, lev, tp0:tp0 + ntp].rearrange("p (t o) -> p t o", o=1).to_broadcast(shape)

        def ssm_group_prompt(l, ct, N, first_block):
            for i in range(4):
                tp = 4 * ct + i
                for ri, X, kX in ((0, XSr, kXSr), (1, XSi, kXSi)):
                    b = bank()
                    S.op("pe", lambda e, tp=tp, ri=ri, b=b: e.matmul(PS[b][:, 0:N], lhsT=W2[:, l, tp, ri, :], rhs=uT[:, ct, 0:N],
                                                                    start=True, stop=True), reads=["W2", "uT"], writes=[("ps", b)])
                    act(lambda e, X=X, i=i, b=b: e.copy(out=X[:, i * TB:i * TB + N], in_=PS[b][:, 0:N]), [("ps", b)], kX[2 * i:2 * i + 2])
            Xr3 = XSr.rearrange("p (t n) -> p t n", t=4)
            Xi3 = XSi.rearrange("p (t n) -> p t n", t=4)
            parts = [("dve", 0, 4, TD)]
            nlev = int(math.log2(N))
            steps = []
            if not first_block:
                steps.append(("carry", 0))
            for lev in range(nlev):
                steps.append(("up", lev))
            for lev in range(nlev - 2, -1, -1):
                steps.append(("down", lev))
            for kind, lev in steps:
                for eng, a0, na, TT in parts:
                    kr = kXSr[2 * a0:2 * (a0 + na)]; ki = kXSi[2 * a0:2 * (a0 + na)]
                    Xr = Xr3[:, a0:a0 + na, :]; Xi = Xi3[:, a0:a0 + na, :]
                    if kind == "carry":
                        m = 1
                        dr, di = Xr[:, :, 0:1], Xi[:, :, 0:1]
                        sr = car_r[:, l, 4 * ct + a0:4 * ct + a0 + na].rearrange("p (t o) -> p t o", o=1)
                        si = car_i[:, l, 4 * ct + a0:4 * ct + a0 + na].rearrange("p (t o) -> p t o", o=1)
                        ksr = ksi = ["car"]
                    else:
                        d = 1 << lev
                        if kind == "up":
                            m = N // (2 * d)
                            Xr4 = Xr.rearrange("p t (m s) -> p t m s", s=2 * d)
                            Xi4 = Xi.rearrange("p t (m s) -> p t m s", s=2 * d)
                        else:
                            m = N // (2 * d) - 1
                            Xr4 = Xr[:, :, d:N - d].rearrange("p t (m s) -> p t m s", s=2 * d)
                            Xi4 = Xi[:, :, d:N - d].rearrange("p t (m s) -> p t m s", s=2 * d)
                        dr, di = Xr4[:, :, :, 2 * d - 1], Xi4[:, :, :, 2 * d - 1]
                        sr, si = Xr4[:, :, :, d - 1], Xi4[:, :, :, d - 1]
                        ksr, ksi = kr, ki
                    sh = [128, na, m]
                    if kind != "carry" and m >= 63:
                        for (dst, src, tab, kd_, ks_) in ((dr, sr, LR, kr, kr), (di, si, LR, ki, ki), (dr, si, LIn, kr, ki), (di, sr, LI, ki, kr)):
                            for tpi in range(na):
                                tg = 4 * ct + a0 + tpi
                                dve(lambda e, dst=dst, src=src, tab=tab, tpi=tpi, tg=tg: e.scalar_tensor_tensor(
                                    out=dst[:, tpi, :], in0=src[:, tpi, :], scalar=tab[:, l, lev, tg:tg + 1], in1=dst[:, tpi, :],
                                    op0=ALU.mult, op1=ALU.add),
                                    ks_[2 * tpi:2 * tpi + 2] + kd_[2 * tpi:2 * tpi + 2] + ["LR", "LI", "LIn"], kd_[2 * tpi:2 * tpi + 2])
                        continue
                    T = [t_[0].rearrange("p (t n) -> p t n", t=na)[:, :, 0:m] for t_ in TT]
                    kT = [t_[1] for t_ in TT]
                    cmul_add(eng, dr, di, sr, si, lam_b(LR, l, lev, 4 * ct + a0, na, sh), lam_b(LI, l, lev, 4 * ct + a0, na, sh),
                             lam_b(LIn, l, lev, 4 * ct + a0, na, sh), kr, ki, ksr, ksi, T, kT)
            dve(lambda e: e.tensor_copy(out=car_r[:, l, 4 * ct:4 * ct + 4], in_=Xr3[:, :, N - 1]), kXSr, ["car"])
            dve(lambda e: e.tensor_copy(out=car_i[:, l, 4 * ct:4 * ct + 4], in_=Xi3[:, :, N - 1]), kXSi, ["car"])
            act(lambda e: e.copy(out=xbr[:], in_=XSr), kXSr, kxbr)
            act(lambda e: e.copy(out=xbi[:], in_=XSi), kXSi, kxbi)
            return ssm_y(l, ct, N)

        def ssm_y(l, ct, N):
            b = bank()
            n = 0
            for i in range(4):
                tp = 4 * ct + i
                for ri, xb_, kx in ((0, xbr, kxbr), (1, xbi, kxbi)):
                    S.op("pe", lambda e, tp=tp, ri=ri, xb_=xb_, i=i, b=b, n=n: e.matmul(
                        PS[b][:, 0:N], lhsT=CP[:, l, tp, ri, :], rhs=xb_[:, i * TB:i * TB + N], start=(n == 0), stop=False),
                        reads=["CP"] + kx, writes=[("ps", b)])
                    n += 1
            S.op("pe", lambda e, b=b: e.matmul(PS[b][:, 0:N], lhsT=Dd[:, l, ct, :], rhs=uT[:, ct, 0:N], start=False, stop=True),
                 reads=["Dd", "uT"], writes=[("ps", b)])
            return b

        def ssm_group_sample(l, ct):
            N = NS
            for i in range(4):
                tp = 4 * ct + i
                for ri, X in ((0, XSr), (1, XSi)):
                    b = bank()
                    S.op("pe", lambda e, tp=tp, ri=ri, b=b: e.matmul(PS[b][:, 0:N], lhsT=W2[:, l, tp, ri, :], rhs=uT[:, ct, 0:N],
                                                                    start=True, stop=True), reads=["W2", "uT"], writes=[("ps", b)])
                    act(lambda e, X=X, i=i, b=b: e.copy(out=X[:, i * TB:i * TB + N], in_=PS[b][:, 0:N]), [("ps", b)], kXS)
            Xr4 = XSr.rearrange("p (t n) -> p t n", t=4)[:, :, 0:NS].rearrange("p t (b i) -> p t b i", i=4)
            Xi4 = XSi.rearrange("p (t n) -> p t n", t=4)[:, :, 0:NS].rearrange("p t (b i) -> p t b i", i=4)
            sh = [128, 4, NSB]
            T = [t_[0][:, 0:4 * NSB].rearrange("p (t n) -> p t n", t=4) for t_ in TD]
            kT = [t_[1] for t_ in TD]
            lr, li, lin = lam_b(LR, l, 0, 4 * ct, 4, sh), lam_b(LI, l, 0, 4 * ct, 4, sh), lam_b(LIn, l, 0, 4 * ct, 4, sh)
            for i in range(4):
                if i == 0:
                    sr, si, ksr, ksi = hs_r[:, 4 * ct:4 * ct + 4, :], hs_i[:, 4 * ct:4 * ct + 4, :], ["hs"], ["hs"]
                else:
                    sr, si, ksr, ksi = Xr4[:, :, :, i - 1], Xi4[:, :, :, i - 1], kXSr, kXSi
                cmul_add("dve", Xr4[:, :, :, i], Xi4[:, :, :, i], sr, si, lr, li, lin, kXSr, kXSi, ksr, ksi, T, kT)
            dve(lambda e: e.tensor_copy(out=hs_r[:, 4 * ct:4 * ct + 4, :], in_=Xr4[:, :, :, 3]), kXS, ["hs"])
            dve(lambda e: e.tensor_copy(out=hs_i[:, 4 * ct:4 * ct + 4, :], in_=Xi4[:, :, :, 3]), kXS, ["hs"])
            act(lambda e: e.copy(out=xbr[:], in_=XSr), kXS, kxbr)
            act(lambda e: e.copy(out=xbi[:], in_=XSi), kXS, kxbi)
            return ssm_y(l, ct, N)

        def layer_block(l, N, sample, blk):
            first_block = (blk == 0)
            last_block = (blk == NBLK - 1)
            fpm[0] = (blk == 0 and not sample)
            fp = fpm[0]
            wvin = (w_in if fp else wb_in)[l].rearrange("(k p) c -> p k c", p=128)
            svin = wb_in[l].rearrange("(k p) c -> p k c", p=128)
            v8 = lambda ws: WS[ws][:].rearrange("p (k c) -> p k c", k=8)
            norm_block(N, gA[:, l, :])
            ws = wslot()
            wq_ = WS[ws][:].rearrange("p (k t two d) -> p k t two d", k=8, t=4, two=2)
            for two in range(2):
                for k in range(8):
                    S.op(wq(), lambda e, two=two, k=k: e.dma_start(
                        out=wq_[:, k, :, two, :], in_=wvin[:, k, 256 * two:256 * two + 256].rearrange("p (t d) -> p t d", d=64)),
                        reads=wr(("wb_in", l)), writes=wk(ws), dma=("ws", ws))
                    if fp:
                        scr_store(lambda e, two=two, k=k: e.dma_start(
                            out=svin[:, k, 256 * two:256 * two + 256].rearrange("p (t d) -> p t d", d=64), in_=wq_[:, k, :, two, :]),
                            ws, ("wb_in", l))
            wqv = v8(ws)
            for t in range(4):
                b = proj_tile(N, wqv, 128 * t, ws)
                headnorm_rope(N, b, l, gq[:, l:l + 1], blk_b, 1.0 / 64, True, qr[:, t, 0:N], ["qr"])
            ws = load_w(lambda w: v8(w)[:, :, 0:256], wvin[:, :, K_OFF:K_OFF + 256], ("wb_in", l), svin[:, :, K_OFF:K_OFF + 256])
            wkv_ = v8(ws)
            b = proj_tile(N, wkv_, 0, ws)
            kdst = kTc[:, l, 128:128 + N] if not sample else kTc[:, l, 0:N]
            headnorm_rope(N, b, l, gk[:, l:l + 1], blk_b, 1.0 / 64, True, kdst, ["kTc"], out32=k32[:, 0:N], out32_keys=["k32"])
            nsub = max(1, N // 128)
            pn = min(N, 128)
            b = bank()
            for s in range(nsub):
                for k in range(8):
                    S.op("pe", lambda e, s=s, k=k, b=b: e.matmul(PS[b][0:pn, 128 * s:128 * s + 128], lhsT=hT[:, k, 128 * s:128 * s + pn],
                                                                rhs=wkv_[:, k, 128:256], start=(k == 0), stop=(k == 7)),
                         reads=["hT", ("ws", ws)], writes=[("ps", b)])
            act(lambda e, b=b: e.copy(out=v32[0:pn, 0:nsub, :], in_=PS[b][0:pn, 0:128 * nsub].rearrange("p (s c) -> p s c", c=128)),
                [("ps", b)], ["v32"])
            vdst = vtc[0:pn, l, 1:1 + nsub, :] if not sample else vtc[0:pn, l, 0:1, :]
            dve(lambda e: e.tensor_copy(out=vdst, in_=v32[0:pn, 0:nsub, :]), ["v32"], ["vtc"])
            if not sample:
                for s in range(nsub):
                    for h in range(2):
                        hs = slice(64 * h, 64 * h + 64)
                        pt = PT[(2 * s + h) % 2]; kpt = "PT%d" % ((2 * s + h) % 2)
                        use_prev = not (first_block and s == 0)
                        parts = ([0] if use_prev else []) + [1]
                        sbk = {}
                        for part in parts:
                            b = bank(); sbk[part] = b
                            c0 = 128 * s + 128 * part
                            S.op("pe", lambda e, b=b, c0=c0, hs=hs, s=s: e.matmul(
                                PS[b][:].rearrange("p (t c) -> p t c", t=4), lhsT=kTc[hs, l, c0:c0 + 128],
                                rhs=qr[hs, :, 128 * s:128 * s + 128], start=True, stop=True),
                                reads=["kTc", "qr"], writes=[("ps", b)])
                            act(lambda e, b=b, part=part, pt=pt: e.activation(out=pt[:, part, :], in_=PS[b][:], func=AF.Exp, scale=0.125),
                                [("ps", b)], [kpt])
                            mk_ = mprev_b if part == 0 else mcur_b
                            S.op("pool", lambda e, part=part, pt=pt, mk_=mk_: e.tensor_tensor(
                                out=pt[:, part, :].rearrange("p (t c) -> p t c", t=4), in0=pt[:, part, :].rearrange("p (t c) -> p t c", t=4),
                                in1=mk_[:].rearrange("p (o c) -> p o c", o=1).to_broadcast([128, 4, 128]), op=ALU.mult),
                                reads=[kpt, "mprev_b", "mcur_b"], writes=[kpt])
                        bo = bank(); bd = bank()
                        for n_, part in enumerate(parts):
                            S.op("pe", lambda e, part=part, n_=n_, bo=bo, pt=pt, s=s, hs=hs: e.matmul(
                                PS[bo][hs, :], lhsT=vtc[:, l, s + part, hs], rhs=pt[:, part, :], start=(n_ == 0), stop=(n_ == len(parts) - 1)),
                                reads=["vtc", kpt], writes=[("ps", bo)])
                        for n_, part in enumerate(parts):
                            S.op("pe", lambda e, part=part, n_=n_, bd=bd, pt=pt, hs=hs: e.matmul(
                                PS[bd][hs, :], lhsT=ones_b[:, hs], rhs=pt[:, part, :], start=(n_ == 0), stop=(n_ == len(parts) - 1)),
                                reads=["ones_b", kpt], writes=[("ps", bd)])
                        dd = dn[h]; kd = "dn%d" % h
                        dve(lambda e, bd=bd, dd=dd, hs=hs: e.tensor_tensor(
                            out=dd[hs, :].rearrange("p (t c) -> p t c", t=4), in0=PS[bd][hs, :].rearrange("p (t c) -> p t c", t=4),
                            in1=esink[hs, l, :].rearrange("p (t o) -> p t o", o=1).to_broadcast([64, 4, 128]), op=ALU.add),
                            [("ps", bd), "esink"], [kd])
                        dve(lambda e, dd=dd, hs=hs: e.reciprocal(out=dd[hs, :], in_=dd[hs, :]), [kd], [kd])
                        dve(lambda e, bo=bo, dd=dd, hs=hs, s=s: e.tensor_tensor(
                            out=oa[hs, :, 128 * s:128 * s + 128], in0=PS[bo][hs, :].rearrange("p (t c) -> p t c", t=4),
                            in1=dd[hs, :].rearrange("p (t c) -> p t c", t=4), op=ALU.mult), [("ps", bo), kd], ["oa"])
                if last_block:
                    b = bank()
                    S.op("pe", lambda e, b=b: e.transpose(out=PS[b][:, 0:128], in_=k32[:, N - 128:N], identity=ident_f[:]),
                         reads=["k32", "ident_f"], writes=[("ps", b)])
                    act(lambda e, b=b: e.copy(out=t1[:, 0:128], in_=PS[b][:, 0:128]), [("ps", b)], ["t1"])
                    out_toks.append(S.op("sp", lambda e: e.dma_start(out=kp[l], in_=t1[:, 0:128]), reads=["t1"], dma="o_kp"))
                    out_toks.append(S.op("sp", lambda e: e.dma_start(out=vp[l], in_=v32[:, 3, :]), reads=["v32"], dma="o_vp"))
                else:
                    act(lambda e: e.copy(out=kTc[:, l, 0:128], in_=kTc[:, l, N:N + 128]), ["kTc"], ["kTc"])
                    act(lambda e: e.copy(out=vtc[:, l, 0, :], in_=vtc[:, l, 4, :]), ["vtc"], ["vtc"])
            else:
                b = bank()
                S.op("pe", lambda e, b=b: e.transpose(out=PS[b][0:NS, 0:128], in_=k32[:, 0:NS], identity=ident_f[:]),
                     reads=["k32", "ident_f"], writes=[("ps", b)])
                act(lambda e, b=b: e.copy(out=t1[0:NS, 0:128], in_=PS[b][0:NS, 0:128]), [("ps", b)], ["t1"])
                out_toks.append(S.op("sp", lambda e: e.dma_start(out=ks[l, :, 124:128, :], in_=t1[0:NS, 0:128]), reads=["t1"], dma="o_ks"))
                out_toks.append(S.op("sp", lambda e: e.dma_start(out=vs[l, :, 124:128, :], in_=v32[0:NS, 0, :]), reads=["v32"], dma="o_vs"))
                out_toks.append(S.op("sp", lambda e: e.dma_start(out=ks[l, :, 0:124, :], in_=csk[l, :, 4:128, :]), dma="o_ks"))
                out_toks.append(S.op("sp", lambda e: e.dma_start(out=vs[l, :, 0:124, :], in_=csv[l, :, 4:128, :]), dma="o_vs"))
                ptn = PT[0]; ptc = PT[1]
                for h in range(2):
                    hs = slice(64 * h, 64 * h + 64)
                    b = bank()
                    S.op("pe", lambda e, b=b, hs=hs: e.matmul(PS[b][0:NS, 0:4 * NS].rearrange("p (t c) -> p t c", t=4),
                                                             lhsT=kTc[hs, l, 0:NS], rhs=qr[hs, :, 0:NS], start=True, stop=True),
                         reads=["kTc", "qr"], writes=[("ps", b)])
                    act(lambda e, b=b, h=h: e.activation(out=ptn[0:NS, h, 0:4 * NS], in_=PS[b][0:NS, 0:4 * NS], func=AF.Exp, scale=0.125),
                        [("ps", b)], ["PT0"])
                    dve(lambda e, h=h: e.tensor_tensor(
                        out=ptn[0:NS, h, 0:4 * NS].rearrange("p (t c) -> p t c", t=4), in0=ptn[0:NS, h, 0:4 * NS].rearrange("p (t c) -> p t c", t=4),
                        in1=mnew_b[:].rearrange("p (o c) -> p o c", o=1).to_broadcast([NS, 4, NS]), op=ALU.mult), ["PT0", "mnew_b"], ["PT0"])
                bsc = bank(); held.add(bsc)
                for bb in range(NSB):
                    kst = sg[bb % 2]; kk_ = "sg%d" % (bb % 2)
                    S.op("pool", lambda e, bb=bb, kst=kst: e.dma_start(out=kst[:, 0:128], in_=csk[l, bb]), writes=[kk_], dma=kk_)
                    S.op("pool", lambda e, bb=bb, kst=kst: e.dma_start(out=kst[:, 128:256], in_=csv[l, bb]), writes=[kk_], dma=kk_)
                    b = bank()
                    pb = PS[b][:].bitcast(BF16)
                    S.op("pe", lambda e, kst=kst, pb=pb: e.transpose(out=pb[:, 0:128], in_=kst[:, 0:128], identity=ident_b[:]),
                         reads=[kk_, "ident_b"], writes=[("ps", b)])
                    act(lambda e, pb=pb, bb=bb: e.copy(out=kcT[:, bb % 2, :], in_=pb[:, 0:128]), [("ps", b)], ["kcT%d" % (bb % 2)])
                    for h in range(2):
                        hs = slice(64 * h, 64 * h + 64)
                        c0 = (bb * 2 + h) * 16
                        S.op("pe", lambda e, bb=bb, hs=hs, c0=c0: e.matmul(
                            PS[bsc][:, c0:c0 + 16].rearrange("p (t i) -> p t i", t=4), lhsT=kcT[hs, bb % 2, :],
                            rhs=qr[hs, :, 4 * bb:4 * bb + 4], start=True, stop=True),
                            reads=["kcT%d" % (bb % 2), "qr"], writes=[("ps", bsc)])
                    dve(lambda e, bb=bb, kst=kst: e.tensor_copy(out=RA[:, 128 * bb:128 * bb + 128], in_=kst[:, 128:256]), [kk_], RAK[0:4])
                act(lambda e: e.activation(out=ptc[:, 0, :], in_=PS[bsc][:], func=AF.Exp, scale=0.125), [("ps", bsc)], ["PT1"])
                held.discard(bsc)
                dve(lambda e: e.tensor_tensor(
                    out=ptc[:, 0, :].rearrange("p (a i) -> p a i", i=4), in0=ptc[:, 0, :].rearrange("p (a i) -> p a i", i=4),
                    in1=mc_b[:].rearrange("p (o i) -> p o i", o=1).to_broadcast([128, 128, 4]), op=ALU.mult), ["PT1", "mc_b"], ["PT1"])
                for h in range(2):
                    hs = slice(64 * h, 64 * h + 64)
                    for which, lw in ((0, None), (1, None)):
                        bo = bank()
                        lhs_new = vtc[0:NS, l, 0, hs] if which == 0 else ones_b[0:NS, hs]
                        S.op("pe", lambda e, bo=bo, hs=hs, h=h, lhs_new=lhs_new: e.matmul(
                            PS[bo][hs, 0:4 * NS], lhsT=lhs_new, rhs=ptn[0:NS, h, 0:4 * NS], start=True, stop=False),
                            reads=["vtc", "ones_b", "PT0"], writes=[("ps", bo)])
                        for bb in range(NSB):
                            c0 = (bb * 2 + h) * 16
                            lhs_c = RA[:, 128 * bb + 64 * h:128 * bb + 64 * h + 64] if which == 0 else ones_b[:, hs]
                            S.op("pe", lambda e, bo=bo, hs=hs, bb=bb, c0=c0, lhs_c=lhs_c: e.matmul(
                                PS[bo][hs, 0:4 * NS].rearrange("p (t c) -> p t c", t=4)[:, :, 4 * bb:4 * bb + 4], lhsT=lhs_c,
                                rhs=ptc[:, 0, c0:c0 + 16].rearrange("p (t i) -> p t i", t=4), start=False, stop=(bb == NSB - 1)),
                                reads=RAK[0:4] + ["ones_b", "PT1"], writes=[("ps", bo)])
                        if which == 0:
                            bnum = bo
                        else:
                            bden = bo
                    dd = dn[h]; kd = "dn%d" % h
                    dve(lambda e, bden=bden, dd=dd, hs=hs: e.tensor_tensor(
                        out=dd[hs, 0:4 * NS].rearrange("p (t c) -> p t c", t=4), in0=PS[bden][hs, 0:4 * NS].rearrange("p (t c) -> p t c", t=4),
                        in1=esink[hs, l, :].rearrange("p (t o) -> p t o", o=1).to_broadcast([64, 4, NS]), op=ALU.add),
                        [("ps", bden), "esink"], [kd])
                    dve(lambda e, dd=dd, hs=hs: e.reciprocal(out=dd[hs, 0:4 * NS], in_=dd[hs, 0:4 * NS]), [kd], [kd])
                    dve(lambda e, bnum=bnum, dd=dd, hs=hs: e.tensor_tensor(
                        out=oa[hs, :, 0:NS], in0=PS[bnum][hs, 0:4 * NS].rearrange("p (t c) -> p t c", t=4),
                        in1=dd[hs, 0:4 * NS].rearrange("p (t c) -> p t c", t=4), op=ALU.mult), [("ps", bnum), kd], ["oa"])
            ws = load_w(lambda w: v8(w), wvin[:, :, U_OFF:U_OFF + 512], ("wb_in", l), svin[:, :, U_OFF:U_OFF + 512])
            for t in range(4):
                b = proj_tile(N, v8(ws), 128 * t, ws)
                act(lambda e, b=b, t=t: e.copy(out=uT[:, t, 0:N], in_=PS[b][:, 0:N]), [("ps", b)], ["uT"])
            if sample:
                for (src_, dst_, stg, kst) in ((sre, hs_r, t1, "t1"), (sim, hs_i, t2, "t2")):
                    stv = stg[:].rearrange("p (a q) -> p a q", a=4)
                    s2 = src_[l].rearrange("b g p -> (b g) p").rearrange("(a q) p -> q a p", q=128)
                    for dup in range(2):
                        S.op("sp", lambda e, dup=dup, stv=stv, s2=s2: e.dma_start(out=stv[:, :, 64 * dup:64 * dup + 64], in_=s2),
                             writes=[kst], dma=kst)
                    for a in range(4):
                        b = bank()
                        S.op("pe", lambda e, a=a, b=b, stv=stv: e.transpose(out=PS[b][:, 0:128], in_=stv[:, a, :], identity=ident_f[:]),
                             reads=[kst, "ident_f"], writes=[("ps", b)])
                        for gl in range(2):
                            act(lambda e, a=a, b=b, gl=gl, dst_=dst_: e.copy(
                                out=dst_[64 * gl:64 * gl + 64, :, 4 * a:4 * a + 4],
                                in_=PS[b][64 * gl:64 * gl + 64, 0:128].rearrange("p (b tp gl) -> p gl tp b", gl=2, tp=16)[:, gl]),
                                [("ps", b)], ["hs"])
            for ct in range(4):
                by = ssm_group_sample(l, ct) if sample else ssm_group_prompt(l, ct, N, first_block)
                act(lambda e, by=by, ct=ct: e.activation(out=zT[:, ct, 0:N], in_=PS[by][:, 0:N], func=AF.Gelu), [("ps", by)], ["zT"])
            if sample:
                for (dram_, buf_, stg, kst) in ((hrs, hs_r, t1, "t1"), (his, hs_i, t2, "t2")):
                    for a in range(4):
                        act(lambda e, a=a, buf_=buf_: e.copy(out=qf[:, 0:64].rearrange("p (b tp) -> p b tp", b=4),
                                                             in_=buf_[:, :, 4 * a:4 * a + 4].rearrange("p tp b -> p b tp")), ["hs"], ["qf"])
                        b = bank()
                        S.op("pe", lambda e, b=b: e.transpose(out=PS[b][0:64, 0:128], in_=qf[:, 0:64], identity=ident_f[:]),
                             reads=["qf", "ident_f"], writes=[("ps", b)])
                        act(lambda e, a=a, b=b, stg=stg: e.copy(out=stg[0:64, 128 * a:128 * a + 128], in_=PS[b][0:64, 0:128]), [("ps", b)], [kst])
                    out_toks.append(S.op("sp", lambda e, dram_=dram_, stg=stg: e.dma_start(
                        out=dram_[l].rearrange("b (tp gl) p -> (b tp) (gl p)", gl=2).rearrange("(a r) c -> r a c", a=4),
                        in_=stg[0:64, :].rearrange("p (a c) -> p a c", a=4)), reads=[kst], dma="o_hs"))
            elif last_block:
                for gl in range(2):
                    sl = slice(64 * gl, 64 * gl + 64)
                    out_toks.append(S.op("sp", lambda e, gl=gl, sl=sl: e.dma_start(
                        out=hrp[l].rearrange("(tp gl) p -> gl p tp", gl=2)[gl], in_=car_r[sl, l, :], allow_slow_non_contiguous=True),
                        reads=["car"], dma="o_hp"))
                    out_toks.append(S.op("sp", lambda e, gl=gl, sl=sl: e.dma_start(
                        out=hip[l].rearrange("(tp gl) p -> gl p tp", gl=2)[gl], in_=car_i[sl, l, :], allow_slow_non_contiguous=True),
                        reads=["car"], dma="o_hp"))
            ws = load_w(lambda w: WS[w][:, 0:2048].rearrange("p (k c) -> p k c", k=4), (w_glu if fp else wb_glu)[l].rearrange("(k p) c -> p k c", p=128),
                        ("wb_glu", l), wb_glu[l].rearrange("(k p) c -> p k c", p=128))
            wg = WS[ws][:, 0:2048].rearrange("p (k c) -> p k c", k=4)
            for t in range(4):
                b = bank()
                for k in range(4):
                    S.op("pe", lambda e, b=b, k=k, t=t: e.matmul(PS[b][:, 0:N], lhsT=wg[:, k, 128 * t:128 * t + 128], rhs=zT[:, k, 0:N],
                                                                start=(k == 0), stop=(k == 3)), reads=["zT", ("ws", ws)], writes=[("ps", b)])
                s_ = sg[t % 2]; ks_ = "sg%d" % (t % 2)
                act(lambda e, b=b, s_=s_: e.activation(out=s_[:, 0:N], in_=PS[b][:, 0:N], func=AF.Sigmoid), [("ps", b)], [ks_])
                dve(lambda e, t=t, s_=s_: e.tensor_tensor(out=ob[:, t, 0:N], in0=zT[:, t, 0:N], in1=s_[:, 0:N], op=ALU.mult), ["zT", ks_], ["ob"])
            ws = load_w(lambda w: v8(w), wvin[:, :, MQ_OFF:MQ_OFF + 512], ("wb_in", l), svin[:, :, MQ_OFF:MQ_OFF + 512])
            for t in range(4):
                b = proj_tile(N, v8(ws), 128 * t, ws)
                headnorm_rope(N, b, l, gmq[:, l:l + 1], ones_b, 1.0 / 128, False, qmn[:, t, 0:N], ["qmn"])
            sc_m = 1.0 / math.sqrt(128.0)
            if not sample:
                for h in range(4):
                    pt = PT[h % 2]; kpt = "PT%d" % (h % 2)
                    for kt in range(2):
                        b = bank()
                        S.op("pe", lambda e, b=b, h=h, kt=kt: e.matmul(PS[b][:, 0:N], lhsT=MKT[:, l, h, 128 * kt:128 * kt + 128], rhs=qmn[:, h, 0:N],
                                                                      start=True, stop=True), reads=["MKT", "qmn"], writes=[("ps", b)])
                        act(lambda e, b=b, kt=kt, pt=pt: e.activation(out=pt[:, kt, 0:N], in_=PS[b][:, 0:N], func=AF.Exp, scale=sc_m),
                            [("ps", b)], [kpt])
                    bo = bank(); bd = bank()
                    for kt in range(2):
                        S.op("pe", lambda e, bo=bo, h=h, kt=kt, pt=pt: e.matmul(PS[bo][:, 0:N], lhsT=MV[:, l, kt, 128 * h:128 * h + 128],
                                                                               rhs=pt[:, kt, 0:N], start=(kt == 0), stop=(kt == 1)),
                             reads=["MV", kpt], writes=[("ps", bo)])
                    for kt in range(2):
                        S.op("pe", lambda e, bd=bd, kt=kt, pt=pt: e.matmul(PS[bd][:, 0:N], lhsT=ones_b[:], rhs=pt[:, kt, 0:N],
                                                                          start=(kt == 0), stop=(kt == 1)),
                             reads=["ones_b", kpt], writes=[("ps", bd)])
                    dd = dn[h % 2]; kd = "dn%d" % (h % 2)
                    dve(lambda e, bd=bd, dd=dd: e.reciprocal(out=dd[:, 0:N], in_=PS[bd][:, 0:N]), [("ps", bd)], [kd])
                    dve(lambda e, bo=bo, dd=dd, h=h: e.tensor_tensor(out=oc[:, h, 0:N], in0=PS[bo][:, 0:N], in1=dd[:, 0:N], op=ALU.mult),
                        [("ps", bo), kd], ["oc"])
            else:
                bsc = bank(); held.add(bsc)
                bo = bank(); held.add(bo)
                ptc = PT[1]
                for bb in range(NSB):
                    wsk = wslot()
                    kcb = WS[wsk][:, 0:1024].rearrange("p (a c) -> p a c", a=2)
                    vcb = WS[wsk][:, 1024:2048].rearrange("p (a c) -> p a c", a=2)
                    S.op("pool", lambda e, bb=bb, kcb=kcb: e.dma_start(out=kcb, in_=cmk[l, bb].rearrange("(a p) c -> p a c", p=128)),
                         writes=wk(wsk), dma=("ws", wsk))
                    S.op("pool", lambda e, bb=bb, vcb=vcb: e.dma_start(out=vcb, in_=cmv[l, bb].rearrange("(a p) c -> p a c", p=128)),
                         writes=wk(wsk), dma=("ws", wsk))
                    for h in range(4):
                        for kt in range(2):
                            b = bank()
                            pb = PS[b][:].bitcast(BF16)
                            o0 = 2048 + 256 * h + 128 * kt
                            S.op("pe", lambda e, pb=pb, kcb=kcb, h=h, kt=kt: e.transpose(out=pb[:, 0:128], in_=kcb[:, kt, 128 * h:128 * h + 128],
                                                                                        identity=ident_b[:]),
                                 reads=[("ws", wsk), "ident_b"], writes=[("ps", b)])
                            act(lambda e, pb=pb, wsk=wsk, o0=o0: e.copy(out=WS[wsk][:, o0:o0 + 128], in_=pb[:, 0:128]),
                                [("ps", b)], [("wsT", wsk)])
                    for h in range(4):
                        for kt in range(2):
                            c0 = ((bb * 4 + h) * 2 + kt) * 4
                            o0 = 2048 + 256 * h + 128 * kt
                            S.op("pe", lambda e, h=h, c0=c0, bb=bb, wsk=wsk, o0=o0: e.matmul(
                                PS[bsc][:, c0:c0 + 4], lhsT=WS[wsk][:, o0:o0 + 128],
                                rhs=qmn[:, h, 4 * bb:4 * bb + 4], start=True, stop=True),
                                reads=[("wsT", wsk), "qmn"], writes=[("ps", bsc)])
                    c0 = bb * 32
                    act(lambda e, c0=c0: e.activation(out=ptc[:, 0, c0:c0 + 32], in_=PS[bsc][:, c0:c0 + 32], func=AF.Exp, scale=sc_m),
                        [("ps", bsc)], ["PT1"])
                    for h in range(4):
                        for kt in range(2):
                            c1 = ((bb * 4 + h) * 2 + kt) * 4
                            S.op("pe", lambda e, h=h, kt=kt, c1=c1, bb=bb, vcb=vcb: e.matmul(
                                PS[bo][:, h * NS + 4 * bb:h * NS + 4 * bb + 4], lhsT=vcb[:, kt, 128 * h:128 * h + 128],
                                rhs=ptc[:, 0, c1:c1 + 4], start=(kt == 0), stop=(kt == 1)),
                                reads=[("ws", wsk), "PT1"], writes=[("ps", bo)])
                bd = bank()
                pv = ptc[:, 0, :].rearrange("p (b h kt i) -> p h kt b i", b=NSB, h=4, kt=2)
                for h in range(4):
                    for kt in range(2):
                        S.op("pe", lambda e, bd=bd, kt=kt, h=h: e.matmul(PS[bd][:, h * NS:(h + 1) * NS].rearrange("p (b i) -> p b i", i=4),
                                                                        lhsT=ones_b[:], rhs=pv[:, h, kt, :, :], start=(kt == 0), stop=(kt == 1)),
                             reads=["ones_b", "PT1"], writes=[("ps", bd)])
                dd = dn[0]
                dve(lambda e, bd=bd: e.reciprocal(out=dd[:, 0:4 * NS], in_=PS[bd][:, 0:4 * NS]), [("ps", bd)], ["dn0"])
                dve(lambda e, bo=bo: e.tensor_tensor(out=oc[:, :, 0:NS], in0=PS[bo][:, 0:4 * NS].rearrange("p (h c) -> p h c", h=4),
                                                     in1=dd[:, 0:4 * NS].rearrange("p (h c) -> p h c", h=4), op=ALU.mult),
                    [("ps", bo), "dn0"], ["oc"])
                held.discard(bsc); held.discard(bo)
            macc3 = macc.rearrange("p (m n) -> p m n", m=8)
            mgT3 = mgT.rearrange("p (m n) -> p m n", m=8)
            for n_, on_ in enumerate((oa, ob, oc)):
                okey = ("oa", "ob", "oc")[n_]
                wsb = wslot()
                wbv = WS[wsb][:].rearrange("p (k c) -> p k c", k=4)
                if n_ == 0:
                    for t in range(4):
                        for two in range(2):
                            r0 = (two * 4 + t) * 64
                            S.op(wq(), lambda e, t=t, two=two, r0=r0: e.dma_start(out=wbv[64 * two:64 * two + 64, t, :],
                                                                                 in_=(w_br if fp else wb_br)[l, 0, r0:r0 + 64, :]),
                                 reads=wr(("wb_br", l)), writes=wk(wsb), dma=("ws", wsb))
                            if fp:
                                scr_store(lambda e, t=t, two=two, r0=r0: e.dma_start(out=wb_br[l, 0, r0:r0 + 64, :],
                                                                                     in_=wbv[64 * two:64 * two + 64, t, :]),
                                          wsb, ("wb_br", l))
                else:
                    S.op(wq(), lambda e, n_=n_: e.dma_start(out=wbv, in_=(w_br if fp else wb_br)[l, n_].rearrange("(k p) c -> p k c", p=128)),
                         reads=wr(("wb_br", l)), writes=wk(wsb), dma=("ws", wsb))
                    if fp:
                        scr_store(lambda e, n_=n_: e.dma_start(out=wb_br[l, n_].rearrange("(k p) c -> p k c", p=128), in_=wbv),
                                  wsb, ("wb_br", l))
                for half in range(2):
                    wsg = load_w(lambda w: v8(w), wvin[:, :, G_OFF + n_ * 1024 + 512 * half:G_OFF + n_ * 1024 + 512 * half + 512], ("wb_in", l),
                                   svin[:, :, G_OFF + n_ * 1024 + 512 * half:G_OFF + n_ * 1024 + 512 * half + 512])
                    for mm_ in range(4):
                        m = 4 * half + mm_
                        bg = proj_tile(N, v8(wsg), 128 * mm_, wsg)
                        s_ = sg[m % 2]; ks_ = "sg%d" % (m % 2)
                        act(lambda e, bg=bg, s_=s_: e.activation(out=s_[:, 0:N], in_=PS[bg][:, 0:N], func=AF.Sigmoid), [("ps", bg)], [ks_])
                        bp = bank()
                        for k in range(4):
                            S.op("pe", lambda e, bp=bp, k=k, m=m, on_=on_: e.matmul(PS[bp][:, 0:N], lhsT=wbv[:, k, 128 * m:128 * m + 128],
                                                                                  rhs=on_[:, k, 0:N], start=(k == 0), stop=(k == 3)),
                                 reads=[okey, ("ws", wsb)], writes=[("ps", bp)])
                        if n_ == 0:
                            dve(lambda e, bp=bp, s_=s_, m=m: e.tensor_tensor(out=macc3[:, m, 0:N], in0=PS[bp][:, 0:N], in1=s_[:, 0:N], op=ALU.mult),
                                [("ps", bp), ks_], kmacc)
                        else:
                            dve(lambda e, bp=bp, s_=s_: e.tensor_tensor(out=t1[:, 0:N], in0=PS[bp][:, 0:N], in1=s_[:, 0:N], op=ALU.mult),
                                [("ps", bp), ks_], ["t1"])
                            if n_ == 1:
                                dve(lambda e, m=m: e.tensor_tensor(out=macc3[:, m, 0:N], in0=macc3[:, m, 0:N], in1=t1[:, 0:N], op=ALU.add),
                                    kmacc + ["t1"], kmacc)
                            else:
                                dve(lambda e, m=m: e.tensor_tensor(out=mgT3[:, m, 0:N], in0=macc3[:, m, 0:N], in1=t1[:, 0:N], op=ALU.add),
                                    kmacc + ["t1"], kmgT)
            for half in range(2):
                wso = load_w(lambda w: v8(w), (w_out if fp else wb_out)[l].rearrange("(k p) c -> p k c", p=128)[:, :, 512 * half:512 * half + 512], ("wb_out", l),
                             wb_out[l].rearrange("(k p) c -> p k c", p=128)[:, :, 512 * half:512 * half + 512])
                for mm_ in range(4):
                    m = 4 * half + mm_
                    b = bank()
                    for k in range(8):
                        S.op("pe", lambda e, b=b, k=k, mm_=mm_, wso=wso: e.matmul(PS[b][:, 0:N], lhsT=v8(wso)[:, k, 128 * mm_:128 * mm_ + 128],
                                                                                 rhs=mgT3[:, k, 0:N], start=(k == 0), stop=(k == 7)),
                             reads=kmgT + [("ws", wso)], writes=[("ps", b)])
                    dve(lambda e, b=b, m=m: e.tensor_tensor(out=xT[:, m, 0:N], in0=xT[:, m, 0:N], in1=PS[b][:, 0:N], op=ALU.add),
                        [("ps", b), "xT"], ["xT"])
            norm_block(N, gF[:, l, :])
            wvup = (w_up if fp else wb_up)[l].rearrange("(k p) c -> p k c", p=128)
            svup = wb_up[l].rearrange("(k p) c -> p k c", p=128)

            def actT(j):
                return RA[:, j * TB:(j + 1) * TB], [RAK[j]]

            for grp in range(6):
                nt = 4 if grp < 5 else 2
                wsg = load_w(lambda w: v8(w)[:, :, 0:128 * nt], wvup[:, :, 512 * grp:512 * grp + 128 * nt], ("wb_up", l),
                             svup[:, :, 512 * grp:512 * grp + 128 * nt])
                wsu = load_w(lambda w: v8(w)[:, :, 0:128 * nt], wvup[:, :, DFF + 512 * grp:DFF + 512 * grp + 128 * nt], ("wb_up", l),
                             svup[:, :, DFF + 512 * grp:DFF + 512 * grp + 128 * nt])
                for jj in range(nt):
                    j = 4 * grp + jj
                    bg = proj_tile(N, v8(wsg), 128 * jj, wsg)
                    bu = proj_tile(N, v8(wsu), 128 * jj, wsu)
                    s_ = sg[j % 2]; ks_ = "sg%d" % (j % 2)
                    act(lambda e, bg=bg, s_=s_: e.activation(out=s_[:, 0:N], in_=PS[bg][:, 0:N], func=AF.Silu), [("ps", bg)], [ks_])
                    av, ak = actT(j)
                    dve(lambda e, bu=bu, s_=s_, av=av: e.tensor_tensor(out=av[:, 0:N], in0=PS[bu][:, 0:N], in1=s_[:, 0:N], op=ALU.mult),
                        [("ps", bu), ks_], ak)
            wvdn = (w_dn if fp else wb_dn)[l].rearrange("(k p) c -> p k c", p=128)
            for q4 in range(4):
                wsl = []
                for hh in range(2):
                    w_ = wslot()
                    S.op(wq(), lambda e, w_=w_, hh=hh, q4=q4: e.dma_start(
                        out=WS[w_][:, 0:11 * 256].rearrange("p (k c) -> p k c", k=11), in_=wvdn[:, 11 * hh:11 * hh + 11, 256 * q4:256 * q4 + 256]),
                        reads=wr(("wb_dn", l)), writes=wk(w_), dma=("ws", w_))
                    if fp:
                        scr_store(lambda e, w_=w_, hh=hh, q4=q4: e.dma_start(
                            out=wb_dn[l].rearrange("(k p) c -> p k c", p=128)[:, 11 * hh:11 * hh + 11, 256 * q4:256 * q4 + 256],
                            in_=WS[w_][:, 0:11 * 256].rearrange("p (k c) -> p k c", k=11)),
                                  w_, ("wb_dn", l))
                    wsl.append(w_)
                for mm_ in range(2):
                    m = 2 * q4 + mm_
                    b = bank()
                    for j in range(22):
                        w_ = wsl[j // 11]
                        wv_ = WS[w_][:, 0:11 * 256].rearrange("p (k c) -> p k c", k=11)
                        av, ak = actT(j)
                        S.op("pe", lambda e, b=b, j=j, mm_=mm_, wv_=wv_, av=av: e.matmul(
                            PS[b][:, 0:N], lhsT=wv_[:, j % 11, 128 * mm_:128 * mm_ + 128], rhs=av[:, 0:N], start=(j == 0), stop=(j == 21)),
                            reads=ak + [("ws", w_)], writes=[("ps", b)])
                    dve(lambda e, b=b, m=m: e.tensor_tensor(out=xT[:, m, 0:N], in0=xT[:, m, 0:N], in1=PS[b][:, 0:N], op=ALU.add),
                        [("ps", b), "xT"], ["xT"])


        def run_block(blk, sample):
            N = NS if sample else TB
            nsub = max(1, N // 128)
            pn = min(N, 128)
            if sample:
                S.op("sp", lambda e: e.dma_start(out=xtok[0:NS, 0, :], in_=xs), writes=["xtok"], dma="xtok")
                S.op("sp", lambda e: e.dma_start(out=cosb[:, 0:NS], in_=c_cos_s), writes=["cosb"], dma="cosb")
                S.op("sp", lambda e: e.dma_start(out=sinb[:, 0:NS], in_=c_sin_s), writes=["sinb"], dma="sinb")
            else:
                t0 = blk * TB
                S.op("sp", lambda e: e.dma_start(out=xtok[:], in_=xp[t0:t0 + TB, :].rearrange("(s p) d -> p s d", p=128)),
                     writes=["xtok"], dma="xtok")
                S.op("sp", lambda e: e.dma_start(out=cosb[:], in_=c_cos[:, t0:t0 + TB]), writes=["cosb"], dma="cosb")
                S.op("sp", lambda e: e.dma_start(out=sinb[:], in_=c_sin[:, t0:t0 + TB]), writes=["sinb"], dma="sinb")
            for s in range(nsub):
                for k in range(8):
                    b = bank()
                    S.op("pe", lambda e, s=s, k=k, b=b: e.transpose(out=PS[b][:, 0:pn], in_=xtok[0:pn, s, 128 * k:128 * k + 128],
                                                                    identity=ident_f[0:pn, 0:pn]),
                         reads=["xtok", "ident_f"], writes=[("ps", b)])
                    act(lambda e, s=s, k=k, b=b: e.copy(out=xT[:, k, 128 * s:128 * s + pn], in_=PS[b][:, 0:pn]), [("ps", b)], ["xT"])
            for l in range(NL):
                layer_block(l, N, sample, blk)
            for s in range(nsub):
                for k in range(8):
                    b = bank()
                    S.op("pe", lambda e, s=s, k=k, b=b: e.transpose(out=PS[b][0:pn, 0:128], in_=xT[:, k, 128 * s:128 * s + pn],
                                                                    identity=ident_f[:]),
                         reads=["xT", "ident_f"], writes=[("ps", b)])
                    act(lambda e, s=s, k=k, b=b: e.copy(out=xtok[0:pn, s, 128 * k:128 * k + 128], in_=PS[b][0:pn, 0:128]),
                        [("ps", b)], ["xtok"])
            if sample:
                out_toks.append(S.op("sp", lambda e: e.dma_start(out=ys, in_=xtok[0:NS, 0, :]), reads=["xtok"], dma="o_y"))
            else:
                t0 = blk * TB
                out_toks.append(S.op("sp", lambda e: e.dma_start(out=yp[t0:t0 + TB, :].rearrange("(s p) d -> p s d", p=128), in_=xtok[:]),
                                     reads=["xtok"], dma="o_y"))

        for blk in range(NBLK):
            run_block(blk, False)
            if blk == 0:
                fpm[0] = False
        run_block(0, True)
        S.wait_all("sp", out_toks)
        with nc.allow_non_contiguous_dma(reason="small strided parameter / state transfers"):
            S.emit()
    return nc


_NC_CACHE = {}


def _consts():
    c = {}
    c["c_ident"] = np.eye(128, dtype=np.float32)
    blk = np.zeros((128, 128), np.float32)
    blk[:64, :64] = 1.0
    blk[64:, 64:] = 1.0
    c["c_blk"] = blk
    p = np.arange(128)
    lo = (p % 64) < 32
    partner = np.where(lo, p + 32, p - 32)
    perm = np.zeros((128, 128), np.float32)
    perm[partner, p] = 1.0
    c["c_perm"] = perm
    j = np.arange(128)[:, None]
    i = np.arange(128)[None, :]
    c["c_mprev"] = (j > i).astype(np.float32)
    c["c_mcur"] = (j <= i).astype(np.float32)
    half = 32
    inv = (np.float32(10000.0) ** (-(np.arange(half, dtype=np.float32) / np.float32(half)))).astype(np.float32)
    invp = inv[p % 32]
    sign = np.where(lo, -1.0, 1.0).astype(np.float32)

    def tables(pos):
        ang = (pos[None, :].astype(np.float32) * invp[:, None]).astype(np.float32)
        return np.cos(ang).astype(np.float32), (np.sin(ang) * sign[:, None]).astype(np.float32)

    c["c_cos"], c["c_sin"] = tables(np.arange(SEQ, dtype=np.float32))
    pos_s = np.float32(PAST) + np.tile(np.arange(4, dtype=np.float32), NSB)
    c["c_cos_s"], c["c_sin_s"] = tables(pos_s)
    r = np.arange(128)[:, None]
    ii = np.arange(4)[None, :]
    c["c_mc"] = (r > ii).astype(np.float32)
    kb, kj = np.divmod(np.arange(NS), 4)
    c["c_mnew"] = ((kb[:, None] == kb[None, :]) & (kj[:, None] <= kj[None, :])).astype(np.float32)
    c["c_rowm"] = (np.arange(128)[:, None] // 32 == np.arange(4)[None, :]).astype(np.float32)
    return c


def kernel(**inputs):
    f = lambda a: np.ascontiguousarray(np.asarray(a, dtype=np.float32))
    inp = {k: f(v) for k, v in inputs.items()}
    if "nc" not in _NC_CACHE:
        _NC_CACHE["nc"] = build_program()
    nc = _NC_CACHE["nc"]
    consts = _consts()
    wnames = ["attn_norm", "w_in", "q_norm", "k_norm", "attn_sinks", "ssm_a_re", "ssm_a_im", "ssm_log_dt", "ssm_b_re", "ssm_b_im",
              "ssm_c_re", "ssm_c_im", "ssm_d", "ssm_w_glu", "mem_norm", "w_mem_kv", "mem_q_norm", "mem_k_norm", "w_branch", "w_out",
              "ffn_norm", "w_ffn_up", "w_ffn_down"]
    in_maps = []
    for c in range(8):
        b0 = NSB * c
        m = {
            "xp": inp["x_prompt"][c % 4],
            "xs": inp["x_sample"][b0:b0 + NSB].reshape(NS, D),
            "csk": inp["cache_swa_k"][:, b0:b0 + NSB].reshape(NL, NSB, 128, 128),
            "csv": inp["cache_swa_v"][:, b0:b0 + NSB].reshape(NL, NSB, 128, 128),
            "sre": inp["state_ssm_re"][:, b0:b0 + NSB],
            "sim": inp["state_ssm_im"][:, b0:b0 + NSB],
            "cmk": inp["cache_mem_k"][:, b0:b0 + NSB].reshape(NL, NSB, 256, 512),
            "cmv": inp["cache_mem_v"][:, b0:b0 + NSB].reshape(NL, NSB, 256, 512),
            "memp": inp["mem_prompt"][c % 4],
        }
        for w in wnames:
            m[w] = inp[w]
        m.update(consts)
        in_maps.append({k: np.ascontiguousarray(v) for k, v in m.items()})
    res = run_bass_kernel_spmd(nc, in_maps, core_ids=list(range(8)))
    R = res.results
    cat = lambda name, cores: np.stack([R[c][name] for c in cores])
    y_p = cat("yp", range(4))
    y_s = np.concatenate([R[c]["ys"].reshape(NSB, 4, D) for c in range(8)], axis=0)
    per_l = lambda name, shape: np.stack([R[c][name] for c in range(4)], axis=1).reshape(shape)
    swa_k_p = per_l("kp", (NL, 4, 128, 2, 64))
    swa_v_p = per_l("vp", (NL, 4, 128, 2, 64))
    ssm_re_p = per_l("hrp", (NL, 4, 32, 64))
    ssm_im_p = per_l("hip", (NL, 4, 32, 64))
    mem_k_p = per_l("mkp", (NL, 4, 256, 4, 128))
    mem_v_p = per_l("mvp", (NL, 4, 256, 4, 128))
    cat_s = lambda name, shape: np.concatenate([R[c][name] for c in range(8)], axis=1).reshape(shape)
    swa_k_s = cat_s("ks", (NL, 128, 128, 2, 64))
    swa_v_s = cat_s("vs", (NL, 128, 128, 2, 64))
    ssm_re_s = cat_s("hrs", (NL, 128, 32, 64))
    ssm_im_s = cat_s("his", (NL, 128, 32, 64))
    outs = (y_p, y_s, swa_k_p, swa_v_p, ssm_re_p, ssm_im_p, mem_k_p, mem_v_p, swa_k_s, swa_v_s, ssm_re_s, ssm_im_s)
    return tuple(np.ascontiguousarray(o, dtype=np.float32) for o in outs)
```
